# Optimizing a Trainium2 kernel written in Bass

```python
import math
import jax, jax.numpy as jnp
from jax import lax
import numpy as np

D_MODEL = 1024
BATCH = 16
SEQ = 4096
DEPTH = 4

MIX_WIDTH = D_MODEL
GMLP_GROUPS = 4
GMLP_GROUP_DIM = 128
GMLP_CHUNK = 128
GMLP_WIDTH = GMLP_GROUPS * GMLP_GROUP_DIM
GDN_HEADS = 4
GDN_HEAD_DIM = 128
GDN_WIDTH = GDN_HEADS * GDN_HEAD_DIM
GDN_CHUNK = 64
GDN_CONV = 4
DIFF_HEADS = 4
DIFF_QK_DIM = 64
DIFF_V_DIM = 2 * DIFF_QK_DIM
DIFF_WIDTH = DIFF_HEADS * DIFF_V_DIM
DSA_HEADS = 4
DSA_HEAD_DIM = 128
DSA_WIDTH = DSA_HEADS * DSA_HEAD_DIM
IDX_HEADS = 8
IDX_DIM = 64
IDX_TOPK_MAX = 256
Q_BLOCK = 128
REL_BUCKETS = 32
REL_MAX_DIST = 128
N_BIAS_HEADS = DIFF_HEADS + DSA_HEADS
D_FF = 2816
FFN_CONV = 3
EPS = 1e-6

EVEN_SIZES = (GMLP_WIDTH, GMLP_WIDTH, 3 * GDN_WIDTH, GDN_HEADS, GDN_HEADS, GDN_WIDTH)
ODD_SIZES = (DIFF_HEADS * 2 * DIFF_QK_DIM, DIFF_HEADS * 2 * DIFF_QK_DIM, DIFF_WIDTH,
             DSA_WIDTH, DSA_HEAD_DIM, DSA_HEAD_DIM, IDX_HEADS * IDX_DIM, IDX_DIM, IDX_HEADS)
EVEN_IN = sum(EVEN_SIZES)
ODD_IN = sum(ODD_SIZES)
N_EVEN = (DEPTH + 1) // 2
N_ODD = DEPTH // 2

kernel_name = 'hybrid_gmlp_gdn_diff_dsa_trunk'


def split_cols(t, sizes):
    return jnp.split(t, [int(s) for s in np.cumsum(sizes)[:-1]], axis=-1)


def rms_norm(x, g=None):
    xf = x.astype(jnp.float32)
    y = xf * lax.rsqrt(jnp.mean(xf * xf, axis=-1, keepdims=True) + EPS)
    if g is not None:
        y = y * g.astype(jnp.float32)
    return y.astype(x.dtype)


def layer_norm_plain(x):
    xf = x.astype(jnp.float32)
    xc = xf - jnp.mean(xf, axis=-1, keepdims=True)
    return (xc * lax.rsqrt(jnp.mean(xc * xc, axis=-1, keepdims=True) + EPS)).astype(x.dtype)


def l2_normalize(x):
    return x * lax.rsqrt(jnp.sum(x * x, axis=-1, keepdims=True) + EPS)


def causal_depthwise_conv(x, w):
    width, L = w.shape[0], x.shape[1]
    xp = jnp.pad(x, ((0, 0), (width - 1, 0), (0, 0)))
    out = xp[:, 0:L] * w[0]
    for j in range(1, width):
        out = out + xp[:, j:j + L] * w[j]
    return out


def rel_bucket(dist):
    exact = REL_BUCKETS // 2
    n = jnp.maximum(dist, 0)
    nf = jnp.maximum(n, exact).astype(jnp.float32)
    far = exact + (jnp.log(nf / exact) / math.log(REL_MAX_DIST / exact) * (REL_BUCKETS - exact)).astype(jnp.int32)
    return jnp.where(n < exact, n, jnp.minimum(far, REL_BUCKETS - 1))


def chunked_gmlp(u, v, w_s, b_s):
    B, L, _ = u.shape
    n = L // GMLP_CHUNK
    shp = (B, n, GMLP_CHUNK, GMLP_GROUPS, GMLP_GROUP_DIM)
    v = layer_norm_plain(v.reshape(shp))
    causal = jnp.tril(jnp.ones((GMLP_CHUNK, GMLP_CHUNK), dtype=bool))
    w = jnp.where(causal[None], w_s, 0.0)
    mixed = jnp.einsum('gts,bnsgc->bntgc', w, v) + b_s.T[None, None, :, :, None]
    return (u.reshape(shp) * mixed).reshape(B, L, GMLP_WIDTH)


def gated_delta_rule(q, k, v, beta, g):
    B, L, H, dk = q.shape
    dv = v.shape[-1]
    C = GDN_CHUNK
    n = L // C
    q = l2_normalize(q) * (dk ** -0.5)
    k = l2_normalize(k)

    def chunks(t):
        return jnp.moveaxis(t.reshape((B, n, C, H) + t.shape[3:]), 3, 1)

    q, k, v, beta, g = chunks(q), chunks(k), chunks(v), chunks(beta), chunks(g)
    g = jnp.cumsum(g, axis=-1)
    k_beta = k * beta[..., None]
    v_beta = v * beta[..., None]
    incl = jnp.tril(jnp.ones((C, C), dtype=bool))
    strict = jnp.tril(jnp.ones((C, C), dtype=bool), -1)
    decay = jnp.exp(jnp.where(incl, g[..., :, None] - g[..., None, :], -jnp.inf))
    lower = jnp.where(strict, jnp.einsum('bhnid,bhnjd->bhnij', k_beta, k) * decay, 0.0)
    eye = jnp.eye(C, dtype=q.dtype)
    t_inv = lax.linalg.triangular_solve(lower + eye, jnp.broadcast_to(eye, lower.shape),
                                        left_side=True, lower=True, unit_diagonal=True)
    u = t_inv @ v_beta
    w = t_inv @ (k_beta * jnp.exp(g)[..., None])
    intra = jnp.where(incl, jnp.einsum('bhnid,bhnjd->bhnij', q, k) * decay, 0.0)

    def step(state, inp):
        q_c, k_c, u_c, w_c, g_c, a_c = inp
        v_new = u_c - w_c @ state
        out = (q_c * jnp.exp(g_c)[..., None]) @ state + a_c @ v_new
        g_last = g_c[..., -1:]
        state = state * jnp.exp(g_last)[..., None] + jnp.einsum(
            'bhck,bhcv->bhkv', k_c * jnp.exp(g_last - g_c)[..., None], v_new)
        return state, out

    xs = tuple(jnp.moveaxis(t, 2, 0) for t in (q, k, u, w, g, intra))
    state0 = jnp.zeros((B, H, dk, dv), q.dtype)
    _, out = lax.scan(step, state0, xs)
    return jnp.transpose(out, (1, 0, 3, 2, 4)).reshape(B, L, H, dv)


def even_mixer(h, w_in, w_out, w_s, b_s, conv_w, a_log, dt_bias, out_norm_g):
    B, L, _ = h.shape
    u, v, qkv, b_raw, a_raw, z = split_cols(h @ w_in, EVEN_SIZES)
    y_a = chunked_gmlp(jax.nn.gelu(u), jax.nn.gelu(v), w_s, b_s)
    qkv = jax.nn.silu(causal_depthwise_conv(qkv, conv_w))
    q, k, vv = [t.reshape(B, L, GDN_HEADS, GDN_HEAD_DIM).astype(jnp.float32) for t in jnp.split(qkv, 3, axis=-1)]
    beta = jax.nn.sigmoid(b_raw.astype(jnp.float32))
    g = -jnp.exp(a_log.astype(jnp.float32)) * jax.nn.softplus(a_raw.astype(jnp.float32) + dt_bias.astype(jnp.float32))
    o = gated_delta_rule(q, k, vv, beta, g).astype(h.dtype)
    o = rms_norm(o, out_norm_g) * jax.nn.silu(z.reshape(B, L, GDN_HEADS, GDN_HEAD_DIM))
    y = jnp.concatenate([y_a, o.reshape(B, L, GDN_WIDTH)], axis=-1)
    return y @ w_out


def diff_attention(q, k, v, lam, bias_tab, sub_g, lambda_init):
    B, L, H, _, d = q.shape
    nb = L // Q_BLOCK
    k_pos = jnp.arange(L, dtype=jnp.int32)
    q_blocks = jnp.moveaxis(q.reshape(B, nb, Q_BLOCK, H, 2, d), 1, 0)

    def one_block(args):
        q_blk, blk = args
        q_pos = blk * Q_BLOCK + jnp.arange(Q_BLOCK, dtype=jnp.int32)
        dist = q_pos[:, None] - k_pos[None, :]
        bias = jnp.moveaxis(bias_tab[rel_bucket(dist)], -1, 0).astype(jnp.float32)
        logits = jnp.einsum('bqhmd,bkhmd->bhmqk', q_blk, k).astype(jnp.float32) * (d ** -0.5) + bias[None, :, None]
        logits = jnp.where(dist >= 0, logits, -jnp.inf)
        probs = jax.nn.softmax(logits, axis=-1)
        weights = probs[:, :, 0] - lam * probs[:, :, 1]
        return jnp.einsum('bhqk,bkhe->bqhe', weights.astype(v.dtype), v)

    out = lax.map(one_block, (q_blocks, jnp.arange(nb, dtype=jnp.int32)))
    out = jnp.moveaxis(out, 0, 1).reshape(B, L, H, v.shape[-1])
    return rms_norm(out, sub_g) * (1.0 - lambda_init)


def dsa_attention(q, k, v, q_idx, k_idx, w_idx, bias_tab):
    B, L, H, d = q.shape
    nb = L // Q_BLOCK
    top_k = min(IDX_TOPK_MAX, L // 4)
    k_pos = jnp.arange(L, dtype=jnp.int32)
    gather = jax.vmap(lambda t, i: t[i])

    def blocks(t):
        return jnp.moveaxis(t.reshape((B, nb, Q_BLOCK) + t.shape[2:]), 1, 0)

    def one_block(args):
        q_blk, qi_blk, wi_blk, blk = args
        q_pos = blk * Q_BLOCK + jnp.arange(Q_BLOCK, dtype=jnp.int32)
        s = jnp.einsum('bqhd,bkd->bqhk', qi_blk, k_idx).astype(jnp.float32) * (IDX_DIM ** -0.5)
        index = jnp.einsum('bqh,bqhk->bqk', wi_blk.astype(jnp.float32), jax.nn.relu(s))
        index = jnp.where(k_pos[None, :] <= q_pos[:, None], index, -jnp.inf)
        _, sel = lax.top_k(index, top_k)
        valid = sel <= q_pos[None, :, None]
        k_sel = gather(k, sel)
        v_sel = gather(v, sel)
        bias = jnp.moveaxis(bias_tab[rel_bucket(q_pos[None, :, None] - sel)], -1, 1).astype(jnp.float32)
        logits = jnp.einsum('bqhd,bqkd->bhqk', q_blk, k_sel).astype(jnp.float32) * (d ** -0.5) + bias
        logits = jnp.where(valid[:, None], logits, -jnp.inf)
        probs = jax.nn.softmax(logits, axis=-1)
        return jnp.einsum('bhqk,bqkd->bqhd', probs.astype(v.dtype), v_sel)

    out = lax.map(one_block, (blocks(q), blocks(q_idx), blocks(w_idx), jnp.arange(nb, dtype=jnp.int32)))
    return jnp.moveaxis(out, 0, 1).reshape(B, L, H * d)


def odd_mixer(h, w_in, w_out, diff_q_g, diff_k_g, diff_lam, diff_sub_g, dsa_q_g, dsa_k_g, rel_bias, lambda_init):
    B, L, _ = h.shape
    dq, dk, dv, sq, sk, sv, iq, ik, iw = split_cols(h @ w_in, ODD_SIZES)
    dq = rms_norm(dq.reshape(B, L, DIFF_HEADS, 2, DIFF_QK_DIM), diff_q_g)
    dk = rms_norm(dk.reshape(B, L, DIFF_HEADS, 2, DIFF_QK_DIM), diff_k_g)
    dv = dv.reshape(B, L, DIFF_HEADS, DIFF_V_DIM)
    lf = diff_lam.astype(jnp.float32)
    lam = jnp.exp(jnp.sum(lf[0] * lf[1])) - jnp.exp(jnp.sum(lf[2] * lf[3])) + lambda_init
    y_c = diff_attention(dq, dk, dv, lam, rel_bias[:, :DIFF_HEADS], diff_sub_g, lambda_init).reshape(B, L, DIFF_WIDTH)
    sq = rms_norm(sq.reshape(B, L, DSA_HEADS, DSA_HEAD_DIM), dsa_q_g)
    sk = rms_norm(sk, dsa_k_g)
    iq = iq.reshape(B, L, IDX_HEADS, IDX_DIM)
    iw = iw * (IDX_HEADS ** -0.5)
    y_d = dsa_attention(sq, sk, sv, iq, ik, iw, rel_bias[:, DIFF_HEADS:])
    y = jnp.concatenate([y_c, y_d], axis=-1)
    return y @ w_out


def conv_ffn(h, w_up, conv_w, conv_b, w_down):
    up = causal_depthwise_conv(h @ w_up, conv_w) + conv_b
    gate, val = jnp.split(up, 2, axis=-1)
    return (jax.nn.silu(gate) * val) @ w_down


def setup_inputs(seed: int = 0) -> dict:
    key = jax.random.key(seed)
    ks = iter(jax.random.split(key, 32))

    def nrm(shape, scale):
        return jax.random.normal(next(ks), shape, jnp.float32) * scale

    def gain(shape):
        return 1.0 + nrm(shape, 0.02)

    out_scale = (2 * DEPTH) ** -0.5
    dt = jnp.exp(jax.random.uniform(next(ks), (N_EVEN, GDN_HEADS), jnp.float32, math.log(1e-3), math.log(1e-1)))
    a_init = jax.random.uniform(next(ks), (N_EVEN, GDN_HEADS), jnp.float32, 1.0, 16.0)
    return {
        'x': nrm((BATCH, SEQ, D_MODEL), 1.0),
        'rel_bias': nrm((REL_BUCKETS, N_BIAS_HEADS), 0.5),
        'mix_norm_g': gain((DEPTH, D_MODEL)),
        'ev_w_in': nrm((N_EVEN, D_MODEL, EVEN_IN), D_MODEL ** -0.5),
        'ev_w_out': nrm((N_EVEN, MIX_WIDTH, D_MODEL), MIX_WIDTH ** -0.5 * out_scale),
        'gmlp_w_s': nrm((N_EVEN, GMLP_GROUPS, GMLP_CHUNK, GMLP_CHUNK), GMLP_CHUNK ** -0.5),
        'gmlp_b_s': 1.0 + nrm((N_EVEN, GMLP_GROUPS, GMLP_CHUNK), 0.1),
        'gdn_conv_w': nrm((N_EVEN, GDN_CONV, 3 * GDN_WIDTH), GDN_CONV ** -0.5),
        'gdn_a_log': jnp.log(a_init),
        'gdn_dt_bias': dt + jnp.log(-jnp.expm1(-dt)),
        'gdn_norm_g': gain((N_EVEN, GDN_HEAD_DIM)),
        'od_w_in': nrm((N_ODD, D_MODEL, ODD_IN), D_MODEL ** -0.5),
        'od_w_out': nrm((N_ODD, MIX_WIDTH, D_MODEL), MIX_WIDTH ** -0.5 * out_scale),
        'diff_q_norm_g': gain((N_ODD, DIFF_QK_DIM)),
        'diff_k_norm_g': gain((N_ODD, DIFF_QK_DIM)),
        'diff_lambda': nrm((N_ODD, 4, DIFF_QK_DIM), 0.1),
        'diff_sub_norm_g': gain((N_ODD, DIFF_V_DIM)),
        'dsa_q_norm_g': gain((N_ODD, DSA_HEAD_DIM)),
        'dsa_k_norm_g': gain((N_ODD, DSA_HEAD_DIM)),
        'ffn_norm_g': gain((DEPTH, D_MODEL)),
        'ffn_w_up': nrm((DEPTH, D_MODEL, 2 * D_FF), D_MODEL ** -0.5),
        'ffn_conv_w': nrm((DEPTH, FFN_CONV, 2 * D_FF), FFN_CONV ** -0.5),
        'ffn_conv_b': nrm((DEPTH, 2 * D_FF), 0.01),
        'ffn_w_down': nrm((DEPTH, D_FF, D_MODEL), D_FF ** -0.5 * out_scale),
    }


def reference(x, rel_bias, mix_norm_g, ev_w_in, ev_w_out, gmlp_w_s, gmlp_b_s, gdn_conv_w, gdn_a_log,
              gdn_dt_bias, gdn_norm_g, od_w_in, od_w_out, diff_q_norm_g, diff_k_norm_g, diff_lambda,
              diff_sub_norm_g, dsa_q_norm_g, dsa_k_norm_g, ffn_norm_g, ffn_w_up, ffn_conv_w, ffn_conv_b,
              ffn_w_down):
    h = x
    for layer in range(DEPTH):
        j = layer // 2
        hn = rms_norm(h, mix_norm_g[layer])
        if layer % 2 == 0:
            h = h + even_mixer(hn, ev_w_in[j], ev_w_out[j], gmlp_w_s[j], gmlp_b_s[j], gdn_conv_w[j],
                               gdn_a_log[j], gdn_dt_bias[j], gdn_norm_g[j])
        else:
            lambda_init = 0.8 - 0.6 * math.exp(-0.3 * layer)
            h = h + odd_mixer(hn, od_w_in[j], od_w_out[j], diff_q_norm_g[j], diff_k_norm_g[j], diff_lambda[j],
                              diff_sub_norm_g[j], dsa_q_norm_g[j], dsa_k_norm_g[j], rel_bias, lambda_init)
        h = h + conv_ffn(rms_norm(h, ffn_norm_g[layer]), ffn_w_up[layer], ffn_conv_w[layer],
                         ffn_conv_b[layer], ffn_w_down[layer])
    return h
```

```python
import math
from contextlib import ExitStack

import numpy as np
import concourse.bass as bass
import concourse.mybir as mybir
from concourse.bass_utils import run_bass_kernel_spmd

F32 = mybir.dt.float32
BF16 = mybir.dt.bfloat16
AF = mybir.ActivationFunctionType
ALU = mybir.AluOpType
AX = mybir.AxisListType


class Res:
    __slots__ = ("name", "last_w", "reads", "dsem", "dcount", "excl")

    def __init__(self, name="r"):
        self.name = name
        self.excl = False
        self.last_w = None
        self.reads = []
        self.dsem = None
        self.dcount = 0


class V:
    __slots__ = ("ap", "res")

    def __init__(self, ap, res):
        self.ap = ap
        self.res = res

    def __getitem__(self, idx):
        return V(self.ap[idx], self.res)

    def r(self, res):
        return V(self.ap, res)


def _res_of(xs):
    out = []
    for x in xs:
        if x is None:
            continue
        out.append(x.res if isinstance(x, V) else x)
    return out


class Sched:
    ENGS = ("pe", "act", "dve", "pool", "sp")

    def __init__(self, nc, es):
        self.nc = nc
        self.es = es
        self.eng = {"pe": nc.tensor, "act": nc.scalar, "dve": nc.vector, "pool": nc.gpsimd, "sp": nc.sync}
        self.sem = {e: es.enter_context(nc.semaphore("sem_" + e)) for e in self.ENGS}
        self.cnt = {e: 0 for e in self.ENGS}
        self.known = {e: {} for e in self.ENGS}
        self.dpool = []
        self.dlive = []
        self.ndsem = 0
        self.nwaits = 0
        self.nops = 0
        self.phase_es = None
        self.uid = 0
        import os
        self.limit = int(os.environ["OPLIMIT"]) if "OPLIMIT" in os.environ else None

    def begin_phase(self):
        self.phase_es = ExitStack()

    def end_phase(self):
        self.barrier()
        for r in self.dlive:
            self.dpool.append((r.dsem, r.dcount))
            r.dsem = None
        self.dlive = []
        self.phase_es.close()
        self.phase_es = None

    def sb(self, name, shape, dt=F32):
        self.uid += 1
        name = "%s_u%d" % (name, self.uid)
        t = self.phase_es.enter_context(self.nc.sbuf_tensor(name, list(shape), dt))
        return V(t[:], Res(name))

    def ps(self, name, shape, dt=F32):
        self.uid += 1
        name = "%s_u%d" % (name, self.uid)
        t = self.phase_es.enter_context(self.nc.psum_tensor(name, list(shape), dt))
        rs = Res(name)
        rs.excl = True
        return V(t[:], rs)

    def _collect(self, eng, r, w, strict):
        waits = {}

        def add(ev, same_ok):
            if ev is None:
                return
            sem, val, src = ev
            if not strict and src == eng and (same_ok or eng == "pe"):
                return
            k = id(sem)
            if k not in waits or waits[k][1] < val:
                waits[k] = (sem, val)

        for res in r:
            add(res.last_w, False)
            if res.excl:
                for ev in res.reads:
                    add(ev, True)
        for res in w:
            add(res.last_w, True)
            for ev in res.reads:
                add(ev, True)
        return waits

    def _emit_waits(self, eng, waits):
        kn = self.known[eng]
        e = self.eng[eng]
        for k, (sem, val) in waits.items():
            if kn.get(k, 0) >= val:
                continue
            kn[k] = val
            e.wait_ge(sem, val)
            self.nwaits += 1

    def op(self, eng, fn, r=(), w=(), inc=True):
        if self.limit is not None and self.nops >= self.limit:
            return None
        r = _res_of(r)
        w = _res_of(w)
        self._emit_waits(eng, self._collect(eng, r, w, False))
        ins = fn(self.eng[eng])
        self.nops += 1
        if inc:
            self.cnt[eng] += 1
            ins.then_inc(self.sem[eng], 1)
            ev = (self.sem[eng], self.cnt[eng], eng)
        else:
            assert eng == "pe"
            ev = (self.sem[eng], self.cnt[eng] + 1, eng)
        for res in r:
            res.reads.append(ev)
        for res in w:
            res.last_w = ev
            res.reads = []
        return ins

    def dma(self, q, out, in_, sres=None, **kw):
        if self.limit is not None and self.nops >= self.limit:
            return None
        sr = sres if sres is not None else out.res
        if sr.dsem is None:
            if self.dpool:
                sr.dsem, sr.dcount = self.dpool.pop()
            else:
                sr.dsem = self.es.enter_context(self.nc.semaphore("dsem%d" % self.ndsem))
                sr.dcount = 0
                self.ndsem += 1
            self.dlive.append(sr)
        waits = self._collect(q, [in_.res], [out.res], True)
        k = id(sr.dsem)
        if sr.dcount > 0 and (k not in waits or waits[k][1] < sr.dcount):
            waits[k] = (sr.dsem, sr.dcount)
        self._emit_waits(q, waits)
        ins = self.eng[q].dma_start(out=out.ap, in_=in_.ap, **kw)
        sr.dcount += 16
        ins.then_inc(sr.dsem, 16)
        self.nops += 1
        ev = (sr.dsem, sr.dcount, "dma")
        in_.res.reads.append(ev)
        out.res.last_w = ev
        out.res.reads = []
        return ins

    def barrier(self):
        evs = [(self.sem[e], self.cnt[e]) for e in self.ENGS if self.cnt[e] > 0]
        evs += [(r.dsem, r.dcount) for r in self.dlive if r.dcount > 0]
        for e in self.ENGS:
            kn = self.known[e]
            for sem, val in evs:
                if sem is self.sem[e]:
                    continue
                if kn.get(id(sem), 0) >= val:
                    continue
                kn[id(sem)] = val
                self.eng[e].wait_ge(sem, val)
                self.nwaits += 1

    def mm(self, out, lhsT, rhs, start=True, stop=True, inc=None, **kw):
        if inc is None:
            inc = bool(stop)
        return self.op("pe", lambda e: e.matmul(out.ap, lhsT.ap, rhs.ap, start=start, stop=stop, **kw),
                       r=[lhsT, rhs], w=[out], inc=inc)

    def tr(self, out, in_, ident):
        return self.op("pe", lambda e: e.transpose(out.ap, in_.ap, ident.ap), r=[in_, ident], w=[out])

    def act(self, out, in_, func, bias=None, scale=None, accum=None, eng="act"):
        kw = {}
        rr = [in_]
        ww = [out]
        if bias is not None:
            if isinstance(bias, V):
                kw["bias"] = bias.ap
                rr.append(bias)
            else:
                kw["bias"] = bias
        if scale is not None:
            if isinstance(scale, V):
                kw["scale"] = scale.ap
                rr.append(scale)
            else:
                kw["scale"] = scale
        if accum is not None:
            kw["accum_out"] = accum.ap
            ww.append(accum)
        return self.op("act", lambda e: e.activation(out.ap, in_.ap, func, **kw), r=rr, w=ww)

    def tt(self, eng, out, a, b, op):
        return self.op(eng, lambda e: e.tensor_tensor(out.ap, a.ap, b.ap, op), r=[a, b], w=[out])

    def ts(self, eng, out, a, s1, op0, s2=None, op1=None, accum=None):
        rr = [a]
        ww = [out]
        a1 = s1
        a2 = s2
        if isinstance(s1, V):
            rr.append(s1)
            a1 = s1.ap
        if isinstance(s2, V):
            rr.append(s2)
            a2 = s2.ap
        kw = {}
        if op1 is not None:
            kw["op1"] = op1
        if accum is not None:
            kw["accum_out"] = accum.ap
            ww.append(accum)
        return self.op(eng, lambda e: e.tensor_scalar(out.ap, a.ap, a1, a2, op0, **kw), r=rr, w=ww)

    def stt(self, eng, out, a, s, b, op0, op1, accum=None):
        rr = [a, b]
        ww = [out]
        sc = s
        if isinstance(s, V):
            rr.append(s)
            sc = s.ap
        kw = {}
        if accum is not None:
            kw["accum_out"] = accum.ap
            ww.append(accum)
        return self.op(eng, lambda e: e.scalar_tensor_tensor(out.ap, a.ap, sc, b.ap, op0, op1, **kw), r=rr, w=ww)

    def copy(self, eng, out, in_):
        if eng == "act":
            return self.op("act", lambda e: e.copy(out.ap, in_.ap), r=[in_], w=[out])
        return self.op(eng, lambda e: e.tensor_copy(out.ap, in_.ap), r=[in_], w=[out])

    def memset(self, eng, out, val):
        return self.op(eng, lambda e: e.memset(out.ap, val), w=[out])

    def reduce(self, eng, out, in_, op, axis=None):
        ax = AX.X if axis is None else axis
        return self.op(eng, lambda e: e.tensor_reduce(out.ap, in_.ap, ax, op), r=[in_], w=[out])

    def asel(self, out, in_, pattern, cmp, fill, base=0, cm=0):
        return self.op("pool", lambda e: e.affine_select(out.ap, in_.ap, pattern=pattern, compare_op=cmp, fill=fill,
                                                         base=base, channel_multiplier=cm), r=[in_], w=[out])


D = 1024
KC = 8
DFF = 2816
FC = 22
EPS = 1e-6
EVEN_IN = 3080
ODD_IN = 2888
NEG = -30000.0


class DT:
    def __init__(self, ap, name):
        self.ap = ap
        self.name = name
        self.res = {}

    def v(self, key, ap):
        if key not in self.res:
            self.res[key] = Res("%s_%s" % (self.name, str(key)))
        return V(ap, self.res[key])


class Ctx:
    pass


def load_rows(S, C, dst, src_dt, key, ap, q="sp"):
    S.dma(q, dst, src_dt.v(key, ap), sres=dst.res)


def rms_to_hnT(S, C, xt, hnT_cols, ptr, hn, ssq, rstd):
    S.act(hn, xt, AF.Square, accum=ssq)
    S.act(rstd, ssq, AF.Sqrt, bias=C.eps, scale=1.0 / D)
    S.op("dve", lambda e: e.reciprocal(rstd.ap, rstd.ap), r=[rstd], w=[rstd])
    S.ts("pool", hn, xt, rstd, ALU.mult, 1.0, ALU.mult)
    for kc in range(KC):
        S.tr(ptr[:, kc * 128:(kc + 1) * 128], hn[:, kc * 128:(kc + 1) * 128], C.identb)
    S.copy("act", hnT_cols, V(ptr.ap.rearrange("p (k t) -> p k t", k=KC), ptr.res))


def load_weight_scaled(S, C, wsb, w_dram_ap, nrows_chunks, ncols, gcol, stg, colchunk, name):
    i = 0
    for rc in range(nrows_chunks):
        for c0 in range(0, ncols, colchunk):
            c1 = min(ncols, c0 + colchunk)
            st = stg[i % len(stg)]
            S.dma("sp", st[:, 0:c1 - c0], V(w_dram_ap[rc * 128:(rc + 1) * 128, c0:c1], C.wres), sres=st.res)
            eng = ("act", "dve", "pool")[i % 3]
            if gcol is None:
                S.copy(eng, wsb[:, rc, c0:c1], st[:, 0:c1 - c0])
            elif eng == "act":
                S.act(wsb[:, rc, c0:c1], st[:, 0:c1 - c0], AF.Copy, scale=gcol[:, rc:rc + 1])
            else:
                S.ts(eng, wsb[:, rc, c0:c1], st[:, 0:c1 - c0], gcol[:, rc:rc + 1], ALU.mult, 1.0, ALU.mult)
            i += 1


def ffn_phase(S, C, layer, src, dst):
    NT = 256
    NSUB = NT // 128
    S.begin_phase()
    wup = S.sb("wup", [128, KC, 2 * DFF], BF16)
    wdn = S.sb("wdn", [128, FC, D], BF16)
    gsb = S.sb("gsb", [128, KC])
    cw = S.sb("cw", [128, 2 * FC, 3])
    cb = S.sb("cb", [128, 2 * FC])
    stg = [S.sb("stg%d" % i, [128, 1408]) for i in range(2)]
    S.dma("sp", gsb, V(C.ffn_g[layer], C.wres), sres=gsb.res)
    S.dma("sp", cw, V(C.ffn_cw[layer], C.wres), sres=cw.res)
    S.dma("sp", cb, V(C.ffn_cb[layer], C.wres), sres=cb.res)
    load_weight_scaled(S, C, wup, C.ffn_wup[layer], KC, 2 * DFF, gsb, stg, 1408, "wup")
    load_weight_scaled(S, C, wdn, C.ffn_wdn[layer], FC, D, None, stg, 1024, "wdn")

    xs = [S.sb("x%d" % i, [128, D]) for i in range(4)]
    hn = [S.sb("hn%d" % i, [128, D], BF16) for i in range(2)]
    ssq = [S.sb("ssq%d" % i, [128, 1]) for i in range(2)]
    rstd = [S.sb("rstd%d" % i, [128, 1]) for i in range(2)]
    hnT = [S.sb("hnT%d" % i, [128, KC, NT], BF16) for i in range(2)]
    hT = S.sb("hT", [128, FC, NT], BF16)
    hal = S.sb("hal", [128, 2 * FC, 2])
    rr = [S.sb("rr%d" % i, [128, NT + 2]) for i in range(4)]
    acc = [S.sb("acc%d" % i, [128, NT]) for i in range(4)]
    sg = [S.sb("sg%d" % i, [128, NT]) for i in range(2)]
    ptr = [S.ps("ptr%d" % i, [128, D], BF16) for i in range(2)]
    pup = [S.ps("pup%d" % i, [128, 512])[:, 0:NT] for i in range(4)]
    pdn = [S.ps("pdn%d" % i, [128, 512]) for i in range(2)]

    gsub = 0
    ipair = 0
    for b in range(C.NSEQ):
        for t0 in range(0, C.L, NT):
            sti = t0 // NT
            hT_ = hnT[(b * (C.L // NT) + sti) % 2]
            xsl = []
            for sub in range(NSUB):
                r0 = b * C.L + t0 + sub * 128
                xt = xs[gsub % 4]
                xsl.append((xt, r0))
                load_rows(S, C, xt, src, (r0 // 128), src.ap[r0:r0 + 128, :])
                j = gsub % 2
                rms_to_hnT(S, C, xt, hT_[:, :, sub * 128:(sub + 1) * 128], ptr[j], hn[j], ssq[j], rstd[j])
                gsub += 1
            for c in range(FC):
                accs = []
                for which in range(2):
                    ch = c + which * FC
                    k = (ipair * 2 + which) % 4
                    ps = pup[k]
                    for kc in range(KC):
                        S.mm(ps, wup[:, kc, ch * 128:(ch + 1) * 128], hT_[:, kc, :], start=(kc == 0), stop=(kc == KC - 1))
                    r = rr[k]
                    a = acc[k]
                    if t0 == 0:
                        S.memset("pool", r[:, 0:2], 0.0)
                    else:
                        S.copy("pool", r[:, 0:2], hal[:, ch, :])
                    S.copy("act", r[:, 2:NT + 2], ps)
                    S.act(a, ps, AF.Identity, bias=cb[:, ch:ch + 1], scale=cw[:, ch, 2:3])
                    S.copy("pool", hal[:, ch, :], r[:, NT:NT + 2])
                    S.stt("dve", a, r[:, 1:NT + 1], cw[:, ch, 1:2], a, ALU.mult, ALU.add)
                    S.stt("dve", a, r[:, 0:NT], cw[:, ch, 0:1], a, ALU.mult, ALU.add)
                    accs.append(a)
                s = sg[ipair % 2]
                S.act(s, accs[0], AF.Silu)
                S.tt("dve", hT[:, c, :], s, accs[1], ALU.mult)
                ipair += 1
            for sub in range(NSUB):
                xt, r0 = xsl[sub]
                for nh in range(2):
                    ps2 = pdn[nh]
                    for fc in range(FC):
                        S.mm(ps2, hT[:, fc, sub * 128:(sub + 1) * 128], wdn[:, fc, nh * 512:(nh + 1) * 512],
                             start=(fc == 0), stop=(fc == FC - 1))
                    S.tt("dve", xt[:, nh * 512:(nh + 1) * 512], xt[:, nh * 512:(nh + 1) * 512], ps2, ALU.add)
                S.dma("sp", dst.v((r0 // 128), dst.ap[r0:r0 + 128, :]), xt, sres=xt.res)
    S.end_phase()


def flat(v):
    return V(v.ap.rearrange("p h j -> p (h j)"), v.res)


def v3(v, h=4):
    return V(v.ap.rearrange("p (h j) -> p h j", h=h), v.res)


def bc_last(v, n):
    H = v.ap.shape[1]
    return V(v.ap.unsqueeze(2).to_broadcast([128, H, n]), v.res)


def bc_mid(v, h):
    n = v.ap.shape[1]
    return V(v.ap.unsqueeze(1).to_broadcast([128, h, n]), v.res)


class Banks:
    def __init__(self, S, n=8):
        self.b = [S.ps("bank%d" % i, [128, 512]) for i in range(n)]
        self.i = 0

    def get(self):
        v = self.b[self.i % len(self.b)]
        self.i += 1
        return v


def bf(v):
    return V(v.ap.bitcast(BF16), v.res)


F32R = mybir.dt.float32r


def r32(v):
    return V(v.ap.bitcast(F32R), v.res)


def run_lanes(gens):
    gens = [g for g in gens if g is not None]
    while gens:
        for g in list(gens):
            try:
                next(g)
            except StopIteration:
                gens.remove(g)


def even_phase(S, C, j, src, dst):
    layer = 2 * j
    NT = 256
    NSUB = NT // 128
    H = 4
    S.begin_phase()
    PB = Banks(S)
    win = S.sb("win", [128, KC, EVEN_IN], BF16)
    wout = S.sb("wout", [128, KC, D], BF16)
    gsb = S.sb("gsb", [128, KC])
    stg = [S.sb("stg%d" % i, [128, 1540]) for i in range(2)]
    S.dma("sp", gsb, V(C.mix_g[layer], C.wres), sres=gsb.res)
    load_weight_scaled(S, C, win, C.ev_win[j], KC, EVEN_IN, gsb, stg, 1540, "win")
    load_weight_scaled(S, C, wout, C.ev_wout[j], KC, D, None, stg, 1024, "wout")
    wTm = S.sb("wTm", [128, H, 128], BF16)
    wTf = S.sb("wTf", [128, H, 128])
    brow = S.sb("brow", [1, 512])
    cw = S.sb("gcw", [128, 12, 4])
    alog = S.sb("alog", [128, 4])
    dtb = S.sb("dtb", [128, 4])
    nega = S.sb("nega", [128, 4])
    gng = S.sb("gng", [128, 128])
    S.dma("sp", wTf, V(C.gm_wT[j], C.wres), sres=wTf.res)
    S.dma("sp", brow, V(C.gm_b[j], C.wres), sres=brow.res)
    S.dma("sp", cw, V(C.gdn_cw[j], C.wres), sres=cw.res)
    S.dma("sp", alog, V(C.gdn_alog[j].partition_broadcast(128), C.wres), sres=alog.res)
    S.dma("sp", dtb, V(C.gdn_dtb[j].partition_broadcast(128), C.wres), sres=dtb.res)
    S.dma("sp", gng, V(C.gdn_ng[j].partition_broadcast(128), C.wres), sres=gng.res)
    ones = S.sb("ones", [128, 128])
    tri = S.sb("tri", [128, 128])
    ntri = S.sb("ntri", [128, 128])
    strict = S.sb("strict", [128, 128])
    incl = S.sb("incl", [128, 128])
    S.memset("pool", ones, 1.0)
    S.asel(tri, ones, [[1, 128]], ALU.is_ge, 0.0, base=0, cm=-1)
    S.ts("pool", ntri, tri, -1.0, ALU.mult, 1.0, ALU.mult)
    S.asel(strict, ones, [[-1, 128]], ALU.is_ge, 0.0, base=-1, cm=1)
    S.asel(incl, ones, [[-1, 128]], ALU.is_ge, 0.0, base=0, cm=1)
    S.tt("pool", wTm, wTf, bc_mid(tri, H), ALU.mult)
    S.act(nega, alog, AF.Exp)
    S.ts("pool", nega, nega, -1.0, ALU.mult, 1.0, ALU.mult)

    xs = [S.sb("x%d" % i, [128, D]) for i in range(4)]
    hn = [S.sb("hn%d" % i, [128, D], BF16) for i in range(2)]
    ssq = [S.sb("ssq%d" % i, [128, 1]) for i in range(2)]
    rstd = [S.sb("rstd%d" % i, [128, 1]) for i in range(2)]
    hnT = [S.sb("hnT%d" % i, [128, KC, NT], BF16) for i in range(2)]
    uTs = [S.sb("uT%d" % i, [128, H, NT], BF16) for i in range(2)]
    yT = S.sb("yT", [128, KC, NT], BF16)
    qTs = [S.sb("qT%d" % i, [128, H, NT], BF16) for i in range(2)]
    kTs = [S.sb("kT%d" % i, [128, H, NT], BF16) for i in range(2)]
    vTs = [S.sb("vT%d" % i, [128, H, NT], BF16) for i in range(2)]
    XSL = [None, None]
    hal = S.sb("hal", [128, 12, 3])
    rr = [S.sb("rr%d" % i, [128, NT + 3]) for i in range(2)]
    ca = [S.sb("ca%d" % i, [128, NT]) for i in range(2)]
    qs = [S.sb("qs%d" % i, [128, NT]) for i in range(2)]
    sq = S.sb("sq", [128, NT])
    rn = S.sb("rn", [128, NT])
    Sst = S.sb("Sst", [128, H, 128])
    Sr = S.sb("Sr", [128, H, 128])

    def F(name, dt=F32):
        return S.sb(name, [128, H, 128], dt)

    ktok, vtok, vg, sqv, zs, G1, G2, E, nbm, usb, osq, on, gz, t1 = [
        F(n) for n in ("ktok", "vtok", "vg", "sqv", "zs", "G1", "G2", "E", "nbm", "usb", "osq", "on", "gz", "t1")]
    Lp0, Lp1, intra, intraT, U0, U1, TT, vb, kbg, wTs, qgT, kd, vnew = [
        F(n) for n in ("Lp0", "Lp1", "intra", "intraT", "U0", "U1", "TT", "vb", "kbg", "wTs", "qgT", "kd", "vnew")]
    vn = F("vn", BF16)
    onb = F("onb", BF16)
    sm = {n: S.sb(n, [128, 4]) for n in ("vsum", "vvar", "vrs", "beta", "nbeta", "xa", "xe", "xm", "sp", "g", "bk", "edl", "oss", "ors")}
    gcs = S.sb("gcs", [128, 8])
    egs = S.sb("egs", [128, 8])

    st = {"gsub": 0, "ich": 0}

    def proj_task(b, t0, k):
        hT_ = hnT[k]
        uT, qT, kT, vT = uTs[k], qTs[k], kTs[k], vTs[k]
        xsl = []
        for sub in range(NSUB):
            r0 = b * C.L + t0 + sub * 128
            xt = xs[st["gsub"] % 4]
            xsl.append((xt, r0))
            load_rows(S, C, xt, src, (r0 // 128), src.ap[r0:r0 + 128, :])
            jj = st["gsub"] % 2
            pt = PB.get()
            rms_to_hnT(S, C, xt, hT_[:, :, sub * 128:(sub + 1) * 128], bf(pt), hn[jj], ssq[jj], rstd[jj])
            st["gsub"] += 1
        for c in range(H):
            ps = PB.get()
            for kc in range(KC):
                S.mm(ps[:, 0:NT], win[:, kc, c * 128:(c + 1) * 128], hT_[:, kc, :], start=(kc == 0), stop=(kc == KC - 1))
            S.act(uT[:, c, :], ps[:, 0:NT], AF.Gelu_apprx_tanh)
            yield
        for c in range(12):
            ps = PB.get()
            for kc in range(KC):
                S.mm(ps[:, 0:NT], win[:, kc, 1024 + c * 128:1024 + (c + 1) * 128], hT_[:, kc, :], start=(kc == 0), stop=(kc == KC - 1))
            r = rr[st["ich"] % 2]
            a = ca[st["ich"] % 2]
            if t0 == 0:
                S.memset("pool", r[:, 0:3], 0.0)
            else:
                S.copy("pool", r[:, 0:3], hal[:, c, :])
            S.copy("act", r[:, 3:NT + 3], ps[:, 0:NT])
            S.act(a, ps[:, 0:NT], AF.Copy, scale=cw[:, c, 3:4])
            S.copy("pool", hal[:, c, :], r[:, NT:NT + 3])
            for tap in (2, 1, 0):
                S.stt("dve", a, r[:, tap:tap + NT], cw[:, c, tap:tap + 1], a, ALU.mult, ALU.add)
            hh = c % 4
            if c >= 8:
                S.act(vT[:, hh, :], a, AF.Silu)
            else:
                q_ = qs[st["ich"] % 2]
                S.act(q_, a, AF.Silu)
                S.act(sq, q_, AF.Square)
                pss = PB.get()
                S.mm(pss[:, 0:NT], ones, sq)
                S.act(rn, pss[:, 0:NT], AF.Sqrt, bias=C.eps, scale=1.0)
                S.op("dve", lambda e: e.reciprocal(rn.ap, rn.ap), r=[rn], w=[rn])
                if c < 4:
                    S.stt("dve", qT[:, hh, :], q_, 128.0 ** -0.5, rn, ALU.mult, ALU.mult)
                else:
                    S.tt("dve", kT[:, hh, :], q_, rn, ALU.mult)
            st["ich"] += 1
            yield
        XSL[k] = xsl
        yield

    def chunk_task(b, t0, k):
        hT_ = hnT[k]
        uT, qT, kT, vT = uTs[k], qTs[k], kTs[k], vTs[k]
        xsl = XSL[k]
        if t0 == 0:
            S.memset("pool", Sst, 0.0)
            S.copy("pool", r32(Sr), Sst)
        for sub in range(NSUB):
            cs = slice(sub * 128, (sub + 1) * 128)
            xt, r0 = xsl[sub]
            pv = PB.get()
            pz = PB.get()
            pba = PB.get()
            for kc in range(KC):
                S.mm(pv, hT_[:, kc, cs], win[:, kc, 512:1024], start=(kc == 0), stop=(kc == KC - 1))
            for kc in range(KC):
                S.mm(pz, hT_[:, kc, cs], win[:, kc, 2568:3080], start=(kc == 0), stop=(kc == KC - 1))
            for kc in range(KC):
                S.mm(pba[:, 0:8], hT_[:, kc, cs], win[:, kc, 2560:2568], start=(kc == 0), stop=(kc == KC - 1))
            S.act(flat(vg), pv, AF.Gelu_apprx_tanh)
            S.act(flat(zs), pz, AF.Silu)
            S.act(sm["beta"], pba[:, 0:4], AF.Sigmoid)
            S.tt("dve", sm["xa"], pba[:, 4:8], dtb, ALU.add)
            S.reduce("dve", sm["vsum"], vg, ALU.add)
            S.stt("dve", vg, bc_last(sm["vsum"], 128), -1.0 / 128, vg, ALU.mult, ALU.add)
            S.tt("pool", sqv, vg, vg, ALU.mult)
            S.reduce("dve", sm["vvar"], sqv, ALU.add)
            S.act(sm["vrs"], sm["vvar"], AF.Sqrt, bias=C.eps, scale=1.0 / 128)
            S.op("dve", lambda e: e.reciprocal(sm["vrs"].ap, sm["vrs"].ap), r=[sm["vrs"]], w=[sm["vrs"]])
            S.tt("dve", vn, vg, bc_last(sm["vrs"], 128), ALU.mult)
            yield
            pm = PB.get()
            for gI in range(H):
                S.mm(pm[:, gI * 128:(gI + 1) * 128], vn[:, gI, :], wTm[:, gI, :], start=True, stop=False)
                S.mm(pm[:, gI * 128:(gI + 1) * 128], ones[0:1, :], brow[0:1, gI * 128:(gI + 1) * 128], start=False, stop=True)
            S.tt("dve", yT[:, 0:4, cs], v3(pm), uT[:, :, cs], ALU.mult)
            yield
            S.ts("dve", sm["nbeta"], sm["beta"], -1.0, ALU.mult)
            S.ts("dve", sm["xm"], sm["xa"], 30.0, ALU.min)
            S.act(sm["xe"], sm["xm"], AF.Exp)
            S.act(sm["sp"], sm["xe"], AF.Ln, bias=1.0, scale=1.0)
            S.ts("dve", sm["xm"], sm["xa"], -30.0, ALU.add, 0.0, ALU.max)
            S.tt("dve", sm["sp"], sm["sp"], sm["xm"], ALU.add)
            S.tt("dve", sm["g"], sm["sp"], nega, ALU.mult)
            g = sm["g"]
            pg = PB.get()
            S.mm(pg[:, 0:4], tri, g)
            S.mm(pg[:, 4:8], ones, g)
            S.copy("dve", gcs, pg[:, 0:8])
            S.act(egs, gcs, AF.Exp)
            S.tt("dve", sm["edl"], gcs[:, 4:8], gcs[:, 0:4], ALU.subtract)
            S.act(sm["edl"], sm["edl"], AF.Exp)
            S.tt("dve", sm["bk"], sm["beta"], egs[:, 0:4], ALU.mult)
            yield
            pk = PB.get()
            for h in range(H):
                S.tr(bf(pk)[:, h * 128:(h + 1) * 128], kT[:, h, cs], C.identb)
            S.copy("act", flat(ktok), bf(pk)[:, 0:512])
            pk = PB.get()
            for h in range(H):
                S.tr(bf(pk)[:, h * 128:(h + 1) * 128], vT[:, h, cs], C.identb)
            S.copy("act", flat(vtok), bf(pk)[:, 0:512])
            yield
            S.copy("pool", G1, bc_last(g, 128))
            S.tt("pool", G2, bc_last(g, 128), bc_mid(ntri, H), ALU.mult)
            pd = PB.get()
            for h in range(H):
                S.mm(pd[:, h * 128:(h + 1) * 128], tri, G1[:, h, :], start=True, stop=False)
                S.mm(pd[:, h * 128:(h + 1) * 128], ones, G2[:, h, :], start=False, stop=True)
            S.ts("dve", flat(E), pd, 0.0, ALU.min)
            S.act(flat(E), flat(E), AF.Exp)
            yield
            pkk = PB.get()
            pqk = PB.get()
            for h in range(H):
                S.mm(pkk[:, h * 128:(h + 1) * 128], kT[:, h, cs], kT[:, h, cs])
            for h in range(H):
                S.mm(pqk[:, h * 128:(h + 1) * 128], qT[:, h, cs], kT[:, h, cs])
            S.tt("pool", nbm, bc_mid(strict, H), bc_last(sm["nbeta"], 128), ALU.mult)
            S.tt("dve", flat(t1), pkk, flat(E), ALU.mult)
            S.tt("pool", r32(Lp0), t1, nbm, ALU.mult)
            S.tt("dve", flat(osq), pqk, flat(E), ALU.mult)
            S.tt("pool", intra, osq, bc_mid(incl, H), ALU.mult)
            yield
            pu = PB.get()
            for h in range(H):
                S.tr(pu[:, h * 128:(h + 1) * 128], Lp0[:, h, :], C.identf)
            S.copy("act", r32(flat(U0)), pu)
            S.tt("dve", r32(TT), v3(pu), bc_mid(C.identf, H), ALU.add)
            pi = PB.get()
            for h in range(H):
                S.tr(pi[:, h * 128:(h + 1) * 128], intra[:, h, :], C.identf)
            S.copy("act", r32(flat(intraT)), pi)
            yield
            Us = [U0, U1]
            Ls = [Lp0, Lp1]
            for k in range(1, 7):
                Uo, Un = Us[(k - 1) % 2], Us[k % 2]
                Lo, Ln_ = Ls[(k - 1) % 2], Ls[k % 2]
                if k <= 5:
                    p1 = PB.get()
                    for h in range(H):
                        S.mm(p1[:, h * 128:(h + 1) * 128], r32(Lo[:, h, :]), r32(Uo[:, h, :]))
                p2 = PB.get()
                for h in range(H):
                    S.mm(p2[:, h * 128:(h + 1) * 128], r32(Uo[:, h, :]), r32(Lo[:, h, :]))
                if k <= 5:
                    S.copy("act", r32(flat(Un)), p1)
                S.copy("dve", r32(flat(Ln_)), p2)
                p3 = PB.get()
                for h in range(H):
                    S.mm(p3[:, h * 128:(h + 1) * 128], r32(Ln_[:, h, :]), r32(TT[:, h, :]))
                S.tt("dve", r32(flat(TT)), flat(TT), p3, ALU.add)
            yield
            yield
            S.tt("pool", r32(vb), vtok, bc_last(sm["beta"], 128), ALU.mult)
            S.tt("pool", r32(kbg), ktok, bc_last(sm["bk"], 128), ALU.mult)
            S.tt("pool", r32(kd), ktok, bc_last(sm["edl"], 128), ALU.mult)
            pU = PB.get()
            for h in range(H):
                S.mm(pU[:, h * 128:(h + 1) * 128], r32(TT[:, h, :]), r32(vb[:, h, :]))
            S.copy("act", flat(usb), pU)
            pW = PB.get()
            for h in range(H):
                S.mm(pW[:, h * 128:(h + 1) * 128], r32(kbg[:, h, :]), r32(TT[:, h, :]))
            S.copy("act", r32(flat(wTs)), pW)
            yield
            S.tt("pool", G1, bc_mid(C.identf, H), bc_last(egs[:, 0:4], 128), ALU.mult)
            pe_ = PB.get()
            S.mm(pe_, ones, flat(G1))
            S.tt("dve", r32(qgT), qT[:, :, cs], v3(pe_), ALU.mult)
            yield
            pws = PB.get()
            for h in range(H):
                S.mm(pws[:, h * 128:(h + 1) * 128], r32(wTs[:, h, :]), r32(Sr[:, h, :]))
            S.tt("dve", r32(flat(vnew)), flat(usb), pws, ALU.subtract)
            po = PB.get()
            for h in range(H):
                S.mm(po[:, h * 128:(h + 1) * 128], r32(qgT[:, h, :]), r32(Sr[:, h, :]), start=True, stop=False)
                S.mm(po[:, h * 128:(h + 1) * 128], r32(intraT[:, h, :]), r32(vnew[:, h, :]), start=False, stop=True)
            pS = PB.get()
            for h in range(H):
                S.mm(pS[:, h * 128:(h + 1) * 128], r32(kd[:, h, :]), r32(vnew[:, h, :]))
            S.tt("dve", Sst, Sst, bc_last(egs[:, 4:8], 128), ALU.mult)
            S.tt("dve", flat(Sst), flat(Sst), pS, ALU.add)
            S.copy("act", r32(Sr), Sst)
            yield
            S.act(flat(osq), po, AF.Square)
            S.reduce("dve", sm["oss"], osq, ALU.add)
            S.act(sm["ors"], sm["oss"], AF.Sqrt, bias=C.eps, scale=1.0 / 128)
            S.op("dve", lambda e: e.reciprocal(sm["ors"].ap, sm["ors"].ap), r=[sm["ors"]], w=[sm["ors"]])
            S.tt("pool", gz, zs, bc_mid(gng, H), ALU.mult)
            S.tt("dve", on, v3(po), bc_last(sm["ors"], 128), ALU.mult)
            S.tt("dve", onb, on, gz, ALU.mult)
            pt2 = PB.get()
            for h in range(H):
                S.tr(bf(pt2)[:, h * 128:(h + 1) * 128], onb[:, h, :], C.identb)
            S.copy("act", yT[:, 4:8, cs], v3(bf(pt2)[:, 0:512]))
            yield
            for nh in range(2):
                pso = PB.get()
                for kc in range(KC):
                    S.mm(pso, yT[:, kc, cs], wout[:, kc, nh * 512:(nh + 1) * 512], start=(kc == 0), stop=(kc == KC - 1))
                S.tt("dve", xt[:, nh * 512:(nh + 1) * 512], xt[:, nh * 512:(nh + 1) * 512], pso, ALU.add)
            S.dma("sp", dst.v((r0 // 128), dst.ap[r0:r0 + 128, :]), xt, sres=xt.res)

    tiles = [(b, t0) for b in range(C.NSEQ) for t0 in range(0, C.L, NT)]
    prev = None
    for i, (b, t0) in enumerate(tiles):
        run_lanes([proj_task(b, t0, i % 2), chunk_task(*prev) if prev is not None else None])
        prev = (b, t0, i % 2)
    run_lanes([chunk_task(*prev)])
    S.end_phase()


OD_EXT = 3016
NQK = 18


def odd_phase(S, C, j, src, dst):
    odd_proj_phase(S, C, j, src)
    odd_attn_phase(S, C, j, src, dst)


def odd_proj_phase(S, C, j, src):
    layer = 2 * j + 1
    NT = 256
    NSUB = NT // 128
    S.begin_phase()
    PB = Banks(S)
    win = S.sb("win", [128, KC, OD_EXT], BF16)
    gsb = S.sb("gsb", [128, KC])
    stg = [S.sb("stg%d" % i, [128, 1508]) for i in range(2)]
    S.dma("sp", gsb, V(C.mix_g[layer], C.wres), sres=gsb.res)
    load_weight_scaled(S, C, win, C.od_win[j], KC, OD_EXT, gsb, stg, 1508, "win")
    gn = S.sb("gn", [128, 4])
    S.dma("sp", gn, V(C.od_gn[j], C.wres), sres=gn.res)
    S.ts("pool", gn[:, 0:1], gn[:, 0:1], 64.0 ** -0.5, ALU.mult, 1.0, ALU.mult)
    S.ts("pool", gn[:, 2:3], gn[:, 2:3], 128.0 ** -0.5, ALU.mult, 1.0, ALU.mult)
    ones = S.sb("ones", [128, 128])
    bd64 = S.sb("bd64", [128, 128])
    S.memset("pool", ones, 1.0)
    S.memset("pool", bd64, 0.0)
    S.memset("pool", bd64[0:64, 0:64], 1.0)
    S.memset("pool", bd64[64:128, 64:128], 1.0)

    xs = [S.sb("x%d" % i, [128, D]) for i in range(2)]
    hn = [S.sb("hn%d" % i, [128, D], BF16) for i in range(2)]
    ssq = [S.sb("ssq%d" % i, [128, 1]) for i in range(2)]
    rstd = [S.sb("rstd%d" % i, [128, 1]) for i in range(2)]
    hnT = [S.sb("hnT%d" % i, [128, KC, NT], BF16) for i in range(2)]
    oT = [S.sb("oT%d" % i, [128, NQK, NT], BF16) for i in range(2)]
    sqb = [S.sb("sqb%d" % i, [128, NT]) for i in range(2)]
    rn = [S.sb("rn%d" % i, [128, NT]) for i in range(2)]
    tokb = [S.sb("tokb%d" % i, [128, 640], BF16) for i in range(2)]
    iwt = [S.sb("iwt%d" % i, [128, 8]) for i in range(2)]

    chunks = []
    for c in range(4):
        chunks.append((c * 128, "n64", 0))
    for c in range(4):
        chunks.append((512 + c * 128, "n64", 1))
    for c in range(4):
        chunks.append((1536 + c * 128, "n128", 2))
    chunks.append((2048, "n128", 3))
    for c in range(4):
        chunks.append((2304 + c * 128, "scale", None))
    chunks.append((2888, "copy", None))

    gsub = 0
    ist = 0
    inorm = 0
    for b in range(C.NSEQ):
        for t0 in range(0, C.L, NT):
            hT_ = hnT[ist % 2]
            o_ = oT[ist % 2]
            for sub in range(NSUB):
                r0 = b * C.L + t0 + sub * 128
                xt = xs[gsub % 2]
                load_rows(S, C, xt, src, (r0 // 128), src.ap[r0:r0 + 128, :])
                jj = gsub % 2
                pt = PB.get()
                rms_to_hnT(S, C, xt, hT_[:, :, sub * 128:(sub + 1) * 128], bf(pt), hn[jj], ssq[jj], rstd[jj])
                gsub += 1
            for ci, (c0, kind, gi) in enumerate(chunks):
                ps = PB.get()
                for kc in range(KC):
                    S.mm(ps[:, 0:NT], win[:, kc, c0:c0 + 128], hT_[:, kc, :], start=(kc == 0), stop=(kc == KC - 1))
                if kind == "scale":
                    S.act(o_[:, ci, :], ps[:, 0:NT], AF.Copy, scale=0.125)
                elif kind == "copy":
                    S.copy("act", o_[:, ci, :], ps[:, 0:NT])
                else:
                    sq_ = sqb[inorm % 2]
                    rn_ = rn[inorm % 2]
                    inorm += 1
                    S.act(sq_, ps[:, 0:NT], AF.Square)
                    pss = PB.get()
                    S.mm(pss[:, 0:NT], bd64 if kind == "n64" else ones, sq_)
                    dim = 64.0 if kind == "n64" else 128.0
                    S.act(rn_, pss[:, 0:NT], AF.Sqrt, bias=C.eps, scale=1.0 / dim)
                    S.op("dve", lambda e, rn_=rn_: e.reciprocal(rn_.ap, rn_.ap), r=[rn_], w=[rn_])
                    S.stt("dve", o_[:, ci, :], ps[:, 0:NT], gn[:, gi:gi + 1], rn_, ALU.mult, ALU.mult)
            for sub in range(NSUB):
                cs = slice(sub * 128, (sub + 1) * 128)
                r0 = t0 + sub * 128
                tb = tokb[sub % 2]
                iw_ = iwt[sub % 2]
                pv = PB.get()
                for kc in range(KC):
                    S.mm(pv, hT_[:, kc, cs], win[:, kc, 1024:1536], start=(kc == 0), stop=(kc == KC - 1))
                p2 = PB.get()
                for kc in range(KC):
                    S.mm(p2[:, 0:128], hT_[:, kc, cs], win[:, kc, 2176:2304], start=(kc == 0), stop=(kc == KC - 1))
                for kc in range(KC):
                    S.mm(p2[:, 128:136], hT_[:, kc, cs], win[:, kc, 2880:2888], start=(kc == 0), stop=(kc == KC - 1))
                S.copy("act", tb[:, 0:512], pv)
                S.copy("act", tb[:, 512:640], p2[:, 0:128])
                S.ts("dve", iw_, p2[:, 128:136], 8.0 ** -0.5, ALU.mult)
                S.dma("sp", C.vtok.v((b, r0 // 128), C.vtok.ap[b, r0:r0 + 128, :]), tb, sres=tb.res)
                S.dma("sp", C.iwd.v((b, r0 // 128), C.iwd.ap[b, r0:r0 + 128, :]), iw_, sres=iw_.res)
            S.dma("sp", C.qkT.v((b, t0 // NT), C.qkT.ap[b, :, :, t0:t0 + NT].rearrange("c p t -> p c t")), o_, sres=o_.res)
            ist += 1
    S.end_phase()


def odd_attn_phase(S, C, j, src, dst):
    layer = 2 * j + 1
    lambda_init = 0.8 - 0.6 * math.exp(-0.3 * layer)
    L = C.L
    NKB = L // 128
    NQ = 512
    NQS = NQ // 128
    TOPK = min(256, L // 4)
    NIT = 18
    S.begin_phase()
    PB1 = Banks(S, 2)
    PB2 = Banks(S, 1)
    PB3 = Banks(S, 1)
    DACC = [S.ps("dacc%d" % i, [128, 512]) for i in range(2)]
    FACC = [S.ps("facc%d" % i, [128, 512]) for i in range(2)]
    wout = S.sb("wout", [128, KC, D], BF16)
    stg = [S.sb("stg%d" % i, [128, 1024]) for i in range(2)]
    load_weight_scaled(S, C, wout, C.od_wout[j], KC, D, None, stg, 1024, "wout")
    S.barrier()
    tmpf = [V(stg[0].ap[:, 0:512], Res("tmpf0")), V(stg[0].ap[:, 512:1024], Res("tmpf1"))]
    R = [V(stg[1].ap[:, 0:512], Res("R0")), V(stg[1].ap[:, 512:1024], Res("R1"))]
    rb = S.sb("rb", [128, 8, 2, 128])
    t31 = S.sb("t31", [128, 8])
    cmT = S.sb("cmT", [128, 128])
    dmask = S.sb("dmask", [128, 128])
    zer = S.sb("zer", [128, 128])
    S.dma("sp", rb, V(C.rb_near, C.wres), sres=rb.res)
    S.dma("sp", t31, V(C.rb_t31.partition_broadcast(128), C.wres), sres=t31.res)
    S.memset("pool", zer, 0.0)
    S.asel(cmT, zer, [[1, 128]], ALU.is_ge, NEG, base=0, cm=-1)
    S.asel(dmask, zer, [[-1, 128]], ALU.is_ge, -1e30, base=0, cm=1)
    rbv = V(rb.ap.rearrange("p h r q -> p h (r q)"), rb.res)
    S.tt("dve", rbv, rbv, bc_last(t31, 256), ALU.subtract)
    S.tt("dve", rb[:, :, 1, :], rb[:, :, 1, :], bc_mid(cmT, 8), ALU.add)
    lf = S.sb("lf", [128, 256])
    lj = S.sb("lj", [128, 64])
    lam = {n: S.sb(n, [128, 1]) for n in ("s01", "s23", "nlam")}
    S.dma("sp", lf, V(C.od_lam[j].partition_broadcast(128), C.wres), sres=lf.res)
    S.memset("dve", lam["s01"], 0.0)
    S.memset("dve", lam["s23"], 0.0)
    S.stt("dve", lj, lf[:, 0:64], 1.0, lf[:, 64:128], ALU.mult, ALU.mult, accum=lam["s01"])
    S.stt("dve", lj, lf[:, 128:192], 1.0, lf[:, 192:256], ALU.mult, ALU.mult, accum=lam["s23"])
    S.act(lam["s01"], lam["s01"], AF.Exp)
    S.act(lam["s23"], lam["s23"], AF.Exp)
    S.tt("dve", lam["nlam"], lam["s23"], lam["s01"], ALU.subtract)
    S.ts("dve", lam["nlam"], lam["nlam"], -lambda_init, ALU.add)
    gsub_ = S.sb("gsubn", [128, 128])
    S.dma("sp", gsub_, V(C.od_gsub[j].partition_broadcast(128), C.wres), sres=gsub_.res)
    S.ts("pool", gsub_, gsub_, 1.0 - lambda_init, ALU.mult, 1.0, ALU.mult)

    dkT = S.sb("dkT", [128, 4, L], BF16)
    skT = S.sb("skT", [128, L], BF16)
    ikT = S.sb("ikT", [128, L], BF16)
    dvA = S.sb("dvA", [128, NKB, 4, 130], BF16)
    svA = S.sb("svA", [128, NKB, 130], BF16)
    S.memset("pool", dvA[:, :, :, 128:130], 1.0)
    S.memset("pool", svA[:, :, 128:130], 1.0)
    dqT = S.sb("dqT", [128, 4, NQ], BF16)
    sqT = S.sb("sqT", [128, 4, NQ], BF16)
    iqT = S.sb("iqT", [128, 4, NQ], BF16)
    iw = S.sb("iw", [128, NQS, 8])
    idx = S.sb("idx", [128, L])
    M = S.sb("M", [128, L], BF16)
    M2 = S.sb("M2", [128, L], BF16)
    MT = S.sb("MT", [128, NKB, 128], BF16)
    PTa = [S.sb("PTa%d" % i, [128, 512], BF16) for i in range(2)]
    PTb = [S.sb("PTb%d" % i, [128, 512], BF16) for i in range(2)]
    xs = [S.sb("x%d" % i, [128, D]) for i in range(1)]
    ytd = [[S.sb("ytd%d_%d" % (a, i), [128, 4, 128], BF16) for i in range(NQS)] for a in range(2)]
    ytf = [[S.sb("ytf%d_%d" % (a, i), [128, 4, 128], BF16) for i in range(NQS)] for a in range(2)]
    yT = S.sb("yT", [128, 8, 128], BF16)
    of0 = S.sb("of0", [128, NQS, 128])
    of = S.sb("of", [128, 128])
    osq = S.sb("osq", [128, 128])
    sc = {n: S.sb(n, [128, 1]) for n in ("rmax", "w0", "lo", "nlo", "nmid", "cnt", "gw", "rec")}
    sd = {n: S.sb(n, [128, 1]) for n in ("rec", "oss", "ors")}
    sc2 = {n: S.sb(n + "2", [128, 1]) for n in ("rec",)}
    ctr = {"R": 0, "Pa": 0, "Pb": 0, "T": 0, "x": 0}

    NQB = L // 128
    NST = (L + NQ - 1) // NQ
    Ms = [M, M2]
    prog = {"s1": 0, "s2": 0, "df": 0}

    def s1_lane(b):
        for qb in range(NQB):
            s0 = (qb // NQS) * NQ
            qs = qb % NQS
            nqs = min(NQS, (L - s0) // 128)
            nq = nqs * 128
            if qs == 0:
                S.dma("sp", iqT[:, :, 0:nq], C.qkT.v((b, "iq", s0), C.qkT.ap[b, 13:17, :, s0:s0 + nq].rearrange("c p t -> p c t")), sres=iqT.res)
                S.dma("sp", iw[:, 0:nqs, :], C.iwd.v((b, "iw", s0), C.iwd.ap[b, s0:s0 + nq, :].rearrange("(s p) e -> p s e", p=128)), sres=iw.res)
                yield
            nk = (qb + 1) * 128
            qc = slice(qs * 128, (qs + 1) * 128)
            if nk > TOPK:
                while prog["s2"] < qb - 1:
                    yield
                Mq = Ms[qb % 2]
                for k0 in range(0, nk, 512):
                    wd = min(512, nk - k0)
                    for h in range(8):
                        pr = slice((h % 2) * 64, (h % 2) * 64 + 64)
                        ps = PB1.get()
                        S.mm(ps[:, 0:wd], iqT[pr, h // 2, qc], ikT[pr, k0:k0 + wd])
                        r_ = R[ctr["R"] % 2]
                        ctr["R"] += 1
                        S.act(r_[:, 0:wd], ps[:, 0:wd], AF.Relu)
                        if h == 0:
                            S.ts("dve", idx[:, k0:k0 + wd], r_[:, 0:wd], iw[:, qs, 0:1], ALU.mult)
                        else:
                            S.stt("dve", idx[:, k0:k0 + wd], r_[:, 0:wd], iw[:, qs, h:h + 1], idx[:, k0:k0 + wd], ALU.mult, ALU.add)
                        if h % 4 == 3:
                            yield
                S.reduce("dve", sc["rmax"], idx[:, 0:nk], ALU.max)
                S.reduce("dve", sc["lo"], idx[:, 0:nk], ALU.min)
                S.tt("dve", sc["w0"], sc["rmax"], sc["lo"], ALU.subtract)
                S.ts("dve", sc["nlo"], sc["lo"], -1.0, ALU.mult)
                S.tt("dve", idx[:, nk - 128:nk], idx[:, nk - 128:nk], dmask, ALU.add)
                yield
                thr = 2.0 * TOPK - nk - 0.5
                for it in range(NIT):
                    hw = 2.0 ** -(it + 1)
                    S.stt("dve", sc["nmid"], sc["w0"], -hw, sc["nlo"], ALU.mult, ALU.add)
                    S.act(Mq[:, 0:nk], idx[:, 0:nk], AF.Sign, bias=sc["nmid"], scale=1.0, accum=sc["cnt"])
                    yield
                    S.stt("dve", sc["gw"], sc["cnt"], thr, sc["w0"], ALU.is_ge, ALU.mult)
                    S.stt("dve", sc["nlo"], sc["gw"], -hw, sc["nlo"], ALU.mult, ALU.add)
                S.ts("dve", sc["lo"], sc["nlo"], -1.0, ALU.mult)
                S.ts("dve", Mq[:, 0:nk], idx[:, 0:nk], sc["lo"], ALU.is_ge)
            prog["s1"] = qb + 1
            yield

    def s2_lane(b):
        for qb in range(NQB):
            st_i = qb // NQS
            s0 = st_i * NQ
            qs = qb % NQS
            nqs = min(NQS, (L - s0) // 128)
            nq = nqs * 128
            if qs == 0:
                while prog["df"] < st_i - 1:
                    yield
                S.dma("sp", sqT[:, :, 0:nq], C.qkT.v((b, "sq", s0), C.qkT.ap[b, 8:12, :, s0:s0 + nq].rearrange("c p t -> p c t")), sres=sqT.res)
                yield
            while prog["s1"] < qb + 1:
                yield
            yb = ytd[st_i % 2]
            nk = (qb + 1) * 128
            qc = slice(qs * 128, (qs + 1) * 128)
            use_topk = nk > TOPK
            Mq = Ms[qb % 2]
            if use_topk:
                for g0 in range(0, qb + 1, 8):
                    ng = min(8, qb + 1 - g0)
                    pm = PB2.get()
                    for i in range(ng):
                        S.tr(bf(pm)[:, i * 128:(i + 1) * 128], Mq[:, (g0 + i) * 128:(g0 + i + 1) * 128], C.identb)
                    S.copy("act", MT[:, g0:g0 + ng, :], v3(bf(pm)[:, 0:ng * 128], ng))
                    yield
            for a in DACC:
                S.memset("dve", a[:, 0:130], 0.0)
                S.memset("dve", a[:, 256:386], 0.0)
            for kb in range(qb + 1):
                kc_ = slice(kb * 128, (kb + 1) * 128)
                ps = PB2.get()
                S.mm(ps, skT[:, kc_], sqT[:, :, qc])
                pt_ = PTa[ctr["Pa"] % 2]
                ctr["Pa"] += 1
                rel = kb - (qb - 1)
                if rel >= 0:
                    t_ = tmpf[ctr["T"] % 2]
                    ctr["T"] += 1
                    S.tt("dve", v3(t_), v3(ps), rb[:, 4:8, rel, :], ALU.add)
                    S.act(pt_, t_, AF.Exp)
                else:
                    S.act(pt_, ps, AF.Exp)
                if use_topk:
                    S.tt("dve", v3(pt_), v3(pt_), bc_mid(MT[:, kb, :], 4), ALU.mult)
                for h in range(4):
                    S.op("pe", lambda e, h=h, pt_=pt_, kb=kb, qb=qb: e.matmul(DACC[h // 2].ap[:, (h % 2) * 256:(h % 2) * 256 + 129], pt_.ap[:, h * 128:(h + 1) * 128],
                                                                             svA.ap[:, kb, 0:129], start=False, stop=(kb == qb), skip_group_check=True),
                         r=[pt_, svA], w=[DACC[h // 2]], inc=(h == 3))
                yield
            for h in range(4):
                a = DACC[h // 2]
                c0 = (h % 2) * 256
                S.op("dve", lambda e, a=a, c0=c0: e.reciprocal(sc2["rec"].ap, a.ap[:, c0 + 128:c0 + 129]), r=[a], w=[sc2["rec"]])
                S.ts("dve", yb[qs][:, h, :], a[:, c0:c0 + 128], sc2["rec"], ALU.mult)
            prog["s2"] = qb + 1
            yield

    def df_lane(b):
        for st_i in range(NST):
            s0 = st_i * NQ
            sblk = s0 // 128
            nqs = min(NQS, (L - s0) // 128)
            nq = nqs * 128
            yf = ytf[st_i % 2]
            yd = ytd[st_i % 2]
            S.dma("sp", dqT[:, :, 0:nq], C.qkT.v((b, "dq", s0), C.qkT.ap[b, 0:4, :, s0:s0 + nq].rearrange("c p t -> p c t")), sres=dqT.res)
            yield
            last_kb = sblk + nqs - 1
            for h in range(4):
                for m in range(2):
                    pr = slice(m * 64, m * 64 + 64)
                    for a in FACC:
                        S.memset("dve", a[:, 0:130], 0.0)
                        S.memset("dve", a[:, 256:386], 0.0)
                    for kb in range(last_kb + 1):
                        kc_ = slice(kb * 128, (kb + 1) * 128)
                        qlo = max(0, kb - sblk)
                        ncol = (nqs - qlo) * 128
                        ps = PB3.get()
                        S.mm(ps[:, 0:ncol], dkT[pr, h, kc_], dqT[pr, h, qlo * 128:nqs * 128])
                        for qs in (kb - sblk, kb - sblk + 1):
                            if 0 <= qs < nqs:
                                rel = kb - (sblk + qs - 1)
                                cc = slice((qs - qlo) * 128, (qs - qlo + 1) * 128)
                                S.tt("dve", ps[:, cc], ps[:, cc], rb[:, h, rel, :], ALU.add)
                        pt_ = PTb[ctr["Pb"] % 2]
                        ctr["Pb"] += 1
                        S.act(pt_[:, 0:ncol], ps[:, 0:ncol], AF.Exp)
                        for qs in range(qlo, nqs):
                            cc = slice((qs - qlo) * 128, (qs - qlo + 1) * 128)
                            S.op("pe", lambda e, qs=qs, cc=cc, pt_=pt_, kb=kb, h=h, sblk=sblk: e.matmul(
                                FACC[qs // 2].ap[:, (qs % 2) * 256:(qs % 2) * 256 + 129], pt_.ap[:, cc], dvA.ap[:, kb, h, 0:129],
                                start=False, stop=(kb == sblk + qs), skip_group_check=True), r=[pt_, dvA], w=[FACC[qs // 2]],
                                inc=(qs == nqs - 1))
                        yield
                    for qs in range(nqs):
                        a = FACC[qs // 2]
                        c0 = (qs % 2) * 256
                        S.op("dve", lambda e, a=a, c0=c0: e.reciprocal(sd["rec"].ap, a.ap[:, c0 + 128:c0 + 129]), r=[a], w=[sd["rec"]])
                        if m == 0:
                            S.ts("dve", of0[:, qs, :], a[:, c0:c0 + 128], sd["rec"], ALU.mult)
                        else:
                            S.tt("dve", sd["rec"], sd["rec"], lam["nlam"], ALU.mult)
                            S.stt("dve", of, a[:, c0:c0 + 128], sd["rec"], of0[:, qs, :], ALU.mult, ALU.add)
                            S.act(osq, of, AF.Square, accum=sd["oss"])
                            S.act(sd["ors"], sd["oss"], AF.Sqrt, bias=C.eps, scale=1.0 / 128)
                            S.op("dve", lambda e: e.reciprocal(sd["ors"].ap, sd["ors"].ap), r=[sd["ors"]], w=[sd["ors"]])
                            S.stt("dve", yf[qs][:, h, :], of, sd["ors"], gsub_, ALU.mult, ALU.mult)
                    yield
            while prog["s2"] < min(NQB, (st_i + 1) * NQS):
                yield
            for qs in range(nqs):
                r0 = b * L + s0 + qs * 128
                xt = xs[0]
                load_rows(S, C, xt, src, (r0 // 128), src.ap[r0:r0 + 128, :])
                pt2 = PB3.get()
                for c in range(4):
                    S.tr(bf(pt2)[:, c * 128:(c + 1) * 128], yf[qs][:, c, :], C.identb)
                for c in range(4):
                    S.tr(bf(pt2)[:, (4 + c) * 128:(5 + c) * 128], yd[qs][:, c, :], C.identb)
                S.copy("act", flat(yT), bf(pt2))
                for nh in range(2):
                    pso = PB3.get()
                    for kc in range(KC):
                        S.mm(pso, yT[:, kc, :], wout[:, kc, nh * 512:(nh + 1) * 512], start=(kc == 0), stop=(kc == KC - 1))
                    S.tt("dve", xt[:, nh * 512:(nh + 1) * 512], xt[:, nh * 512:(nh + 1) * 512], pso, ALU.add)
                S.dma("sp", dst.v((r0 // 128), dst.ap[r0:r0 + 128, :]), xt, sres=xt.res)
                yield
            prog["df"] = st_i + 1
            yield

    for b in range(C.NSEQ):
        S.dma("sp", dkT, C.qkT.v((b, "dk"), C.qkT.ap[b, 4:8, :, :].rearrange("c p t -> p c t")), sres=dkT.res)
        S.dma("sp", skT, C.qkT.v((b, "sk"), C.qkT.ap[b, 12, :, :]), sres=skT.res)
        S.dma("sp", ikT, C.qkT.v((b, "ik"), C.qkT.ap[b, 17, :, :]), sres=ikT.res)
        for kb in range(NKB):
            S.dma("sp", dvA[:, kb, :, 0:128], C.vtok.v((b, "dv", kb), C.vtok.ap[b, kb * 128:(kb + 1) * 128, 0:512].rearrange("p (h d) -> p h d", h=4)), sres=dvA.res)
        S.dma("sp", svA[:, :, 0:128], C.vtok.v((b, "sv"), C.vtok.ap[b, :, 512:640].rearrange("(k p) d -> p k d", p=128)), sres=svA.res)
        prog["s1"] = prog["s2"] = prog["df"] = 0
        run_lanes([s1_lane(b), s2_lane(b), df_lane(b)])
    S.end_phase()


W_SPECS = {
    "ffn_g": [4, 128, KC],
    "ffn_cw": [4, 128, 2 * FC, 3],
    "ffn_cb": [4, 128, 2 * FC],
    "ffn_wup": [4, D, 2 * DFF],
    "ffn_wdn": [4, DFF, D],
    "mix_g": [4, 128, KC],
    "ev_win": [2, D, EVEN_IN],
    "ev_wout": [2, D, D],
    "gm_wT": [2, 128, 4, 128],
    "gm_b": [2, 1, 512],
    "gdn_cw": [2, 128, 12, 4],
    "gdn_alog": [2, 1, 4],
    "gdn_dtb": [2, 1, 4],
    "gdn_ng": [2, 1, 128],
    "od_win": [2, D, OD_EXT],
    "od_wout": [2, D, D],
    "od_gn": [2, 128, 4],
    "od_lam": [2, 1, 256],
    "od_gsub": [2, 1, 128],
    "rb_near": [128, 8, 2, 128],
    "rb_t31": [1, 8],
}


def _rel_bucket_np(dist):
    n = np.maximum(dist, 0)
    nf = np.maximum(n, 16).astype(np.float32)
    far = 16 + (np.log(nf / np.float32(16)) / np.float32(math.log(128 / 16)) * np.float32(16)).astype(np.int32)
    return np.where(n < 16, n, np.minimum(far, 31))


def prep_weights(inp):
    f = lambda a: np.ascontiguousarray(np.asarray(a, dtype=np.float32))
    w = {}
    w["ffn_g"] = f(inp["ffn_norm_g"].reshape(4, KC, 128).transpose(0, 2, 1))
    w["ffn_cw"] = f(inp["ffn_conv_w"].reshape(4, 3, 2 * FC, 128).transpose(0, 3, 2, 1))
    w["ffn_cb"] = f(inp["ffn_conv_b"].reshape(4, 2 * FC, 128).transpose(0, 2, 1))
    w["ffn_wup"] = f(inp["ffn_w_up"])
    w["ffn_wdn"] = f(inp["ffn_w_down"])
    w["mix_g"] = f(inp["mix_norm_g"].reshape(4, KC, 128).transpose(0, 2, 1))
    w["ev_win"] = f(inp["ev_w_in"])
    w["ev_wout"] = f(inp["ev_w_out"])
    w["gm_wT"] = f(inp["gmlp_w_s"].transpose(0, 3, 1, 2))
    w["gm_b"] = f(inp["gmlp_b_s"].reshape(2, 1, 512))
    w["gdn_cw"] = f(inp["gdn_conv_w"].reshape(2, 4, 12, 128).transpose(0, 3, 2, 1))
    w["gdn_alog"] = f(inp["gdn_a_log"].reshape(2, 1, 4))
    w["gdn_dtb"] = f(inp["gdn_dt_bias"].reshape(2, 1, 4))
    w["gdn_ng"] = f(inp["gdn_norm_g"].reshape(2, 1, 128))
    ow = np.asarray(inp["od_w_in"], dtype=np.float32)
    w["od_win"] = f(np.concatenate([ow, ow[:, :, 2816:2880], ow[:, :, 2816:2880]], axis=2))
    w["od_wout"] = f(inp["od_w_out"])
    gq = np.asarray(inp["diff_q_norm_g"], dtype=np.float32)
    gk = np.asarray(inp["diff_k_norm_g"], dtype=np.float32)
    w["od_gn"] = f(np.stack([np.tile(gq, (1, 2)), np.tile(gk, (1, 2)), np.asarray(inp["dsa_q_norm_g"]), np.asarray(inp["dsa_k_norm_g"])], axis=2))
    w["od_lam"] = f(inp["diff_lambda"].reshape(2, 1, 256))
    w["od_gsub"] = f(inp["diff_sub_norm_g"].reshape(2, 1, 128))
    kk = np.arange(128)[:, None]
    qq = np.arange(128)[None, :]
    tab = np.asarray(inp["rel_bias"], dtype=np.float32)
    near = np.zeros((128, 8, 2, 128), np.float32)
    for rel in range(2):
        dist = qq - kk + (128 if rel == 0 else 0)
        near[:, :, rel, :] = tab[_rel_bucket_np(dist)].transpose(0, 2, 1)
    w["rb_near"] = f(near)
    w["rb_t31"] = f(tab[31:32, :])
    return w


def build_program(L, NSEQ, plan):
    nc = bass.Bass("TRN2", target_bir_lowering=False)
    NTOK = L * NSEQ
    C = Ctx()
    C.L, C.NSEQ, C.NTOK = L, NSEQ, NTOK
    x = nc.dram_tensor("x", [NTOK, D], F32, kind="ExternalInput").ap()
    y = nc.dram_tensor("y", [NTOK, D], F32, kind="ExternalOutput").ap()
    for name, shape in W_SPECS.items():
        setattr(C, name, nc.dram_tensor(name, shape, F32, kind="ExternalInput").ap())
    C.wres = Res("weights")
    xdt = DT(x, "x")
    ydt = DT(y, "y")
    C.qkT = DT(nc.dram_tensor("qkT", [NSEQ, NQK, 128, L], BF16, kind="Internal").ap(), "qkT")
    C.vtok = DT(nc.dram_tensor("vtok", [NSEQ, L, 640], BF16, kind="Internal").ap(), "vtok")
    C.iwd = DT(nc.dram_tensor("iwd", [NSEQ, L, 8], F32, kind="Internal").ap(), "iwd")
    with ExitStack() as es:
        S = Sched(nc, es)
        ct = es.enter_context(nc.sbuf_tensor("identb", [128, 128], BF16))
        C.identb = V(ct[:], Res("identb"))
        ct = es.enter_context(nc.sbuf_tensor("identf", [128, 128], F32))
        C.identf = V(ct[:], Res("identf"))
        ct = es.enter_context(nc.sbuf_tensor("eps", [128, 1], F32))
        C.eps = V(ct[:], Res("eps"))
        S.memset("pool", C.identf, 1.0)
        S.asel(C.identf, C.identf, [[-1, 128]], ALU.is_equal, 0.0, base=0, cm=1)
        S.copy("pool", C.identb, C.identf)
        S.memset("pool", C.eps, EPS)
        src = xdt
        for kind, idx in plan:
            if kind == "ffn":
                ffn_phase(S, C, idx, src, ydt)
            elif kind == "even":
                even_phase(S, C, idx, src, ydt)
            elif kind == "odd":
                odd_phase(S, C, idx, src, ydt)
            src = ydt
        S.barrier()
        print("program: ops=%d waits=%d dma_sems=%d" % (S.nops, S.nwaits, S.ndsem))
    return nc


FULL_PLAN = [("even", 0), ("ffn", 0), ("odd", 0), ("ffn", 1), ("even", 1), ("ffn", 2), ("odd", 1), ("ffn", 3)]


N_CORES = 8
_PROG = {}


def kernel(**inputs):
    x = np.asarray(inputs["x"], dtype=np.float32)
    B, L, Dm = x.shape
    nseq = B // N_CORES
    key = (L, nseq)
    if key not in _PROG:
        _PROG[key] = build_program(L, nseq, FULL_PLAN)
    nc = _PROG[key]
    w = prep_weights(inputs)
    in_maps = []
    for c in range(N_CORES):
        m = {"x": np.ascontiguousarray(x[c * nseq:(c + 1) * nseq].reshape(nseq * L, Dm))}
        m.update(w)
        in_maps.append(m)
    res = run_bass_kernel_spmd(nc, in_maps, core_ids=list(range(N_CORES)))
    out = np.concatenate([np.asarray(r["y"]).reshape(nseq, L, Dm) for r in res.results], axis=0)
    return out.astype(np.float32)
```

```python
import math
from contextlib import ExitStack

import numpy as np
import concourse.bass as bass
import concourse.mybir as mybir
from concourse.bass_utils import run_bass_kernel_spmd

F32 = mybir.dt.float32
BF16 = mybir.dt.bfloat16
AF = mybir.ActivationFunctionType
ALU = mybir.AluOpType
AX = mybir.AxisListType


class Res:
    __slots__ = ("name", "last_w", "reads", "dsem", "dcount", "excl")

    def __init__(self, name="r"):
        self.name = name
        self.excl = False
        self.last_w = None
        self.reads = []
        self.dsem = None
        self.dcount = 0


class V:
    __slots__ = ("ap", "res")

    def __init__(self, ap, res):
        self.ap = ap
        self.res = res

    def __getitem__(self, idx):
        return V(self.ap[idx], self.res)

    def r(self, res):
        return V(self.ap, res)


def _res_of(xs):
    out = []
    for x in xs:
        if x is None:
            continue
        out.append(x.res if isinstance(x, V) else x)
    return out


class Sched:
    ENGS = ("pe", "act", "dve", "pool", "sp")

    def __init__(self, nc, es):
        self.nc = nc
        self.es = es
        self.eng = {"pe": nc.tensor, "act": nc.scalar, "dve": nc.vector, "pool": nc.gpsimd, "sp": nc.sync}
        self.sem = {e: es.enter_context(nc.semaphore("sem_" + e)) for e in self.ENGS}
        self.cnt = {e: 0 for e in self.ENGS}
        self.known = {e: {} for e in self.ENGS}
        self.dpool = []
        self.dlive = []
        self.ndsem = 0
        self.nwaits = 0
        self.nops = 0
        self.phase_es = None
        self.uid = 0
        import os
        self.limit = int(os.environ["OPLIMIT"]) if "OPLIMIT" in os.environ else None

    def begin_phase(self):
        self.phase_es = ExitStack()

    def end_phase(self):
        self.barrier()
        for r in self.dlive:
            self.dpool.append((r.dsem, r.dcount))
            r.dsem = None
        self.dlive = []
        self.phase_es.close()
        self.phase_es = None

    def sb(self, name, shape, dt=F32):
        self.uid += 1
        name = "%s_u%d" % (name, self.uid)
        t = self.phase_es.enter_context(self.nc.sbuf_tensor(name, list(shape), dt))
        return V(t[:], Res(name))

    def ps(self, name, shape, dt=F32):
        self.uid += 1
        name = "%s_u%d" % (name, self.uid)
        t = self.phase_es.enter_context(self.nc.psum_tensor(name, list(shape), dt))
        rs = Res(name)
        rs.excl = True
        return V(t[:], rs)

    def _collect(self, eng, r, w, strict):
        waits = {}

        def add(ev, same_ok):
            if ev is None:
                return
            sem, val, src = ev
            if not strict and src == eng and (same_ok or eng == "pe"):
                return
            k = id(sem)
            if k not in waits or waits[k][1] < val:
                waits[k] = (sem, val)

        for res in r:
            add(res.last_w, False)
            if res.excl:
                for ev in res.reads:
                    add(ev, True)
        for res in w:
            add(res.last_w, True)
            for ev in res.reads:
                add(ev, True)
        return waits

    def _emit_waits(self, eng, waits):
        kn = self.known[eng]
        e = self.eng[eng]
        for k, (sem, val) in waits.items():
            if kn.get(k, 0) >= val:
                continue
            kn[k] = val
            e.wait_ge(sem, val)
            self.nwaits += 1

    def op(self, eng, fn, r=(), w=(), inc=True):
        if self.limit is not None and self.nops >= self.limit:
            return None
        r = _res_of(r)
        w = _res_of(w)
        self._emit_waits(eng, self._collect(eng, r, w, False))
        ins = fn(self.eng[eng])
        self.nops += 1
        if inc:
            self.cnt[eng] += 1
            ins.then_inc(self.sem[eng], 1)
            ev = (self.sem[eng], self.cnt[eng], eng)
        else:
            assert eng == "pe"
            ev = (self.sem[eng], self.cnt[eng] + 1, eng)
        for res in r:
            res.reads.append(ev)
        for res in w:
            res.last_w = ev
            res.reads = []
        return ins

    def dma(self, q, out, in_, sres=None, **kw):
        if self.limit is not None and self.nops >= self.limit:
            return None
        sr = sres if sres is not None else out.res
        if sr.dsem is None:
            if self.dpool:
                sr.dsem, sr.dcount = self.dpool.pop()
            else:
                sr.dsem = self.es.enter_context(self.nc.semaphore("dsem%d" % self.ndsem))
                sr.dcount = 0
                self.ndsem += 1
            self.dlive.append(sr)
        waits = self._collect(q, [in_.res], [out.res], True)
        k = id(sr.dsem)
        if sr.dcount > 0 and (k not in waits or waits[k][1] < sr.dcount):
            waits[k] = (sr.dsem, sr.dcount)
        self._emit_waits(q, waits)
        ins = self.eng[q].dma_start(out=out.ap, in_=in_.ap, **kw)
        sr.dcount += 16
        ins.then_inc(sr.dsem, 16)
        self.nops += 1
        ev = (sr.dsem, sr.dcount, "dma")
        in_.res.reads.append(ev)
        out.res.last_w = ev
        out.res.reads = []
        return ins

    def barrier(self):
        evs = [(self.sem[e], self.cnt[e]) for e in self.ENGS if self.cnt[e] > 0]
        evs += [(r.dsem, r.dcount) for r in self.dlive if r.dcount > 0]
        for e in self.ENGS:
            kn = self.known[e]
            for sem, val in evs:
                if sem is self.sem[e]:
                    continue
                if kn.get(id(sem), 0) >= val:
                    continue
                kn[id(sem)] = val
                self.eng[e].wait_ge(sem, val)
                self.nwaits += 1

    def mm(self, out, lhsT, rhs, start=True, stop=True, inc=None, **kw):
        if inc is None:
            inc = bool(stop)
        return self.op("pe", lambda e: e.matmul(out.ap, lhsT.ap, rhs.ap, start=start, stop=stop, **kw),
                       r=[lhsT, rhs], w=[out], inc=inc)

    def tr(self, out, in_, ident):
        return self.op("pe", lambda e: e.transpose(out.ap, in_.ap, ident.ap), r=[in_, ident], w=[out])

    def act(self, out, in_, func, bias=None, scale=None, accum=None, eng="act"):
        kw = {}
        rr = [in_]
        ww = [out]
        if bias is not None:
            if isinstance(bias, V):
                kw["bias"] = bias.ap
                rr.append(bias)
            else:
                kw["bias"] = bias
        if scale is not None:
            if isinstance(scale, V):
                kw["scale"] = scale.ap
                rr.append(scale)
            else:
                kw["scale"] = scale
        if accum is not None:
            kw["accum_out"] = accum.ap
            ww.append(accum)
        return self.op("act", lambda e: e.activation(out.ap, in_.ap, func, **kw), r=rr, w=ww)

    def tt(self, eng, out, a, b, op):
        return self.op(eng, lambda e: e.tensor_tensor(out.ap, a.ap, b.ap, op), r=[a, b], w=[out])

    def ts(self, eng, out, a, s1, op0, s2=None, op1=None, accum=None):
        rr = [a]
        ww = [out]
        a1 = s1
        a2 = s2
        if isinstance(s1, V):
            rr.append(s1)
            a1 = s1.ap
        if isinstance(s2, V):
            rr.append(s2)
            a2 = s2.ap
        kw = {}
        if op1 is not None:
            kw["op1"] = op1
        if accum is not None:
            kw["accum_out"] = accum.ap
            ww.append(accum)
        return self.op(eng, lambda e: e.tensor_scalar(out.ap, a.ap, a1, a2, op0, **kw), r=rr, w=ww)

    def stt(self, eng, out, a, s, b, op0, op1, accum=None):
        rr = [a, b]
        ww = [out]
        sc = s
        if isinstance(s, V):
            rr.append(s)
            sc = s.ap
        kw = {}
        if accum is not None:
            kw["accum_out"] = accum.ap
            ww.append(accum)
        return self.op(eng, lambda e: e.scalar_tensor_tensor(out.ap, a.ap, sc, b.ap, op0, op1, **kw), r=rr, w=ww)

    def copy(self, eng, out, in_):
        if eng == "act":
            return self.op("act", lambda e: e.copy(out.ap, in_.ap), r=[in_], w=[out])
        return self.op(eng, lambda e: e.tensor_copy(out.ap, in_.ap), r=[in_], w=[out])

    def memset(self, eng, out, val):
        return self.op(eng, lambda e: e.memset(out.ap, val), w=[out])

    def reduce(self, eng, out, in_, op, axis=None):
        ax = AX.X if axis is None else axis
        return self.op(eng, lambda e: e.tensor_reduce(out.ap, in_.ap, ax, op), r=[in_], w=[out])

    def asel(self, out, in_, pattern, cmp, fill, base=0, cm=0):
        return self.op("pool", lambda e: e.affine_select(out.ap, in_.ap, pattern=pattern, compare_op=cmp, fill=fill,
                                                         base=base, channel_multiplier=cm), r=[in_], w=[out])


D = 1024
KC = 8
DFF = 2816
FC = 22
EPS = 1e-6
EVEN_IN = 3080
ODD_IN = 2888
NEG = -30000.0


class DT:
    def __init__(self, ap, name):
        self.ap = ap
        self.name = name
        self.res = {}

    def v(self, key, ap):
        if key not in self.res:
            self.res[key] = Res("%s_%s" % (self.name, str(key)))
        return V(ap, self.res[key])


class Ctx:
    pass


def load_rows(S, C, dst, src_dt, key, ap, q="sp"):
    S.dma(q, dst, src_dt.v(key, ap), sres=dst.res)


def rms_to_hnT(S, C, xt, hnT_cols, ptr, hn, ssq, rstd):
    S.act(hn, xt, AF.Square, accum=ssq)
    S.act(rstd, ssq, AF.Sqrt, bias=C.eps, scale=1.0 / D)
    S.op("dve", lambda e: e.reciprocal(rstd.ap, rstd.ap), r=[rstd], w=[rstd])
    S.ts("pool", hn, xt, rstd, ALU.mult, 1.0, ALU.mult)
    for kc in range(KC):
        S.tr(ptr[:, kc * 128:(kc + 1) * 128], hn[:, kc * 128:(kc + 1) * 128], C.identb)
    S.copy("act", hnT_cols, V(ptr.ap.rearrange("p (k t) -> p k t", k=KC), ptr.res))


def load_weight_scaled(S, C, wsb, w_dram_ap, nrows_chunks, ncols, gcol, stg, colchunk, name):
    i = 0
    width = stg[0].ap.shape[1]
    colchunk = min(colchunk, width)
    for rc in range(nrows_chunks):
        for c0 in range(0, ncols, colchunk):
            c1 = min(ncols, c0 + colchunk)
            st = stg[i % len(stg)]
            S.dma(("sp", "act")[i % 2], st[:, 0:c1 - c0], V(w_dram_ap[rc * 128:(rc + 1) * 128, c0:c1], C.wres), sres=st.res)
            eng = ("dve", "act", "dve", "pool")[i % 4]
            if gcol is None:
                S.copy(eng, wsb[:, rc, c0:c1], st[:, 0:c1 - c0])
            elif eng == "act":
                S.act(wsb[:, rc, c0:c1], st[:, 0:c1 - c0], AF.Copy, scale=gcol[:, rc:rc + 1])
            else:
                S.ts(eng, wsb[:, rc, c0:c1], st[:, 0:c1 - c0], gcol[:, rc:rc + 1], ALU.mult, 1.0, ALU.mult)
            i += 1


def ffn_phase(S, C, layer, src, dst):
    NT = 256
    NSUB = NT // 128
    S.begin_phase()
    wup = S.sb("wup", [128, KC, 2 * DFF], BF16)
    wdn = S.sb("wdn", [128, FC, D], BF16)
    gsb = S.sb("gsb", [128, KC])
    cw = S.sb("cw", [128, 2 * FC, 3])
    cb = S.sb("cb", [128, 2 * FC])
    stg = [S.sb("stg%d" % i, [128, 704]) for i in range(4)]
    S.dma("sp", gsb, V(C.ffn_g[layer], C.wres), sres=gsb.res)
    S.dma("sp", cw, V(C.ffn_cw[layer], C.wres), sres=cw.res)
    S.dma("sp", cb, V(C.ffn_cb[layer], C.wres), sres=cb.res)
    load_weight_scaled(S, C, wup, C.ffn_wup[layer], KC, 2 * DFF, gsb, stg, 1408, "wup")
    load_weight_scaled(S, C, wdn, C.ffn_wdn[layer], FC, D, None, stg, 1024, "wdn")

    xs = [S.sb("x%d" % i, [128, D]) for i in range(4)]
    hn = [S.sb("hn%d" % i, [128, D], BF16) for i in range(2)]
    ssq = [S.sb("ssq%d" % i, [128, 1]) for i in range(2)]
    rstd = [S.sb("rstd%d" % i, [128, 1]) for i in range(2)]
    hnT = [S.sb("hnT%d" % i, [128, KC, NT], BF16) for i in range(2)]
    hT = S.sb("hT", [128, FC, NT], BF16)
    hal = S.sb("hal", [128, 2 * FC, 2])
    rr = [S.sb("rr%d" % i, [128, NT + 2]) for i in range(4)]
    acc = [S.sb("acc%d" % i, [128, NT]) for i in range(4)]
    sg = [S.sb("sg%d" % i, [128, NT]) for i in range(2)]
    ptr = [S.ps("ptr%d" % i, [128, D], BF16) for i in range(2)]
    pup = [S.ps("pup%d" % i, [128, 512])[:, 0:NT] for i in range(4)]
    pdn = [S.ps("pdn%d" % i, [128, 512]) for i in range(2)]

    gsub = 0
    ipair = 0
    for b in range(C.NSEQ):
        for t0 in range(0, C.L, NT):
            sti = t0 // NT
            hT_ = hnT[(b * (C.L // NT) + sti) % 2]
            xsl = []
            for sub in range(NSUB):
                r0 = b * C.L + t0 + sub * 128
                xt = xs[gsub % 4]
                xsl.append((xt, r0))
                load_rows(S, C, xt, src, (r0 // 128), src.ap[r0:r0 + 128, :])
                j = gsub % 2
                rms_to_hnT(S, C, xt, hT_[:, :, sub * 128:(sub + 1) * 128], ptr[j], hn[j], ssq[j], rstd[j])
                gsub += 1
            halves = [(c, which) for c in range(FC) for which in range(2)]
            nh_ = len(halves)

            def fA(n):
                c, which = halves[n]
                ch = c + which * FC
                k = n % 4
                ps = pup[k]
                for kc in range(KC):
                    S.mm(ps, wup[:, kc, ch * 128:(ch + 1) * 128], hT_[:, kc, :], start=(kc == 0), stop=(kc == KC - 1))
                r = rr[k]
                a = acc[k]
                if t0 == 0:
                    S.memset("pool", r[:, 0:2], 0.0)
                else:
                    S.copy("pool", r[:, 0:2], hal[:, ch, :])
                S.copy("act", r[:, 2:NT + 2], ps)
                S.act(a, ps, AF.Identity, bias=cb[:, ch:ch + 1], scale=cw[:, ch, 2:3])
                S.copy("pool", hal[:, ch, :], r[:, NT:NT + 2])

            def fB(n):
                c, which = halves[n]
                ch = c + which * FC
                k = n % 4
                r = rr[k]
                a = acc[k]
                S.stt("dve", a, r[:, 1:NT + 1], cw[:, ch, 1:2], a, ALU.mult, ALU.add)
                S.stt("dve", a, r[:, 0:NT], cw[:, ch, 0:1], a, ALU.mult, ALU.add)

            def fC(p):
                S.act(sg[p % 2], acc[(2 * p) % 4], AF.Silu)

            def fD(p):
                S.tt("dve", hT[:, p, :], sg[p % 2], acc[(2 * p + 1) % 4], ALU.mult)

            for n in range(nh_ + 3):
                if n < nh_:
                    fA(n)
                if 0 <= n - 1 < nh_:
                    fB(n - 1)
                if n >= 2 and n % 2 == 0 and (n - 2) // 2 < FC:
                    fC((n - 2) // 2)
                if n >= 3 and n % 2 == 1 and (n - 3) // 2 < FC:
                    fD((n - 3) // 2)
            for sub in range(NSUB):
                xt, r0 = xsl[sub]
                for nh in range(2):
                    ps2 = pdn[nh]
                    for fc in range(FC):
                        S.mm(ps2, hT[:, fc, sub * 128:(sub + 1) * 128], wdn[:, fc, nh * 512:(nh + 1) * 512],
                             start=(fc == 0), stop=(fc == FC - 1))
                    S.tt("dve", xt[:, nh * 512:(nh + 1) * 512], xt[:, nh * 512:(nh + 1) * 512], ps2, ALU.add)
                S.dma("sp", dst.v((r0 // 128), dst.ap[r0:r0 + 128, :]), xt, sres=xt.res)
    S.end_phase()


def flat(v):
    return V(v.ap.rearrange("p h j -> p (h j)"), v.res)


def v3(v, h=4):
    return V(v.ap.rearrange("p (h j) -> p h j", h=h), v.res)


def bc_last(v, n):
    H = v.ap.shape[1]
    return V(v.ap.unsqueeze(2).to_broadcast([128, H, n]), v.res)


def bc_mid(v, h):
    n = v.ap.shape[1]
    return V(v.ap.unsqueeze(1).to_broadcast([128, h, n]), v.res)


class Banks:
    def __init__(self, S, n=8):
        self.b = [S.ps("bank%d" % i, [128, 512]) for i in range(n)]
        self.i = 0

    def get(self):
        v = self.b[self.i % len(self.b)]
        self.i += 1
        return v


def bf(v):
    return V(v.ap.bitcast(BF16), v.res)


F32R = mybir.dt.float32r


def r32(v):
    return V(v.ap.bitcast(F32R), v.res)


def run_lanes(gens):
    gens = [g for g in gens if g is not None]
    while gens:
        for g in list(gens):
            try:
                next(g)
            except StopIteration:
                gens.remove(g)


def even_phase(S, C, j, src, dst):
    layer = 2 * j
    NT = 256
    NSUB = NT // 128
    H = 4
    S.begin_phase()
    PB = Banks(S)
    win = S.sb("win", [128, KC, EVEN_IN], BF16)
    wout = S.sb("wout", [128, KC, D], BF16)
    gsb = S.sb("gsb", [128, KC])
    stg = [S.sb("stg%d" % i, [128, 770]) for i in range(4)]
    S.dma("sp", gsb, V(C.mix_g[layer], C.wres), sres=gsb.res)
    load_weight_scaled(S, C, win, C.ev_win[j], KC, EVEN_IN, gsb, stg, 1540, "win")
    load_weight_scaled(S, C, wout, C.ev_wout[j], KC, D, None, stg, 1024, "wout")
    wTm = S.sb("wTm", [128, H, 128], BF16)
    wTf = S.sb("wTf", [128, H, 128])
    brow = S.sb("brow", [1, 512])
    cw = S.sb("gcw", [128, 12, 4])
    alog = S.sb("alog", [128, 4])
    dtb = S.sb("dtb", [128, 4])
    nega = S.sb("nega", [128, 4])
    gng = S.sb("gng", [128, 128])
    S.dma("sp", wTf, V(C.gm_wT[j], C.wres), sres=wTf.res)
    S.dma("sp", brow, V(C.gm_b[j], C.wres), sres=brow.res)
    S.dma("sp", cw, V(C.gdn_cw[j], C.wres), sres=cw.res)
    S.dma("sp", alog, V(C.gdn_alog[j].partition_broadcast(128), C.wres), sres=alog.res)
    S.dma("sp", dtb, V(C.gdn_dtb[j].partition_broadcast(128), C.wres), sres=dtb.res)
    S.dma("sp", gng, V(C.gdn_ng[j].partition_broadcast(128), C.wres), sres=gng.res)
    ones = S.sb("ones", [128, 128])
    tri = S.sb("tri", [128, 128])
    ntri = S.sb("ntri", [128, 128])
    strict = S.sb("strict", [128, 128])
    incl = S.sb("incl", [128, 128])
    S.memset("pool", ones, 1.0)
    S.asel(tri, ones, [[1, 128]], ALU.is_ge, 0.0, base=0, cm=-1)
    S.ts("pool", ntri, tri, -1.0, ALU.mult, 1.0, ALU.mult)
    S.asel(strict, ones, [[-1, 128]], ALU.is_ge, 0.0, base=-1, cm=1)
    S.asel(incl, ones, [[-1, 128]], ALU.is_ge, 0.0, base=0, cm=1)
    S.tt("pool", wTm, wTf, bc_mid(tri, H), ALU.mult)
    S.act(nega, alog, AF.Exp)
    S.ts("pool", nega, nega, -1.0, ALU.mult, 1.0, ALU.mult)

    xs = [S.sb("x%d" % i, [128, D]) for i in range(4)]
    hn = [S.sb("hn%d" % i, [128, D], BF16) for i in range(2)]
    ssq = [S.sb("ssq%d" % i, [128, 1]) for i in range(2)]
    rstd = [S.sb("rstd%d" % i, [128, 1]) for i in range(2)]
    hnT = [S.sb("hnT%d" % i, [128, KC, NT], BF16) for i in range(2)]
    uTs = [S.sb("uT%d" % i, [128, H, NT], BF16) for i in range(2)]
    yT = S.sb("yT", [128, KC, NT], BF16)
    qTs = [S.sb("qT%d" % i, [128, H, NT], BF16) for i in range(2)]
    kTs = [S.sb("kT%d" % i, [128, H, NT], BF16) for i in range(2)]
    vTs = [S.sb("vT%d" % i, [128, H, NT], BF16) for i in range(2)]
    XSL = [None, None]
    hal = S.sb("hal", [128, 12, 3])
    rr = [S.sb("rr%d" % i, [128, NT + 3]) for i in range(2)]
    ca = [S.sb("ca%d" % i, [128, NT]) for i in range(2)]
    qs = [S.sb("qs%d" % i, [128, NT]) for i in range(2)]
    sq = S.sb("sq", [128, NT])
    rn = S.sb("rn", [128, NT])
    Sst = S.sb("Sst", [128, H, 128])
    Sr = S.sb("Sr", [128, H, 128])

    def F(name, dt=F32):
        return S.sb(name, [128, H, 128], dt)

    ktok, vtok, vg, sqv, zs, G1, G2, E, nbm, usb, osq, on, gz, t1 = [
        F(n) for n in ("ktok", "vtok", "vg", "sqv", "zs", "G1", "G2", "E", "nbm", "usb", "osq", "on", "gz", "t1")]
    Lp0, Lp1, intra, intraT, U0, U1, TT, vb, kbg, wTs, qgT, kd, vnew = [
        F(n) for n in ("Lp0", "Lp1", "intra", "intraT", "U0", "U1", "TT", "vb", "kbg", "wTs", "qgT", "kd", "vnew")]
    vn = F("vn", BF16)
    onb = F("onb", BF16)
    sm = {n: S.sb(n, [128, 4]) for n in ("vsum", "vvar", "vrs", "beta", "nbeta", "xa", "xe", "xm", "sp", "g", "bk", "edl", "oss", "ors")}
    gcs = S.sb("gcs", [128, 8])
    egs = S.sb("egs", [128, 8])

    st = {"gsub": 0, "ich": 0}

    def proj_task(b, t0, k):
        hT_ = hnT[k]
        uT, qT, kT, vT = uTs[k], qTs[k], kTs[k], vTs[k]
        xsl = []
        for sub in range(NSUB):
            r0 = b * C.L + t0 + sub * 128
            xt = xs[st["gsub"] % 4]
            xsl.append((xt, r0))
            load_rows(S, C, xt, src, (r0 // 128), src.ap[r0:r0 + 128, :])
            jj = st["gsub"] % 2
            pt = PB.get()
            rms_to_hnT(S, C, xt, hT_[:, :, sub * 128:(sub + 1) * 128], bf(pt), hn[jj], ssq[jj], rstd[jj])
            st["gsub"] += 1
        for c in range(H):
            ps = PB.get()
            for kc in range(KC):
                S.mm(ps[:, 0:NT], win[:, kc, c * 128:(c + 1) * 128], hT_[:, kc, :], start=(kc == 0), stop=(kc == KC - 1))
            S.act(uT[:, c, :], ps[:, 0:NT], AF.Gelu_apprx_tanh)
            yield
        for c in range(12):
            ps = PB.get()
            for kc in range(KC):
                S.mm(ps[:, 0:NT], win[:, kc, 1024 + c * 128:1024 + (c + 1) * 128], hT_[:, kc, :], start=(kc == 0), stop=(kc == KC - 1))
            r = rr[st["ich"] % 2]
            a = ca[st["ich"] % 2]
            if t0 == 0:
                S.memset("pool", r[:, 0:3], 0.0)
            else:
                S.copy("pool", r[:, 0:3], hal[:, c, :])
            S.copy("act", r[:, 3:NT + 3], ps[:, 0:NT])
            S.act(a, ps[:, 0:NT], AF.Copy, scale=cw[:, c, 3:4])
            S.copy("pool", hal[:, c, :], r[:, NT:NT + 3])
            for tap in (2, 1, 0):
                S.stt("dve", a, r[:, tap:tap + NT], cw[:, c, tap:tap + 1], a, ALU.mult, ALU.add)
            hh = c % 4
            if c >= 8:
                S.act(vT[:, hh, :], a, AF.Silu)
            else:
                q_ = qs[st["ich"] % 2]
                S.act(q_, a, AF.Silu)
                S.act(sq, q_, AF.Square)
                pss = PB.get()
                S.mm(pss[:, 0:NT], ones, sq)
                S.act(rn, pss[:, 0:NT], AF.Sqrt, bias=C.eps, scale=1.0)
                S.op("dve", lambda e: e.reciprocal(rn.ap, rn.ap), r=[rn], w=[rn])
                if c < 4:
                    S.stt("dve", qT[:, hh, :], q_, 128.0 ** -0.5, rn, ALU.mult, ALU.mult)
                else:
                    S.tt("dve", kT[:, hh, :], q_, rn, ALU.mult)
            st["ich"] += 1
            yield
        XSL[k] = xsl
        yield

    def chunk_task(b, t0, k):
        hT_ = hnT[k]
        uT, qT, kT, vT = uTs[k], qTs[k], kTs[k], vTs[k]
        xsl = XSL[k]
        if t0 == 0:
            S.memset("pool", Sst, 0.0)
            S.copy("pool", r32(Sr), Sst)
        for sub in range(NSUB):
            cs = slice(sub * 128, (sub + 1) * 128)
            xt, r0 = xsl[sub]
            pv = PB.get()
            pz = PB.get()
            pba = PB.get()
            for kc in range(KC):
                S.mm(pv, hT_[:, kc, cs], win[:, kc, 512:1024], start=(kc == 0), stop=(kc == KC - 1))
            for kc in range(KC):
                S.mm(pz, hT_[:, kc, cs], win[:, kc, 2568:3080], start=(kc == 0), stop=(kc == KC - 1))
            for kc in range(KC):
                S.mm(pba[:, 0:8], hT_[:, kc, cs], win[:, kc, 2560:2568], start=(kc == 0), stop=(kc == KC - 1))
            S.act(flat(vg), pv, AF.Gelu_apprx_tanh)
            S.act(flat(zs), pz, AF.Silu)
            S.act(sm["beta"], pba[:, 0:4], AF.Sigmoid)
            S.tt("dve", sm["xa"], pba[:, 4:8], dtb, ALU.add)
            S.reduce("dve", sm["vsum"], vg, ALU.add)
            S.stt("dve", vg, bc_last(sm["vsum"], 128), -1.0 / 128, vg, ALU.mult, ALU.add)
            S.tt("pool", sqv, vg, vg, ALU.mult)
            S.reduce("dve", sm["vvar"], sqv, ALU.add)
            S.act(sm["vrs"], sm["vvar"], AF.Sqrt, bias=C.eps, scale=1.0 / 128)
            S.op("dve", lambda e: e.reciprocal(sm["vrs"].ap, sm["vrs"].ap), r=[sm["vrs"]], w=[sm["vrs"]])
            S.tt("dve", vn, vg, bc_last(sm["vrs"], 128), ALU.mult)
            yield
            pm = PB.get()
            for gI in range(H):
                S.mm(pm[:, gI * 128:(gI + 1) * 128], vn[:, gI, :], wTm[:, gI, :], start=True, stop=False)
                S.mm(pm[:, gI * 128:(gI + 1) * 128], ones[0:1, :], brow[0:1, gI * 128:(gI + 1) * 128], start=False, stop=True)
            S.tt("dve", yT[:, 0:4, cs], v3(pm), uT[:, :, cs], ALU.mult)
            yield
            S.ts("dve", sm["nbeta"], sm["beta"], -1.0, ALU.mult)
            S.ts("dve", sm["xm"], sm["xa"], 30.0, ALU.min)
            S.act(sm["xe"], sm["xm"], AF.Exp)
            S.act(sm["sp"], sm["xe"], AF.Ln, bias=1.0, scale=1.0)
            S.ts("dve", sm["xm"], sm["xa"], -30.0, ALU.add, 0.0, ALU.max)
            S.tt("dve", sm["sp"], sm["sp"], sm["xm"], ALU.add)
            S.tt("dve", sm["g"], sm["sp"], nega, ALU.mult)
            g = sm["g"]
            pg = PB.get()
            S.mm(pg[:, 0:4], tri, g)
            S.mm(pg[:, 4:8], ones, g)
            S.copy("dve", gcs, pg[:, 0:8])
            S.act(egs, gcs, AF.Exp)
            S.tt("dve", sm["edl"], gcs[:, 4:8], gcs[:, 0:4], ALU.subtract)
            S.act(sm["edl"], sm["edl"], AF.Exp)
            S.tt("dve", sm["bk"], sm["beta"], egs[:, 0:4], ALU.mult)
            yield
            pk = PB.get()
            for h in range(H):
                S.tr(bf(pk)[:, h * 128:(h + 1) * 128], kT[:, h, cs], C.identb)
            S.copy("act", flat(ktok), bf(pk)[:, 0:512])
            pk = PB.get()
            for h in range(H):
                S.tr(bf(pk)[:, h * 128:(h + 1) * 128], vT[:, h, cs], C.identb)
            S.copy("act", flat(vtok), bf(pk)[:, 0:512])
            yield
            S.copy("pool", G1, bc_last(g, 128))
            S.tt("pool", G2, bc_last(g, 128), bc_mid(ntri, H), ALU.mult)
            pd = PB.get()
            for h in range(H):
                S.mm(pd[:, h * 128:(h + 1) * 128], tri, G1[:, h, :], start=True, stop=False)
                S.mm(pd[:, h * 128:(h + 1) * 128], ones, G2[:, h, :], start=False, stop=True)
            S.ts("dve", flat(E), pd, 0.0, ALU.min)
            S.act(flat(E), flat(E), AF.Exp)
            yield
            pkk = PB.get()
            pqk = PB.get()
            for h in range(H):
                S.mm(pkk[:, h * 128:(h + 1) * 128], kT[:, h, cs], kT[:, h, cs])
            for h in range(H):
                S.mm(pqk[:, h * 128:(h + 1) * 128], qT[:, h, cs], kT[:, h, cs])
            S.tt("pool", nbm, bc_mid(strict, H), bc_last(sm["nbeta"], 128), ALU.mult)
            S.tt("dve", flat(t1), pkk, flat(E), ALU.mult)
            S.tt("pool", r32(Lp0), t1, nbm, ALU.mult)
            S.tt("dve", flat(osq), pqk, flat(E), ALU.mult)
            S.tt("pool", intra, osq, bc_mid(incl, H), ALU.mult)
            yield
            pu = PB.get()
            for h in range(H):
                S.tr(pu[:, h * 128:(h + 1) * 128], Lp0[:, h, :], C.identf)
            S.copy("act", r32(flat(U0)), pu)
            S.tt("dve", r32(TT), v3(pu), bc_mid(C.identf, H), ALU.add)
            pi = PB.get()
            for h in range(H):
                S.tr(pi[:, h * 128:(h + 1) * 128], intra[:, h, :], C.identf)
            S.copy("act", r32(flat(intraT)), pi)
            yield
            Us = [U0, U1]
            Ls = [Lp0, Lp1]
            for k in range(1, 7):
                Uo, Un = Us[(k - 1) % 2], Us[k % 2]
                Lo, Ln_ = Ls[(k - 1) % 2], Ls[k % 2]
                if k <= 5:
                    p1 = PB.get()
                    for h in range(H):
                        S.mm(p1[:, h * 128:(h + 1) * 128], r32(Lo[:, h, :]), r32(Uo[:, h, :]))
                p2 = PB.get()
                for h in range(H):
                    S.mm(p2[:, h * 128:(h + 1) * 128], r32(Uo[:, h, :]), r32(Lo[:, h, :]))
                if k <= 5:
                    S.copy("act", r32(flat(Un)), p1)
                S.copy("dve", r32(flat(Ln_)), p2)
                p3 = PB.get()
                for h in range(H):
                    S.mm(p3[:, h * 128:(h + 1) * 128], r32(Ln_[:, h, :]), r32(TT[:, h, :]))
                S.tt("dve", r32(flat(TT)), flat(TT), p3, ALU.add)
            yield
            yield
            S.tt("pool", r32(vb), vtok, bc_last(sm["beta"], 128), ALU.mult)
            S.tt("pool", r32(kbg), ktok, bc_last(sm["bk"], 128), ALU.mult)
            S.tt("pool", r32(kd), ktok, bc_last(sm["edl"], 128), ALU.mult)
            pU = PB.get()
            for h in range(H):
                S.mm(pU[:, h * 128:(h + 1) * 128], r32(TT[:, h, :]), r32(vb[:, h, :]))
            S.copy("act", flat(usb), pU)
            pW = PB.get()
            for h in range(H):
                S.mm(pW[:, h * 128:(h + 1) * 128], r32(kbg[:, h, :]), r32(TT[:, h, :]))
            S.copy("act", r32(flat(wTs)), pW)
            yield
            S.tt("pool", G1, bc_mid(C.identf, H), bc_last(egs[:, 0:4], 128), ALU.mult)
            pe_ = PB.get()
            S.mm(pe_, ones, flat(G1))
            S.tt("dve", r32(qgT), qT[:, :, cs], v3(pe_), ALU.mult)
            yield
            pws = PB.get()
            for h in range(H):
                S.mm(pws[:, h * 128:(h + 1) * 128], r32(wTs[:, h, :]), r32(Sr[:, h, :]))
            S.tt("dve", r32(flat(vnew)), flat(usb), pws, ALU.subtract)
            po = PB.get()
            for h in range(H):
                S.mm(po[:, h * 128:(h + 1) * 128], r32(qgT[:, h, :]), r32(Sr[:, h, :]), start=True, stop=False)
                S.mm(po[:, h * 128:(h + 1) * 128], r32(intraT[:, h, :]), r32(vnew[:, h, :]), start=False, stop=True)
            pS = PB.get()
            for h in range(H):
                S.mm(pS[:, h * 128:(h + 1) * 128], r32(kd[:, h, :]), r32(vnew[:, h, :]))
            S.tt("dve", Sst, Sst, bc_last(egs[:, 4:8], 128), ALU.mult)
            S.tt("dve", flat(Sst), flat(Sst), pS, ALU.add)
            S.copy("act", r32(Sr), Sst)
            yield
            S.act(flat(osq), po, AF.Square)
            S.reduce("dve", sm["oss"], osq, ALU.add)
            S.act(sm["ors"], sm["oss"], AF.Sqrt, bias=C.eps, scale=1.0 / 128)
            S.op("dve", lambda e: e.reciprocal(sm["ors"].ap, sm["ors"].ap), r=[sm["ors"]], w=[sm["ors"]])
            S.tt("pool", gz, zs, bc_mid(gng, H), ALU.mult)
            S.tt("dve", on, v3(po), bc_last(sm["ors"], 128), ALU.mult)
            S.tt("dve", onb, on, gz, ALU.mult)
            pt2 = PB.get()
            for h in range(H):
                S.tr(bf(pt2)[:, h * 128:(h + 1) * 128], onb[:, h, :], C.identb)
            S.copy("act", yT[:, 4:8, cs], v3(bf(pt2)[:, 0:512]))
            yield
            for nh in range(2):
                pso = PB.get()
                for kc in range(KC):
                    S.mm(pso, yT[:, kc, cs], wout[:, kc, nh * 512:(nh + 1) * 512], start=(kc == 0), stop=(kc == KC - 1))
                S.tt("dve", xt[:, nh * 512:(nh + 1) * 512], xt[:, nh * 512:(nh + 1) * 512], pso, ALU.add)
            S.dma("sp", dst.v((r0 // 128), dst.ap[r0:r0 + 128, :]), xt, sres=xt.res)

    tiles = [(b, t0) for b in range(C.NSEQ) for t0 in range(0, C.L, NT)]
    prev = None
    for i, (b, t0) in enumerate(tiles):
        run_lanes([proj_task(b, t0, i % 2), chunk_task(*prev) if prev is not None else None])
        prev = (b, t0, i % 2)
    run_lanes([chunk_task(*prev)])
    S.end_phase()


OD_EXT = 3016
NQK = 18


def odd_phase(S, C, j, src, dst):
    odd_proj_phase(S, C, j, src)
    odd_attn_phase(S, C, j, src, dst)


def odd_proj_phase(S, C, j, src):
    layer = 2 * j + 1
    NT = 512
    NSUB = NT // 128
    S.begin_phase()
    PB = Banks(S)
    win = S.sb("win", [128, KC, OD_EXT], BF16)
    gsb = S.sb("gsb", [128, KC])
    stg = [S.sb("stg%d" % i, [128, 754]) for i in range(4)]
    S.dma("sp", gsb, V(C.mix_g[layer], C.wres), sres=gsb.res)
    load_weight_scaled(S, C, win, C.od_win[j], KC, OD_EXT, gsb, stg, 1508, "win")
    gn = S.sb("gn", [128, 4])
    S.dma("sp", gn, V(C.od_gn[j], C.wres), sres=gn.res)
    S.ts("pool", gn[:, 0:1], gn[:, 0:1], 64.0 ** -0.5, ALU.mult, 1.0, ALU.mult)
    S.ts("pool", gn[:, 2:3], gn[:, 2:3], 128.0 ** -0.5, ALU.mult, 1.0, ALU.mult)
    ones = S.sb("ones", [128, 128])
    bd64 = S.sb("bd64", [128, 128])
    S.memset("pool", ones, 1.0)
    S.memset("pool", bd64, 0.0)
    S.memset("pool", bd64[0:64, 0:64], 1.0)
    S.memset("pool", bd64[64:128, 64:128], 1.0)

    xs = [S.sb("x%d" % i, [128, D]) for i in range(2)]
    hn = [S.sb("hn%d" % i, [128, D], BF16) for i in range(2)]
    ssq = [S.sb("ssq%d" % i, [128, 1]) for i in range(2)]
    rstd = [S.sb("rstd%d" % i, [128, 1]) for i in range(2)]
    hnT = [S.sb("hnT%d" % i, [128, KC, NT], BF16) for i in range(2)]
    oT = [S.sb("oT%d" % i, [128, NQK, NT], BF16) for i in range(2)]
    sqb = [S.sb("sqb%d" % i, [128, NT]) for i in range(2)]
    rn = [S.sb("rn%d" % i, [128, NT]) for i in range(2)]
    tokb = [S.sb("tokb%d" % i, [128, 640], BF16) for i in range(2)]
    iwt = [S.sb("iwt%d" % i, [128, 8]) for i in range(2)]

    chunks = []
    for c in range(4):
        chunks.append((c * 128, "n64", 0))
    for c in range(4):
        chunks.append((512 + c * 128, "n64", 1))
    for c in range(4):
        chunks.append((1536 + c * 128, "n128", 2))
    chunks.append((2048, "n128", 3))
    for c in range(4):
        chunks.append((2304 + c * 128, "scale", None))
    chunks.append((2888, "copy", None))

    gsub = 0
    ist = 0
    inorm = 0
    for b in range(C.NSEQ):
        for t0 in range(0, C.L, NT):
            hT_ = hnT[ist % 2]
            o_ = oT[ist % 2]
            for sub in range(NSUB):
                r0 = b * C.L + t0 + sub * 128
                xt = xs[gsub % 2]
                load_rows(S, C, xt, src, (r0 // 128), src.ap[r0:r0 + 128, :])
                jj = gsub % 2
                pt = PB.get()
                rms_to_hnT(S, C, xt, hT_[:, :, sub * 128:(sub + 1) * 128], bf(pt), hn[jj], ssq[jj], rstd[jj])
                gsub += 1
            for ci, (c0, kind, gi) in enumerate(chunks):
                ps = PB.get()
                for kc in range(KC):
                    S.mm(ps[:, 0:NT], win[:, kc, c0:c0 + 128], hT_[:, kc, :], start=(kc == 0), stop=(kc == KC - 1))
                if kind == "scale":
                    S.act(o_[:, ci, :], ps[:, 0:NT], AF.Copy, scale=0.125)
                elif kind == "copy":
                    S.copy("act", o_[:, ci, :], ps[:, 0:NT])
                else:
                    sq_ = sqb[inorm % 2]
                    rn_ = rn[inorm % 2]
                    inorm += 1
                    S.act(sq_, ps[:, 0:NT], AF.Square)
                    pss = PB.get()
                    S.mm(pss[:, 0:NT], bd64 if kind == "n64" else ones, sq_)
                    dim = 64.0 if kind == "n64" else 128.0
                    S.act(rn_, pss[:, 0:NT], AF.Sqrt, bias=C.eps, scale=1.0 / dim)
                    S.op("dve", lambda e, rn_=rn_: e.reciprocal(rn_.ap, rn_.ap), r=[rn_], w=[rn_])
                    S.stt("dve", o_[:, ci, :], ps[:, 0:NT], gn[:, gi:gi + 1], rn_, ALU.mult, ALU.mult)
            for sub in range(NSUB):
                cs = slice(sub * 128, (sub + 1) * 128)
                r0 = t0 + sub * 128
                tb = tokb[sub % 2]
                iw_ = iwt[sub % 2]
                pv = PB.get()
                for kc in range(KC):
                    S.mm(pv, hT_[:, kc, cs], win[:, kc, 1024:1536], start=(kc == 0), stop=(kc == KC - 1))
                p2 = PB.get()
                for kc in range(KC):
                    S.mm(p2[:, 0:128], hT_[:, kc, cs], win[:, kc, 2176:2304], start=(kc == 0), stop=(kc == KC - 1))
                for kc in range(KC):
                    S.mm(p2[:, 128:136], hT_[:, kc, cs], win[:, kc, 2880:2888], start=(kc == 0), stop=(kc == KC - 1))
                S.copy("act", tb[:, 0:512], pv)
                S.copy("act", tb[:, 512:640], p2[:, 0:128])
                S.ts("dve", iw_, p2[:, 128:136], 8.0 ** -0.5, ALU.mult)
                S.dma("sp", C.vtok.v((b, r0 // 128), C.vtok.ap[b, r0:r0 + 128, :]), tb, sres=tb.res)
                S.dma("sp", C.iwd.v((b, r0 // 128), C.iwd.ap[b, r0:r0 + 128, :]), iw_, sres=iw_.res)
            S.dma("sp", C.qkT.v((b, t0 // NT), C.qkT.ap[b, :, :, t0:t0 + NT].rearrange("c p t -> p c t")), o_, sres=o_.res)
            ist += 1
    S.end_phase()


def odd_attn_phase(S, C, j, src, dst):
    layer = 2 * j + 1
    lambda_init = 0.8 - 0.6 * math.exp(-0.3 * layer)
    L = C.L
    NKB = L // 128
    NQ = 512
    NQS = NQ // 128
    TOPK = min(256, L // 4)
    NIT = 18
    S.begin_phase()
    PB1 = Banks(S, 2)
    PB2 = Banks(S, 1)
    PB3 = Banks(S, 1)
    DACC = [S.ps("dacc%d" % i, [128, 512]) for i in range(2)]
    FACC = [S.ps("facc%d" % i, [128, 512]) for i in range(2)]
    wout = S.sb("wout", [128, KC, D], BF16)
    stg = [S.sb("stg%d" % i, [128, 1024]) for i in range(2)]
    load_weight_scaled(S, C, wout, C.od_wout[j], KC, D, None, stg, 1024, "wout")
    S.barrier()
    tmpf = [V(stg[0].ap[:, 0:512], Res("tmpf0"))]
    R = [V(stg[1].ap[:, 0:512], Res("R0")), V(stg[1].ap[:, 512:1024], Res("R1")), V(stg[0].ap[:, 512:1024], Res("R2"))]
    rb = S.sb("rb", [128, 8, 2, 128])
    t31 = S.sb("t31", [128, 8])
    cmT = S.sb("cmT", [128, 128])
    dmask = S.sb("dmask", [128, 128])
    zer = S.sb("zer", [128, 128])
    S.dma("sp", rb, V(C.rb_near, C.wres), sres=rb.res)
    S.dma("sp", t31, V(C.rb_t31.partition_broadcast(128), C.wres), sres=t31.res)
    S.memset("pool", zer, 0.0)
    S.asel(cmT, zer, [[1, 128]], ALU.is_ge, NEG, base=0, cm=-1)
    S.asel(dmask, zer, [[-1, 128]], ALU.is_ge, -1e30, base=0, cm=1)
    rbv = V(rb.ap.rearrange("p h r q -> p h (r q)"), rb.res)
    S.tt("dve", rbv, rbv, bc_last(t31, 256), ALU.subtract)
    S.tt("dve", rb[:, :, 1, :], rb[:, :, 1, :], bc_mid(cmT, 8), ALU.add)
    lf = S.sb("lf", [128, 256])
    lj = S.sb("lj", [128, 64])
    lam = {n: S.sb(n, [128, 1]) for n in ("s01", "s23", "nlam")}
    S.dma("sp", lf, V(C.od_lam[j].partition_broadcast(128), C.wres), sres=lf.res)
    S.memset("dve", lam["s01"], 0.0)
    S.memset("dve", lam["s23"], 0.0)
    S.stt("dve", lj, lf[:, 0:64], 1.0, lf[:, 64:128], ALU.mult, ALU.mult, accum=lam["s01"])
    S.stt("dve", lj, lf[:, 128:192], 1.0, lf[:, 192:256], ALU.mult, ALU.mult, accum=lam["s23"])
    S.act(lam["s01"], lam["s01"], AF.Exp)
    S.act(lam["s23"], lam["s23"], AF.Exp)
    S.tt("dve", lam["nlam"], lam["s23"], lam["s01"], ALU.subtract)
    S.ts("dve", lam["nlam"], lam["nlam"], -lambda_init, ALU.add)
    gsub_ = S.sb("gsubn", [128, 128])
    S.dma("sp", gsub_, V(C.od_gsub[j].partition_broadcast(128), C.wres), sres=gsub_.res)
    S.ts("pool", gsub_, gsub_, 1.0 - lambda_init, ALU.mult, 1.0, ALU.mult)

    dkT = S.sb("dkT", [128, 4, L], BF16)
    skT = S.sb("skT", [128, L], BF16)
    ikT = S.sb("ikT", [128, L], BF16)
    dvA = S.sb("dvA", [128, NKB, 4, 130], BF16)
    svA = S.sb("svA", [128, NKB, 130], BF16)
    S.memset("pool", dvA[:, :, :, 128:130], 1.0)
    S.memset("pool", svA[:, :, 128:130], 1.0)
    dqT = S.sb("dqT", [128, 4, NQ], BF16)
    sqT = S.sb("sqT", [128, 4, NQ], BF16)
    iqT = S.sb("iqT", [128, 4, NQ], BF16)
    iw = S.sb("iw", [128, NQS, 8])
    idx = S.sb("idx", [128, L])
    M = S.sb("M", [128, L], BF16)
    M2 = S.sb("M2", [128, L], BF16)
    MT = S.sb("MT", [128, NKB, 128], BF16)
    PTa = [S.sb("PTa%d" % i, [128, 512], BF16) for i in range(3)]
    PTb = [S.sb("PTb%d" % i, [128, 512], BF16) for i in range(2)]
    xs = [S.sb("x%d" % i, [128, D]) for i in range(1)]
    ytd = [[S.sb("ytd%d_%d" % (a, i), [128, 4, 128], BF16) for i in range(NQS)] for a in range(2)]
    ytf = [[S.sb("ytf%d_%d" % (a, i), [128, 4, 128], BF16) for i in range(NQS)] for a in range(2)]
    yT = S.sb("yT", [128, 8, 128], BF16)
    of0 = S.sb("of0", [128, NQS, 128])
    of = S.sb("of", [128, 128])
    osq = S.sb("osq", [128, 128])
    sc = {n: S.sb(n, [128, 1]) for n in ("rmax", "w0", "lo", "nlo", "nmid", "cnt", "gw", "rec")}
    sd = {n: S.sb(n, [128, 1]) for n in ("rec", "oss", "ors")}
    sc2 = {n: S.sb(n + "2", [128, 1]) for n in ("rec",)}
    ctr = {"R": 0, "Pa": 0, "Pb": 0, "T": 0, "x": 0}

    NQB = L // 128
    NST = (L + NQ - 1) // NQ
    Ms = [M, M2]
    prog = {"s1": 0, "s2": 0, "df": 0}

    def s1_lane(b):
        for qb in range(NQB):
            s0 = (qb // NQS) * NQ
            qs = qb % NQS
            nqs = min(NQS, (L - s0) // 128)
            nq = nqs * 128
            if qs == 0:
                S.dma("sp", iqT[:, :, 0:nq], C.qkT.v((b, "iq", s0), C.qkT.ap[b, 13:17, :, s0:s0 + nq].rearrange("c p t -> p c t")), sres=iqT.res)
                S.dma("sp", iw[:, 0:nqs, :], C.iwd.v((b, "iw", s0), C.iwd.ap[b, s0:s0 + nq, :].rearrange("(s p) e -> p s e", p=128)), sres=iw.res)
                yield
            nk = (qb + 1) * 128
            qc = slice(qs * 128, (qs + 1) * 128)
            if nk > TOPK:
                while prog["s2"] < qb - 1:
                    yield
                Mq = Ms[qb % 2]
                pend = None

                def fma(p):
                    r_, k0, wd, h = p
                    if h == 0:
                        S.ts("dve", idx[:, k0:k0 + wd], r_[:, 0:wd], iw[:, qs, 0:1], ALU.mult)
                    else:
                        S.stt("dve", idx[:, k0:k0 + wd], r_[:, 0:wd], iw[:, qs, h:h + 1], idx[:, k0:k0 + wd], ALU.mult, ALU.add)

                for k0 in range(0, nk, 512):
                    wd = min(512, nk - k0)
                    for h in range(8):
                        pr = slice((h % 2) * 64, (h % 2) * 64 + 64)
                        ps = PB1.get()
                        S.mm(ps[:, 0:wd], iqT[pr, h // 2, qc], ikT[pr, k0:k0 + wd])
                        r_ = R[ctr["R"] % len(R)]
                        ctr["R"] += 1
                        S.act(r_[:, 0:wd], ps[:, 0:wd], AF.Relu)
                        if pend is not None:
                            fma(pend)
                        pend = (r_, k0, wd, h)
                        yield
                fma(pend)
                S.reduce("dve", sc["rmax"], idx[:, 0:nk], ALU.max)
                S.reduce("dve", sc["lo"], idx[:, 0:nk], ALU.min)
                S.tt("dve", sc["w0"], sc["rmax"], sc["lo"], ALU.subtract)
                S.ts("dve", sc["nlo"], sc["lo"], -1.0, ALU.mult)
                S.tt("dve", idx[:, nk - 128:nk], idx[:, nk - 128:nk], dmask, ALU.add)
                yield
                thr = 2.0 * TOPK - nk - 0.5
                for it in range(NIT):
                    hw = 2.0 ** -(it + 1)
                    S.stt("dve", sc["nmid"], sc["w0"], -hw, sc["nlo"], ALU.mult, ALU.add)
                    S.act(Mq[:, 0:nk], idx[:, 0:nk], AF.Sign, bias=sc["nmid"], scale=1.0, accum=sc["cnt"])
                    yield
                    S.stt("dve", sc["gw"], sc["cnt"], thr, sc["w0"], ALU.is_ge, ALU.mult)
                    S.stt("dve", sc["nlo"], sc["gw"], -hw, sc["nlo"], ALU.mult, ALU.add)
                S.ts("dve", sc["lo"], sc["nlo"], -1.0, ALU.mult)
                S.ts("dve", Mq[:, 0:nk], idx[:, 0:nk], sc["lo"], ALU.is_ge)
            prog["s1"] = qb + 1
            yield

    def s2_lane(b):
        for qb in range(NQB):
            st_i = qb // NQS
            s0 = st_i * NQ
            qs = qb % NQS
            nqs = min(NQS, (L - s0) // 128)
            nq = nqs * 128
            if qs == 0:
                while prog["df"] < st_i - 1:
                    yield
                S.dma("sp", sqT[:, :, 0:nq], C.qkT.v((b, "sq", s0), C.qkT.ap[b, 8:12, :, s0:s0 + nq].rearrange("c p t -> p c t")), sres=sqT.res)
                yield
            while prog["s1"] < qb + 1:
                yield
            yb = ytd[st_i % 2]
            nk = (qb + 1) * 128
            qc = slice(qs * 128, (qs + 1) * 128)
            use_topk = nk > TOPK
            Mq = Ms[qb % 2]
            if use_topk:
                for g0 in range(0, qb + 1, 8):
                    ng = min(8, qb + 1 - g0)
                    pm = PB2.get()
                    for i in range(ng):
                        S.tr(bf(pm)[:, i * 128:(i + 1) * 128], Mq[:, (g0 + i) * 128:(g0 + i + 1) * 128], C.identb)
                    S.copy("act", MT[:, g0:g0 + ng, :], v3(bf(pm)[:, 0:ng * 128], ng))
                    yield
            for a in DACC:
                S.memset("dve", a[:, 0:130], 0.0)
                S.memset("dve", a[:, 256:386], 0.0)
            nkb_ = qb + 1

            def st1(kb):
                kc_ = slice(kb * 128, (kb + 1) * 128)
                ps = PB2.get()
                S.mm(ps, skT[:, kc_], sqT[:, :, qc])
                pt_ = PTa[kb % 3]
                rel = kb - (qb - 1)
                if rel >= 0:
                    t_ = tmpf[ctr["T"] % len(tmpf)]
                    ctr["T"] += 1
                    S.tt("dve", v3(t_), v3(ps), rb[:, 4:8, rel, :], ALU.add)
                    S.act(pt_, t_, AF.Exp)
                else:
                    S.act(pt_, ps, AF.Exp)

            def st2(kb):
                if use_topk:
                    pt_ = PTa[kb % 3]
                    S.tt("dve", v3(pt_), v3(pt_), bc_mid(MT[:, kb, :], 4), ALU.mult)

            def st3(kb):
                pt_ = PTa[kb % 3]
                for h in range(4):
                    S.op("pe", lambda e, h=h, pt_=pt_, kb=kb, qb=qb: e.matmul(DACC[h // 2].ap[:, (h % 2) * 256:(h % 2) * 256 + 129], pt_.ap[:, h * 128:(h + 1) * 128],
                                                                             svA.ap[:, kb, 0:129], start=False, stop=(kb == qb), skip_group_check=True),
                         r=[pt_, svA], w=[DACC[h // 2]], inc=(h == 3))

            for t in range(nkb_ + 2):
                if 0 <= t - 2 < nkb_:
                    st3(t - 2)
                if 0 <= t - 1 < nkb_:
                    st2(t - 1)
                if t < nkb_:
                    st1(t)
                yield
            for h in range(4):
                a = DACC[h // 2]
                c0 = (h % 2) * 256
                S.op("dve", lambda e, a=a, c0=c0: e.reciprocal(sc2["rec"].ap, a.ap[:, c0 + 128:c0 + 129]), r=[a], w=[sc2["rec"]])
                S.ts("dve", yb[qs][:, h, :], a[:, c0:c0 + 128], sc2["rec"], ALU.mult)
            prog["s2"] = qb + 1
            yield

    def df_lane(b):
        for st_i in range(NST):
            s0 = st_i * NQ
            sblk = s0 // 128
            nqs = min(NQS, (L - s0) // 128)
            nq = nqs * 128
            yf = ytf[st_i % 2]
            yd = ytd[st_i % 2]
            S.dma("sp", dqT[:, :, 0:nq], C.qkT.v((b, "dq", s0), C.qkT.ap[b, 0:4, :, s0:s0 + nq].rearrange("c p t -> p c t")), sres=dqT.res)
            yield
            last_kb = sblk + nqs - 1
            for h in range(4):
                for m in range(2):
                    pr = slice(m * 64, m * 64 + 64)
                    for a in FACC:
                        S.memset("dve", a[:, 0:130], 0.0)
                        S.memset("dve", a[:, 256:386], 0.0)
                    def d1(kb):
                        kc_ = slice(kb * 128, (kb + 1) * 128)
                        qlo = max(0, kb - sblk)
                        ncol = (nqs - qlo) * 128
                        ps = PB3.get()
                        S.mm(ps[:, 0:ncol], dkT[pr, h, kc_], dqT[pr, h, qlo * 128:nqs * 128])
                        for qs in (kb - sblk, kb - sblk + 1):
                            if 0 <= qs < nqs:
                                rel = kb - (sblk + qs - 1)
                                cc = slice((qs - qlo) * 128, (qs - qlo + 1) * 128)
                                S.tt("dve", ps[:, cc], ps[:, cc], rb[:, h, rel, :], ALU.add)
                        pt_ = PTb[kb % 2]
                        S.act(pt_[:, 0:ncol], ps[:, 0:ncol], AF.Exp)

                    def d2(kb):
                        qlo = max(0, kb - sblk)
                        pt_ = PTb[kb % 2]
                        for qs in range(qlo, nqs):
                            cc = slice((qs - qlo) * 128, (qs - qlo + 1) * 128)
                            S.op("pe", lambda e, qs=qs, cc=cc, pt_=pt_, kb=kb, h=h, sblk=sblk: e.matmul(
                                FACC[qs // 2].ap[:, (qs % 2) * 256:(qs % 2) * 256 + 129], pt_.ap[:, cc], dvA.ap[:, kb, h, 0:129],
                                start=False, stop=(kb == sblk + qs), skip_group_check=True), r=[pt_, dvA], w=[FACC[qs // 2]],
                                inc=(qs == nqs - 1))

                    for t in range(last_kb + 2):
                        if 0 <= t - 1 <= last_kb:
                            d2(t - 1)
                        if t <= last_kb:
                            d1(t)
                        yield
                    for qs in range(nqs):
                        a = FACC[qs // 2]
                        c0 = (qs % 2) * 256
                        S.op("dve", lambda e, a=a, c0=c0: e.reciprocal(sd["rec"].ap, a.ap[:, c0 + 128:c0 + 129]), r=[a], w=[sd["rec"]])
                        if m == 0:
                            S.ts("dve", of0[:, qs, :], a[:, c0:c0 + 128], sd["rec"], ALU.mult)
                        else:
                            S.tt("dve", sd["rec"], sd["rec"], lam["nlam"], ALU.mult)
                            S.stt("dve", of, a[:, c0:c0 + 128], sd["rec"], of0[:, qs, :], ALU.mult, ALU.add)
                            S.act(osq, of, AF.Square, accum=sd["oss"])
                            S.act(sd["ors"], sd["oss"], AF.Sqrt, bias=C.eps, scale=1.0 / 128)
                            S.op("dve", lambda e: e.reciprocal(sd["ors"].ap, sd["ors"].ap), r=[sd["ors"]], w=[sd["ors"]])
                            S.stt("dve", yf[qs][:, h, :], of, sd["ors"], gsub_, ALU.mult, ALU.mult)
                    yield
            while prog["s2"] < min(NQB, (st_i + 1) * NQS):
                yield
            for qs in range(nqs):
                r0 = b * L + s0 + qs * 128
                xt = xs[0]
                load_rows(S, C, xt, src, (r0 // 128), src.ap[r0:r0 + 128, :])
                pt2 = PB3.get()
                for c in range(4):
                    S.tr(bf(pt2)[:, c * 128:(c + 1) * 128], yf[qs][:, c, :], C.identb)
                for c in range(4):
                    S.tr(bf(pt2)[:, (4 + c) * 128:(5 + c) * 128], yd[qs][:, c, :], C.identb)
                yield
                S.copy("act", flat(yT), bf(pt2))
                yield
                for nh in range(2):
                    pso = PB3.get()
                    for kc in range(KC):
                        S.mm(pso, yT[:, kc, :], wout[:, kc, nh * 512:(nh + 1) * 512], start=(kc == 0), stop=(kc == KC - 1))
                    yield
                    S.tt("dve", xt[:, nh * 512:(nh + 1) * 512], xt[:, nh * 512:(nh + 1) * 512], pso, ALU.add)
                S.dma("sp", dst.v((r0 // 128), dst.ap[r0:r0 + 128, :]), xt, sres=xt.res)
                yield
            prog["df"] = st_i + 1
            yield

    for b in range(C.NSEQ):
        S.dma("sp", dkT, C.qkT.v((b, "dk"), C.qkT.ap[b, 4:8, :, :].rearrange("c p t -> p c t")), sres=dkT.res)
        S.dma("sp", skT, C.qkT.v((b, "sk"), C.qkT.ap[b, 12, :, :]), sres=skT.res)
        S.dma("sp", ikT, C.qkT.v((b, "ik"), C.qkT.ap[b, 17, :, :]), sres=ikT.res)
        for kb in range(NKB):
            S.dma("sp", dvA[:, kb, :, 0:128], C.vtok.v((b, "dv", kb), C.vtok.ap[b, kb * 128:(kb + 1) * 128, 0:512].rearrange("p (h d) -> p h d", h=4)), sres=dvA.res)
        S.dma("sp", svA[:, :, 0:128], C.vtok.v((b, "sv"), C.vtok.ap[b, :, 512:640].rearrange("(k p) d -> p k d", p=128)), sres=svA.res)
        prog["s1"] = prog["s2"] = prog["df"] = 0
        run_lanes([s1_lane(b), s2_lane(b), df_lane(b)])
    S.end_phase()


W_SPECS = {
    "ffn_g": [4, 128, KC],
    "ffn_cw": [4, 128, 2 * FC, 3],
    "ffn_cb": [4, 128, 2 * FC],
    "ffn_wup": [4, D, 2 * DFF],
    "ffn_wdn": [4, DFF, D],
    "mix_g": [4, 128, KC],
    "ev_win": [2, D, EVEN_IN],
    "ev_wout": [2, D, D],
    "gm_wT": [2, 128, 4, 128],
    "gm_b": [2, 1, 512],
    "gdn_cw": [2, 128, 12, 4],
    "gdn_alog": [2, 1, 4],
    "gdn_dtb": [2, 1, 4],
    "gdn_ng": [2, 1, 128],
    "od_win": [2, D, OD_EXT],
    "od_wout": [2, D, D],
    "od_gn": [2, 128, 4],
    "od_lam": [2, 1, 256],
    "od_gsub": [2, 1, 128],
    "rb_near": [128, 8, 2, 128],
    "rb_t31": [1, 8],
}


def _rel_bucket_np(dist):
    n = np.maximum(dist, 0)
    nf = np.maximum(n, 16).astype(np.float32)
    far = 16 + (np.log(nf / np.float32(16)) / np.float32(math.log(128 / 16)) * np.float32(16)).astype(np.int32)
    return np.where(n < 16, n, np.minimum(far, 31))


def prep_weights(inp):
    f = lambda a: np.ascontiguousarray(np.asarray(a, dtype=np.float32))
    w = {}
    w["ffn_g"] = f(inp["ffn_norm_g"].reshape(4, KC, 128).transpose(0, 2, 1))
    w["ffn_cw"] = f(inp["ffn_conv_w"].reshape(4, 3, 2 * FC, 128).transpose(0, 3, 2, 1))
    w["ffn_cb"] = f(inp["ffn_conv_b"].reshape(4, 2 * FC, 128).transpose(0, 2, 1))
    w["ffn_wup"] = f(inp["ffn_w_up"])
    w["ffn_wdn"] = f(inp["ffn_w_down"])
    w["mix_g"] = f(inp["mix_norm_g"].reshape(4, KC, 128).transpose(0, 2, 1))
    w["ev_win"] = f(inp["ev_w_in"])
    w["ev_wout"] = f(inp["ev_w_out"])
    w["gm_wT"] = f(inp["gmlp_w_s"].transpose(0, 3, 1, 2))
    w["gm_b"] = f(inp["gmlp_b_s"].reshape(2, 1, 512))
    w["gdn_cw"] = f(inp["gdn_conv_w"].reshape(2, 4, 12, 128).transpose(0, 3, 2, 1))
    w["gdn_alog"] = f(inp["gdn_a_log"].reshape(2, 1, 4))
    w["gdn_dtb"] = f(inp["gdn_dt_bias"].reshape(2, 1, 4))
    w["gdn_ng"] = f(inp["gdn_norm_g"].reshape(2, 1, 128))
    ow = np.asarray(inp["od_w_in"], dtype=np.float32)
    w["od_win"] = f(np.concatenate([ow, ow[:, :, 2816:2880], ow[:, :, 2816:2880]], axis=2))
    w["od_wout"] = f(inp["od_w_out"])
    gq = np.asarray(inp["diff_q_norm_g"], dtype=np.float32)
    gk = np.asarray(inp["diff_k_norm_g"], dtype=np.float32)
    w["od_gn"] = f(np.stack([np.tile(gq, (1, 2)), np.tile(gk, (1, 2)), np.asarray(inp["dsa_q_norm_g"]), np.asarray(inp["dsa_k_norm_g"])], axis=2))
    w["od_lam"] = f(inp["diff_lambda"].reshape(2, 1, 256))
    w["od_gsub"] = f(inp["diff_sub_norm_g"].reshape(2, 1, 128))
    kk = np.arange(128)[:, None]
    qq = np.arange(128)[None, :]
    tab = np.asarray(inp["rel_bias"], dtype=np.float32)
    near = np.zeros((128, 8, 2, 128), np.float32)
    for rel in range(2):
        dist = qq - kk + (128 if rel == 0 else 0)
        near[:, :, rel, :] = tab[_rel_bucket_np(dist)].transpose(0, 2, 1)
    w["rb_near"] = f(near)
    w["rb_t31"] = f(tab[31:32, :])
    return w


def build_program(L, NSEQ, plan):
    nc = bass.Bass("TRN2", target_bir_lowering=False)
    NTOK = L * NSEQ
    C = Ctx()
    C.L, C.NSEQ, C.NTOK = L, NSEQ, NTOK
    x = nc.dram_tensor("x", [NTOK, D], F32, kind="ExternalInput").ap()
    y = nc.dram_tensor("y", [NTOK, D], F32, kind="ExternalOutput").ap()
    for name, shape in W_SPECS.items():
        setattr(C, name, nc.dram_tensor(name, shape, F32, kind="ExternalInput").ap())
    C.wres = Res("weights")
    xdt = DT(x, "x")
    ydt = DT(y, "y")
    C.qkT = DT(nc.dram_tensor("qkT", [NSEQ, NQK, 128, L], BF16, kind="Internal").ap(), "qkT")
    C.vtok = DT(nc.dram_tensor("vtok", [NSEQ, L, 640], BF16, kind="Internal").ap(), "vtok")
    C.iwd = DT(nc.dram_tensor("iwd", [NSEQ, L, 8], F32, kind="Internal").ap(), "iwd")
    with ExitStack() as es:
        S = Sched(nc, es)
        ct = es.enter_context(nc.sbuf_tensor("identb", [128, 128], BF16))
        C.identb = V(ct[:], Res("identb"))
        ct = es.enter_context(nc.sbuf_tensor("identf", [128, 128], F32))
        C.identf = V(ct[:], Res("identf"))
        ct = es.enter_context(nc.sbuf_tensor("eps", [128, 1], F32))
        C.eps = V(ct[:], Res("eps"))
        S.memset("pool", C.identf, 1.0)
        S.asel(C.identf, C.identf, [[-1, 128]], ALU.is_equal, 0.0, base=0, cm=1)
        S.copy("pool", C.identb, C.identf)
        S.memset("pool", C.eps, EPS)
        src = xdt
        for kind, idx in plan:
            if kind == "ffn":
                ffn_phase(S, C, idx, src, ydt)
            elif kind == "even":
                even_phase(S, C, idx, src, ydt)
            elif kind == "odd":
                odd_phase(S, C, idx, src, ydt)
            src = ydt
        S.barrier()
        print("program: ops=%d waits=%d dma_sems=%d" % (S.nops, S.nwaits, S.ndsem))
    return nc


FULL_PLAN = [("even", 0), ("ffn", 0), ("odd", 0), ("ffn", 1), ("even", 1), ("ffn", 2), ("odd", 1), ("ffn", 3)]


N_CORES = 8
_PROG = {}


def kernel(**inputs):
    x = np.asarray(inputs["x"], dtype=np.float32)
    B, L, Dm = x.shape
    nseq = B // N_CORES
    key = (L, nseq)
    if key not in _PROG:
        _PROG[key] = build_program(L, nseq, FULL_PLAN)
    nc = _PROG[key]
    w = prep_weights(inputs)
    in_maps = []
    for c in range(N_CORES):
        m = {"x": np.ascontiguousarray(x[c * nseq:(c + 1) * nseq].reshape(nseq * L, Dm))}
        m.update(w)
        in_maps.append(m)
    res = run_bass_kernel_spmd(nc, in_maps, core_ids=list(range(N_CORES)))
    out = np.concatenate([np.asarray(r["y"]).reshape(nseq, L, Dm) for r in res.results], axis=0)
    return out.astype(np.float32)
```

```python
import math
from contextlib import ExitStack

import numpy as np
import concourse.bass as bass
import concourse.mybir as mybir
from concourse.bass_utils import run_bass_kernel_spmd

F32 = mybir.dt.float32
BF16 = mybir.dt.bfloat16
AF = mybir.ActivationFunctionType
ALU = mybir.AluOpType
AX = mybir.AxisListType


class Res:
    __slots__ = ("name", "last_w", "reads", "dsem", "dcount", "excl")

    def __init__(self, name="r"):
        self.name = name
        self.excl = False
        self.last_w = None
        self.reads = []
        self.dsem = None
        self.dcount = 0


class V:
    __slots__ = ("ap", "res")

    def __init__(self, ap, res):
        self.ap = ap
        self.res = res

    def __getitem__(self, idx):
        return V(self.ap[idx], self.res)

    def r(self, res):
        return V(self.ap, res)


def _res_of(xs):
    out = []
    for x in xs:
        if x is None:
            continue
        out.append(x.res if isinstance(x, V) else x)
    return out


class Sched:
    ENGS = ("pe", "act", "dve", "pool", "sp")

    def __init__(self, nc, es):
        self.nc = nc
        self.es = es
        self.eng = {"pe": nc.tensor, "act": nc.scalar, "dve": nc.vector, "pool": nc.gpsimd, "sp": nc.sync}
        self.sem = {e: es.enter_context(nc.semaphore("sem_" + e)) for e in self.ENGS}
        self.cnt = {e: 0 for e in self.ENGS}
        self.known = {e: {} for e in self.ENGS}
        self.dpool = []
        self.dlive = []
        self.ndsem = 0
        self.nwaits = 0
        self.nops = 0
        self.phase_es = None
        self.uid = 0
        import os
        self.limit = int(os.environ["OPLIMIT"]) if "OPLIMIT" in os.environ else None

    def begin_phase(self):
        self.phase_es = ExitStack()

    def end_phase(self):
        self.barrier()
        for r in self.dlive:
            self.dpool.append((r.dsem, r.dcount))
            r.dsem = None
        self.dlive = []
        self.phase_es.close()
        self.phase_es = None

    def sb(self, name, shape, dt=F32):
        self.uid += 1
        name = "%s_u%d" % (name, self.uid)
        t = self.phase_es.enter_context(self.nc.sbuf_tensor(name, list(shape), dt))
        return V(t[:], Res(name))

    def ps(self, name, shape, dt=F32):
        self.uid += 1
        name = "%s_u%d" % (name, self.uid)
        t = self.phase_es.enter_context(self.nc.psum_tensor(name, list(shape), dt))
        rs = Res(name)
        rs.excl = True
        return V(t[:], rs)

    def _collect(self, eng, r, w, strict):
        waits = {}

        def add(ev, same_ok):
            if ev is None:
                return
            sem, val, src = ev
            if not strict and src == eng and (same_ok or eng == "pe"):
                return
            k = id(sem)
            if k not in waits or waits[k][1] < val:
                waits[k] = (sem, val)

        for res in r:
            add(res.last_w, False)
            if res.excl:
                for ev in res.reads:
                    add(ev, True)
        for res in w:
            add(res.last_w, True)
            for ev in res.reads:
                add(ev, True)
        return waits

    def _emit_waits(self, eng, waits):
        kn = self.known[eng]
        e = self.eng[eng]
        for k, (sem, val) in waits.items():
            if kn.get(k, 0) >= val:
                continue
            kn[k] = val
            e.wait_ge(sem, val)
            self.nwaits += 1

    def op(self, eng, fn, r=(), w=(), inc=True):
        if self.limit is not None and self.nops >= self.limit:
            return None
        r = _res_of(r)
        w = _res_of(w)
        self._emit_waits(eng, self._collect(eng, r, w, False))
        ins = fn(self.eng[eng])
        self.nops += 1
        if inc:
            self.cnt[eng] += 1
            ins.then_inc(self.sem[eng], 1)
            ev = (self.sem[eng], self.cnt[eng], eng)
        else:
            assert eng == "pe"
            ev = (self.sem[eng], self.cnt[eng] + 1, eng)
        for res in r:
            res.reads.append(ev)
        for res in w:
            res.last_w = ev
            res.reads = []
        return ins

    def dma(self, q, out, in_, sres=None, **kw):
        if self.limit is not None and self.nops >= self.limit:
            return None
        sr = sres if sres is not None else out.res
        if sr.dsem is None:
            if self.dpool:
                sr.dsem, sr.dcount = self.dpool.pop()
            else:
                sr.dsem = self.es.enter_context(self.nc.semaphore("dsem%d" % self.ndsem))
                sr.dcount = 0
                self.ndsem += 1
            self.dlive.append(sr)
        waits = self._collect(q, [in_.res], [out.res], True)
        k = id(sr.dsem)
        if sr.dcount > 0 and (k not in waits or waits[k][1] < sr.dcount):
            waits[k] = (sr.dsem, sr.dcount)
        self._emit_waits(q, waits)
        ins = self.eng[q].dma_start(out=out.ap, in_=in_.ap, **kw)
        sr.dcount += 16
        ins.then_inc(sr.dsem, 16)
        self.nops += 1
        ev = (sr.dsem, sr.dcount, "dma")
        in_.res.reads.append(ev)
        out.res.last_w = ev
        out.res.reads = []
        return ins

    def barrier(self):
        evs = [(self.sem[e], self.cnt[e]) for e in self.ENGS if self.cnt[e] > 0]
        evs += [(r.dsem, r.dcount) for r in self.dlive if r.dcount > 0]
        for e in self.ENGS:
            kn = self.known[e]
            for sem, val in evs:
                if sem is self.sem[e]:
                    continue
                if kn.get(id(sem), 0) >= val:
                    continue
                kn[id(sem)] = val
                self.eng[e].wait_ge(sem, val)
                self.nwaits += 1

    def mm(self, out, lhsT, rhs, start=True, stop=True, inc=None, **kw):
        if inc is None:
            inc = bool(stop)
        return self.op("pe", lambda e: e.matmul(out.ap, lhsT.ap, rhs.ap, start=start, stop=stop, **kw),
                       r=[lhsT, rhs], w=[out], inc=inc)

    def tr(self, out, in_, ident):
        return self.op("pe", lambda e: e.transpose(out.ap, in_.ap, ident.ap), r=[in_, ident], w=[out])

    def act(self, out, in_, func, bias=None, scale=None, accum=None, eng="act"):
        kw = {}
        rr = [in_]
        ww = [out]
        if bias is not None:
            if isinstance(bias, V):
                kw["bias"] = bias.ap
                rr.append(bias)
            else:
                kw["bias"] = bias
        if scale is not None:
            if isinstance(scale, V):
                kw["scale"] = scale.ap
                rr.append(scale)
            else:
                kw["scale"] = scale
        if accum is not None:
            kw["accum_out"] = accum.ap
            ww.append(accum)
        return self.op("act", lambda e: e.activation(out.ap, in_.ap, func, **kw), r=rr, w=ww)

    def tt(self, eng, out, a, b, op):
        return self.op(eng, lambda e: e.tensor_tensor(out.ap, a.ap, b.ap, op), r=[a, b], w=[out])

    def ts(self, eng, out, a, s1, op0, s2=None, op1=None, accum=None):
        rr = [a]
        ww = [out]
        a1 = s1
        a2 = s2
        if isinstance(s1, V):
            rr.append(s1)
            a1 = s1.ap
        if isinstance(s2, V):
            rr.append(s2)
            a2 = s2.ap
        kw = {}
        if op1 is not None:
            kw["op1"] = op1
        if accum is not None:
            kw["accum_out"] = accum.ap
            ww.append(accum)
        return self.op(eng, lambda e: e.tensor_scalar(out.ap, a.ap, a1, a2, op0, **kw), r=rr, w=ww)

    def stt(self, eng, out, a, s, b, op0, op1, accum=None):
        rr = [a, b]
        ww = [out]
        sc = s
        if isinstance(s, V):
            rr.append(s)
            sc = s.ap
        kw = {}
        if accum is not None:
            kw["accum_out"] = accum.ap
            ww.append(accum)
        return self.op(eng, lambda e: e.scalar_tensor_tensor(out.ap, a.ap, sc, b.ap, op0, op1, **kw), r=rr, w=ww)

    def copy(self, eng, out, in_):
        if eng == "act":
            return self.op("act", lambda e: e.copy(out.ap, in_.ap), r=[in_], w=[out])
        return self.op(eng, lambda e: e.tensor_copy(out.ap, in_.ap), r=[in_], w=[out])

    def memset(self, eng, out, val):
        return self.op(eng, lambda e: e.memset(out.ap, val), w=[out])

    def reduce(self, eng, out, in_, op, axis=None):
        ax = AX.X if axis is None else axis
        return self.op(eng, lambda e: e.tensor_reduce(out.ap, in_.ap, ax, op), r=[in_], w=[out])

    def asel(self, out, in_, pattern, cmp, fill, base=0, cm=0):
        return self.op("pool", lambda e: e.affine_select(out.ap, in_.ap, pattern=pattern, compare_op=cmp, fill=fill,
                                                         base=base, channel_multiplier=cm), r=[in_], w=[out])


D = 1024
KC = 8
DFF = 2816
FC = 22
EPS = 1e-6
EVEN_IN = 3080
ODD_IN = 2888
NEG = -30000.0


class DT:
    def __init__(self, ap, name):
        self.ap = ap
        self.name = name
        self.res = {}

    def v(self, key, ap):
        if key not in self.res:
            self.res[key] = Res("%s_%s" % (self.name, str(key)))
        return V(ap, self.res[key])


class Ctx:
    pass


def load_rows(S, C, dst, src_dt, key, ap, q="sp"):
    S.dma(q, dst, src_dt.v(key, ap), sres=dst.res)


def rms_to_hnT(S, C, xt, hnT_cols, ptr, hn, ssq, rstd):
    S.act(hn, xt, AF.Square, accum=ssq)
    S.act(rstd, ssq, AF.Sqrt, bias=C.eps, scale=1.0 / D)
    S.op("dve", lambda e: e.reciprocal(rstd.ap, rstd.ap), r=[rstd], w=[rstd])
    S.ts("pool", hn, xt, rstd, ALU.mult, 1.0, ALU.mult)
    for kc in range(KC):
        S.tr(ptr[:, kc * 128:(kc + 1) * 128], hn[:, kc * 128:(kc + 1) * 128], C.identb)
    S.copy("act", hnT_cols, V(ptr.ap.rearrange("p (k t) -> p k t", k=KC), ptr.res))


def load_weight_scaled(S, C, wsb, w_dram_ap, nrows_chunks, ncols, gcol, stg, colchunk, name):
    i = 0
    width = stg[0].ap.shape[1]
    colchunk = min(colchunk, width)
    for rc in range(nrows_chunks):
        for c0 in range(0, ncols, colchunk):
            c1 = min(ncols, c0 + colchunk)
            st = stg[i % len(stg)]
            S.dma(("sp", "act")[i % 2], st[:, 0:c1 - c0], V(w_dram_ap[rc * 128:(rc + 1) * 128, c0:c1], C.wres), sres=st.res)
            eng = ("dve", "act", "dve", "pool")[i % 4]
            if gcol is None:
                S.copy(eng, wsb[:, rc, c0:c1], st[:, 0:c1 - c0])
            elif eng == "act":
                S.act(wsb[:, rc, c0:c1], st[:, 0:c1 - c0], AF.Copy, scale=gcol[:, rc:rc + 1])
            else:
                S.ts(eng, wsb[:, rc, c0:c1], st[:, 0:c1 - c0], gcol[:, rc:rc + 1], ALU.mult, 1.0, ALU.mult)
            i += 1


def ffn_phase(S, C, layer, src, dst):
    NT = 256
    NSUB = NT // 128
    S.begin_phase()
    wup = S.sb("wup", [128, KC, 2 * DFF], BF16)
    wdn = S.sb("wdn", [128, FC, D], BF16)
    gsb = S.sb("gsb", [128, KC])
    cw = S.sb("cw", [128, 2 * FC, 3])
    cb = S.sb("cb", [128, 2 * FC])
    stg = [S.sb("stg%d" % i, [128, 704]) for i in range(4)]
    S.dma("sp", gsb, V(C.ffn_g[layer], C.wres), sres=gsb.res)
    S.dma("sp", cw, V(C.ffn_cw[layer], C.wres), sres=cw.res)
    S.dma("sp", cb, V(C.ffn_cb[layer], C.wres), sres=cb.res)
    load_weight_scaled(S, C, wup, C.ffn_wup[layer], KC, 2 * DFF, gsb, stg, 1408, "wup")
    load_weight_scaled(S, C, wdn, C.ffn_wdn[layer], FC, D, None, stg, 1024, "wdn")

    xs = [S.sb("x%d" % i, [128, D]) for i in range(4)]
    hn = [S.sb("hn%d" % i, [128, D], BF16) for i in range(2)]
    ssq = [S.sb("ssq%d" % i, [128, 1]) for i in range(2)]
    rstd = [S.sb("rstd%d" % i, [128, 1]) for i in range(2)]
    hnT = [S.sb("hnT%d" % i, [128, KC, NT], BF16) for i in range(2)]
    hT = S.sb("hT", [128, FC, NT], BF16)
    hal = S.sb("hal", [128, 2 * FC, 2])
    rr = [S.sb("rr%d" % i, [128, NT + 2]) for i in range(4)]
    acc = [S.sb("acc%d" % i, [128, NT]) for i in range(4)]
    sg = [S.sb("sg%d" % i, [128, NT]) for i in range(2)]
    ptr = [S.ps("ptr%d" % i, [128, D], BF16) for i in range(2)]
    pup = [S.ps("pup%d" % i, [128, 512])[:, 0:NT] for i in range(4)]
    pdn = [S.ps("pdn%d" % i, [128, 512]) for i in range(2)]

    gsub = 0
    ipair = 0
    for b in range(C.NSEQ):
        for t0 in range(0, C.L, NT):
            sti = t0 // NT
            hT_ = hnT[(b * (C.L // NT) + sti) % 2]
            xsl = []
            for sub in range(NSUB):
                r0 = b * C.L + t0 + sub * 128
                xt = xs[gsub % 4]
                xsl.append((xt, r0))
                load_rows(S, C, xt, src, (r0 // 128), src.ap[r0:r0 + 128, :])
                j = gsub % 2
                rms_to_hnT(S, C, xt, hT_[:, :, sub * 128:(sub + 1) * 128], ptr[j], hn[j], ssq[j], rstd[j])
                gsub += 1
            halves = [(c, which) for c in range(FC) for which in range(2)]
            nh_ = len(halves)

            def fA(n):
                c, which = halves[n]
                ch = c + which * FC
                k = n % 4
                ps = pup[k]
                for kc in range(KC):
                    S.mm(ps, wup[:, kc, ch * 128:(ch + 1) * 128], hT_[:, kc, :], start=(kc == 0), stop=(kc == KC - 1))
                r = rr[k]
                a = acc[k]
                if t0 == 0:
                    S.memset("pool", r[:, 0:2], 0.0)
                else:
                    S.copy("pool", r[:, 0:2], hal[:, ch, :])
                S.copy("act", r[:, 2:NT + 2], ps)
                S.act(a, ps, AF.Identity, bias=cb[:, ch:ch + 1], scale=cw[:, ch, 2:3])
                S.copy("pool", hal[:, ch, :], r[:, NT:NT + 2])

            def fB(n):
                c, which = halves[n]
                ch = c + which * FC
                k = n % 4
                r = rr[k]
                a = acc[k]
                S.stt("dve", a, r[:, 1:NT + 1], cw[:, ch, 1:2], a, ALU.mult, ALU.add)
                S.stt("dve", a, r[:, 0:NT], cw[:, ch, 0:1], a, ALU.mult, ALU.add)

            def fC(p):
                S.act(sg[p % 2], acc[(2 * p) % 4], AF.Silu)

            def fD(p):
                S.tt("dve", hT[:, p, :], sg[p % 2], acc[(2 * p + 1) % 4], ALU.mult)

            for n in range(nh_ + 3):
                if n < nh_:
                    fA(n)
                if 0 <= n - 1 < nh_:
                    fB(n - 1)
                if n >= 2 and n % 2 == 0 and (n - 2) // 2 < FC:
                    fC((n - 2) // 2)
                if n >= 3 and n % 2 == 1 and (n - 3) // 2 < FC:
                    fD((n - 3) // 2)
            for sub in range(NSUB):
                xt, r0 = xsl[sub]
                for nh in range(2):
                    ps2 = pdn[nh]
                    for fc in range(FC):
                        S.mm(ps2, hT[:, fc, sub * 128:(sub + 1) * 128], wdn[:, fc, nh * 512:(nh + 1) * 512],
                             start=(fc == 0), stop=(fc == FC - 1))
                    S.tt("dve", xt[:, nh * 512:(nh + 1) * 512], xt[:, nh * 512:(nh + 1) * 512], ps2, ALU.add)
                S.dma("sp", dst.v((r0 // 128), dst.ap[r0:r0 + 128, :]), xt, sres=xt.res)
    S.end_phase()


def flat(v):
    return V(v.ap.rearrange("p h j -> p (h j)"), v.res)


def v3(v, h=4):
    return V(v.ap.rearrange("p (h j) -> p h j", h=h), v.res)


def bc_last(v, n):
    H = v.ap.shape[1]
    return V(v.ap.unsqueeze(2).to_broadcast([128, H, n]), v.res)


def bc_mid(v, h):
    n = v.ap.shape[1]
    return V(v.ap.unsqueeze(1).to_broadcast([128, h, n]), v.res)


class Banks:
    def __init__(self, S, n=8):
        self.b = [S.ps("bank%d" % i, [128, 512]) for i in range(n)]
        self.i = 0

    def get(self):
        v = self.b[self.i % len(self.b)]
        self.i += 1
        return v


def bf(v):
    return V(v.ap.bitcast(BF16), v.res)


F32R = mybir.dt.float32r


def r32(v):
    return V(v.ap.bitcast(F32R), v.res)


def run_lanes(gens):
    gens = [g for g in gens if g is not None]
    while gens:
        for g in list(gens):
            try:
                next(g)
            except StopIteration:
                gens.remove(g)


def even_phase(S, C, j, src, dst):
    layer = 2 * j
    NT = 256
    NSUB = NT // 128
    H = 4
    S.begin_phase()
    PB = Banks(S)
    win = S.sb("win", [128, KC, EVEN_IN], BF16)
    wout = S.sb("wout", [128, KC, D], BF16)
    gsb = S.sb("gsb", [128, KC])
    stg = [S.sb("stg%d" % i, [128, 770]) for i in range(4)]
    S.dma("sp", gsb, V(C.mix_g[layer], C.wres), sres=gsb.res)
    load_weight_scaled(S, C, win, C.ev_win[j], KC, EVEN_IN, gsb, stg, 1540, "win")
    load_weight_scaled(S, C, wout, C.ev_wout[j], KC, D, None, stg, 1024, "wout")
    wTm = S.sb("wTm", [128, H, 128], BF16)
    wTf = S.sb("wTf", [128, H, 128])
    brow = S.sb("brow", [1, 512])
    cw = S.sb("gcw", [128, 12, 4])
    alog = S.sb("alog", [128, 4])
    dtb = S.sb("dtb", [128, 4])
    nega = S.sb("nega", [128, 4])
    gng = S.sb("gng", [128, 128])
    S.dma("sp", wTf, V(C.gm_wT[j], C.wres), sres=wTf.res)
    S.dma("sp", brow, V(C.gm_b[j], C.wres), sres=brow.res)
    S.dma("sp", cw, V(C.gdn_cw[j], C.wres), sres=cw.res)
    S.dma("sp", alog, V(C.gdn_alog[j].partition_broadcast(128), C.wres), sres=alog.res)
    S.dma("sp", dtb, V(C.gdn_dtb[j].partition_broadcast(128), C.wres), sres=dtb.res)
    S.dma("sp", gng, V(C.gdn_ng[j].partition_broadcast(128), C.wres), sres=gng.res)
    ones = S.sb("ones", [128, 128])
    tri = S.sb("tri", [128, 128])
    ntri = S.sb("ntri", [128, 128])
    strict = S.sb("strict", [128, 128])
    incl = S.sb("incl", [128, 128])
    S.memset("pool", ones, 1.0)
    S.asel(tri, ones, [[1, 128]], ALU.is_ge, 0.0, base=0, cm=-1)
    S.ts("pool", ntri, tri, -1.0, ALU.mult, 1.0, ALU.mult)
    S.asel(strict, ones, [[-1, 128]], ALU.is_ge, 0.0, base=-1, cm=1)
    S.asel(incl, ones, [[-1, 128]], ALU.is_ge, 0.0, base=0, cm=1)
    S.tt("pool", wTm, wTf, bc_mid(tri, H), ALU.mult)
    S.act(nega, alog, AF.Exp)
    S.ts("pool", nega, nega, -1.0, ALU.mult, 1.0, ALU.mult)

    xs = [S.sb("x%d" % i, [128, D]) for i in range(4)]
    hn = [S.sb("hn%d" % i, [128, D], BF16) for i in range(2)]
    ssq = [S.sb("ssq%d" % i, [128, 1]) for i in range(2)]
    rstd = [S.sb("rstd%d" % i, [128, 1]) for i in range(2)]
    hnT = [S.sb("hnT%d" % i, [128, KC, NT], BF16) for i in range(2)]
    uTs = [S.sb("uT%d" % i, [128, H, NT], BF16) for i in range(2)]
    yT = S.sb("yT", [128, KC, NT], BF16)
    qTs = [S.sb("qT%d" % i, [128, H, NT], BF16) for i in range(2)]
    kTs = [S.sb("kT%d" % i, [128, H, NT], BF16) for i in range(2)]
    vTs = [S.sb("vT%d" % i, [128, H, NT], BF16) for i in range(2)]
    XSL = [None, None]
    hal = S.sb("hal", [128, 12, 3])
    rr = [S.sb("rr%d" % i, [128, NT + 3]) for i in range(2)]
    ca = [S.sb("ca%d" % i, [128, NT]) for i in range(2)]
    qs = [S.sb("qs%d" % i, [128, NT]) for i in range(2)]
    sq = S.sb("sq", [128, NT])
    rn = S.sb("rn", [128, NT])
    Sst = S.sb("Sst", [128, H, 128])
    Sr = S.sb("Sr", [128, H, 128])

    def F(name, dt=F32):
        return S.sb(name, [128, H, 128], dt)

    ktok, vtok, vg, sqv, zs, G1, G2, E, nbm, usb, osq, on, gz, t1 = [
        F(n) for n in ("ktok", "vtok", "vg", "sqv", "zs", "G1", "G2", "E", "nbm", "usb", "osq", "on", "gz", "t1")]
    Lp0, Lp1, intra, intraT, U0, U1, TT, vb, kbg, wTs, qgT, kd, vnew = [
        F(n) for n in ("Lp0", "Lp1", "intra", "intraT", "U0", "U1", "TT", "vb", "kbg", "wTs", "qgT", "kd", "vnew")]
    vn = F("vn", BF16)
    onb = F("onb", BF16)
    sm = {n: S.sb(n, [128, 4]) for n in ("vsum", "vvar", "vrs", "beta", "nbeta", "xa", "xe", "xm", "sp", "g", "bk", "edl", "oss", "ors")}
    gcs = S.sb("gcs", [128, 8])
    egs = S.sb("egs", [128, 8])

    st = {"gsub": 0, "ich": 0}

    def proj_task(b, t0, k):
        hT_ = hnT[k]
        uT, qT, kT, vT = uTs[k], qTs[k], kTs[k], vTs[k]
        xsl = []
        for sub in range(NSUB):
            r0 = b * C.L + t0 + sub * 128
            xt = xs[st["gsub"] % 4]
            xsl.append((xt, r0))
            load_rows(S, C, xt, src, (r0 // 128), src.ap[r0:r0 + 128, :])
            jj = st["gsub"] % 2
            pt = PB.get()
            rms_to_hnT(S, C, xt, hT_[:, :, sub * 128:(sub + 1) * 128], bf(pt), hn[jj], ssq[jj], rstd[jj])
            st["gsub"] += 1
        for c in range(H):
            ps = PB.get()
            for kc in range(KC):
                S.mm(ps[:, 0:NT], win[:, kc, c * 128:(c + 1) * 128], hT_[:, kc, :], start=(kc == 0), stop=(kc == KC - 1))
            S.act(uT[:, c, :], ps[:, 0:NT], AF.Gelu_apprx_tanh)
            yield
        for c in range(12):
            ps = PB.get()
            for kc in range(KC):
                S.mm(ps[:, 0:NT], win[:, kc, 1024 + c * 128:1024 + (c + 1) * 128], hT_[:, kc, :], start=(kc == 0), stop=(kc == KC - 1))
            yield
            r = rr[st["ich"] % 2]
            a = ca[st["ich"] % 2]
            if t0 == 0:
                S.memset("pool", r[:, 0:3], 0.0)
            else:
                S.copy("pool", r[:, 0:3], hal[:, c, :])
            S.copy("act", r[:, 3:NT + 3], ps[:, 0:NT])
            S.act(a, ps[:, 0:NT], AF.Copy, scale=cw[:, c, 3:4])
            S.copy("pool", hal[:, c, :], r[:, NT:NT + 3])
            yield
            for tap in (2, 1, 0):
                S.stt("dve", a, r[:, tap:tap + NT], cw[:, c, tap:tap + 1], a, ALU.mult, ALU.add)
            hh = c % 4
            yield
            if c >= 8:
                S.act(vT[:, hh, :], a, AF.Silu)
            else:
                q_ = qs[st["ich"] % 2]
                S.act(q_, a, AF.Silu)
                S.act(sq, q_, AF.Square)
                yield
                pss = PB.get()
                S.mm(pss[:, 0:NT], ones, sq)
                yield
                S.act(rn, pss[:, 0:NT], AF.Sqrt, bias=C.eps, scale=1.0)
                yield
                S.op("dve", lambda e: e.reciprocal(rn.ap, rn.ap), r=[rn], w=[rn])
                if c < 4:
                    S.stt("dve", qT[:, hh, :], q_, 128.0 ** -0.5, rn, ALU.mult, ALU.mult)
                else:
                    S.tt("dve", kT[:, hh, :], q_, rn, ALU.mult)
            st["ich"] += 1
            yield
        XSL[k] = xsl
        yield

    def chunk_task(b, t0, k):
        hT_ = hnT[k]
        uT, qT, kT, vT = uTs[k], qTs[k], kTs[k], vTs[k]
        xsl = XSL[k]
        if t0 == 0:
            S.memset("pool", Sst, 0.0)
            S.copy("pool", r32(Sr), Sst)
        for sub in range(NSUB):
            cs = slice(sub * 128, (sub + 1) * 128)
            xt, r0 = xsl[sub]
            pv = PB.get()
            pz = PB.get()
            pba = PB.get()
            for kc in range(KC):
                S.mm(pv, hT_[:, kc, cs], win[:, kc, 512:1024], start=(kc == 0), stop=(kc == KC - 1))
            for kc in range(KC):
                S.mm(pz, hT_[:, kc, cs], win[:, kc, 2568:3080], start=(kc == 0), stop=(kc == KC - 1))
            for kc in range(KC):
                S.mm(pba[:, 0:8], hT_[:, kc, cs], win[:, kc, 2560:2568], start=(kc == 0), stop=(kc == KC - 1))
            yield
            S.act(flat(vg), pv, AF.Gelu_apprx_tanh)
            S.act(flat(zs), pz, AF.Silu)
            S.act(sm["beta"], pba[:, 0:4], AF.Sigmoid)
            S.tt("dve", sm["xa"], pba[:, 4:8], dtb, ALU.add)
            S.reduce("dve", sm["vsum"], vg, ALU.add)
            S.stt("dve", vg, bc_last(sm["vsum"], 128), -1.0 / 128, vg, ALU.mult, ALU.add)
            S.tt("pool", sqv, vg, vg, ALU.mult)
            S.reduce("dve", sm["vvar"], sqv, ALU.add)
            S.act(sm["vrs"], sm["vvar"], AF.Sqrt, bias=C.eps, scale=1.0 / 128)
            S.op("dve", lambda e: e.reciprocal(sm["vrs"].ap, sm["vrs"].ap), r=[sm["vrs"]], w=[sm["vrs"]])
            S.tt("dve", vn, vg, bc_last(sm["vrs"], 128), ALU.mult)
            yield
            pm = PB.get()
            for gI in range(H):
                S.mm(pm[:, gI * 128:(gI + 1) * 128], vn[:, gI, :], wTm[:, gI, :], start=True, stop=False)
                S.mm(pm[:, gI * 128:(gI + 1) * 128], ones[0:1, :], brow[0:1, gI * 128:(gI + 1) * 128], start=False, stop=True)
            yield
            S.tt("dve", yT[:, 0:4, cs], v3(pm), uT[:, :, cs], ALU.mult)
            yield
            S.ts("dve", sm["nbeta"], sm["beta"], -1.0, ALU.mult)
            S.ts("dve", sm["xm"], sm["xa"], 30.0, ALU.min)
            S.act(sm["xe"], sm["xm"], AF.Exp)
            S.act(sm["sp"], sm["xe"], AF.Ln, bias=1.0, scale=1.0)
            S.ts("dve", sm["xm"], sm["xa"], -30.0, ALU.add, 0.0, ALU.max)
            S.tt("dve", sm["sp"], sm["sp"], sm["xm"], ALU.add)
            S.tt("dve", sm["g"], sm["sp"], nega, ALU.mult)
            g = sm["g"]
            pg = PB.get()
            S.mm(pg[:, 0:4], tri, g)
            S.mm(pg[:, 4:8], ones, g)
            yield
            S.copy("dve", gcs, pg[:, 0:8])
            S.act(egs, gcs, AF.Exp)
            S.tt("dve", sm["edl"], gcs[:, 4:8], gcs[:, 0:4], ALU.subtract)
            S.act(sm["edl"], sm["edl"], AF.Exp)
            S.tt("dve", sm["bk"], sm["beta"], egs[:, 0:4], ALU.mult)
            yield
            pk = PB.get()
            for h in range(H):
                S.tr(bf(pk)[:, h * 128:(h + 1) * 128], kT[:, h, cs], C.identb)
            yield
            S.copy("act", flat(ktok), bf(pk)[:, 0:512])
            pk = PB.get()
            for h in range(H):
                S.tr(bf(pk)[:, h * 128:(h + 1) * 128], vT[:, h, cs], C.identb)
            yield
            S.copy("act", flat(vtok), bf(pk)[:, 0:512])
            yield
            S.copy("pool", G1, bc_last(g, 128))
            S.tt("pool", G2, bc_last(g, 128), bc_mid(ntri, H), ALU.mult)
            pd = PB.get()
            for h in range(H):
                S.mm(pd[:, h * 128:(h + 1) * 128], tri, G1[:, h, :], start=True, stop=False)
                S.mm(pd[:, h * 128:(h + 1) * 128], ones, G2[:, h, :], start=False, stop=True)
            yield
            S.ts("dve", flat(E), pd, 0.0, ALU.min)
            S.act(flat(E), flat(E), AF.Exp)
            yield
            pkk = PB.get()
            pqk = PB.get()
            for h in range(H):
                S.mm(pkk[:, h * 128:(h + 1) * 128], kT[:, h, cs], kT[:, h, cs])
            for h in range(H):
                S.mm(pqk[:, h * 128:(h + 1) * 128], qT[:, h, cs], kT[:, h, cs])
            yield
            S.tt("pool", nbm, bc_mid(strict, H), bc_last(sm["nbeta"], 128), ALU.mult)
            S.tt("dve", flat(t1), pkk, flat(E), ALU.mult)
            S.tt("pool", r32(Lp0), t1, nbm, ALU.mult)
            S.tt("dve", flat(osq), pqk, flat(E), ALU.mult)
            S.tt("pool", intra, osq, bc_mid(incl, H), ALU.mult)
            yield
            pu = PB.get()
            for h in range(H):
                S.tr(pu[:, h * 128:(h + 1) * 128], Lp0[:, h, :], C.identf)
            yield
            S.copy("act", r32(flat(U0)), pu)
            S.tt("dve", r32(TT), v3(pu), bc_mid(C.identf, H), ALU.add)
            pi = PB.get()
            for h in range(H):
                S.tr(pi[:, h * 128:(h + 1) * 128], intra[:, h, :], C.identf)
            yield
            S.copy("act", r32(flat(intraT)), pi)
            yield
            Us = [U0, U1]
            Ls = [Lp0, Lp1]
            for k in range(1, 7):
                Uo, Un = Us[(k - 1) % 2], Us[k % 2]
                Lo, Ln_ = Ls[(k - 1) % 2], Ls[k % 2]
                if k <= 5:
                    p1 = PB.get()
                    for h in range(H):
                        S.mm(p1[:, h * 128:(h + 1) * 128], r32(Lo[:, h, :]), r32(Uo[:, h, :]))
                p2 = PB.get()
                for h in range(H):
                    S.mm(p2[:, h * 128:(h + 1) * 128], r32(Uo[:, h, :]), r32(Lo[:, h, :]))
                yield
                if k <= 5:
                    S.copy("act", r32(flat(Un)), p1)
                S.copy("dve", r32(flat(Ln_)), p2)
                yield
                p3 = PB.get()
                for h in range(H):
                    S.mm(p3[:, h * 128:(h + 1) * 128], r32(Ln_[:, h, :]), r32(TT[:, h, :]))
                yield
                S.tt("dve", r32(flat(TT)), flat(TT), p3, ALU.add)
                yield
            yield
            S.tt("pool", r32(vb), vtok, bc_last(sm["beta"], 128), ALU.mult)
            S.tt("pool", r32(kbg), ktok, bc_last(sm["bk"], 128), ALU.mult)
            S.tt("pool", r32(kd), ktok, bc_last(sm["edl"], 128), ALU.mult)
            pU = PB.get()
            for h in range(H):
                S.mm(pU[:, h * 128:(h + 1) * 128], r32(TT[:, h, :]), r32(vb[:, h, :]))
            yield
            S.copy("act", flat(usb), pU)
            pW = PB.get()
            for h in range(H):
                S.mm(pW[:, h * 128:(h + 1) * 128], r32(kbg[:, h, :]), r32(TT[:, h, :]))
            yield
            S.copy("act", r32(flat(wTs)), pW)
            yield
            S.tt("pool", G1, bc_mid(C.identf, H), bc_last(egs[:, 0:4], 128), ALU.mult)
            pe_ = PB.get()
            S.mm(pe_, ones, flat(G1))
            yield
            S.tt("dve", r32(qgT), qT[:, :, cs], v3(pe_), ALU.mult)
            yield
            pws = PB.get()
            for h in range(H):
                S.mm(pws[:, h * 128:(h + 1) * 128], r32(wTs[:, h, :]), r32(Sr[:, h, :]))
            yield
            S.tt("dve", r32(flat(vnew)), flat(usb), pws, ALU.subtract)
            po = PB.get()
            for h in range(H):
                S.mm(po[:, h * 128:(h + 1) * 128], r32(qgT[:, h, :]), r32(Sr[:, h, :]), start=True, stop=False)
                S.mm(po[:, h * 128:(h + 1) * 128], r32(intraT[:, h, :]), r32(vnew[:, h, :]), start=False, stop=True)
            pS = PB.get()
            for h in range(H):
                S.mm(pS[:, h * 128:(h + 1) * 128], r32(kd[:, h, :]), r32(vnew[:, h, :]))
            yield
            S.tt("dve", Sst, Sst, bc_last(egs[:, 4:8], 128), ALU.mult)
            S.tt("dve", flat(Sst), flat(Sst), pS, ALU.add)
            S.copy("act", r32(Sr), Sst)
            yield
            yield
            S.act(flat(osq), po, AF.Square)
            S.reduce("dve", sm["oss"], osq, ALU.add)
            S.act(sm["ors"], sm["oss"], AF.Sqrt, bias=C.eps, scale=1.0 / 128)
            S.op("dve", lambda e: e.reciprocal(sm["ors"].ap, sm["ors"].ap), r=[sm["ors"]], w=[sm["ors"]])
            S.tt("pool", gz, zs, bc_mid(gng, H), ALU.mult)
            S.tt("dve", on, v3(po), bc_last(sm["ors"], 128), ALU.mult)
            S.tt("dve", onb, on, gz, ALU.mult)
            pt2 = PB.get()
            for h in range(H):
                S.tr(bf(pt2)[:, h * 128:(h + 1) * 128], onb[:, h, :], C.identb)
            yield
            S.copy("act", yT[:, 4:8, cs], v3(bf(pt2)[:, 0:512]))
            yield
            for nh in range(2):
                pso = PB.get()
                for kc in range(KC):
                    S.mm(pso, yT[:, kc, cs], wout[:, kc, nh * 512:(nh + 1) * 512], start=(kc == 0), stop=(kc == KC - 1))
                S.tt("dve", xt[:, nh * 512:(nh + 1) * 512], xt[:, nh * 512:(nh + 1) * 512], pso, ALU.add)
            S.dma("sp", dst.v((r0 // 128), dst.ap[r0:r0 + 128, :]), xt, sres=xt.res)

    tiles = [(b, t0) for b in range(C.NSEQ) for t0 in range(0, C.L, NT)]
    prev = None
    for i, (b, t0) in enumerate(tiles):
        run_lanes([proj_task(b, t0, i % 2), chunk_task(*prev) if prev is not None else None])
        prev = (b, t0, i % 2)
    run_lanes([chunk_task(*prev)])
    S.end_phase()


OD_EXT = 3016
NQK = 18


def odd_phase(S, C, j, src, dst):
    odd_proj_phase(S, C, j, src)
    odd_attn_phase(S, C, j, src, dst)


def odd_proj_phase(S, C, j, src):
    layer = 2 * j + 1
    NT = 512
    NSUB = NT // 128
    S.begin_phase()
    PB = Banks(S)
    win = S.sb("win", [128, KC, OD_EXT], BF16)
    gsb = S.sb("gsb", [128, KC])
    stg = [S.sb("stg%d" % i, [128, 754]) for i in range(4)]
    S.dma("sp", gsb, V(C.mix_g[layer], C.wres), sres=gsb.res)
    load_weight_scaled(S, C, win, C.od_win[j], KC, OD_EXT, gsb, stg, 1508, "win")
    gn = S.sb("gn", [128, 4])
    S.dma("sp", gn, V(C.od_gn[j], C.wres), sres=gn.res)
    S.ts("pool", gn[:, 0:1], gn[:, 0:1], 64.0 ** -0.5, ALU.mult, 1.0, ALU.mult)
    S.ts("pool", gn[:, 2:3], gn[:, 2:3], 128.0 ** -0.5, ALU.mult, 1.0, ALU.mult)
    ones = S.sb("ones", [128, 128])
    bd64 = S.sb("bd64", [128, 128])
    S.memset("pool", ones, 1.0)
    S.memset("pool", bd64, 0.0)
    S.memset("pool", bd64[0:64, 0:64], 1.0)
    S.memset("pool", bd64[64:128, 64:128], 1.0)

    xs = [S.sb("x%d" % i, [128, D]) for i in range(2)]
    hn = [S.sb("hn%d" % i, [128, D], BF16) for i in range(2)]
    ssq = [S.sb("ssq%d" % i, [128, 1]) for i in range(2)]
    rstd = [S.sb("rstd%d" % i, [128, 1]) for i in range(2)]
    hnT = [S.sb("hnT%d" % i, [128, KC, NT], BF16) for i in range(2)]
    oT = [S.sb("oT%d" % i, [128, NQK, NT], BF16) for i in range(2)]
    sqb = [S.sb("sqb%d" % i, [128, NT]) for i in range(2)]
    rn = [S.sb("rn%d" % i, [128, NT]) for i in range(2)]
    tokb = [S.sb("tokb%d" % i, [128, 640], BF16) for i in range(2)]
    iwt = [S.sb("iwt%d" % i, [128, 8]) for i in range(2)]

    chunks = []
    for c in range(4):
        chunks.append((c * 128, "n64", 0))
    for c in range(4):
        chunks.append((512 + c * 128, "n64", 1))
    for c in range(4):
        chunks.append((1536 + c * 128, "n128", 2))
    chunks.append((2048, "n128", 3))
    for c in range(4):
        chunks.append((2304 + c * 128, "scale", None))
    chunks.append((2888, "copy", None))

    gsub = 0
    ist = 0
    inorm = 0
    for b in range(C.NSEQ):
        for t0 in range(0, C.L, NT):
            hT_ = hnT[ist % 2]
            o_ = oT[ist % 2]
            for sub in range(NSUB):
                r0 = b * C.L + t0 + sub * 128
                xt = xs[gsub % 2]
                load_rows(S, C, xt, src, (r0 // 128), src.ap[r0:r0 + 128, :])
                jj = gsub % 2
                pt = PB.get()
                rms_to_hnT(S, C, xt, hT_[:, :, sub * 128:(sub + 1) * 128], bf(pt), hn[jj], ssq[jj], rstd[jj])
                gsub += 1
            for ci, (c0, kind, gi) in enumerate(chunks):
                ps = PB.get()
                for kc in range(KC):
                    S.mm(ps[:, 0:NT], win[:, kc, c0:c0 + 128], hT_[:, kc, :], start=(kc == 0), stop=(kc == KC - 1))
                if kind == "scale":
                    S.act(o_[:, ci, :], ps[:, 0:NT], AF.Copy, scale=0.125)
                elif kind == "copy":
                    S.copy("act", o_[:, ci, :], ps[:, 0:NT])
                else:
                    sq_ = sqb[inorm % 2]
                    rn_ = rn[inorm % 2]
                    inorm += 1
                    S.act(sq_, ps[:, 0:NT], AF.Square)
                    pss = PB.get()
                    S.mm(pss[:, 0:NT], bd64 if kind == "n64" else ones, sq_)
                    dim = 64.0 if kind == "n64" else 128.0
                    S.act(rn_, pss[:, 0:NT], AF.Sqrt, bias=C.eps, scale=1.0 / dim)
                    S.op("dve", lambda e, rn_=rn_: e.reciprocal(rn_.ap, rn_.ap), r=[rn_], w=[rn_])
                    S.stt("dve", o_[:, ci, :], ps[:, 0:NT], gn[:, gi:gi + 1], rn_, ALU.mult, ALU.mult)
            for sub in range(NSUB):
                cs = slice(sub * 128, (sub + 1) * 128)
                r0 = t0 + sub * 128
                tb = tokb[sub % 2]
                iw_ = iwt[sub % 2]
                pv = PB.get()
                for kc in range(KC):
                    S.mm(pv, hT_[:, kc, cs], win[:, kc, 1024:1536], start=(kc == 0), stop=(kc == KC - 1))
                p2 = PB.get()
                for kc in range(KC):
                    S.mm(p2[:, 0:128], hT_[:, kc, cs], win[:, kc, 2176:2304], start=(kc == 0), stop=(kc == KC - 1))
                for kc in range(KC):
                    S.mm(p2[:, 128:136], hT_[:, kc, cs], win[:, kc, 2880:2888], start=(kc == 0), stop=(kc == KC - 1))
                S.copy("act", tb[:, 0:512], pv)
                S.copy("act", tb[:, 512:640], p2[:, 0:128])
                S.ts("dve", iw_, p2[:, 128:136], 8.0 ** -0.5, ALU.mult)
                S.dma("sp", C.vtok.v((b, r0 // 128), C.vtok.ap[b, r0:r0 + 128, :]), tb, sres=tb.res)
                S.dma("sp", C.iwd.v((b, r0 // 128), C.iwd.ap[b, r0:r0 + 128, :]), iw_, sres=iw_.res)
            S.dma("sp", C.qkT.v((b, t0 // NT), C.qkT.ap[b, :, :, t0:t0 + NT].rearrange("c p t -> p c t")), o_, sres=o_.res)
            ist += 1
    S.end_phase()


def odd_attn_phase(S, C, j, src, dst):
    layer = 2 * j + 1
    lambda_init = 0.8 - 0.6 * math.exp(-0.3 * layer)
    L = C.L
    NKB = L // 128
    NQ = 512
    NQS = NQ // 128
    TOPK = min(256, L // 4)
    NIT = 18
    S.begin_phase()
    PB1 = Banks(S, 2)
    PB2 = Banks(S, 1)
    PB3 = Banks(S, 1)
    DACC = [S.ps("dacc%d" % i, [128, 512]) for i in range(2)]
    FACC = [S.ps("facc%d" % i, [128, 512]) for i in range(2)]
    wout = S.sb("wout", [128, KC, D], BF16)
    stg = [S.sb("stg%d" % i, [128, 1024]) for i in range(2)]
    load_weight_scaled(S, C, wout, C.od_wout[j], KC, D, None, stg, 1024, "wout")
    S.barrier()
    tmpf = [V(stg[0].ap[:, 0:512], Res("tmpf0"))]
    R = [V(stg[1].ap[:, 0:512], Res("R0")), V(stg[1].ap[:, 512:1024], Res("R1")), V(stg[0].ap[:, 512:1024], Res("R2"))]
    rb = S.sb("rb", [128, 8, 2, 128])
    t31 = S.sb("t31", [128, 8])
    cmT = S.sb("cmT", [128, 128])
    dmask = S.sb("dmask", [128, 128])
    zer = S.sb("zer", [128, 128])
    S.dma("sp", rb, V(C.rb_near, C.wres), sres=rb.res)
    S.dma("sp", t31, V(C.rb_t31.partition_broadcast(128), C.wres), sres=t31.res)
    S.memset("pool", zer, 0.0)
    S.asel(cmT, zer, [[1, 128]], ALU.is_ge, NEG, base=0, cm=-1)
    S.asel(dmask, zer, [[-1, 128]], ALU.is_ge, -1e30, base=0, cm=1)
    rbv = V(rb.ap.rearrange("p h r q -> p h (r q)"), rb.res)
    S.tt("dve", rbv, rbv, bc_last(t31, 256), ALU.subtract)
    S.tt("dve", rb[:, :, 1, :], rb[:, :, 1, :], bc_mid(cmT, 8), ALU.add)
    lf = S.sb("lf", [128, 256])
    lj = S.sb("lj", [128, 64])
    lam = {n: S.sb(n, [128, 1]) for n in ("s01", "s23", "nlam")}
    S.dma("sp", lf, V(C.od_lam[j].partition_broadcast(128), C.wres), sres=lf.res)
    S.memset("dve", lam["s01"], 0.0)
    S.memset("dve", lam["s23"], 0.0)
    S.stt("dve", lj, lf[:, 0:64], 1.0, lf[:, 64:128], ALU.mult, ALU.mult, accum=lam["s01"])
    S.stt("dve", lj, lf[:, 128:192], 1.0, lf[:, 192:256], ALU.mult, ALU.mult, accum=lam["s23"])
    S.act(lam["s01"], lam["s01"], AF.Exp)
    S.act(lam["s23"], lam["s23"], AF.Exp)
    S.tt("dve", lam["nlam"], lam["s23"], lam["s01"], ALU.subtract)
    S.ts("dve", lam["nlam"], lam["nlam"], -lambda_init, ALU.add)
    gsub_ = S.sb("gsubn", [128, 128])
    S.dma("sp", gsub_, V(C.od_gsub[j].partition_broadcast(128), C.wres), sres=gsub_.res)
    S.ts("pool", gsub_, gsub_, 1.0 - lambda_init, ALU.mult, 1.0, ALU.mult)

    dkT = S.sb("dkT", [128, 4, L], BF16)
    skT = S.sb("skT", [128, L], BF16)
    ikT = S.sb("ikT", [128, L], BF16)
    dvA = S.sb("dvA", [128, NKB, 4, 130], BF16)
    svA = S.sb("svA", [128, NKB, 130], BF16)
    S.memset("pool", dvA[:, :, :, 128:130], 1.0)
    S.memset("pool", svA[:, :, 128:130], 1.0)
    dqT = S.sb("dqT", [128, 4, NQ], BF16)
    sqT = S.sb("sqT", [128, 4, NQ], BF16)
    iqT = S.sb("iqT", [128, 4, NQ], BF16)
    iw = S.sb("iw", [128, NQS, 8])
    idx = S.sb("idx", [128, L])
    M = S.sb("M", [128, L], BF16)
    M2 = S.sb("M2", [128, L], BF16)
    MT = S.sb("MT", [128, NKB, 128], BF16)
    PTa = [S.sb("PTa%d" % i, [128, 512], BF16) for i in range(3)]
    PTb = [S.sb("PTb%d" % i, [128, 512], BF16) for i in range(2)]
    xs = [S.sb("x%d" % i, [128, D]) for i in range(1)]
    ytd = [[S.sb("ytd%d_%d" % (a, i), [128, 4, 128], BF16) for i in range(NQS)] for a in range(2)]
    ytf = [[S.sb("ytf%d_%d" % (a, i), [128, 4, 128], BF16) for i in range(NQS)] for a in range(2)]
    yT = S.sb("yT", [128, 8, 128], BF16)
    of0 = S.sb("of0", [128, NQS, 128])
    of = S.sb("of", [128, 128])
    osq = S.sb("osq", [128, 128])
    sc = {n: S.sb(n, [128, 1]) for n in ("rmax", "w0", "lo", "nlo", "nmid", "cnt", "gw", "rec")}
    sd = {n: S.sb(n, [128, 1]) for n in ("rec", "oss", "ors")}
    sc2 = {n: S.sb(n + "2", [128, 1]) for n in ("rec",)}
    ctr = {"R": 0, "Pa": 0, "Pb": 0, "T": 0, "x": 0}

    NQB = L // 128
    NST = (L + NQ - 1) // NQ
    Ms = [M, M2]
    prog = {"s1": 0, "s2": 0, "df": 0}

    def s1_lane(b):
        for qb in range(NQB):
            s0 = (qb // NQS) * NQ
            qs = qb % NQS
            nqs = min(NQS, (L - s0) // 128)
            nq = nqs * 128
            if qs == 0:
                S.dma("sp", iqT[:, :, 0:nq], C.qkT.v((b, "iq", s0), C.qkT.ap[b, 13:17, :, s0:s0 + nq].rearrange("c p t -> p c t")), sres=iqT.res)
                S.dma("sp", iw[:, 0:nqs, :], C.iwd.v((b, "iw", s0), C.iwd.ap[b, s0:s0 + nq, :].rearrange("(s p) e -> p s e", p=128)), sres=iw.res)
                yield
            nk = (qb + 1) * 128
            qc = slice(qs * 128, (qs + 1) * 128)
            if nk > TOPK:
                while prog["s2"] < qb - 1:
                    yield
                Mq = Ms[qb % 2]
                pend = None

                def fma(p):
                    r_, k0, wd, h = p
                    if h == 0:
                        S.ts("dve", idx[:, k0:k0 + wd], r_[:, 0:wd], iw[:, qs, 0:1], ALU.mult)
                    else:
                        S.stt("dve", idx[:, k0:k0 + wd], r_[:, 0:wd], iw[:, qs, h:h + 1], idx[:, k0:k0 + wd], ALU.mult, ALU.add)

                for k0 in range(0, nk, 512):
                    wd = min(512, nk - k0)
                    for h in range(8):
                        pr = slice((h % 2) * 64, (h % 2) * 64 + 64)
                        ps = PB1.get()
                        S.mm(ps[:, 0:wd], iqT[pr, h // 2, qc], ikT[pr, k0:k0 + wd])
                        r_ = R[ctr["R"] % len(R)]
                        ctr["R"] += 1
                        S.act(r_[:, 0:wd], ps[:, 0:wd], AF.Relu)
                        if pend is not None:
                            fma(pend)
                        pend = (r_, k0, wd, h)
                        if h % 2 == 1:
                            yield
                fma(pend)
                S.reduce("dve", sc["rmax"], idx[:, 0:nk], ALU.max)
                S.reduce("dve", sc["lo"], idx[:, 0:nk], ALU.min)
                S.tt("dve", sc["w0"], sc["rmax"], sc["lo"], ALU.subtract)
                S.ts("dve", sc["nlo"], sc["lo"], -1.0, ALU.mult)
                S.tt("dve", idx[:, nk - 128:nk], idx[:, nk - 128:nk], dmask, ALU.add)
                yield
                thr = 2.0 * TOPK - nk - 0.5
                for it in range(NIT):
                    hw = 2.0 ** -(it + 1)
                    S.stt("dve", sc["nmid"], sc["w0"], -hw, sc["nlo"], ALU.mult, ALU.add)
                    S.act(Mq[:, 0:nk], idx[:, 0:nk], AF.Sign, bias=sc["nmid"], scale=1.0, accum=sc["cnt"])
                    yield
                    S.stt("dve", sc["gw"], sc["cnt"], thr, sc["w0"], ALU.is_ge, ALU.mult)
                    S.stt("dve", sc["nlo"], sc["gw"], -hw, sc["nlo"], ALU.mult, ALU.add)
                S.ts("dve", sc["lo"], sc["nlo"], -1.0, ALU.mult)
                S.ts("dve", Mq[:, 0:nk], idx[:, 0:nk], sc["lo"], ALU.is_ge)
            prog["s1"] = qb + 1
            yield

    def s2_lane(b):
        for qb in range(NQB):
            st_i = qb // NQS
            s0 = st_i * NQ
            qs = qb % NQS
            nqs = min(NQS, (L - s0) // 128)
            nq = nqs * 128
            if qs == 0:
                while prog["df"] < st_i - 1:
                    yield
                S.dma("sp", sqT[:, :, 0:nq], C.qkT.v((b, "sq", s0), C.qkT.ap[b, 8:12, :, s0:s0 + nq].rearrange("c p t -> p c t")), sres=sqT.res)
                yield
            while prog["s1"] < qb + 1:
                yield
            yb = ytd[st_i % 2]
            nk = (qb + 1) * 128
            qc = slice(qs * 128, (qs + 1) * 128)
            use_topk = nk > TOPK
            Mq = Ms[qb % 2]
            if use_topk:
                for g0 in range(0, qb + 1, 8):
                    ng = min(8, qb + 1 - g0)
                    pm = PB2.get()
                    for i in range(ng):
                        S.tr(bf(pm)[:, i * 128:(i + 1) * 128], Mq[:, (g0 + i) * 128:(g0 + i + 1) * 128], C.identb)
                    S.copy("act", MT[:, g0:g0 + ng, :], v3(bf(pm)[:, 0:ng * 128], ng))
                    yield
            for a in DACC:
                S.memset("dve", a[:, 0:130], 0.0)
                S.memset("dve", a[:, 256:386], 0.0)
            nkb_ = qb + 1

            def st1(kb):
                kc_ = slice(kb * 128, (kb + 1) * 128)
                ps = PB2.get()
                S.mm(ps, skT[:, kc_], sqT[:, :, qc])
                pt_ = PTa[kb % 3]
                rel = kb - (qb - 1)
                if rel >= 0:
                    t_ = tmpf[ctr["T"] % len(tmpf)]
                    ctr["T"] += 1
                    S.tt("dve", v3(t_), v3(ps), rb[:, 4:8, rel, :], ALU.add)
                    S.act(pt_, t_, AF.Exp)
                else:
                    S.act(pt_, ps, AF.Exp)

            def st2(kb):
                if use_topk:
                    pt_ = PTa[kb % 3]
                    S.tt("dve", v3(pt_), v3(pt_), bc_mid(MT[:, kb, :], 4), ALU.mult)

            def st3(kb):
                pt_ = PTa[kb % 3]
                for h in range(4):
                    S.op("pe", lambda e, h=h, pt_=pt_, kb=kb, qb=qb: e.matmul(DACC[h // 2].ap[:, (h % 2) * 256:(h % 2) * 256 + 129], pt_.ap[:, h * 128:(h + 1) * 128],
                                                                             svA.ap[:, kb, 0:129], start=False, stop=(kb == qb), skip_group_check=True),
                         r=[pt_, svA], w=[DACC[h // 2]], inc=(h == 3))

            for t in range(nkb_ + 2):
                if 0 <= t - 2 < nkb_:
                    st3(t - 2)
                if 0 <= t - 1 < nkb_:
                    st2(t - 1)
                if t < nkb_:
                    st1(t)
                yield
            for h in range(4):
                a = DACC[h // 2]
                c0 = (h % 2) * 256
                S.op("dve", lambda e, a=a, c0=c0: e.reciprocal(sc2["rec"].ap, a.ap[:, c0 + 128:c0 + 129]), r=[a], w=[sc2["rec"]])
                S.ts("dve", yb[qs][:, h, :], a[:, c0:c0 + 128], sc2["rec"], ALU.mult)
            prog["s2"] = qb + 1
            yield

    def df_lane(b):
        for st_i in range(NST):
            s0 = st_i * NQ
            sblk = s0 // 128
            nqs = min(NQS, (L - s0) // 128)
            nq = nqs * 128
            yf = ytf[st_i % 2]
            yd = ytd[st_i % 2]
            S.dma("sp", dqT[:, :, 0:nq], C.qkT.v((b, "dq", s0), C.qkT.ap[b, 0:4, :, s0:s0 + nq].rearrange("c p t -> p c t")), sres=dqT.res)
            yield
            last_kb = sblk + nqs - 1
            for h in range(4):
                for m in range(2):
                    pr = slice(m * 64, m * 64 + 64)
                    for a in FACC:
                        S.memset("dve", a[:, 0:130], 0.0)
                        S.memset("dve", a[:, 256:386], 0.0)
                    def d1(kb):
                        kc_ = slice(kb * 128, (kb + 1) * 128)
                        qlo = max(0, kb - sblk)
                        ncol = (nqs - qlo) * 128
                        ps = PB3.get()
                        S.mm(ps[:, 0:ncol], dkT[pr, h, kc_], dqT[pr, h, qlo * 128:nqs * 128])
                        for qs in (kb - sblk, kb - sblk + 1):
                            if 0 <= qs < nqs:
                                rel = kb - (sblk + qs - 1)
                                cc = slice((qs - qlo) * 128, (qs - qlo + 1) * 128)
                                S.tt("dve", ps[:, cc], ps[:, cc], rb[:, h, rel, :], ALU.add)
                        pt_ = PTb[kb % 2]
                        S.act(pt_[:, 0:ncol], ps[:, 0:ncol], AF.Exp)

                    def d2(kb):
                        qlo = max(0, kb - sblk)
                        pt_ = PTb[kb % 2]
                        for qs in range(qlo, nqs):
                            cc = slice((qs - qlo) * 128, (qs - qlo + 1) * 128)
                            S.op("pe", lambda e, qs=qs, cc=cc, pt_=pt_, kb=kb, h=h, sblk=sblk: e.matmul(
                                FACC[qs // 2].ap[:, (qs % 2) * 256:(qs % 2) * 256 + 129], pt_.ap[:, cc], dvA.ap[:, kb, h, 0:129],
                                start=False, stop=(kb == sblk + qs), skip_group_check=True), r=[pt_, dvA], w=[FACC[qs // 2]],
                                inc=(qs == nqs - 1))

                    for t in range(last_kb + 2):
                        if 0 <= t - 1 <= last_kb:
                            d2(t - 1)
                        if t <= last_kb:
                            d1(t)
                        yield
                    for qs in range(nqs):
                        a = FACC[qs // 2]
                        c0 = (qs % 2) * 256
                        S.op("dve", lambda e, a=a, c0=c0: e.reciprocal(sd["rec"].ap, a.ap[:, c0 + 128:c0 + 129]), r=[a], w=[sd["rec"]])
                        if m == 0:
                            S.ts("dve", of0[:, qs, :], a[:, c0:c0 + 128], sd["rec"], ALU.mult)
                        else:
                            S.tt("dve", sd["rec"], sd["rec"], lam["nlam"], ALU.mult)
                            S.stt("dve", of, a[:, c0:c0 + 128], sd["rec"], of0[:, qs, :], ALU.mult, ALU.add)
                            S.act(osq, of, AF.Square, accum=sd["oss"])
                            S.act(sd["ors"], sd["oss"], AF.Sqrt, bias=C.eps, scale=1.0 / 128)
                            S.op("dve", lambda e: e.reciprocal(sd["ors"].ap, sd["ors"].ap), r=[sd["ors"]], w=[sd["ors"]])
                            S.stt("dve", yf[qs][:, h, :], of, sd["ors"], gsub_, ALU.mult, ALU.mult)
                    yield
            while prog["s2"] < min(NQB, (st_i + 1) * NQS):
                yield
            for qs in range(nqs):
                r0 = b * L + s0 + qs * 128
                xt = xs[0]
                load_rows(S, C, xt, src, (r0 // 128), src.ap[r0:r0 + 128, :])
                pt2 = PB3.get()
                for c in range(4):
                    S.tr(bf(pt2)[:, c * 128:(c + 1) * 128], yf[qs][:, c, :], C.identb)
                for c in range(4):
                    S.tr(bf(pt2)[:, (4 + c) * 128:(5 + c) * 128], yd[qs][:, c, :], C.identb)
                yield
                S.copy("act", flat(yT), bf(pt2))
                yield
                for nh in range(2):
                    pso = PB3.get()
                    for kc in range(KC):
                        S.mm(pso, yT[:, kc, :], wout[:, kc, nh * 512:(nh + 1) * 512], start=(kc == 0), stop=(kc == KC - 1))
                    yield
                    S.tt("dve", xt[:, nh * 512:(nh + 1) * 512], xt[:, nh * 512:(nh + 1) * 512], pso, ALU.add)
                S.dma("sp", dst.v((r0 // 128), dst.ap[r0:r0 + 128, :]), xt, sres=xt.res)
                yield
            prog["df"] = st_i + 1
            yield

    for b in range(C.NSEQ):
        S.dma("sp", dkT, C.qkT.v((b, "dk"), C.qkT.ap[b, 4:8, :, :].rearrange("c p t -> p c t")), sres=dkT.res)
        S.dma("sp", skT, C.qkT.v((b, "sk"), C.qkT.ap[b, 12, :, :]), sres=skT.res)
        S.dma("sp", ikT, C.qkT.v((b, "ik"), C.qkT.ap[b, 17, :, :]), sres=ikT.res)
        for kb in range(NKB):
            S.dma("sp", dvA[:, kb, :, 0:128], C.vtok.v((b, "dv", kb), C.vtok.ap[b, kb * 128:(kb + 1) * 128, 0:512].rearrange("p (h d) -> p h d", h=4)), sres=dvA.res)
        S.dma("sp", svA[:, :, 0:128], C.vtok.v((b, "sv"), C.vtok.ap[b, :, 512:640].rearrange("(k p) d -> p k d", p=128)), sres=svA.res)
        prog["s1"] = prog["s2"] = prog["df"] = 0
        run_lanes([s1_lane(b), s2_lane(b), df_lane(b)])
    S.end_phase()


W_SPECS = {
    "ffn_g": [4, 128, KC],
    "ffn_cw": [4, 128, 2 * FC, 3],
    "ffn_cb": [4, 128, 2 * FC],
    "ffn_wup": [4, D, 2 * DFF],
    "ffn_wdn": [4, DFF, D],
    "mix_g": [4, 128, KC],
    "ev_win": [2, D, EVEN_IN],
    "ev_wout": [2, D, D],
    "gm_wT": [2, 128, 4, 128],
    "gm_b": [2, 1, 512],
    "gdn_cw": [2, 128, 12, 4],
    "gdn_alog": [2, 1, 4],
    "gdn_dtb": [2, 1, 4],
    "gdn_ng": [2, 1, 128],
    "od_win": [2, D, OD_EXT],
    "od_wout": [2, D, D],
    "od_gn": [2, 128, 4],
    "od_lam": [2, 1, 256],
    "od_gsub": [2, 1, 128],
    "rb_near": [128, 8, 2, 128],
    "rb_t31": [1, 8],
}


def _rel_bucket_np(dist):
    n = np.maximum(dist, 0)
    nf = np.maximum(n, 16).astype(np.float32)
    far = 16 + (np.log(nf / np.float32(16)) / np.float32(math.log(128 / 16)) * np.float32(16)).astype(np.int32)
    return np.where(n < 16, n, np.minimum(far, 31))


def prep_weights(inp):
    f = lambda a: np.ascontiguousarray(np.asarray(a, dtype=np.float32))
    w = {}
    w["ffn_g"] = f(inp["ffn_norm_g"].reshape(4, KC, 128).transpose(0, 2, 1))
    w["ffn_cw"] = f(inp["ffn_conv_w"].reshape(4, 3, 2 * FC, 128).transpose(0, 3, 2, 1))
    w["ffn_cb"] = f(inp["ffn_conv_b"].reshape(4, 2 * FC, 128).transpose(0, 2, 1))
    w["ffn_wup"] = f(inp["ffn_w_up"])
    w["ffn_wdn"] = f(inp["ffn_w_down"])
    w["mix_g"] = f(inp["mix_norm_g"].reshape(4, KC, 128).transpose(0, 2, 1))
    w["ev_win"] = f(inp["ev_w_in"])
    w["ev_wout"] = f(inp["ev_w_out"])
    w["gm_wT"] = f(inp["gmlp_w_s"].transpose(0, 3, 1, 2))
    w["gm_b"] = f(inp["gmlp_b_s"].reshape(2, 1, 512))
    w["gdn_cw"] = f(inp["gdn_conv_w"].reshape(2, 4, 12, 128).transpose(0, 3, 2, 1))
    w["gdn_alog"] = f(inp["gdn_a_log"].reshape(2, 1, 4))
    w["gdn_dtb"] = f(inp["gdn_dt_bias"].reshape(2, 1, 4))
    w["gdn_ng"] = f(inp["gdn_norm_g"].reshape(2, 1, 128))
    ow = np.asarray(inp["od_w_in"], dtype=np.float32)
    w["od_win"] = f(np.concatenate([ow, ow[:, :, 2816:2880], ow[:, :, 2816:2880]], axis=2))
    w["od_wout"] = f(inp["od_w_out"])
    gq = np.asarray(inp["diff_q_norm_g"], dtype=np.float32)
    gk = np.asarray(inp["diff_k_norm_g"], dtype=np.float32)
    w["od_gn"] = f(np.stack([np.tile(gq, (1, 2)), np.tile(gk, (1, 2)), np.asarray(inp["dsa_q_norm_g"]), np.asarray(inp["dsa_k_norm_g"])], axis=2))
    w["od_lam"] = f(inp["diff_lambda"].reshape(2, 1, 256))
    w["od_gsub"] = f(inp["diff_sub_norm_g"].reshape(2, 1, 128))
    kk = np.arange(128)[:, None]
    qq = np.arange(128)[None, :]
    tab = np.asarray(inp["rel_bias"], dtype=np.float32)
    near = np.zeros((128, 8, 2, 128), np.float32)
    for rel in range(2):
        dist = qq - kk + (128 if rel == 0 else 0)
        near[:, :, rel, :] = tab[_rel_bucket_np(dist)].transpose(0, 2, 1)
    w["rb_near"] = f(near)
    w["rb_t31"] = f(tab[31:32, :])
    return w


def build_program(L, NSEQ, plan):
    nc = bass.Bass("TRN2", target_bir_lowering=False)
    NTOK = L * NSEQ
    C = Ctx()
    C.L, C.NSEQ, C.NTOK = L, NSEQ, NTOK
    x = nc.dram_tensor("x", [NTOK, D], F32, kind="ExternalInput").ap()
    y = nc.dram_tensor("y", [NTOK, D], F32, kind="ExternalOutput").ap()
    for name, shape in W_SPECS.items():
        setattr(C, name, nc.dram_tensor(name, shape, F32, kind="ExternalInput").ap())
    C.wres = Res("weights")
    xdt = DT(x, "x")
    ydt = DT(y, "y")
    C.qkT = DT(nc.dram_tensor("qkT", [NSEQ, NQK, 128, L], BF16, kind="Internal").ap(), "qkT")
    C.vtok = DT(nc.dram_tensor("vtok", [NSEQ, L, 640], BF16, kind="Internal").ap(), "vtok")
    C.iwd = DT(nc.dram_tensor("iwd", [NSEQ, L, 8], F32, kind="Internal").ap(), "iwd")
    with ExitStack() as es:
        S = Sched(nc, es)
        ct = es.enter_context(nc.sbuf_tensor("identb", [128, 128], BF16))
        C.identb = V(ct[:], Res("identb"))
        ct = es.enter_context(nc.sbuf_tensor("identf", [128, 128], F32))
        C.identf = V(ct[:], Res("identf"))
        ct = es.enter_context(nc.sbuf_tensor("eps", [128, 1], F32))
        C.eps = V(ct[:], Res("eps"))
        S.memset("pool", C.identf, 1.0)
        S.asel(C.identf, C.identf, [[-1, 128]], ALU.is_equal, 0.0, base=0, cm=1)
        S.copy("pool", C.identb, C.identf)
        S.memset("pool", C.eps, EPS)
        src = xdt
        for kind, idx in plan:
            if kind == "ffn":
                ffn_phase(S, C, idx, src, ydt)
            elif kind == "even":
                even_phase(S, C, idx, src, ydt)
            elif kind == "odd":
                odd_phase(S, C, idx, src, ydt)
            src = ydt
        S.barrier()
        print("program: ops=%d waits=%d dma_sems=%d" % (S.nops, S.nwaits, S.ndsem))
    return nc


FULL_PLAN = [("even", 0), ("ffn", 0), ("odd", 0), ("ffn", 1), ("even", 1), ("ffn", 2), ("odd", 1), ("ffn", 3)]


N_CORES = 8
_PROG = {}


def kernel(**inputs):
    x = np.asarray(inputs["x"], dtype=np.float32)
    B, L, Dm = x.shape
    nseq = B // N_CORES
    key = (L, nseq)
    if key not in _PROG:
        _PROG[key] = build_program(L, nseq, FULL_PLAN)
    nc = _PROG[key]
    w = prep_weights(inputs)
    in_maps = []
    for c in range(N_CORES):
        m = {"x": np.ascontiguousarray(x[c * nseq:(c + 1) * nseq].reshape(nseq * L, Dm))}
        m.update(w)
        in_maps.append(m)
    res = run_bass_kernel_spmd(nc, in_maps, core_ids=list(range(N_CORES)))
    out = np.concatenate([np.asarray(r["y"]).reshape(nseq, L, Dm) for r in res.results], axis=0)
    return out.astype(np.float32)
```

```python
import math
from contextlib import ExitStack

import numpy as np
import concourse.bass as bass
import concourse.mybir as mybir
from concourse.bass_utils import run_bass_kernel_spmd

F32 = mybir.dt.float32
BF16 = mybir.dt.bfloat16
AF = mybir.ActivationFunctionType
ALU = mybir.AluOpType
AX = mybir.AxisListType


class Res:
    __slots__ = ("name", "last_w", "reads", "dsem", "dcount", "excl")

    def __init__(self, name="r"):
        self.name = name
        self.excl = False
        self.last_w = None
        self.reads = []
        self.dsem = None
        self.dcount = 0


class V:
    __slots__ = ("ap", "res")

    def __init__(self, ap, res):
        self.ap = ap
        self.res = res

    def __getitem__(self, idx):
        return V(self.ap[idx], self.res)

    def r(self, res):
        return V(self.ap, res)


def _res_of(xs):
    out = []
    for x in xs:
        if x is None:
            continue
        out.append(x.res if isinstance(x, V) else x)
    return out


class Sched:
    ENGS = ("pe", "act", "dve", "pool", "sp")

    def __init__(self, nc, es):
        self.nc = nc
        self.es = es
        self.eng = {"pe": nc.tensor, "act": nc.scalar, "dve": nc.vector, "pool": nc.gpsimd, "sp": nc.sync}
        self.sem = {e: es.enter_context(nc.semaphore("sem_" + e)) for e in self.ENGS}
        self.cnt = {e: 0 for e in self.ENGS}
        self.known = {e: {} for e in self.ENGS}
        self.dpool = []
        self.dlive = []
        self.ndsem = 0
        self.nwaits = 0
        self.nops = 0
        self.phase_es = None
        self.uid = 0
        import os
        self.limit = int(os.environ["OPLIMIT"]) if "OPLIMIT" in os.environ else None

    def begin_phase(self):
        self.phase_es = ExitStack()

    def end_phase(self):
        self.barrier()
        for r in self.dlive:
            self.dpool.append((r.dsem, r.dcount))
            r.dsem = None
        self.dlive = []
        self.phase_es.close()
        self.phase_es = None

    def sb(self, name, shape, dt=F32):
        self.uid += 1
        name = "%s_u%d" % (name, self.uid)
        t = self.phase_es.enter_context(self.nc.sbuf_tensor(name, list(shape), dt))
        return V(t[:], Res(name))

    def ps(self, name, shape, dt=F32):
        self.uid += 1
        name = "%s_u%d" % (name, self.uid)
        t = self.phase_es.enter_context(self.nc.psum_tensor(name, list(shape), dt))
        rs = Res(name)
        rs.excl = True
        return V(t[:], rs)

    def _collect(self, eng, r, w, strict):
        waits = {}

        def add(ev, same_ok):
            if ev is None:
                return
            sem, val, src = ev
            if not strict and src == eng and (same_ok or eng == "pe"):
                return
            k = id(sem)
            if k not in waits or waits[k][1] < val:
                waits[k] = (sem, val)

        for res in r:
            add(res.last_w, False)
            if res.excl:
                for ev in res.reads:
                    add(ev, True)
        for res in w:
            add(res.last_w, True)
            for ev in res.reads:
                add(ev, True)
        return waits

    def _emit_waits(self, eng, waits):
        kn = self.known[eng]
        e = self.eng[eng]
        for k, (sem, val) in waits.items():
            if kn.get(k, 0) >= val:
                continue
            kn[k] = val
            e.wait_ge(sem, val)
            self.nwaits += 1

    def op(self, eng, fn, r=(), w=(), inc=True):
        if self.limit is not None and self.nops >= self.limit:
            return None
        r = _res_of(r)
        w = _res_of(w)
        self._emit_waits(eng, self._collect(eng, r, w, False))
        ins = fn(self.eng[eng])
        self.nops += 1
        if inc:
            self.cnt[eng] += 1
            ins.then_inc(self.sem[eng], 1)
            ev = (self.sem[eng], self.cnt[eng], eng)
        else:
            assert eng == "pe"
            ev = (self.sem[eng], self.cnt[eng] + 1, eng)
        for res in r:
            res.reads.append(ev)
        for res in w:
            res.last_w = ev
            res.reads = []
        return ins

    def dma(self, q, out, in_, sres=None, **kw):
        if self.limit is not None and self.nops >= self.limit:
            return None
        sr = sres if sres is not None else out.res
        if sr.dsem is None:
            if self.dpool:
                sr.dsem, sr.dcount = self.dpool.pop()
            else:
                sr.dsem = self.es.enter_context(self.nc.semaphore("dsem%d" % self.ndsem))
                sr.dcount = 0
                self.ndsem += 1
            self.dlive.append(sr)
        waits = self._collect(q, [in_.res], [out.res], True)
        k = id(sr.dsem)
        if sr.dcount > 0 and (k not in waits or waits[k][1] < sr.dcount):
            waits[k] = (sr.dsem, sr.dcount)
        self._emit_waits(q, waits)
        ins = self.eng[q].dma_start(out=out.ap, in_=in_.ap, **kw)
        sr.dcount += 16
        ins.then_inc(sr.dsem, 16)
        self.nops += 1
        ev = (sr.dsem, sr.dcount, "dma")
        in_.res.reads.append(ev)
        out.res.last_w = ev
        out.res.reads = []
        return ins

    def barrier(self):
        evs = [(self.sem[e], self.cnt[e]) for e in self.ENGS if self.cnt[e] > 0]
        evs += [(r.dsem, r.dcount) for r in self.dlive if r.dcount > 0]
        for e in self.ENGS:
            kn = self.known[e]
            for sem, val in evs:
                if sem is self.sem[e]:
                    continue
                if kn.get(id(sem), 0) >= val:
                    continue
                kn[id(sem)] = val
                self.eng[e].wait_ge(sem, val)
                self.nwaits += 1

    def mm(self, out, lhsT, rhs, start=True, stop=True, inc=None, **kw):
        if inc is None:
            inc = bool(stop)
        return self.op("pe", lambda e: e.matmul(out.ap, lhsT.ap, rhs.ap, start=start, stop=stop, **kw),
                       r=[lhsT, rhs], w=[out], inc=inc)

    def tr(self, out, in_, ident):
        return self.op("pe", lambda e: e.transpose(out.ap, in_.ap, ident.ap), r=[in_, ident], w=[out])

    def act(self, out, in_, func, bias=None, scale=None, accum=None, eng="act"):
        kw = {}
        rr = [in_]
        ww = [out]
        if bias is not None:
            if isinstance(bias, V):
                kw["bias"] = bias.ap
                rr.append(bias)
            else:
                kw["bias"] = bias
        if scale is not None:
            if isinstance(scale, V):
                kw["scale"] = scale.ap
                rr.append(scale)
            else:
                kw["scale"] = scale
        if accum is not None:
            kw["accum_out"] = accum.ap
            ww.append(accum)
        return self.op("act", lambda e: e.activation(out.ap, in_.ap, func, **kw), r=rr, w=ww)

    def tt(self, eng, out, a, b, op):
        return self.op(eng, lambda e: e.tensor_tensor(out.ap, a.ap, b.ap, op), r=[a, b], w=[out])

    def ts(self, eng, out, a, s1, op0, s2=None, op1=None, accum=None):
        rr = [a]
        ww = [out]
        a1 = s1
        a2 = s2
        if isinstance(s1, V):
            rr.append(s1)
            a1 = s1.ap
        if isinstance(s2, V):
            rr.append(s2)
            a2 = s2.ap
        kw = {}
        if op1 is not None:
            kw["op1"] = op1
        if accum is not None:
            kw["accum_out"] = accum.ap
            ww.append(accum)
        return self.op(eng, lambda e: e.tensor_scalar(out.ap, a.ap, a1, a2, op0, **kw), r=rr, w=ww)

    def stt(self, eng, out, a, s, b, op0, op1, accum=None):
        rr = [a, b]
        ww = [out]
        sc = s
        if isinstance(s, V):
            rr.append(s)
            sc = s.ap
        kw = {}
        if accum is not None:
            kw["accum_out"] = accum.ap
            ww.append(accum)
        return self.op(eng, lambda e: e.scalar_tensor_tensor(out.ap, a.ap, sc, b.ap, op0, op1, **kw), r=rr, w=ww)

    def copy(self, eng, out, in_):
        if eng == "act":
            return self.op("act", lambda e: e.copy(out.ap, in_.ap), r=[in_], w=[out])
        return self.op(eng, lambda e: e.tensor_copy(out.ap, in_.ap), r=[in_], w=[out])

    def memset(self, eng, out, val):
        return self.op(eng, lambda e: e.memset(out.ap, val), w=[out])

    def reduce(self, eng, out, in_, op, axis=None):
        ax = AX.X if axis is None else axis
        return self.op(eng, lambda e: e.tensor_reduce(out.ap, in_.ap, ax, op), r=[in_], w=[out])

    def asel(self, out, in_, pattern, cmp, fill, base=0, cm=0):
        return self.op("pool", lambda e: e.affine_select(out.ap, in_.ap, pattern=pattern, compare_op=cmp, fill=fill,
                                                         base=base, channel_multiplier=cm), r=[in_], w=[out])


D = 1024
KC = 8
DFF = 2816
FC = 22
EPS = 1e-6
EVEN_IN = 3080
ODD_IN = 2888
NEG = -30000.0


class DT:
    def __init__(self, ap, name):
        self.ap = ap
        self.name = name
        self.res = {}

    def v(self, key, ap):
        if key not in self.res:
            self.res[key] = Res("%s_%s" % (self.name, str(key)))
        return V(ap, self.res[key])


class Ctx:
    pass


def load_rows(S, C, dst, src_dt, key, ap, q="sp"):
    S.dma(q, dst, src_dt.v(key, ap), sres=dst.res)


def rms_to_hnT(S, C, xt, hnT_cols, ptr, hn, ssq, rstd):
    S.act(hn, xt, AF.Square, accum=ssq)
    S.act(rstd, ssq, AF.Sqrt, bias=C.eps, scale=1.0 / D)
    S.op("dve", lambda e: e.reciprocal(rstd.ap, rstd.ap), r=[rstd], w=[rstd])
    S.ts("pool", hn, xt, rstd, ALU.mult, 1.0, ALU.mult)
    for kc in range(KC):
        S.tr(ptr[:, kc * 128:(kc + 1) * 128], hn[:, kc * 128:(kc + 1) * 128], C.identb)
    S.copy("act", hnT_cols, V(ptr.ap.rearrange("p (k t) -> p k t", k=KC), ptr.res))


def load_weight_scaled(S, C, wsb, w_dram_ap, nrows_chunks, ncols, gcol, stg, colchunk, name):
    i = 0
    width = stg[0].ap.shape[1]
    colchunk = min(colchunk, width)
    for rc in range(nrows_chunks):
        for c0 in range(0, ncols, colchunk):
            c1 = min(ncols, c0 + colchunk)
            st = stg[i % len(stg)]
            S.dma(("sp", "act")[i % 2], st[:, 0:c1 - c0], V(w_dram_ap[rc * 128:(rc + 1) * 128, c0:c1], C.wres), sres=st.res)
            eng = ("dve", "act", "dve", "pool")[i % 4]
            if gcol is None:
                S.copy(eng, wsb[:, rc, c0:c1], st[:, 0:c1 - c0])
            elif eng == "act":
                S.act(wsb[:, rc, c0:c1], st[:, 0:c1 - c0], AF.Copy, scale=gcol[:, rc:rc + 1])
            else:
                S.ts(eng, wsb[:, rc, c0:c1], st[:, 0:c1 - c0], gcol[:, rc:rc + 1], ALU.mult, 1.0, ALU.mult)
            i += 1


def ffn_phase(S, C, layer, src, dst):
    NT = 512
    HW = 256
    NSUB = NT // 128
    S.begin_phase()
    wup = S.sb("wup", [128, KC, 2 * DFF], BF16)
    wdn = S.sb("wdn", [128, FC, D], BF16)
    gsb = S.sb("gsb", [128, KC])
    cw = S.sb("cw", [128, 2 * FC, 3])
    cb = S.sb("cb", [128, 2 * FC])
    stg = [S.sb("stg%d" % i, [128, 352]) for i in range(4)]
    S.dma("sp", gsb, V(C.ffn_g[layer], C.wres), sres=gsb.res)
    S.dma("sp", cw, V(C.ffn_cw[layer], C.wres), sres=cw.res)
    S.dma("sp", cb, V(C.ffn_cb[layer], C.wres), sres=cb.res)
    load_weight_scaled(S, C, wup, C.ffn_wup[layer], KC, 2 * DFF, gsb, stg, 352, "wup")
    load_weight_scaled(S, C, wdn, C.ffn_wdn[layer], FC, D, None, stg, 352, "wdn")

    xs = [S.sb("x%d" % i, [128, D]) for i in range(2)]
    xr = [S.sb("xr%d" % i, [128, D]) for i in range(2)]
    hn = [S.sb("hn%d" % i, [128, D], BF16) for i in range(2)]
    ssq = [S.sb("ssq%d" % i, [128, 1]) for i in range(2)]
    rstd = [S.sb("rstd%d" % i, [128, 1]) for i in range(2)]
    hT_ = S.sb("hnT", [128, KC, NT], BF16)
    hT = S.sb("hT", [128, FC, NT], BF16)
    hal = S.sb("hal", [128, 2 * FC, 2])
    rr = [S.sb("rr%d" % i, [128, HW + 2]) for i in range(4)]
    acc = [S.sb("acc%d" % i, [128, HW]) for i in range(4)]
    sg = [S.sb("sg%d" % i, [128, HW]) for i in range(2)]
    ptr = [S.ps("ptr%d" % i, [128, D], BF16) for i in range(2)]
    pup = [S.ps("pup%d" % i, [128, 512]) for i in range(4)]
    pdn = [S.ps("pdn%d" % i, [128, 512]) for i in range(2)]

    gsub = 0
    for b in range(C.NSEQ):
        for t0 in range(0, C.L, NT):
            rows = []
            for sub in range(NSUB):
                r0 = b * C.L + t0 + sub * 128
                xt = xs[gsub % 2]
                rows.append(r0)
                load_rows(S, C, xt, src, (r0 // 128), src.ap[r0:r0 + 128, :])
                j = gsub % 2
                rms_to_hnT(S, C, xt, hT_[:, :, sub * 128:(sub + 1) * 128], ptr[j], hn[j], ssq[j], rstd[j])
                gsub += 1
            items = [(c, hf, which) for c in range(FC) for hf in range(2) for which in range(2)]
            ni = len(items)

            def fA(n):
                c, hf, which = items[n]
                ch = c + which * FC
                ps = pup[(2 * c + which) % 4]
                if hf == 0:
                    for kc in range(KC):
                        S.mm(ps, wup[:, kc, ch * 128:(ch + 1) * 128], hT_[:, kc, :], start=(kc == 0), stop=(kc == KC - 1))
                psh = ps[:, hf * HW:(hf + 1) * HW]
                r = rr[n % 4]
                a = acc[n % 4]
                if t0 == 0 and hf == 0:
                    S.memset("pool", r[:, 0:2], 0.0)
                else:
                    S.copy("pool", r[:, 0:2], hal[:, ch, :])
                S.copy("act", r[:, 2:HW + 2], psh)
                S.act(a, psh, AF.Identity, bias=cb[:, ch:ch + 1], scale=cw[:, ch, 2:3])
                S.copy("pool", hal[:, ch, :], r[:, HW:HW + 2])

            def fB(n):
                c, hf, which = items[n]
                ch = c + which * FC
                r = rr[n % 4]
                a = acc[n % 4]
                S.stt("dve", a, r[:, 1:HW + 1], cw[:, ch, 1:2], a, ALU.mult, ALU.add)
                S.stt("dve", a, r[:, 0:HW], cw[:, ch, 0:1], a, ALU.mult, ALU.add)

            def fC(p):
                S.act(sg[p % 2], acc[(2 * p) % 4], AF.Silu)

            def fD(p):
                c, hf = p // 2, p % 2
                S.tt("dve", hT[:, c, hf * HW:(hf + 1) * HW], sg[p % 2], acc[(2 * p + 1) % 4], ALU.mult)

            npair = ni // 2
            for n in range(ni + 3):
                if n < ni:
                    fA(n)
                if 0 <= n - 1 < ni:
                    fB(n - 1)
                if n >= 2 and n % 2 == 0 and (n - 2) // 2 < npair:
                    fC((n - 2) // 2)
                if n >= 3 and n % 2 == 1 and (n - 3) // 2 < npair:
                    fD((n - 3) // 2)
            for sub in range(NSUB):
                r0 = rows[sub]
                xt = xr[sub % 2]
                load_rows(S, C, xt, src, (r0 // 128), src.ap[r0:r0 + 128, :])
                for nh in range(2):
                    ps2 = pdn[nh]
                    for fc in range(FC):
                        S.mm(ps2, hT[:, fc, sub * 128:(sub + 1) * 128], wdn[:, fc, nh * 512:(nh + 1) * 512],
                             start=(fc == 0), stop=(fc == FC - 1))
                    S.tt("dve", xt[:, nh * 512:(nh + 1) * 512], xt[:, nh * 512:(nh + 1) * 512], ps2, ALU.add)
                S.dma("sp", dst.v((r0 // 128), dst.ap[r0:r0 + 128, :]), xt, sres=xt.res)
    S.end_phase()


def flat(v):
    return V(v.ap.rearrange("p h j -> p (h j)"), v.res)


def v3(v, h=4):
    return V(v.ap.rearrange("p (h j) -> p h j", h=h), v.res)


def bc_last(v, n):
    H = v.ap.shape[1]
    return V(v.ap.unsqueeze(2).to_broadcast([128, H, n]), v.res)


def bc_mid(v, h):
    n = v.ap.shape[1]
    return V(v.ap.unsqueeze(1).to_broadcast([128, h, n]), v.res)


class Banks:
    def __init__(self, S, n=8):
        self.b = [S.ps("bank%d" % i, [128, 512]) for i in range(n)]
        self.i = 0

    def get(self):
        v = self.b[self.i % len(self.b)]
        self.i += 1
        return v


def bf(v):
    return V(v.ap.bitcast(BF16), v.res)


F32R = mybir.dt.float32r


def r32(v):
    return V(v.ap.bitcast(F32R), v.res)


def run_lanes(gens):
    gens = [g for g in gens if g is not None]
    while gens:
        for g in list(gens):
            try:
                next(g)
            except StopIteration:
                gens.remove(g)


def even_phase(S, C, j, src, dst):
    layer = 2 * j
    NT = 256
    NSUB = NT // 128
    H = 4
    S.begin_phase()
    PB = Banks(S)
    win = S.sb("win", [128, KC, EVEN_IN], BF16)
    wout = S.sb("wout", [128, KC, D], BF16)
    gsb = S.sb("gsb", [128, KC])
    stg = [S.sb("stg%d" % i, [128, 770]) for i in range(4)]
    S.dma("sp", gsb, V(C.mix_g[layer], C.wres), sres=gsb.res)
    load_weight_scaled(S, C, win, C.ev_win[j], KC, EVEN_IN, gsb, stg, 1540, "win")
    load_weight_scaled(S, C, wout, C.ev_wout[j], KC, D, None, stg, 1024, "wout")
    wTm = S.sb("wTm", [128, H, 128], BF16)
    wTf = S.sb("wTf", [128, H, 128])
    brow = S.sb("brow", [1, 512])
    cw = S.sb("gcw", [128, 12, 4])
    alog = S.sb("alog", [128, 4])
    dtb = S.sb("dtb", [128, 4])
    nega = S.sb("nega", [128, 4])
    gng = S.sb("gng", [128, 128])
    S.dma("sp", wTf, V(C.gm_wT[j], C.wres), sres=wTf.res)
    S.dma("sp", brow, V(C.gm_b[j], C.wres), sres=brow.res)
    S.dma("sp", cw, V(C.gdn_cw[j], C.wres), sres=cw.res)
    S.dma("sp", alog, V(C.gdn_alog[j].partition_broadcast(128), C.wres), sres=alog.res)
    S.dma("sp", dtb, V(C.gdn_dtb[j].partition_broadcast(128), C.wres), sres=dtb.res)
    S.dma("sp", gng, V(C.gdn_ng[j].partition_broadcast(128), C.wres), sres=gng.res)
    ones = S.sb("ones", [128, 128])
    tri = S.sb("tri", [128, 128])
    ntri = S.sb("ntri", [128, 128])
    strict = S.sb("strict", [128, 128])
    incl = S.sb("incl", [128, 128])
    S.memset("pool", ones, 1.0)
    S.asel(tri, ones, [[1, 128]], ALU.is_ge, 0.0, base=0, cm=-1)
    S.ts("pool", ntri, tri, -1.0, ALU.mult, 1.0, ALU.mult)
    S.asel(strict, ones, [[-1, 128]], ALU.is_ge, 0.0, base=-1, cm=1)
    S.asel(incl, ones, [[-1, 128]], ALU.is_ge, 0.0, base=0, cm=1)
    S.tt("pool", wTm, wTf, bc_mid(tri, H), ALU.mult)
    S.act(nega, alog, AF.Exp)
    S.ts("pool", nega, nega, -1.0, ALU.mult, 1.0, ALU.mult)

    xs = [S.sb("x%d" % i, [128, D]) for i in range(4)]
    hn = [S.sb("hn%d" % i, [128, D], BF16) for i in range(2)]
    ssq = [S.sb("ssq%d" % i, [128, 1]) for i in range(2)]
    rstd = [S.sb("rstd%d" % i, [128, 1]) for i in range(2)]
    hnT = [S.sb("hnT%d" % i, [128, KC, NT], BF16) for i in range(2)]
    uTs = [S.sb("uT%d" % i, [128, H, NT], BF16) for i in range(2)]
    yT = S.sb("yT", [128, KC, NT], BF16)
    qTs = [S.sb("qT%d" % i, [128, H, NT], BF16) for i in range(2)]
    kTs = [S.sb("kT%d" % i, [128, H, NT], BF16) for i in range(2)]
    vTs = [S.sb("vT%d" % i, [128, H, NT], BF16) for i in range(2)]
    XSL = [None, None]
    hal = S.sb("hal", [128, 12, 3])
    rr = [S.sb("rr%d" % i, [128, NT + 3]) for i in range(2)]
    ca = [S.sb("ca%d" % i, [128, NT]) for i in range(2)]
    qs = [S.sb("qs%d" % i, [128, NT]) for i in range(2)]
    sq = S.sb("sq", [128, NT])
    rn = S.sb("rn", [128, NT])
    Sst = S.sb("Sst", [128, H, 128])
    Sr = S.sb("Sr", [128, H, 128])

    def F(name, dt=F32):
        return S.sb(name, [128, H, 128], dt)

    ktok, vtok, vg, sqv, zs, G1, G2, E, nbm, usb, osq, on, gz, t1 = [
        F(n) for n in ("ktok", "vtok", "vg", "sqv", "zs", "G1", "G2", "E", "nbm", "usb", "osq", "on", "gz", "t1")]
    Lp0, Lp1, intra, intraT, U0, U1, TT, vb, kbg, wTs, qgT, kd, vnew = [
        F(n) for n in ("Lp0", "Lp1", "intra", "intraT", "U0", "U1", "TT", "vb", "kbg", "wTs", "qgT", "kd", "vnew")]
    vn = F("vn", BF16)
    onb = F("onb", BF16)
    sm = {n: S.sb(n, [128, 4]) for n in ("vsum", "vvar", "vrs", "beta", "nbeta", "xa", "xe", "xm", "sp", "g", "bk", "edl", "oss", "ors")}
    gcs = S.sb("gcs", [128, 8])
    egs = S.sb("egs", [128, 8])

    st = {"gsub": 0, "ich": 0}

    def proj_task(b, t0, k):
        hT_ = hnT[k]
        uT, qT, kT, vT = uTs[k], qTs[k], kTs[k], vTs[k]
        xsl = []
        for sub in range(NSUB):
            r0 = b * C.L + t0 + sub * 128
            xt = xs[st["gsub"] % 4]
            xsl.append((xt, r0))
            load_rows(S, C, xt, src, (r0 // 128), src.ap[r0:r0 + 128, :])
            jj = st["gsub"] % 2
            pt = PB.get()
            rms_to_hnT(S, C, xt, hT_[:, :, sub * 128:(sub + 1) * 128], bf(pt), hn[jj], ssq[jj], rstd[jj])
            st["gsub"] += 1
        for c in range(H):
            ps = PB.get()
            for kc in range(KC):
                S.mm(ps[:, 0:NT], win[:, kc, c * 128:(c + 1) * 128], hT_[:, kc, :], start=(kc == 0), stop=(kc == KC - 1))
            S.act(uT[:, c, :], ps[:, 0:NT], AF.Gelu_apprx_tanh)
            yield
        for c in range(12):
            ps = PB.get()
            for kc in range(KC):
                S.mm(ps[:, 0:NT], win[:, kc, 1024 + c * 128:1024 + (c + 1) * 128], hT_[:, kc, :], start=(kc == 0), stop=(kc == KC - 1))
            yield
            r = rr[st["ich"] % 2]
            a = ca[st["ich"] % 2]
            if t0 == 0:
                S.memset("pool", r[:, 0:3], 0.0)
            else:
                S.copy("pool", r[:, 0:3], hal[:, c, :])
            S.copy("act", r[:, 3:NT + 3], ps[:, 0:NT])
            S.act(a, ps[:, 0:NT], AF.Copy, scale=cw[:, c, 3:4])
            S.copy("pool", hal[:, c, :], r[:, NT:NT + 3])
            yield
            for tap in (2, 1, 0):
                S.stt("dve", a, r[:, tap:tap + NT], cw[:, c, tap:tap + 1], a, ALU.mult, ALU.add)
            hh = c % 4
            yield
            if c >= 8:
                S.act(vT[:, hh, :], a, AF.Silu)
            else:
                q_ = qs[st["ich"] % 2]
                S.act(q_, a, AF.Silu)
                S.act(sq, q_, AF.Square)
                yield
                pss = PB.get()
                S.mm(pss[:, 0:NT], ones, sq)
                yield
                S.act(rn, pss[:, 0:NT], AF.Sqrt, bias=C.eps, scale=1.0)
                yield
                S.op("dve", lambda e: e.reciprocal(rn.ap, rn.ap), r=[rn], w=[rn])
                if c < 4:
                    S.stt("dve", qT[:, hh, :], q_, 128.0 ** -0.5, rn, ALU.mult, ALU.mult)
                else:
                    S.tt("dve", kT[:, hh, :], q_, rn, ALU.mult)
            st["ich"] += 1
            yield
        XSL[k] = xsl
        yield

    def chunk_task(b, t0, k):
        hT_ = hnT[k]
        uT, qT, kT, vT = uTs[k], qTs[k], kTs[k], vTs[k]
        xsl = XSL[k]
        if t0 == 0:
            S.memset("pool", Sst, 0.0)
            S.copy("pool", r32(Sr), Sst)
        for sub in range(NSUB):
            cs = slice(sub * 128, (sub + 1) * 128)
            xt, r0 = xsl[sub]
            pv = PB.get()
            pz = PB.get()
            pba = PB.get()
            for kc in range(KC):
                S.mm(pv, hT_[:, kc, cs], win[:, kc, 512:1024], start=(kc == 0), stop=(kc == KC - 1))
            for kc in range(KC):
                S.mm(pz, hT_[:, kc, cs], win[:, kc, 2568:3080], start=(kc == 0), stop=(kc == KC - 1))
            for kc in range(KC):
                S.mm(pba[:, 0:8], hT_[:, kc, cs], win[:, kc, 2560:2568], start=(kc == 0), stop=(kc == KC - 1))
            yield
            S.act(flat(vg), pv, AF.Gelu_apprx_tanh)
            S.act(flat(zs), pz, AF.Silu)
            S.act(sm["beta"], pba[:, 0:4], AF.Sigmoid)
            S.tt("dve", sm["xa"], pba[:, 4:8], dtb, ALU.add)
            S.reduce("dve", sm["vsum"], vg, ALU.add)
            S.stt("dve", vg, bc_last(sm["vsum"], 128), -1.0 / 128, vg, ALU.mult, ALU.add)
            S.tt("pool", sqv, vg, vg, ALU.mult)
            S.reduce("dve", sm["vvar"], sqv, ALU.add)
            S.act(sm["vrs"], sm["vvar"], AF.Sqrt, bias=C.eps, scale=1.0 / 128)
            S.op("dve", lambda e: e.reciprocal(sm["vrs"].ap, sm["vrs"].ap), r=[sm["vrs"]], w=[sm["vrs"]])
            S.tt("dve", vn, vg, bc_last(sm["vrs"], 128), ALU.mult)
            yield
            pm = PB.get()
            for gI in range(H):
                S.mm(pm[:, gI * 128:(gI + 1) * 128], vn[:, gI, :], wTm[:, gI, :], start=True, stop=False)
                S.mm(pm[:, gI * 128:(gI + 1) * 128], ones[0:1, :], brow[0:1, gI * 128:(gI + 1) * 128], start=False, stop=True)
            yield
            S.tt("dve", yT[:, 0:4, cs], v3(pm), uT[:, :, cs], ALU.mult)
            yield
            S.ts("dve", sm["nbeta"], sm["beta"], -1.0, ALU.mult)
            S.ts("dve", sm["xm"], sm["xa"], 30.0, ALU.min)
            S.act(sm["xe"], sm["xm"], AF.Exp)
            S.act(sm["sp"], sm["xe"], AF.Ln, bias=1.0, scale=1.0)
            S.ts("dve", sm["xm"], sm["xa"], -30.0, ALU.add, 0.0, ALU.max)
            S.tt("dve", sm["sp"], sm["sp"], sm["xm"], ALU.add)
            S.tt("dve", sm["g"], sm["sp"], nega, ALU.mult)
            g = sm["g"]
            pg = PB.get()
            S.mm(pg[:, 0:4], tri, g)
            S.mm(pg[:, 4:8], ones, g)
            yield
            S.copy("dve", gcs, pg[:, 0:8])
            S.act(egs, gcs, AF.Exp)
            S.tt("dve", sm["edl"], gcs[:, 4:8], gcs[:, 0:4], ALU.subtract)
            S.act(sm["edl"], sm["edl"], AF.Exp)
            S.tt("dve", sm["bk"], sm["beta"], egs[:, 0:4], ALU.mult)
            yield
            pk = PB.get()
            for h in range(H):
                S.tr(bf(pk)[:, h * 128:(h + 1) * 128], kT[:, h, cs], C.identb)
            yield
            S.copy("act", flat(ktok), bf(pk)[:, 0:512])
            pk = PB.get()
            for h in range(H):
                S.tr(bf(pk)[:, h * 128:(h + 1) * 128], vT[:, h, cs], C.identb)
            yield
            S.copy("act", flat(vtok), bf(pk)[:, 0:512])
            yield
            S.tt("pool", G2, bc_mid(C.identf, H), bc_last(gcs[:, 0:4], 128), ALU.mult)
            pd = PB.get()
            S.mm(pd, ones, flat(G2))
            yield
            S.tt("dve", E, bc_last(gcs[:, 0:4], 128), v3(pd), ALU.subtract)
            S.ts("dve", flat(E), flat(E), 0.0, ALU.min)
            S.act(flat(E), flat(E), AF.Exp)
            yield
            pkk = PB.get()
            pqk = PB.get()
            for h in range(H):
                S.mm(pkk[:, h * 128:(h + 1) * 128], kT[:, h, cs], kT[:, h, cs])
            for h in range(H):
                S.mm(pqk[:, h * 128:(h + 1) * 128], qT[:, h, cs], kT[:, h, cs])
            yield
            S.tt("pool", nbm, bc_mid(strict, H), bc_last(sm["nbeta"], 128), ALU.mult)
            S.tt("dve", flat(t1), pkk, flat(E), ALU.mult)
            S.tt("pool", r32(Lp0), t1, nbm, ALU.mult)
            S.tt("dve", flat(osq), pqk, flat(E), ALU.mult)
            S.tt("pool", intra, osq, bc_mid(incl, H), ALU.mult)
            yield
            pu = PB.get()
            for h in range(H):
                S.tr(pu[:, h * 128:(h + 1) * 128], Lp0[:, h, :], C.identf)
            yield
            S.copy("act", r32(flat(U0)), pu)
            S.tt("dve", r32(TT), v3(pu), bc_mid(C.identf, H), ALU.add)
            pi = PB.get()
            for h in range(H):
                S.tr(pi[:, h * 128:(h + 1) * 128], intra[:, h, :], C.identf)
            yield
            S.copy("act", r32(flat(intraT)), pi)
            yield
            Us = [U0, U1]
            Ls = [Lp0, Lp1]
            for k in range(1, 7):
                Uo, Un = Us[(k - 1) % 2], Us[k % 2]
                Lo, Ln_ = Ls[(k - 1) % 2], Ls[k % 2]
                if k <= 5:
                    p1 = PB.get()
                    for h in range(H):
                        S.mm(p1[:, h * 128:(h + 1) * 128], r32(Lo[:, h, :]), r32(Uo[:, h, :]))
                p2 = PB.get()
                for h in range(H):
                    S.mm(p2[:, h * 128:(h + 1) * 128], r32(Uo[:, h, :]), r32(Lo[:, h, :]))
                yield
                if k <= 5:
                    S.copy("act", r32(flat(Un)), p1)
                S.copy("dve", r32(flat(Ln_)), p2)
                yield
                p3 = PB.get()
                for h in range(H):
                    S.mm(p3[:, h * 128:(h + 1) * 128], r32(Ln_[:, h, :]), r32(TT[:, h, :]))
                yield
                S.tt("dve", r32(flat(TT)), flat(TT), p3, ALU.add)
                yield
            yield
            S.tt("pool", r32(vb), vtok, bc_last(sm["beta"], 128), ALU.mult)
            S.tt("pool", r32(kbg), ktok, bc_last(sm["bk"], 128), ALU.mult)
            S.tt("pool", r32(kd), ktok, bc_last(sm["edl"], 128), ALU.mult)
            pU = PB.get()
            for h in range(H):
                S.mm(pU[:, h * 128:(h + 1) * 128], r32(TT[:, h, :]), r32(vb[:, h, :]))
            yield
            S.copy("act", flat(usb), pU)
            pW = PB.get()
            for h in range(H):
                S.mm(pW[:, h * 128:(h + 1) * 128], r32(kbg[:, h, :]), r32(TT[:, h, :]))
            yield
            S.copy("act", r32(flat(wTs)), pW)
            yield
            S.tt("pool", G1, bc_mid(C.identf, H), bc_last(egs[:, 0:4], 128), ALU.mult)
            pe_ = PB.get()
            S.mm(pe_, ones, flat(G1))
            yield
            S.tt("dve", r32(qgT), qT[:, :, cs], v3(pe_), ALU.mult)
            yield
            pws = PB.get()
            for h in range(H):
                S.mm(pws[:, h * 128:(h + 1) * 128], r32(wTs[:, h, :]), r32(Sr[:, h, :]))
            yield
            S.tt("dve", r32(flat(vnew)), flat(usb), pws, ALU.subtract)
            po = PB.get()
            for h in range(H):
                S.mm(po[:, h * 128:(h + 1) * 128], r32(qgT[:, h, :]), r32(Sr[:, h, :]), start=True, stop=False)
                S.mm(po[:, h * 128:(h + 1) * 128], r32(intraT[:, h, :]), r32(vnew[:, h, :]), start=False, stop=True)
            pS = PB.get()
            for h in range(H):
                S.mm(pS[:, h * 128:(h + 1) * 128], r32(kd[:, h, :]), r32(vnew[:, h, :]))
            yield
            S.tt("dve", Sst, Sst, bc_last(egs[:, 4:8], 128), ALU.mult)
            S.tt("dve", flat(Sst), flat(Sst), pS, ALU.add)
            S.copy("act", r32(Sr), Sst)
            yield
            yield
            S.act(flat(osq), po, AF.Square)
            S.reduce("dve", sm["oss"], osq, ALU.add)
            S.act(sm["ors"], sm["oss"], AF.Sqrt, bias=C.eps, scale=1.0 / 128)
            S.op("dve", lambda e: e.reciprocal(sm["ors"].ap, sm["ors"].ap), r=[sm["ors"]], w=[sm["ors"]])
            S.tt("pool", gz, zs, bc_mid(gng, H), ALU.mult)
            S.tt("dve", on, v3(po), bc_last(sm["ors"], 128), ALU.mult)
            S.tt("dve", onb, on, gz, ALU.mult)
            pt2 = PB.get()
            for h in range(H):
                S.tr(bf(pt2)[:, h * 128:(h + 1) * 128], onb[:, h, :], C.identb)
            yield
            S.copy("act", yT[:, 4:8, cs], v3(bf(pt2)[:, 0:512]))
            yield
            for nh in range(2):
                pso = PB.get()
                for kc in range(KC):
                    S.mm(pso, yT[:, kc, cs], wout[:, kc, nh * 512:(nh + 1) * 512], start=(kc == 0), stop=(kc == KC - 1))
                S.tt("dve", xt[:, nh * 512:(nh + 1) * 512], xt[:, nh * 512:(nh + 1) * 512], pso, ALU.add)
            S.dma("sp", dst.v((r0 // 128), dst.ap[r0:r0 + 128, :]), xt, sres=xt.res)

    tiles = [(b, t0) for b in range(C.NSEQ) for t0 in range(0, C.L, NT)]
    prev = None
    for i, (b, t0) in enumerate(tiles):
        run_lanes([proj_task(b, t0, i % 2), chunk_task(*prev) if prev is not None else None])
        prev = (b, t0, i % 2)
    run_lanes([chunk_task(*prev)])
    S.end_phase()


OD_EXT = 3016
NQK = 18


def odd_phase(S, C, j, src, dst):
    odd_proj_phase(S, C, j, src)
    odd_attn_phase(S, C, j, src, dst)


def odd_proj_phase(S, C, j, src):
    layer = 2 * j + 1
    NT = 512
    NSUB = NT // 128
    S.begin_phase()
    PB = Banks(S)
    win = S.sb("win", [128, KC, OD_EXT], BF16)
    gsb = S.sb("gsb", [128, KC])
    stg = [S.sb("stg%d" % i, [128, 754]) for i in range(4)]
    S.dma("sp", gsb, V(C.mix_g[layer], C.wres), sres=gsb.res)
    load_weight_scaled(S, C, win, C.od_win[j], KC, OD_EXT, gsb, stg, 1508, "win")
    gn = S.sb("gn", [128, 4])
    S.dma("sp", gn, V(C.od_gn[j], C.wres), sres=gn.res)
    S.ts("pool", gn[:, 0:1], gn[:, 0:1], 64.0 ** -0.5, ALU.mult, 1.0, ALU.mult)
    S.ts("pool", gn[:, 2:3], gn[:, 2:3], 128.0 ** -0.5, ALU.mult, 1.0, ALU.mult)
    ones = S.sb("ones", [128, 128])
    bd64 = S.sb("bd64", [128, 128])
    S.memset("pool", ones, 1.0)
    S.memset("pool", bd64, 0.0)
    S.memset("pool", bd64[0:64, 0:64], 1.0)
    S.memset("pool", bd64[64:128, 64:128], 1.0)

    xs = [S.sb("x%d" % i, [128, D]) for i in range(2)]
    hn = [S.sb("hn%d" % i, [128, D], BF16) for i in range(2)]
    ssq = [S.sb("ssq%d" % i, [128, 1]) for i in range(2)]
    rstd = [S.sb("rstd%d" % i, [128, 1]) for i in range(2)]
    hnT = [S.sb("hnT%d" % i, [128, KC, NT], BF16) for i in range(2)]
    oT = [S.sb("oT%d" % i, [128, NQK, NT], BF16) for i in range(2)]
    sqb = [S.sb("sqb%d" % i, [128, NT]) for i in range(2)]
    rn = [S.sb("rn%d" % i, [128, NT]) for i in range(2)]
    tokb = [S.sb("tokb%d" % i, [128, 640], BF16) for i in range(2)]
    iwt = [S.sb("iwt%d" % i, [128, 8]) for i in range(2)]

    chunks = []
    for c in range(4):
        chunks.append((c * 128, "n64", 0))
    for c in range(4):
        chunks.append((512 + c * 128, "n64", 1))
    for c in range(4):
        chunks.append((1536 + c * 128, "n128", 2))
    chunks.append((2048, "n128", 3))
    for c in range(4):
        chunks.append((2304 + c * 128, "scale", None))
    chunks.append((2888, "copy", None))

    gsub = 0
    ist = 0
    inorm = 0
    for b in range(C.NSEQ):
        for t0 in range(0, C.L, NT):
            hT_ = hnT[ist % 2]
            o_ = oT[ist % 2]
            for sub in range(NSUB):
                r0 = b * C.L + t0 + sub * 128
                xt = xs[gsub % 2]
                load_rows(S, C, xt, src, (r0 // 128), src.ap[r0:r0 + 128, :])
                jj = gsub % 2
                pt = PB.get()
                rms_to_hnT(S, C, xt, hT_[:, :, sub * 128:(sub + 1) * 128], bf(pt), hn[jj], ssq[jj], rstd[jj])
                gsub += 1
            for ci, (c0, kind, gi) in enumerate(chunks):
                ps = PB.get()
                for kc in range(KC):
                    S.mm(ps[:, 0:NT], win[:, kc, c0:c0 + 128], hT_[:, kc, :], start=(kc == 0), stop=(kc == KC - 1))
                if kind == "scale":
                    S.act(o_[:, ci, :], ps[:, 0:NT], AF.Copy, scale=0.125)
                elif kind == "copy":
                    S.copy("act", o_[:, ci, :], ps[:, 0:NT])
                else:
                    sq_ = sqb[inorm % 2]
                    rn_ = rn[inorm % 2]
                    inorm += 1
                    S.act(sq_, ps[:, 0:NT], AF.Square)
                    pss = PB.get()
                    S.mm(pss[:, 0:NT], bd64 if kind == "n64" else ones, sq_)
                    dim = 64.0 if kind == "n64" else 128.0
                    S.act(rn_, pss[:, 0:NT], AF.Sqrt, bias=C.eps, scale=1.0 / dim)
                    S.op("dve", lambda e, rn_=rn_: e.reciprocal(rn_.ap, rn_.ap), r=[rn_], w=[rn_])
                    S.stt("dve", o_[:, ci, :], ps[:, 0:NT], gn[:, gi:gi + 1], rn_, ALU.mult, ALU.mult)
            for sub in range(NSUB):
                cs = slice(sub * 128, (sub + 1) * 128)
                r0 = t0 + sub * 128
                tb = tokb[sub % 2]
                iw_ = iwt[sub % 2]
                pv = PB.get()
                for kc in range(KC):
                    S.mm(pv, hT_[:, kc, cs], win[:, kc, 1024:1536], start=(kc == 0), stop=(kc == KC - 1))
                p2 = PB.get()
                for kc in range(KC):
                    S.mm(p2[:, 0:128], hT_[:, kc, cs], win[:, kc, 2176:2304], start=(kc == 0), stop=(kc == KC - 1))
                for kc in range(KC):
                    S.mm(p2[:, 128:136], hT_[:, kc, cs], win[:, kc, 2880:2888], start=(kc == 0), stop=(kc == KC - 1))
                S.copy("act", tb[:, 0:512], pv)
                S.copy("act", tb[:, 512:640], p2[:, 0:128])
                S.ts("dve", iw_, p2[:, 128:136], 8.0 ** -0.5, ALU.mult)
                S.dma("sp", C.vtok.v((b, r0 // 128), C.vtok.ap[b, r0:r0 + 128, :]), tb, sres=tb.res)
                S.dma("sp", C.iwd.v((b, r0 // 128), C.iwd.ap[b, r0:r0 + 128, :]), iw_, sres=iw_.res)
            S.dma("sp", C.qkT.v((b, t0 // NT), C.qkT.ap[b, :, :, t0:t0 + NT].rearrange("c p t -> p c t")), o_, sres=o_.res)
            ist += 1
    S.end_phase()


def odd_attn_phase(S, C, j, src, dst):
    layer = 2 * j + 1
    lambda_init = 0.8 - 0.6 * math.exp(-0.3 * layer)
    L = C.L
    NKB = L // 128
    NQ = 512
    NQS = NQ // 128
    TOPK = min(256, L // 4)
    NIT = 18
    S.begin_phase()
    PB1 = Banks(S, 2)
    PB2 = Banks(S, 1)
    PB3 = Banks(S, 1)
    DACC = [S.ps("dacc%d" % i, [128, 512]) for i in range(2)]
    FACC = [S.ps("facc%d" % i, [128, 512]) for i in range(2)]
    wout = S.sb("wout", [128, KC, D], BF16)
    stg = [S.sb("stg%d" % i, [128, 1024]) for i in range(2)]
    load_weight_scaled(S, C, wout, C.od_wout[j], KC, D, None, stg, 1024, "wout")
    S.barrier()
    tmpf = [V(stg[0].ap[:, 0:512], Res("tmpf0"))]
    R = [V(stg[1].ap[:, 0:512], Res("R0")), V(stg[1].ap[:, 512:1024], Res("R1")), V(stg[0].ap[:, 512:1024], Res("R2"))]
    rb = S.sb("rb", [128, 8, 2, 128])
    t31 = S.sb("t31", [128, 8])
    cmT = S.sb("cmT", [128, 128])
    dmask = S.sb("dmask", [128, 128])
    zer = S.sb("zer", [128, 128])
    S.dma("sp", rb, V(C.rb_near, C.wres), sres=rb.res)
    S.dma("sp", t31, V(C.rb_t31.partition_broadcast(128), C.wres), sres=t31.res)
    S.memset("pool", zer, 0.0)
    S.asel(cmT, zer, [[1, 128]], ALU.is_ge, NEG, base=0, cm=-1)
    S.asel(dmask, zer, [[-1, 128]], ALU.is_ge, -1e30, base=0, cm=1)
    rbv = V(rb.ap.rearrange("p h r q -> p h (r q)"), rb.res)
    S.tt("dve", rbv, rbv, bc_last(t31, 256), ALU.subtract)
    S.tt("dve", rb[:, :, 1, :], rb[:, :, 1, :], bc_mid(cmT, 8), ALU.add)
    lf = S.sb("lf", [128, 256])
    lj = S.sb("lj", [128, 64])
    lam = {n: S.sb(n, [128, 1]) for n in ("s01", "s23", "nlam")}
    S.dma("sp", lf, V(C.od_lam[j].partition_broadcast(128), C.wres), sres=lf.res)
    S.memset("dve", lam["s01"], 0.0)
    S.memset("dve", lam["s23"], 0.0)
    S.stt("dve", lj, lf[:, 0:64], 1.0, lf[:, 64:128], ALU.mult, ALU.mult, accum=lam["s01"])
    S.stt("dve", lj, lf[:, 128:192], 1.0, lf[:, 192:256], ALU.mult, ALU.mult, accum=lam["s23"])
    S.act(lam["s01"], lam["s01"], AF.Exp)
    S.act(lam["s23"], lam["s23"], AF.Exp)
    S.tt("dve", lam["nlam"], lam["s23"], lam["s01"], ALU.subtract)
    S.ts("dve", lam["nlam"], lam["nlam"], -lambda_init, ALU.add)
    gsub_ = S.sb("gsubn", [128, 128])
    S.dma("sp", gsub_, V(C.od_gsub[j].partition_broadcast(128), C.wres), sres=gsub_.res)
    S.ts("pool", gsub_, gsub_, 1.0 - lambda_init, ALU.mult, 1.0, ALU.mult)

    dkT = S.sb("dkT", [128, 4, L], BF16)
    skT = S.sb("skT", [128, L], BF16)
    ikT = S.sb("ikT", [128, L], BF16)
    dvA = S.sb("dvA", [128, NKB, 4, 130], BF16)
    svA = S.sb("svA", [128, NKB, 130], BF16)
    S.memset("pool", dvA[:, :, :, 128:130], 1.0)
    S.memset("pool", svA[:, :, 128:130], 1.0)
    dqT = S.sb("dqT", [128, 4, NQ], BF16)
    sqT = S.sb("sqT", [128, 4, NQ], BF16)
    iqT = S.sb("iqT", [128, 4, NQ], BF16)
    iw = S.sb("iw", [128, NQS, 8])
    idx = S.sb("idx", [128, L])
    M = S.sb("M", [128, L], BF16)
    M2 = S.sb("M2", [128, L], BF16)
    MT = S.sb("MT", [128, NKB, 128], BF16)
    PTa = [S.sb("PTa%d" % i, [128, 512], BF16) for i in range(3)]
    PTb = [S.sb("PTb%d" % i, [128, 512], BF16) for i in range(2)]
    xs = [S.sb("x%d" % i, [128, D]) for i in range(1)]
    ytd = [[S.sb("ytd%d_%d" % (a, i), [128, 4, 128], BF16) for i in range(NQS)] for a in range(2)]
    ytf = [[S.sb("ytf%d_%d" % (a, i), [128, 4, 128], BF16) for i in range(NQS)] for a in range(2)]
    yT = S.sb("yT", [128, 8, 128], BF16)
    of0 = S.sb("of0", [128, NQS, 128])
    of = S.sb("of", [128, 128])
    osq = S.sb("osq", [128, 128])
    sc = {n: S.sb(n, [128, 1]) for n in ("rmax", "w0", "lo", "nlo", "nmid", "cnt", "gw", "rec")}
    sd = {n: S.sb(n, [128, 1]) for n in ("rec", "oss", "ors")}
    sc2 = {n: S.sb(n + "2", [128, 1]) for n in ("rec",)}
    ctr = {"R": 0, "Pa": 0, "Pb": 0, "T": 0, "x": 0}

    NQB = L // 128
    NST = (L + NQ - 1) // NQ
    Ms = [M, M2]
    prog = {"s1": 0, "s2": 0, "df": 0}

    def s1_lane(b):
        for qb in range(NQB):
            s0 = (qb // NQS) * NQ
            qs = qb % NQS
            nqs = min(NQS, (L - s0) // 128)
            nq = nqs * 128
            if qs == 0:
                S.dma("sp", iqT[:, :, 0:nq], C.qkT.v((b, "iq", s0), C.qkT.ap[b, 13:17, :, s0:s0 + nq].rearrange("c p t -> p c t")), sres=iqT.res)
                S.dma("sp", iw[:, 0:nqs, :], C.iwd.v((b, "iw", s0), C.iwd.ap[b, s0:s0 + nq, :].rearrange("(s p) e -> p s e", p=128)), sres=iw.res)
                yield
            nk = (qb + 1) * 128
            qc = slice(qs * 128, (qs + 1) * 128)
            if nk > TOPK:
                while prog["s2"] < qb - 1:
                    yield
                Mq = Ms[qb % 2]
                pend = None

                def fma(p):
                    r_, k0, wd, h = p
                    if h == 0:
                        S.ts("dve", idx[:, k0:k0 + wd], r_[:, 0:wd], iw[:, qs, 0:1], ALU.mult)
                    else:
                        S.stt("dve", idx[:, k0:k0 + wd], r_[:, 0:wd], iw[:, qs, h:h + 1], idx[:, k0:k0 + wd], ALU.mult, ALU.add)

                for k0 in range(0, nk, 512):
                    wd = min(512, nk - k0)
                    for h in range(8):
                        pr = slice((h % 2) * 64, (h % 2) * 64 + 64)
                        ps = PB1.get()
                        S.mm(ps[:, 0:wd], iqT[pr, h // 2, qc], ikT[pr, k0:k0 + wd])
                        r_ = R[ctr["R"] % len(R)]
                        ctr["R"] += 1
                        S.act(r_[:, 0:wd], ps[:, 0:wd], AF.Relu)
                        if pend is not None:
                            fma(pend)
                        pend = (r_, k0, wd, h)
                        if h % 2 == 1:
                            yield
                fma(pend)
                S.reduce("dve", sc["rmax"], idx[:, 0:nk], ALU.max)
                S.reduce("dve", sc["lo"], idx[:, 0:nk], ALU.min)
                S.tt("dve", sc["w0"], sc["rmax"], sc["lo"], ALU.subtract)
                S.ts("dve", sc["nlo"], sc["lo"], -1.0, ALU.mult)
                S.tt("dve", idx[:, nk - 128:nk], idx[:, nk - 128:nk], dmask, ALU.add)
                yield
                thr = 2.0 * TOPK - nk - 0.5
                for it in range(NIT):
                    hw = 2.0 ** -(it + 1)
                    S.stt("dve", sc["nmid"], sc["w0"], -hw, sc["nlo"], ALU.mult, ALU.add)
                    S.act(Mq[:, 0:nk], idx[:, 0:nk], AF.Sign, bias=sc["nmid"], scale=1.0, accum=sc["cnt"])
                    yield
                    S.stt("dve", sc["gw"], sc["cnt"], thr, sc["w0"], ALU.is_ge, ALU.mult)
                    S.stt("dve", sc["nlo"], sc["gw"], -hw, sc["nlo"], ALU.mult, ALU.add)
                S.ts("dve", sc["lo"], sc["nlo"], -1.0, ALU.mult)
                S.ts("dve", Mq[:, 0:nk], idx[:, 0:nk], sc["lo"], ALU.is_ge)
            prog["s1"] = qb + 1
            yield

    def s2_lane(b):
        for qb in range(NQB):
            st_i = qb // NQS
            s0 = st_i * NQ
            qs = qb % NQS
            nqs = min(NQS, (L - s0) // 128)
            nq = nqs * 128
            if qs == 0:
                while prog["df"] < st_i - 1:
                    yield
                S.dma("sp", sqT[:, :, 0:nq], C.qkT.v((b, "sq", s0), C.qkT.ap[b, 8:12, :, s0:s0 + nq].rearrange("c p t -> p c t")), sres=sqT.res)
                yield
            while prog["s1"] < qb + 1:
                yield
            yb = ytd[st_i % 2]
            nk = (qb + 1) * 128
            qc = slice(qs * 128, (qs + 1) * 128)
            use_topk = nk > TOPK
            Mq = Ms[qb % 2]
            if use_topk:
                for g0 in range(0, qb + 1, 8):
                    ng = min(8, qb + 1 - g0)
                    pm = PB2.get()
                    for i in range(ng):
                        S.tr(bf(pm)[:, i * 128:(i + 1) * 128], Mq[:, (g0 + i) * 128:(g0 + i + 1) * 128], C.identb)
                    S.copy("act", MT[:, g0:g0 + ng, :], v3(bf(pm)[:, 0:ng * 128], ng))
                    yield
            for a in DACC:
                S.memset("dve", a[:, 0:130], 0.0)
                S.memset("dve", a[:, 256:386], 0.0)
            nkb_ = qb + 1

            def st1(kb):
                kc_ = slice(kb * 128, (kb + 1) * 128)
                ps = PB2.get()
                S.mm(ps, skT[:, kc_], sqT[:, :, qc])
                pt_ = PTa[kb % 3]
                rel = kb - (qb - 1)
                if rel >= 0:
                    t_ = tmpf[ctr["T"] % len(tmpf)]
                    ctr["T"] += 1
                    S.tt("dve", v3(t_), v3(ps), rb[:, 4:8, rel, :], ALU.add)
                    S.act(pt_, t_, AF.Exp)
                else:
                    S.act(pt_, ps, AF.Exp)

            def st2(kb):
                if use_topk:
                    pt_ = PTa[kb % 3]
                    S.tt("dve", v3(pt_), v3(pt_), bc_mid(MT[:, kb, :], 4), ALU.mult)

            def st3(kb):
                pt_ = PTa[kb % 3]
                for h in range(4):
                    S.op("pe", lambda e, h=h, pt_=pt_, kb=kb, qb=qb: e.matmul(DACC[h // 2].ap[:, (h % 2) * 256:(h % 2) * 256 + 129], pt_.ap[:, h * 128:(h + 1) * 128],
                                                                             svA.ap[:, kb, 0:129], start=False, stop=(kb == qb), skip_group_check=True),
                         r=[pt_, svA], w=[DACC[h // 2]], inc=(h == 3))

            for t in range(nkb_ + 2):
                if 0 <= t - 2 < nkb_:
                    st3(t - 2)
                if 0 <= t - 1 < nkb_:
                    st2(t - 1)
                if t < nkb_:
                    st1(t)
                yield
            for h in range(4):
                a = DACC[h // 2]
                c0 = (h % 2) * 256
                S.op("dve", lambda e, a=a, c0=c0: e.reciprocal(sc2["rec"].ap, a.ap[:, c0 + 128:c0 + 129]), r=[a], w=[sc2["rec"]])
                S.ts("dve", yb[qs][:, h, :], a[:, c0:c0 + 128], sc2["rec"], ALU.mult)
            prog["s2"] = qb + 1
            yield

    def df_lane(b):
        for st_i in range(NST):
            s0 = st_i * NQ
            sblk = s0 // 128
            nqs = min(NQS, (L - s0) // 128)
            nq = nqs * 128
            yf = ytf[st_i % 2]
            yd = ytd[st_i % 2]
            S.dma("sp", dqT[:, :, 0:nq], C.qkT.v((b, "dq", s0), C.qkT.ap[b, 0:4, :, s0:s0 + nq].rearrange("c p t -> p c t")), sres=dqT.res)
            yield
            last_kb = sblk + nqs - 1
            for h in range(4):
                for m in range(2):
                    pr = slice(m * 64, m * 64 + 64)
                    for a in FACC:
                        S.memset("dve", a[:, 0:130], 0.0)
                        S.memset("dve", a[:, 256:386], 0.0)
                    def d1(kb):
                        kc_ = slice(kb * 128, (kb + 1) * 128)
                        qlo = max(0, kb - sblk)
                        ncol = (nqs - qlo) * 128
                        ps = PB3.get()
                        S.mm(ps[:, 0:ncol], dkT[pr, h, kc_], dqT[pr, h, qlo * 128:nqs * 128])
                        for qs in (kb - sblk, kb - sblk + 1):
                            if 0 <= qs < nqs:
                                rel = kb - (sblk + qs - 1)
                                cc = slice((qs - qlo) * 128, (qs - qlo + 1) * 128)
                                S.tt("dve", ps[:, cc], ps[:, cc], rb[:, h, rel, :], ALU.add)
                        pt_ = PTb[kb % 2]
                        S.act(pt_[:, 0:ncol], ps[:, 0:ncol], AF.Exp)

                    def d2(kb):
                        qlo = max(0, kb - sblk)
                        pt_ = PTb[kb % 2]
                        for qs in range(qlo, nqs):
                            cc = slice((qs - qlo) * 128, (qs - qlo + 1) * 128)
                            S.op("pe", lambda e, qs=qs, cc=cc, pt_=pt_, kb=kb, h=h, sblk=sblk: e.matmul(
                                FACC[qs // 2].ap[:, (qs % 2) * 256:(qs % 2) * 256 + 129], pt_.ap[:, cc], dvA.ap[:, kb, h, 0:129],
                                start=False, stop=(kb == sblk + qs), skip_group_check=True), r=[pt_, dvA], w=[FACC[qs // 2]],
                                inc=(qs == nqs - 1))

                    for t in range(last_kb + 2):
                        if 0 <= t - 1 <= last_kb:
                            d2(t - 1)
                        if t <= last_kb:
                            d1(t)
                        yield
                    for qs in range(nqs):
                        a = FACC[qs // 2]
                        c0 = (qs % 2) * 256
                        S.op("dve", lambda e, a=a, c0=c0: e.reciprocal(sd["rec"].ap, a.ap[:, c0 + 128:c0 + 129]), r=[a], w=[sd["rec"]])
                        if m == 0:
                            S.ts("dve", of0[:, qs, :], a[:, c0:c0 + 128], sd["rec"], ALU.mult)
                        else:
                            S.tt("dve", sd["rec"], sd["rec"], lam["nlam"], ALU.mult)
                            S.stt("dve", of, a[:, c0:c0 + 128], sd["rec"], of0[:, qs, :], ALU.mult, ALU.add)
                            S.act(osq, of, AF.Square, accum=sd["oss"])
                            S.act(sd["ors"], sd["oss"], AF.Sqrt, bias=C.eps, scale=1.0 / 128)
                            S.op("dve", lambda e: e.reciprocal(sd["ors"].ap, sd["ors"].ap), r=[sd["ors"]], w=[sd["ors"]])
                            S.stt("dve", yf[qs][:, h, :], of, sd["ors"], gsub_, ALU.mult, ALU.mult)
                    yield
            while prog["s2"] < min(NQB, (st_i + 1) * NQS):
                yield
            for qs in range(nqs):
                r0 = b * L + s0 + qs * 128
                xt = xs[0]
                load_rows(S, C, xt, src, (r0 // 128), src.ap[r0:r0 + 128, :])
                pt2 = PB3.get()
                for c in range(4):
                    S.tr(bf(pt2)[:, c * 128:(c + 1) * 128], yf[qs][:, c, :], C.identb)
                for c in range(4):
                    S.tr(bf(pt2)[:, (4 + c) * 128:(5 + c) * 128], yd[qs][:, c, :], C.identb)
                yield
                S.copy("act", flat(yT), bf(pt2))
                yield
                for nh in range(2):
                    pso = PB3.get()
                    for kc in range(KC):
                        S.mm(pso, yT[:, kc, :], wout[:, kc, nh * 512:(nh + 1) * 512], start=(kc == 0), stop=(kc == KC - 1))
                    yield
                    S.tt("dve", xt[:, nh * 512:(nh + 1) * 512], xt[:, nh * 512:(nh + 1) * 512], pso, ALU.add)
                S.dma("sp", dst.v((r0 // 128), dst.ap[r0:r0 + 128, :]), xt, sres=xt.res)
                yield
            prog["df"] = st_i + 1
            yield

    for b in range(C.NSEQ):
        S.dma("sp", dkT, C.qkT.v((b, "dk"), C.qkT.ap[b, 4:8, :, :].rearrange("c p t -> p c t")), sres=dkT.res)
        S.dma("sp", skT, C.qkT.v((b, "sk"), C.qkT.ap[b, 12, :, :]), sres=skT.res)
        S.dma("sp", ikT, C.qkT.v((b, "ik"), C.qkT.ap[b, 17, :, :]), sres=ikT.res)
        for kb in range(NKB):
            S.dma("sp", dvA[:, kb, :, 0:128], C.vtok.v((b, "dv", kb), C.vtok.ap[b, kb * 128:(kb + 1) * 128, 0:512].rearrange("p (h d) -> p h d", h=4)), sres=dvA.res)
        S.dma("sp", svA[:, :, 0:128], C.vtok.v((b, "sv"), C.vtok.ap[b, :, 512:640].rearrange("(k p) d -> p k d", p=128)), sres=svA.res)
        prog["s1"] = prog["s2"] = prog["df"] = 0
        run_lanes([s1_lane(b), s2_lane(b), df_lane(b)])
    S.end_phase()


W_SPECS = {
    "ffn_g": [4, 128, KC],
    "ffn_cw": [4, 128, 2 * FC, 3],
    "ffn_cb": [4, 128, 2 * FC],
    "ffn_wup": [4, D, 2 * DFF],
    "ffn_wdn": [4, DFF, D],
    "mix_g": [4, 128, KC],
    "ev_win": [2, D, EVEN_IN],
    "ev_wout": [2, D, D],
    "gm_wT": [2, 128, 4, 128],
    "gm_b": [2, 1, 512],
    "gdn_cw": [2, 128, 12, 4],
    "gdn_alog": [2, 1, 4],
    "gdn_dtb": [2, 1, 4],
    "gdn_ng": [2, 1, 128],
    "od_win": [2, D, OD_EXT],
    "od_wout": [2, D, D],
    "od_gn": [2, 128, 4],
    "od_lam": [2, 1, 256],
    "od_gsub": [2, 1, 128],
    "rb_near": [128, 8, 2, 128],
    "rb_t31": [1, 8],
}


def _rel_bucket_np(dist):
    n = np.maximum(dist, 0)
    nf = np.maximum(n, 16).astype(np.float32)
    far = 16 + (np.log(nf / np.float32(16)) / np.float32(math.log(128 / 16)) * np.float32(16)).astype(np.int32)
    return np.where(n < 16, n, np.minimum(far, 31))


def prep_weights(inp):
    f = lambda a: np.ascontiguousarray(np.asarray(a, dtype=np.float32))
    w = {}
    w["ffn_g"] = f(inp["ffn_norm_g"].reshape(4, KC, 128).transpose(0, 2, 1))
    w["ffn_cw"] = f(inp["ffn_conv_w"].reshape(4, 3, 2 * FC, 128).transpose(0, 3, 2, 1))
    w["ffn_cb"] = f(inp["ffn_conv_b"].reshape(4, 2 * FC, 128).transpose(0, 2, 1))
    w["ffn_wup"] = f(inp["ffn_w_up"])
    w["ffn_wdn"] = f(inp["ffn_w_down"])
    w["mix_g"] = f(inp["mix_norm_g"].reshape(4, KC, 128).transpose(0, 2, 1))
    w["ev_win"] = f(inp["ev_w_in"])
    w["ev_wout"] = f(inp["ev_w_out"])
    w["gm_wT"] = f(inp["gmlp_w_s"].transpose(0, 3, 1, 2))
    w["gm_b"] = f(inp["gmlp_b_s"].reshape(2, 1, 512))
    w["gdn_cw"] = f(inp["gdn_conv_w"].reshape(2, 4, 12, 128).transpose(0, 3, 2, 1))
    w["gdn_alog"] = f(inp["gdn_a_log"].reshape(2, 1, 4))
    w["gdn_dtb"] = f(inp["gdn_dt_bias"].reshape(2, 1, 4))
    w["gdn_ng"] = f(inp["gdn_norm_g"].reshape(2, 1, 128))
    ow = np.asarray(inp["od_w_in"], dtype=np.float32)
    w["od_win"] = f(np.concatenate([ow, ow[:, :, 2816:2880], ow[:, :, 2816:2880]], axis=2))
    w["od_wout"] = f(inp["od_w_out"])
    gq = np.asarray(inp["diff_q_norm_g"], dtype=np.float32)
    gk = np.asarray(inp["diff_k_norm_g"], dtype=np.float32)
    w["od_gn"] = f(np.stack([np.tile(gq, (1, 2)), np.tile(gk, (1, 2)), np.asarray(inp["dsa_q_norm_g"]), np.asarray(inp["dsa_k_norm_g"])], axis=2))
    w["od_lam"] = f(inp["diff_lambda"].reshape(2, 1, 256))
    w["od_gsub"] = f(inp["diff_sub_norm_g"].reshape(2, 1, 128))
    kk = np.arange(128)[:, None]
    qq = np.arange(128)[None, :]
    tab = np.asarray(inp["rel_bias"], dtype=np.float32)
    near = np.zeros((128, 8, 2, 128), np.float32)
    for rel in range(2):
        dist = qq - kk + (128 if rel == 0 else 0)
        near[:, :, rel, :] = tab[_rel_bucket_np(dist)].transpose(0, 2, 1)
    w["rb_near"] = f(near)
    w["rb_t31"] = f(tab[31:32, :])
    return w


def build_program(L, NSEQ, plan):
    nc = bass.Bass("TRN2", target_bir_lowering=False)
    NTOK = L * NSEQ
    C = Ctx()
    C.L, C.NSEQ, C.NTOK = L, NSEQ, NTOK
    x = nc.dram_tensor("x", [NTOK, D], F32, kind="ExternalInput").ap()
    y = nc.dram_tensor("y", [NTOK, D], F32, kind="ExternalOutput").ap()
    for name, shape in W_SPECS.items():
        setattr(C, name, nc.dram_tensor(name, shape, F32, kind="ExternalInput").ap())
    C.wres = Res("weights")
    xdt = DT(x, "x")
    ydt = DT(y, "y")
    C.qkT = DT(nc.dram_tensor("qkT", [NSEQ, NQK, 128, L], BF16, kind="Internal").ap(), "qkT")
    C.vtok = DT(nc.dram_tensor("vtok", [NSEQ, L, 640], BF16, kind="Internal").ap(), "vtok")
    C.iwd = DT(nc.dram_tensor("iwd", [NSEQ, L, 8], F32, kind="Internal").ap(), "iwd")
    with ExitStack() as es:
        S = Sched(nc, es)
        ct = es.enter_context(nc.sbuf_tensor("identb", [128, 128], BF16))
        C.identb = V(ct[:], Res("identb"))
        ct = es.enter_context(nc.sbuf_tensor("identf", [128, 128], F32))
        C.identf = V(ct[:], Res("identf"))
        ct = es.enter_context(nc.sbuf_tensor("eps", [128, 1], F32))
        C.eps = V(ct[:], Res("eps"))
        S.memset("pool", C.identf, 1.0)
        S.asel(C.identf, C.identf, [[-1, 128]], ALU.is_equal, 0.0, base=0, cm=1)
        S.copy("pool", C.identb, C.identf)
        S.memset("pool", C.eps, EPS)
        src = xdt
        for kind, idx in plan:
            if kind == "ffn":
                ffn_phase(S, C, idx, src, ydt)
            elif kind == "even":
                even_phase(S, C, idx, src, ydt)
            elif kind == "odd":
                odd_phase(S, C, idx, src, ydt)
            src = ydt
        S.barrier()
        print("program: ops=%d waits=%d dma_sems=%d" % (S.nops, S.nwaits, S.ndsem))
    return nc


FULL_PLAN = [("even", 0), ("ffn", 0), ("odd", 0), ("ffn", 1), ("even", 1), ("ffn", 2), ("odd", 1), ("ffn", 3)]


N_CORES = 8
_PROG = {}


def kernel(**inputs):
    x = np.asarray(inputs["x"], dtype=np.float32)
    B, L, Dm = x.shape
    nseq = B // N_CORES
    key = (L, nseq)
    if key not in _PROG:
        _PROG[key] = build_program(L, nseq, FULL_PLAN)
    nc = _PROG[key]
    w = prep_weights(inputs)
    in_maps = []
    for c in range(N_CORES):
        m = {"x": np.ascontiguousarray(x[c * nseq:(c + 1) * nseq].reshape(nseq * L, Dm))}
        m.update(w)
        in_maps.append(m)
    res = run_bass_kernel_spmd(nc, in_maps, core_ids=list(range(N_CORES)))
    out = np.concatenate([np.asarray(r["y"]).reshape(nseq, L, Dm) for r in res.results], axis=0)
    return out.astype(np.float32)
```

```python
import math
from contextlib import ExitStack

import numpy as np
import concourse.bass as bass
import concourse.mybir as mybir
from concourse.bass_utils import run_bass_kernel_spmd

F32 = mybir.dt.float32
BF16 = mybir.dt.bfloat16
AF = mybir.ActivationFunctionType
ALU = mybir.AluOpType
AX = mybir.AxisListType


class Res:
    __slots__ = ("name", "last_w", "reads", "dsem", "dcount", "excl")

    def __init__(self, name="r"):
        self.name = name
        self.excl = False
        self.last_w = None
        self.reads = []
        self.dsem = None
        self.dcount = 0


class V:
    __slots__ = ("ap", "res")

    def __init__(self, ap, res):
        self.ap = ap
        self.res = res

    def __getitem__(self, idx):
        return V(self.ap[idx], self.res)

    def r(self, res):
        return V(self.ap, res)


def _res_of(xs):
    out = []
    for x in xs:
        if x is None:
            continue
        out.append(x.res if isinstance(x, V) else x)
    return out


class Sched:
    ENGS = ("pe", "act", "dve", "pool", "sp")

    def __init__(self, nc, es):
        self.nc = nc
        self.es = es
        self.eng = {"pe": nc.tensor, "act": nc.scalar, "dve": nc.vector, "pool": nc.gpsimd, "sp": nc.sync}
        self.sem = {e: es.enter_context(nc.semaphore("sem_" + e)) for e in self.ENGS}
        self.cnt = {e: 0 for e in self.ENGS}
        self.known = {e: {} for e in self.ENGS}
        self.dpool = []
        self.dlive = []
        self.ndsem = 0
        self.nwaits = 0
        self.nops = 0
        self.phase_es = None
        self.uid = 0
        import os
        self.limit = int(os.environ["OPLIMIT"]) if "OPLIMIT" in os.environ else None

    def begin_phase(self):
        self.phase_es = ExitStack()

    def end_phase(self):
        self.barrier()
        for r in self.dlive:
            self.dpool.append((r.dsem, r.dcount))
            r.dsem = None
        self.dlive = []
        self.phase_es.close()
        self.phase_es = None

    def sb(self, name, shape, dt=F32):
        self.uid += 1
        name = "%s_u%d" % (name, self.uid)
        t = self.phase_es.enter_context(self.nc.sbuf_tensor(name, list(shape), dt))
        return V(t[:], Res(name))

    def ps(self, name, shape, dt=F32):
        self.uid += 1
        name = "%s_u%d" % (name, self.uid)
        t = self.phase_es.enter_context(self.nc.psum_tensor(name, list(shape), dt))
        rs = Res(name)
        rs.excl = True
        return V(t[:], rs)

    def _collect(self, eng, r, w, strict):
        waits = {}

        def add(ev, same_ok):
            if ev is None:
                return
            sem, val, src = ev
            if not strict and src == eng and (same_ok or eng == "pe"):
                return
            k = id(sem)
            if k not in waits or waits[k][1] < val:
                waits[k] = (sem, val)

        for res in r:
            add(res.last_w, False)
            if res.excl:
                for ev in res.reads:
                    add(ev, True)
        for res in w:
            add(res.last_w, True)
            for ev in res.reads:
                add(ev, True)
        return waits

    def _emit_waits(self, eng, waits):
        kn = self.known[eng]
        e = self.eng[eng]
        for k, (sem, val) in waits.items():
            if kn.get(k, 0) >= val:
                continue
            kn[k] = val
            e.wait_ge(sem, val)
            self.nwaits += 1

    def op(self, eng, fn, r=(), w=(), inc=True):
        if self.limit is not None and self.nops >= self.limit:
            return None
        r = _res_of(r)
        w = _res_of(w)
        self._emit_waits(eng, self._collect(eng, r, w, False))
        ins = fn(self.eng[eng])
        self.nops += 1
        if inc:
            self.cnt[eng] += 1
            ins.then_inc(self.sem[eng], 1)
            ev = (self.sem[eng], self.cnt[eng], eng)
        else:
            assert eng == "pe"
            ev = (self.sem[eng], self.cnt[eng] + 1, eng)
        for res in r:
            res.reads.append(ev)
        for res in w:
            res.last_w = ev
            res.reads = []
        return ins

    def dma(self, q, out, in_, sres=None, **kw):
        if self.limit is not None and self.nops >= self.limit:
            return None
        sr = sres if sres is not None else out.res
        if sr.dsem is None:
            if self.dpool:
                sr.dsem, sr.dcount = self.dpool.pop()
            else:
                sr.dsem = self.es.enter_context(self.nc.semaphore("dsem%d" % self.ndsem))
                sr.dcount = 0
                self.ndsem += 1
            self.dlive.append(sr)
        waits = self._collect(q, [in_.res], [out.res], True)
        k = id(sr.dsem)
        if sr.dcount > 0 and (k not in waits or waits[k][1] < sr.dcount):
            waits[k] = (sr.dsem, sr.dcount)
        self._emit_waits(q, waits)
        ins = self.eng[q].dma_start(out=out.ap, in_=in_.ap, **kw)
        sr.dcount += 16
        ins.then_inc(sr.dsem, 16)
        self.nops += 1
        ev = (sr.dsem, sr.dcount, "dma")
        in_.res.reads.append(ev)
        out.res.last_w = ev
        out.res.reads = []
        return ins

    def barrier(self):
        evs = [(self.sem[e], self.cnt[e]) for e in self.ENGS if self.cnt[e] > 0]
        evs += [(r.dsem, r.dcount) for r in self.dlive if r.dcount > 0]
        for e in self.ENGS:
            kn = self.known[e]
            for sem, val in evs:
                if sem is self.sem[e]:
                    continue
                if kn.get(id(sem), 0) >= val:
                    continue
                kn[id(sem)] = val
                self.eng[e].wait_ge(sem, val)
                self.nwaits += 1

    def mm(self, out, lhsT, rhs, start=True, stop=True, inc=None, **kw):
        if inc is None:
            inc = bool(stop)
        return self.op("pe", lambda e: e.matmul(out.ap, lhsT.ap, rhs.ap, start=start, stop=stop, **kw),
                       r=[lhsT, rhs], w=[out], inc=inc)

    def tr(self, out, in_, ident):
        return self.op("pe", lambda e: e.transpose(out.ap, in_.ap, ident.ap), r=[in_, ident], w=[out])

    def act(self, out, in_, func, bias=None, scale=None, accum=None, eng="act"):
        kw = {}
        rr = [in_]
        ww = [out]
        if bias is not None:
            if isinstance(bias, V):
                kw["bias"] = bias.ap
                rr.append(bias)
            else:
                kw["bias"] = bias
        if scale is not None:
            if isinstance(scale, V):
                kw["scale"] = scale.ap
                rr.append(scale)
            else:
                kw["scale"] = scale
        if accum is not None:
            kw["accum_out"] = accum.ap
            ww.append(accum)
        return self.op("act", lambda e: e.activation(out.ap, in_.ap, func, **kw), r=rr, w=ww)

    def tt(self, eng, out, a, b, op):
        return self.op(eng, lambda e: e.tensor_tensor(out.ap, a.ap, b.ap, op), r=[a, b], w=[out])

    def ts(self, eng, out, a, s1, op0, s2=None, op1=None, accum=None):
        rr = [a]
        ww = [out]
        a1 = s1
        a2 = s2
        if isinstance(s1, V):
            rr.append(s1)
            a1 = s1.ap
        if isinstance(s2, V):
            rr.append(s2)
            a2 = s2.ap
        kw = {}
        if op1 is not None:
            kw["op1"] = op1
        if accum is not None:
            kw["accum_out"] = accum.ap
            ww.append(accum)
        return self.op(eng, lambda e: e.tensor_scalar(out.ap, a.ap, a1, a2, op0, **kw), r=rr, w=ww)

    def stt(self, eng, out, a, s, b, op0, op1, accum=None):
        rr = [a, b]
        ww = [out]
        sc = s
        if isinstance(s, V):
            rr.append(s)
            sc = s.ap
        kw = {}
        if accum is not None:
            kw["accum_out"] = accum.ap
            ww.append(accum)
        return self.op(eng, lambda e: e.scalar_tensor_tensor(out.ap, a.ap, sc, b.ap, op0, op1, **kw), r=rr, w=ww)

    def copy(self, eng, out, in_):
        if eng == "act":
            return self.op("act", lambda e: e.copy(out.ap, in_.ap), r=[in_], w=[out])
        return self.op(eng, lambda e: e.tensor_copy(out.ap, in_.ap), r=[in_], w=[out])

    def memset(self, eng, out, val):
        return self.op(eng, lambda e: e.memset(out.ap, val), w=[out])

    def reduce(self, eng, out, in_, op, axis=None):
        ax = AX.X if axis is None else axis
        return self.op(eng, lambda e: e.tensor_reduce(out.ap, in_.ap, ax, op), r=[in_], w=[out])

    def asel(self, out, in_, pattern, cmp, fill, base=0, cm=0):
        return self.op("pool", lambda e: e.affine_select(out.ap, in_.ap, pattern=pattern, compare_op=cmp, fill=fill,
                                                         base=base, channel_multiplier=cm), r=[in_], w=[out])


D = 1024
KC = 8
DFF = 2816
FC = 22
EPS = 1e-6
EVEN_IN = 3080
ODD_IN = 2888
NEG = -30000.0


class DT:
    def __init__(self, ap, name):
        self.ap = ap
        self.name = name
        self.res = {}

    def v(self, key, ap):
        if key not in self.res:
            self.res[key] = Res("%s_%s" % (self.name, str(key)))
        return V(ap, self.res[key])


class Ctx:
    pass


def load_rows(S, C, dst, src_dt, key, ap, q="sp"):
    S.dma(q, dst, src_dt.v(key, ap), sres=dst.res)


def rms_to_hnT(S, C, xt, hnT_cols, ptr, hn, ssq, rstd):
    S.act(hn, xt, AF.Square, accum=ssq)
    S.act(rstd, ssq, AF.Sqrt, bias=C.eps, scale=1.0 / D)
    S.op("dve", lambda e: e.reciprocal(rstd.ap, rstd.ap), r=[rstd], w=[rstd])
    S.ts("pool", hn, xt, rstd, ALU.mult, 1.0, ALU.mult)
    for kc in range(KC):
        S.tr(ptr[:, kc * 128:(kc + 1) * 128], hn[:, kc * 128:(kc + 1) * 128], C.identb)
    S.copy("act", hnT_cols, V(ptr.ap.rearrange("p (k t) -> p k t", k=KC), ptr.res))


def rms_to_hnT_gen(S, C, xt, hnT_cols, ptr, hn, ssq, rstd):
    S.act(hn, xt, AF.Square, accum=ssq)
    yield
    S.act(rstd, ssq, AF.Sqrt, bias=C.eps, scale=1.0 / D)
    yield
    S.op("dve", lambda e: e.reciprocal(rstd.ap, rstd.ap), r=[rstd], w=[rstd])
    yield
    S.ts("pool", hn, xt, rstd, ALU.mult, 1.0, ALU.mult)
    yield
    for kc in range(KC):
        S.tr(ptr[:, kc * 128:(kc + 1) * 128], hn[:, kc * 128:(kc + 1) * 128], C.identb)
    yield
    S.copy("act", hnT_cols, V(ptr.ap.rearrange("p (k t) -> p k t", k=KC), ptr.res))
    yield


def load_weight_scaled(S, C, wsb, w_dram_ap, nrows_chunks, ncols, gcol, stg, colchunk, name):
    i = 0
    width = stg[0].ap.shape[1]
    colchunk = min(colchunk, width)
    for rc in range(nrows_chunks):
        for c0 in range(0, ncols, colchunk):
            c1 = min(ncols, c0 + colchunk)
            st = stg[i % len(stg)]
            S.dma(("sp", "act")[i % 2], st[:, 0:c1 - c0], V(w_dram_ap[rc * 128:(rc + 1) * 128, c0:c1], C.wres), sres=st.res)
            eng = ("dve", "act", "dve", "pool")[i % 4]
            if gcol is None:
                S.copy(eng, wsb[:, rc, c0:c1], st[:, 0:c1 - c0])
            elif eng == "act":
                S.act(wsb[:, rc, c0:c1], st[:, 0:c1 - c0], AF.Copy, scale=gcol[:, rc:rc + 1])
            else:
                S.ts(eng, wsb[:, rc, c0:c1], st[:, 0:c1 - c0], gcol[:, rc:rc + 1], ALU.mult, 1.0, ALU.mult)
            i += 1


def ffn_phase(S, C, layer, src, dst):
    NT = 512
    HW = 256
    NSUB = NT // 128
    S.begin_phase()
    wup = S.sb("wup", [128, KC, 2 * DFF], BF16)
    wdn = S.sb("wdn", [128, FC, D], BF16)
    gsb = S.sb("gsb", [128, KC])
    cw = S.sb("cw", [128, 2 * FC, 3])
    cb = S.sb("cb", [128, 2 * FC])
    stg = [S.sb("stg%d" % i, [128, 352]) for i in range(4)]
    S.dma("sp", gsb, V(C.ffn_g[layer], C.wres), sres=gsb.res)
    S.dma("sp", cw, V(C.ffn_cw[layer], C.wres), sres=cw.res)
    S.dma("sp", cb, V(C.ffn_cb[layer], C.wres), sres=cb.res)
    load_weight_scaled(S, C, wup, C.ffn_wup[layer], KC, 2 * DFF, gsb, stg, 352, "wup")
    load_weight_scaled(S, C, wdn, C.ffn_wdn[layer], FC, D, None, stg, 352, "wdn")

    xs = [S.sb("x%d" % i, [128, D]) for i in range(2)]
    xr = [S.sb("xr%d" % i, [128, D]) for i in range(2)]
    hn = [S.sb("hn%d" % i, [128, D], BF16) for i in range(2)]
    ssq = [S.sb("ssq%d" % i, [128, 1]) for i in range(2)]
    rstd = [S.sb("rstd%d" % i, [128, 1]) for i in range(2)]
    hT_ = S.sb("hnT", [128, KC, NT], BF16)
    hT = S.sb("hT", [128, FC, NT], BF16)
    hal = S.sb("hal", [128, 2 * FC, 2])
    rr = [S.sb("rr%d" % i, [128, HW + 2]) for i in range(4)]
    acc = [S.sb("acc%d" % i, [128, HW]) for i in range(4)]
    sg = [S.sb("sg%d" % i, [128, HW]) for i in range(2)]
    ptr = [S.ps("ptr%d" % i, [128, D], BF16) for i in range(2)]
    pup = [S.ps("pup%d" % i, [128, 512]) for i in range(4)]
    pdn = [S.ps("pdn%d" % i, [128, 512]) for i in range(2)]

    gsub = 0
    for b in range(C.NSEQ):
        for t0 in range(0, C.L, NT):
            rows = []
            for sub in range(NSUB):
                r0 = b * C.L + t0 + sub * 128
                xt = xs[gsub % 2]
                rows.append(r0)
                load_rows(S, C, xt, src, (r0 // 128), src.ap[r0:r0 + 128, :])
                j = gsub % 2
                rms_to_hnT(S, C, xt, hT_[:, :, sub * 128:(sub + 1) * 128], ptr[j], hn[j], ssq[j], rstd[j])
                gsub += 1
            items = [(c, hf, which) for c in range(FC) for hf in range(2) for which in range(2)]
            ni = len(items)

            def fA(n):
                c, hf, which = items[n]
                ch = c + which * FC
                ps = pup[(2 * c + which) % 4]
                if hf == 0:
                    for kc in range(KC):
                        S.mm(ps, wup[:, kc, ch * 128:(ch + 1) * 128], hT_[:, kc, :], start=(kc == 0), stop=(kc == KC - 1))
                psh = ps[:, hf * HW:(hf + 1) * HW]
                r = rr[n % 4]
                a = acc[n % 4]
                if t0 == 0 and hf == 0:
                    S.memset("pool", r[:, 0:2], 0.0)
                else:
                    S.copy("pool", r[:, 0:2], hal[:, ch, :])
                S.copy("act", r[:, 2:HW + 2], psh)
                S.act(a, psh, AF.Identity, bias=cb[:, ch:ch + 1], scale=cw[:, ch, 2:3])
                S.copy("pool", hal[:, ch, :], r[:, HW:HW + 2])

            def fB(n):
                c, hf, which = items[n]
                ch = c + which * FC
                r = rr[n % 4]
                a = acc[n % 4]
                S.stt("dve", a, r[:, 1:HW + 1], cw[:, ch, 1:2], a, ALU.mult, ALU.add)
                S.stt("dve", a, r[:, 0:HW], cw[:, ch, 0:1], a, ALU.mult, ALU.add)

            def fC(p):
                S.act(sg[p % 2], acc[(2 * p) % 4], AF.Silu)

            def fD(p):
                c, hf = p // 2, p % 2
                S.tt("dve", hT[:, c, hf * HW:(hf + 1) * HW], sg[p % 2], acc[(2 * p + 1) % 4], ALU.mult)

            npair = ni // 2
            for n in range(ni + 3):
                if n < ni:
                    fA(n)
                if 0 <= n - 1 < ni:
                    fB(n - 1)
                if n >= 2 and n % 2 == 0 and (n - 2) // 2 < npair:
                    fC((n - 2) // 2)
                if n >= 3 and n % 2 == 1 and (n - 3) // 2 < npair:
                    fD((n - 3) // 2)
            for sub in range(NSUB):
                r0 = rows[sub]
                xt = xr[sub % 2]
                load_rows(S, C, xt, src, (r0 // 128), src.ap[r0:r0 + 128, :])
                for nh in range(2):
                    ps2 = pdn[nh]
                    for fc in range(FC):
                        S.mm(ps2, hT[:, fc, sub * 128:(sub + 1) * 128], wdn[:, fc, nh * 512:(nh + 1) * 512],
                             start=(fc == 0), stop=(fc == FC - 1))
                    S.tt("dve", xt[:, nh * 512:(nh + 1) * 512], xt[:, nh * 512:(nh + 1) * 512], ps2, ALU.add)
                S.dma("sp", dst.v((r0 // 128), dst.ap[r0:r0 + 128, :]), xt, sres=xt.res)
    S.end_phase()


def flat(v):
    return V(v.ap.rearrange("p h j -> p (h j)"), v.res)


def v3(v, h=4):
    return V(v.ap.rearrange("p (h j) -> p h j", h=h), v.res)


def bc_last(v, n):
    H = v.ap.shape[1]
    return V(v.ap.unsqueeze(2).to_broadcast([128, H, n]), v.res)


def bc_mid(v, h):
    n = v.ap.shape[1]
    return V(v.ap.unsqueeze(1).to_broadcast([128, h, n]), v.res)


class Banks:
    def __init__(self, S, n=8):
        self.b = [S.ps("bank%d" % i, [128, 512]) for i in range(n)]
        self.i = 0

    def get(self):
        v = self.b[self.i % len(self.b)]
        self.i += 1
        return v


def bf(v):
    return V(v.ap.bitcast(BF16), v.res)


F32R = mybir.dt.float32r


def r32(v):
    return V(v.ap.bitcast(F32R), v.res)


def run_lanes(gens):
    gens = [g for g in gens if g is not None]
    while gens:
        for g in list(gens):
            try:
                next(g)
            except StopIteration:
                gens.remove(g)


def even_phase(S, C, j, src, dst):
    layer = 2 * j
    NT = 256
    NSUB = NT // 128
    H = 4
    S.begin_phase()
    PB = Banks(S)
    win = S.sb("win", [128, KC, EVEN_IN], BF16)
    wout = S.sb("wout", [128, KC, D], BF16)
    gsb = S.sb("gsb", [128, KC])
    stg = [S.sb("stg%d" % i, [128, 770]) for i in range(4)]
    S.dma("sp", gsb, V(C.mix_g[layer], C.wres), sres=gsb.res)
    load_weight_scaled(S, C, win, C.ev_win[j], KC, EVEN_IN, gsb, stg, 1540, "win")
    load_weight_scaled(S, C, wout, C.ev_wout[j], KC, D, None, stg, 1024, "wout")
    wTm = S.sb("wTm", [128, H, 128], BF16)
    wTf = S.sb("wTf", [128, H, 128])
    brow = S.sb("brow", [1, 512])
    cw = S.sb("gcw", [128, 12, 4])
    alog = S.sb("alog", [128, 4])
    dtb = S.sb("dtb", [128, 4])
    nega = S.sb("nega", [128, 4])
    gng = S.sb("gng", [128, 128])
    S.dma("sp", wTf, V(C.gm_wT[j], C.wres), sres=wTf.res)
    S.dma("sp", brow, V(C.gm_b[j], C.wres), sres=brow.res)
    S.dma("sp", cw, V(C.gdn_cw[j], C.wres), sres=cw.res)
    S.dma("sp", alog, V(C.gdn_alog[j].partition_broadcast(128), C.wres), sres=alog.res)
    S.dma("sp", dtb, V(C.gdn_dtb[j].partition_broadcast(128), C.wres), sres=dtb.res)
    S.dma("sp", gng, V(C.gdn_ng[j].partition_broadcast(128), C.wres), sres=gng.res)
    ones = S.sb("ones", [128, 128])
    tri = S.sb("tri", [128, 128])
    ntri = S.sb("ntri", [128, 128])
    strict = S.sb("strict", [128, 128])
    incl = S.sb("incl", [128, 128])
    S.memset("pool", ones, 1.0)
    S.asel(tri, ones, [[1, 128]], ALU.is_ge, 0.0, base=0, cm=-1)
    S.ts("pool", ntri, tri, -1.0, ALU.mult, 1.0, ALU.mult)
    S.asel(strict, ones, [[-1, 128]], ALU.is_ge, 0.0, base=-1, cm=1)
    S.asel(incl, ones, [[-1, 128]], ALU.is_ge, 0.0, base=0, cm=1)
    S.tt("pool", wTm, wTf, bc_mid(tri, H), ALU.mult)
    S.act(nega, alog, AF.Exp)
    S.ts("pool", nega, nega, -1.0, ALU.mult, 1.0, ALU.mult)

    xs = [S.sb("x%d" % i, [128, D]) for i in range(4)]
    hn = [S.sb("hn%d" % i, [128, D], BF16) for i in range(2)]
    ssq = [S.sb("ssq%d" % i, [128, 1]) for i in range(2)]
    rstd = [S.sb("rstd%d" % i, [128, 1]) for i in range(2)]
    hnT = [S.sb("hnT%d" % i, [128, KC, NT], BF16) for i in range(2)]
    uTs = [S.sb("uT%d" % i, [128, H, NT], BF16) for i in range(2)]
    yT = S.sb("yT", [128, KC, NT], BF16)
    qTs = [S.sb("qT%d" % i, [128, H, NT], BF16) for i in range(2)]
    kTs = [S.sb("kT%d" % i, [128, H, NT], BF16) for i in range(2)]
    vTs = [S.sb("vT%d" % i, [128, H, NT], BF16) for i in range(2)]
    XSL = [None, None]
    hal = S.sb("hal", [128, 12, 3])
    rr = [S.sb("rr%d" % i, [128, NT + 3]) for i in range(2)]
    ca = [S.sb("ca%d" % i, [128, NT]) for i in range(2)]
    qs = [S.sb("qs%d" % i, [128, NT]) for i in range(2)]
    sq = S.sb("sq", [128, NT])
    rn = S.sb("rn", [128, NT])
    Sst = S.sb("Sst", [128, H, 128])
    Sr = S.sb("Sr", [128, H, 128])

    def F(name, dt=F32):
        return S.sb(name, [128, H, 128], dt)

    ktok, vtok, vg, sqv, zs, G1, G2, E, nbm, usb, osq, on, gz, t1 = [
        F(n) for n in ("ktok", "vtok", "vg", "sqv", "zs", "G1", "G2", "E", "nbm", "usb", "osq", "on", "gz", "t1")]
    Lp0, Lp1, intra, intraT, U0, U1, TT, vb, kbg, wTs, qgT, kd, vnew = [
        F(n) for n in ("Lp0", "Lp1", "intra", "intraT", "U0", "U1", "TT", "vb", "kbg", "wTs", "qgT", "kd", "vnew")]
    vn = F("vn", BF16)
    onb = F("onb", BF16)
    sm = {n: S.sb(n, [128, 4]) for n in ("vsum", "vvar", "vrs", "beta", "nbeta", "xa", "xe", "xm", "sp", "g", "bk", "edl", "oss", "ors")}
    gcs = S.sb("gcs", [128, 8])
    egs = S.sb("egs", [128, 8])

    st = {"gsub": 0, "ich": 0}

    def proj_task(b, t0, k):
        hT_ = hnT[k]
        uT, qT, kT, vT = uTs[k], qTs[k], kTs[k], vTs[k]
        xsl = []
        for sub in range(NSUB):
            r0 = b * C.L + t0 + sub * 128
            xt = xs[st["gsub"] % 4]
            xsl.append((xt, r0))
            load_rows(S, C, xt, src, (r0 // 128), src.ap[r0:r0 + 128, :])
            jj = st["gsub"] % 2
            pt = PB.get()
            yield from rms_to_hnT_gen(S, C, xt, hT_[:, :, sub * 128:(sub + 1) * 128], bf(pt), hn[jj], ssq[jj], rstd[jj])
            st["gsub"] += 1
        for c in range(H):
            ps = PB.get()
            for kc in range(KC):
                S.mm(ps[:, 0:NT], win[:, kc, c * 128:(c + 1) * 128], hT_[:, kc, :], start=(kc == 0), stop=(kc == KC - 1))
            yield
            S.act(uT[:, c, :], ps[:, 0:NT], AF.Gelu_apprx_tanh)
            yield
        for c in range(12):
            ps = PB.get()
            for kc in range(KC):
                S.mm(ps[:, 0:NT], win[:, kc, 1024 + c * 128:1024 + (c + 1) * 128], hT_[:, kc, :], start=(kc == 0), stop=(kc == KC - 1))
            yield
            r = rr[st["ich"] % 2]
            a = ca[st["ich"] % 2]
            if t0 == 0:
                S.memset("pool", r[:, 0:3], 0.0)
            else:
                S.copy("pool", r[:, 0:3], hal[:, c, :])
            S.copy("act", r[:, 3:NT + 3], ps[:, 0:NT])
            S.act(a, ps[:, 0:NT], AF.Copy, scale=cw[:, c, 3:4])
            S.copy("pool", hal[:, c, :], r[:, NT:NT + 3])
            yield
            for tap in (2, 1, 0):
                S.stt("dve", a, r[:, tap:tap + NT], cw[:, c, tap:tap + 1], a, ALU.mult, ALU.add)
            hh = c % 4
            yield
            if c >= 8:
                S.act(vT[:, hh, :], a, AF.Silu)
            else:
                q_ = qs[st["ich"] % 2]
                S.act(q_, a, AF.Silu)
                S.act(sq, q_, AF.Square)
                yield
                pss = PB.get()
                S.mm(pss[:, 0:NT], ones, sq)
                yield
                S.act(rn, pss[:, 0:NT], AF.Sqrt, bias=C.eps, scale=1.0)
                yield
                S.op("dve", lambda e: e.reciprocal(rn.ap, rn.ap), r=[rn], w=[rn])
                if c < 4:
                    S.stt("dve", qT[:, hh, :], q_, 128.0 ** -0.5, rn, ALU.mult, ALU.mult)
                else:
                    S.tt("dve", kT[:, hh, :], q_, rn, ALU.mult)
            st["ich"] += 1
            yield
        XSL[k] = xsl
        yield

    def chunk_task(b, t0, k):
        hT_ = hnT[k]
        uT, qT, kT, vT = uTs[k], qTs[k], kTs[k], vTs[k]
        xsl = XSL[k]
        if t0 == 0:
            S.memset("pool", Sst, 0.0)
            S.copy("pool", r32(Sr), Sst)
        for sub in range(NSUB):
            cs = slice(sub * 128, (sub + 1) * 128)
            xt, r0 = xsl[sub]
            pv = PB.get()
            pz = PB.get()
            pba = PB.get()
            for kc in range(KC):
                S.mm(pv, hT_[:, kc, cs], win[:, kc, 512:1024], start=(kc == 0), stop=(kc == KC - 1))
            for kc in range(KC):
                S.mm(pz, hT_[:, kc, cs], win[:, kc, 2568:3080], start=(kc == 0), stop=(kc == KC - 1))
            for kc in range(KC):
                S.mm(pba[:, 0:8], hT_[:, kc, cs], win[:, kc, 2560:2568], start=(kc == 0), stop=(kc == KC - 1))
            yield
            S.act(flat(vg), pv, AF.Gelu_apprx_tanh)
            S.act(flat(zs), pz, AF.Silu)
            S.act(sm["beta"], pba[:, 0:4], AF.Sigmoid)
            S.tt("dve", sm["xa"], pba[:, 4:8], dtb, ALU.add)
            S.reduce("dve", sm["vsum"], vg, ALU.add)
            S.stt("dve", vg, bc_last(sm["vsum"], 128), -1.0 / 128, vg, ALU.mult, ALU.add)
            S.tt("pool", sqv, vg, vg, ALU.mult)
            S.reduce("dve", sm["vvar"], sqv, ALU.add)
            S.act(sm["vrs"], sm["vvar"], AF.Sqrt, bias=C.eps, scale=1.0 / 128)
            S.op("dve", lambda e: e.reciprocal(sm["vrs"].ap, sm["vrs"].ap), r=[sm["vrs"]], w=[sm["vrs"]])
            S.tt("dve", vn, vg, bc_last(sm["vrs"], 128), ALU.mult)
            yield
            pm = PB.get()
            for gI in range(H):
                S.mm(pm[:, gI * 128:(gI + 1) * 128], vn[:, gI, :], wTm[:, gI, :], start=True, stop=False)
                S.mm(pm[:, gI * 128:(gI + 1) * 128], ones[0:1, :], brow[0:1, gI * 128:(gI + 1) * 128], start=False, stop=True)
            yield
            S.tt("dve", yT[:, 0:4, cs], v3(pm), uT[:, :, cs], ALU.mult)
            yield
            S.ts("dve", sm["nbeta"], sm["beta"], -1.0, ALU.mult)
            S.ts("dve", sm["xm"], sm["xa"], 30.0, ALU.min)
            S.act(sm["xe"], sm["xm"], AF.Exp)
            S.act(sm["sp"], sm["xe"], AF.Ln, bias=1.0, scale=1.0)
            S.ts("dve", sm["xm"], sm["xa"], -30.0, ALU.add, 0.0, ALU.max)
            S.tt("dve", sm["sp"], sm["sp"], sm["xm"], ALU.add)
            S.tt("dve", sm["g"], sm["sp"], nega, ALU.mult)
            g = sm["g"]
            pg = PB.get()
            S.mm(pg[:, 0:4], tri, g)
            S.mm(pg[:, 4:8], ones, g)
            yield
            S.copy("dve", gcs, pg[:, 0:8])
            S.act(egs, gcs, AF.Exp)
            S.tt("dve", sm["edl"], gcs[:, 4:8], gcs[:, 0:4], ALU.subtract)
            S.act(sm["edl"], sm["edl"], AF.Exp)
            S.tt("dve", sm["bk"], sm["beta"], egs[:, 0:4], ALU.mult)
            yield
            pk = PB.get()
            for h in range(H):
                S.tr(bf(pk)[:, h * 128:(h + 1) * 128], kT[:, h, cs], C.identb)
            yield
            S.copy("act", flat(ktok), bf(pk)[:, 0:512])
            pk = PB.get()
            for h in range(H):
                S.tr(bf(pk)[:, h * 128:(h + 1) * 128], vT[:, h, cs], C.identb)
            yield
            S.copy("act", flat(vtok), bf(pk)[:, 0:512])
            yield
            S.tt("pool", G2, bc_mid(C.identf, H), bc_last(gcs[:, 0:4], 128), ALU.mult)
            pd = PB.get()
            S.mm(pd, ones, flat(G2))
            yield
            S.tt("dve", E, bc_last(gcs[:, 0:4], 128), v3(pd), ALU.subtract)
            S.ts("dve", flat(E), flat(E), 0.0, ALU.min)
            S.act(flat(E), flat(E), AF.Exp)
            yield
            pkk = PB.get()
            pqk = PB.get()
            for h in range(H):
                S.mm(pkk[:, h * 128:(h + 1) * 128], kT[:, h, cs], kT[:, h, cs])
            for h in range(H):
                S.mm(pqk[:, h * 128:(h + 1) * 128], qT[:, h, cs], kT[:, h, cs])
            yield
            S.tt("pool", nbm, bc_mid(strict, H), bc_last(sm["nbeta"], 128), ALU.mult)
            S.tt("dve", flat(t1), pkk, flat(E), ALU.mult)
            S.tt("pool", r32(Lp0), t1, nbm, ALU.mult)
            S.tt("dve", flat(osq), pqk, flat(E), ALU.mult)
            S.tt("pool", intra, osq, bc_mid(incl, H), ALU.mult)
            yield
            pu = PB.get()
            for h in range(H):
                S.tr(pu[:, h * 128:(h + 1) * 128], Lp0[:, h, :], C.identf)
            yield
            S.copy("act", r32(flat(U0)), pu)
            S.tt("dve", r32(TT), v3(pu), bc_mid(C.identf, H), ALU.add)
            pi = PB.get()
            for h in range(H):
                S.tr(pi[:, h * 128:(h + 1) * 128], intra[:, h, :], C.identf)
            yield
            S.copy("act", r32(flat(intraT)), pi)
            yield
            Us = [U0, U1]
            Ls = [Lp0, Lp1]
            for k in range(1, 7):
                Uo, Un = Us[(k - 1) % 2], Us[k % 2]
                Lo, Ln_ = Ls[(k - 1) % 2], Ls[k % 2]
                if k <= 5:
                    p1 = PB.get()
                    for h in range(H):
                        S.mm(p1[:, h * 128:(h + 1) * 128], r32(Lo[:, h, :]), r32(Uo[:, h, :]))
                p2 = PB.get()
                for h in range(H):
                    S.mm(p2[:, h * 128:(h + 1) * 128], r32(Uo[:, h, :]), r32(Lo[:, h, :]))
                yield
                if k <= 5:
                    S.copy("act", r32(flat(Un)), p1)
                S.copy("dve", r32(flat(Ln_)), p2)
                yield
                p3 = PB.get()
                for h in range(H):
                    S.mm(p3[:, h * 128:(h + 1) * 128], r32(Ln_[:, h, :]), r32(TT[:, h, :]))
                yield
                S.tt("dve", r32(flat(TT)), flat(TT), p3, ALU.add)
                yield
            yield
            S.tt("pool", r32(vb), vtok, bc_last(sm["beta"], 128), ALU.mult)
            S.tt("pool", r32(kbg), ktok, bc_last(sm["bk"], 128), ALU.mult)
            S.tt("pool", r32(kd), ktok, bc_last(sm["edl"], 128), ALU.mult)
            pU = PB.get()
            for h in range(H):
                S.mm(pU[:, h * 128:(h + 1) * 128], r32(TT[:, h, :]), r32(vb[:, h, :]))
            yield
            S.copy("act", flat(usb), pU)
            pW = PB.get()
            for h in range(H):
                S.mm(pW[:, h * 128:(h + 1) * 128], r32(kbg[:, h, :]), r32(TT[:, h, :]))
            yield
            S.copy("act", r32(flat(wTs)), pW)
            yield
            S.tt("pool", G1, bc_mid(C.identf, H), bc_last(egs[:, 0:4], 128), ALU.mult)
            pe_ = PB.get()
            S.mm(pe_, ones, flat(G1))
            yield
            S.tt("dve", r32(qgT), qT[:, :, cs], v3(pe_), ALU.mult)
            yield
            pws = PB.get()
            for h in range(H):
                S.mm(pws[:, h * 128:(h + 1) * 128], r32(wTs[:, h, :]), r32(Sr[:, h, :]))
            yield
            S.tt("dve", r32(flat(vnew)), flat(usb), pws, ALU.subtract)
            po = PB.get()
            for h in range(H):
                S.mm(po[:, h * 128:(h + 1) * 128], r32(qgT[:, h, :]), r32(Sr[:, h, :]), start=True, stop=False)
                S.mm(po[:, h * 128:(h + 1) * 128], r32(intraT[:, h, :]), r32(vnew[:, h, :]), start=False, stop=True)
            pS = PB.get()
            for h in range(H):
                S.mm(pS[:, h * 128:(h + 1) * 128], r32(kd[:, h, :]), r32(vnew[:, h, :]))
            yield
            S.tt("dve", Sst, Sst, bc_last(egs[:, 4:8], 128), ALU.mult)
            S.tt("dve", flat(Sst), flat(Sst), pS, ALU.add)
            S.copy("act", r32(Sr), Sst)
            yield
            yield
            S.act(flat(osq), po, AF.Square)
            S.reduce("dve", sm["oss"], osq, ALU.add)
            S.act(sm["ors"], sm["oss"], AF.Sqrt, bias=C.eps, scale=1.0 / 128)
            S.op("dve", lambda e: e.reciprocal(sm["ors"].ap, sm["ors"].ap), r=[sm["ors"]], w=[sm["ors"]])
            S.tt("pool", gz, zs, bc_mid(gng, H), ALU.mult)
            S.tt("dve", on, v3(po), bc_last(sm["ors"], 128), ALU.mult)
            S.tt("dve", onb, on, gz, ALU.mult)
            pt2 = PB.get()
            for h in range(H):
                S.tr(bf(pt2)[:, h * 128:(h + 1) * 128], onb[:, h, :], C.identb)
            yield
            S.copy("act", yT[:, 4:8, cs], v3(bf(pt2)[:, 0:512]))
            yield
            for nh in range(2):
                pso = PB.get()
                for kc in range(KC):
                    S.mm(pso, yT[:, kc, cs], wout[:, kc, nh * 512:(nh + 1) * 512], start=(kc == 0), stop=(kc == KC - 1))
                S.tt("dve", xt[:, nh * 512:(nh + 1) * 512], xt[:, nh * 512:(nh + 1) * 512], pso, ALU.add)
            S.dma("sp", dst.v((r0 // 128), dst.ap[r0:r0 + 128, :]), xt, sres=xt.res)

    tiles = [(b, t0) for b in range(C.NSEQ) for t0 in range(0, C.L, NT)]
    prev = None
    for i, (b, t0) in enumerate(tiles):
        run_lanes([proj_task(b, t0, i % 2), chunk_task(*prev) if prev is not None else None])
        prev = (b, t0, i % 2)
    run_lanes([chunk_task(*prev)])
    S.end_phase()


OD_EXT = 3016
NQK = 18


def odd_phase(S, C, j, src, dst):
    odd_proj_phase(S, C, j, src)
    odd_attn_phase(S, C, j, src, dst)


def odd_proj_phase(S, C, j, src):
    layer = 2 * j + 1
    NT = 512
    NSUB = NT // 128
    S.begin_phase()
    PB = Banks(S)
    win = S.sb("win", [128, KC, OD_EXT], BF16)
    gsb = S.sb("gsb", [128, KC])
    stg = [S.sb("stg%d" % i, [128, 754]) for i in range(4)]
    S.dma("sp", gsb, V(C.mix_g[layer], C.wres), sres=gsb.res)
    load_weight_scaled(S, C, win, C.od_win[j], KC, OD_EXT, gsb, stg, 1508, "win")
    gn = S.sb("gn", [128, 4])
    S.dma("sp", gn, V(C.od_gn[j], C.wres), sres=gn.res)
    S.ts("pool", gn[:, 0:1], gn[:, 0:1], 64.0 ** -0.5, ALU.mult, 1.0, ALU.mult)
    S.ts("pool", gn[:, 2:3], gn[:, 2:3], 128.0 ** -0.5, ALU.mult, 1.0, ALU.mult)
    ones = S.sb("ones", [128, 128])
    bd64 = S.sb("bd64", [128, 128])
    S.memset("pool", ones, 1.0)
    S.memset("pool", bd64, 0.0)
    S.memset("pool", bd64[0:64, 0:64], 1.0)
    S.memset("pool", bd64[64:128, 64:128], 1.0)

    xs = [S.sb("x%d" % i, [128, D]) for i in range(2)]
    hn = [S.sb("hn%d" % i, [128, D], BF16) for i in range(2)]
    ssq = [S.sb("ssq%d" % i, [128, 1]) for i in range(2)]
    rstd = [S.sb("rstd%d" % i, [128, 1]) for i in range(2)]
    hnT = [S.sb("hnT%d" % i, [128, KC, NT], BF16) for i in range(2)]
    oT = [S.sb("oT%d" % i, [128, NQK, NT], BF16) for i in range(2)]
    sqb = [S.sb("sqb%d" % i, [128, NT]) for i in range(2)]
    rn = [S.sb("rn%d" % i, [128, NT]) for i in range(2)]
    tokb = [S.sb("tokb%d" % i, [128, 640], BF16) for i in range(2)]
    iwt = [S.sb("iwt%d" % i, [128, 8]) for i in range(2)]

    chunks = []
    for c in range(4):
        chunks.append((c * 128, "n64", 0))
    for c in range(4):
        chunks.append((512 + c * 128, "n64", 1))
    for c in range(4):
        chunks.append((1536 + c * 128, "n128", 2))
    chunks.append((2048, "n128", 3))
    for c in range(4):
        chunks.append((2304 + c * 128, "scale", None))
    chunks.append((2888, "copy", None))

    gsub = 0
    ist = 0
    inorm = 0
    for b in range(C.NSEQ):
        for t0 in range(0, C.L, NT):
            hT_ = hnT[ist % 2]
            o_ = oT[ist % 2]
            for sub in range(NSUB):
                r0 = b * C.L + t0 + sub * 128
                xt = xs[gsub % 2]
                load_rows(S, C, xt, src, (r0 // 128), src.ap[r0:r0 + 128, :])
                jj = gsub % 2
                pt = PB.get()
                rms_to_hnT(S, C, xt, hT_[:, :, sub * 128:(sub + 1) * 128], bf(pt), hn[jj], ssq[jj], rstd[jj])
                gsub += 1
            for ci, (c0, kind, gi) in enumerate(chunks):
                ps = PB.get()
                for kc in range(KC):
                    S.mm(ps[:, 0:NT], win[:, kc, c0:c0 + 128], hT_[:, kc, :], start=(kc == 0), stop=(kc == KC - 1))
                if kind == "scale":
                    S.act(o_[:, ci, :], ps[:, 0:NT], AF.Copy, scale=0.125)
                elif kind == "copy":
                    S.copy("act", o_[:, ci, :], ps[:, 0:NT])
                else:
                    sq_ = sqb[inorm % 2]
                    rn_ = rn[inorm % 2]
                    inorm += 1
                    S.act(sq_, ps[:, 0:NT], AF.Square)
                    pss = PB.get()
                    S.mm(pss[:, 0:NT], bd64 if kind == "n64" else ones, sq_)
                    dim = 64.0 if kind == "n64" else 128.0
                    S.act(rn_, pss[:, 0:NT], AF.Sqrt, bias=C.eps, scale=1.0 / dim)
                    S.op("dve", lambda e, rn_=rn_: e.reciprocal(rn_.ap, rn_.ap), r=[rn_], w=[rn_])
                    S.stt("dve", o_[:, ci, :], ps[:, 0:NT], gn[:, gi:gi + 1], rn_, ALU.mult, ALU.mult)
            for sub in range(NSUB):
                cs = slice(sub * 128, (sub + 1) * 128)
                r0 = t0 + sub * 128
                tb = tokb[sub % 2]
                iw_ = iwt[sub % 2]
                pv = PB.get()
                for kc in range(KC):
                    S.mm(pv, hT_[:, kc, cs], win[:, kc, 1024:1536], start=(kc == 0), stop=(kc == KC - 1))
                p2 = PB.get()
                for kc in range(KC):
                    S.mm(p2[:, 0:128], hT_[:, kc, cs], win[:, kc, 2176:2304], start=(kc == 0), stop=(kc == KC - 1))
                for kc in range(KC):
                    S.mm(p2[:, 128:136], hT_[:, kc, cs], win[:, kc, 2880:2888], start=(kc == 0), stop=(kc == KC - 1))
                S.copy("act", tb[:, 0:512], pv)
                S.copy("act", tb[:, 512:640], p2[:, 0:128])
                S.ts("dve", iw_, p2[:, 128:136], 8.0 ** -0.5, ALU.mult)
                S.dma("sp", C.vtok.v((b, r0 // 128), C.vtok.ap[b, r0:r0 + 128, :]), tb, sres=tb.res)
                S.dma("sp", C.iwd.v((b, r0 // 128), C.iwd.ap[b, r0:r0 + 128, :]), iw_, sres=iw_.res)
            S.dma("sp", C.qkT.v((b, t0 // NT), C.qkT.ap[b, :, :, t0:t0 + NT].rearrange("c p t -> p c t")), o_, sres=o_.res)
            ist += 1
    S.end_phase()


def odd_attn_phase(S, C, j, src, dst):
    layer = 2 * j + 1
    lambda_init = 0.8 - 0.6 * math.exp(-0.3 * layer)
    L = C.L
    NKB = L // 128
    NQ = 512
    NQS = NQ // 128
    TOPK = min(256, L // 4)
    NIT = 18
    S.begin_phase()
    PB1 = Banks(S, 2)
    PB2 = Banks(S, 1)
    PB3 = Banks(S, 1)
    DACC = [S.ps("dacc%d" % i, [128, 512]) for i in range(2)]
    FACC = [S.ps("facc%d" % i, [128, 512]) for i in range(2)]
    wout = S.sb("wout", [128, KC, D], BF16)
    stg = [S.sb("stg%d" % i, [128, 1024]) for i in range(2)]
    load_weight_scaled(S, C, wout, C.od_wout[j], KC, D, None, stg, 1024, "wout")
    S.barrier()
    tmpf = [V(stg[0].ap[:, 0:512], Res("tmpf0"))]
    R = [V(stg[1].ap[:, 0:512], Res("R0")), V(stg[1].ap[:, 512:1024], Res("R1")), V(stg[0].ap[:, 512:1024], Res("R2"))]
    rb = S.sb("rb", [128, 8, 2, 128])
    t31 = S.sb("t31", [128, 8])
    cmT = S.sb("cmT", [128, 128])
    dmask = S.sb("dmask", [128, 128])
    zer = S.sb("zer", [128, 128])
    S.dma("sp", rb, V(C.rb_near, C.wres), sres=rb.res)
    S.dma("sp", t31, V(C.rb_t31.partition_broadcast(128), C.wres), sres=t31.res)
    S.memset("pool", zer, 0.0)
    S.asel(cmT, zer, [[1, 128]], ALU.is_ge, NEG, base=0, cm=-1)
    S.asel(dmask, zer, [[-1, 128]], ALU.is_ge, -1e30, base=0, cm=1)
    rbv = V(rb.ap.rearrange("p h r q -> p h (r q)"), rb.res)
    S.tt("dve", rbv, rbv, bc_last(t31, 256), ALU.subtract)
    S.tt("dve", rb[:, :, 1, :], rb[:, :, 1, :], bc_mid(cmT, 8), ALU.add)
    lf = S.sb("lf", [128, 256])
    lj = S.sb("lj", [128, 64])
    lam = {n: S.sb(n, [128, 1]) for n in ("s01", "s23", "nlam")}
    S.dma("sp", lf, V(C.od_lam[j].partition_broadcast(128), C.wres), sres=lf.res)
    S.memset("dve", lam["s01"], 0.0)
    S.memset("dve", lam["s23"], 0.0)
    S.stt("dve", lj, lf[:, 0:64], 1.0, lf[:, 64:128], ALU.mult, ALU.mult, accum=lam["s01"])
    S.stt("dve", lj, lf[:, 128:192], 1.0, lf[:, 192:256], ALU.mult, ALU.mult, accum=lam["s23"])
    S.act(lam["s01"], lam["s01"], AF.Exp)
    S.act(lam["s23"], lam["s23"], AF.Exp)
    S.tt("dve", lam["nlam"], lam["s23"], lam["s01"], ALU.subtract)
    S.ts("dve", lam["nlam"], lam["nlam"], -lambda_init, ALU.add)
    gsub_ = S.sb("gsubn", [128, 128])
    S.dma("sp", gsub_, V(C.od_gsub[j].partition_broadcast(128), C.wres), sres=gsub_.res)
    S.ts("pool", gsub_, gsub_, 1.0 - lambda_init, ALU.mult, 1.0, ALU.mult)

    dkT = S.sb("dkT", [128, 4, L], BF16)
    skT = S.sb("skT", [128, L], BF16)
    ikT = S.sb("ikT", [128, L], BF16)
    dvA = S.sb("dvA", [128, NKB, 4, 130], BF16)
    svA = S.sb("svA", [128, NKB, 130], BF16)
    S.memset("pool", dvA[:, :, :, 128:130], 1.0)
    S.memset("pool", svA[:, :, 128:130], 1.0)
    dqT = S.sb("dqT", [128, 4, NQ], BF16)
    sqT = S.sb("sqT", [128, 4, NQ], BF16)
    iqT = S.sb("iqT", [128, 4, NQ], BF16)
    iw = S.sb("iw", [128, NQS, 8])
    idx = S.sb("idx", [128, L])
    M = S.sb("M", [128, L], BF16)
    M2 = S.sb("M2", [128, L], BF16)
    MT = S.sb("MT", [128, NKB, 128], BF16)
    PTa = [S.sb("PTa%d" % i, [128, 512], BF16) for i in range(3)]
    PTb = [S.sb("PTb%d" % i, [128, 512], BF16) for i in range(2)]
    xs = [S.sb("x%d" % i, [128, D]) for i in range(1)]
    ytd = [[S.sb("ytd%d_%d" % (a, i), [128, 4, 128], BF16) for i in range(NQS)] for a in range(2)]
    ytf = [[S.sb("ytf%d_%d" % (a, i), [128, 4, 128], BF16) for i in range(NQS)] for a in range(2)]
    yT = S.sb("yT", [128, 8, 128], BF16)
    of0 = S.sb("of0", [128, NQS, 128])
    of = S.sb("of", [128, 128])
    osq = S.sb("osq", [128, 128])
    sc = {n: S.sb(n, [128, 1]) for n in ("rmax", "w0", "lo", "nlo", "nmid", "cnt", "gw", "rec")}
    sd = {n: S.sb(n, [128, 1]) for n in ("rec", "oss", "ors")}
    sc2 = {n: S.sb(n + "2", [128, 1]) for n in ("rec",)}
    ctr = {"R": 0, "Pa": 0, "Pb": 0, "T": 0, "x": 0}

    NQB = L // 128
    NST = (L + NQ - 1) // NQ
    Ms = [M, M2]
    prog = {"s1": 0, "s2": 0, "df": 0}

    def s1_lane(b):
        for qb in range(NQB):
            s0 = (qb // NQS) * NQ
            qs = qb % NQS
            nqs = min(NQS, (L - s0) // 128)
            nq = nqs * 128
            if qs == 0:
                S.dma("sp", iqT[:, :, 0:nq], C.qkT.v((b, "iq", s0), C.qkT.ap[b, 13:17, :, s0:s0 + nq].rearrange("c p t -> p c t")), sres=iqT.res)
                S.dma("sp", iw[:, 0:nqs, :], C.iwd.v((b, "iw", s0), C.iwd.ap[b, s0:s0 + nq, :].rearrange("(s p) e -> p s e", p=128)), sres=iw.res)
                yield
            nk = (qb + 1) * 128
            qc = slice(qs * 128, (qs + 1) * 128)
            if nk > TOPK:
                while prog["s2"] < qb - 1:
                    yield
                Mq = Ms[qb % 2]
                pend = None

                def fma(p):
                    r_, k0, wd, h = p
                    if h == 0:
                        S.ts("dve", idx[:, k0:k0 + wd], r_[:, 0:wd], iw[:, qs, 0:1], ALU.mult)
                    else:
                        S.stt("dve", idx[:, k0:k0 + wd], r_[:, 0:wd], iw[:, qs, h:h + 1], idx[:, k0:k0 + wd], ALU.mult, ALU.add)

                for k0 in range(0, nk, 512):
                    wd = min(512, nk - k0)
                    for h in range(8):
                        pr = slice((h % 2) * 64, (h % 2) * 64 + 64)
                        ps = PB1.get()
                        S.mm(ps[:, 0:wd], iqT[pr, h // 2, qc], ikT[pr, k0:k0 + wd])
                        r_ = R[ctr["R"] % len(R)]
                        ctr["R"] += 1
                        S.act(r_[:, 0:wd], ps[:, 0:wd], AF.Relu)
                        if pend is not None:
                            fma(pend)
                        pend = (r_, k0, wd, h)
                        if h % 2 == 1:
                            yield
                fma(pend)
                S.reduce("dve", sc["rmax"], idx[:, 0:nk], ALU.max)
                S.reduce("dve", sc["lo"], idx[:, 0:nk], ALU.min)
                S.tt("dve", sc["w0"], sc["rmax"], sc["lo"], ALU.subtract)
                S.ts("dve", sc["nlo"], sc["lo"], -1.0, ALU.mult)
                S.tt("dve", idx[:, nk - 128:nk], idx[:, nk - 128:nk], dmask, ALU.add)
                yield
                thr = 2.0 * TOPK - nk - 0.5
                for it in range(NIT):
                    hw = 2.0 ** -(it + 1)
                    S.stt("dve", sc["nmid"], sc["w0"], -hw, sc["nlo"], ALU.mult, ALU.add)
                    S.act(Mq[:, 0:nk], idx[:, 0:nk], AF.Sign, bias=sc["nmid"], scale=1.0, accum=sc["cnt"])
                    yield
                    S.stt("dve", sc["gw"], sc["cnt"], thr, sc["w0"], ALU.is_ge, ALU.mult)
                    S.stt("dve", sc["nlo"], sc["gw"], -hw, sc["nlo"], ALU.mult, ALU.add)
                S.ts("dve", sc["lo"], sc["nlo"], -1.0, ALU.mult)
                S.ts("dve", Mq[:, 0:nk], idx[:, 0:nk], sc["lo"], ALU.is_ge)
            prog["s1"] = qb + 1
            yield

    def s2_lane(b):
        for qb in range(NQB):
            st_i = qb // NQS
            s0 = st_i * NQ
            qs = qb % NQS
            nqs = min(NQS, (L - s0) // 128)
            nq = nqs * 128
            if qs == 0:
                while prog["df"] < st_i - 1:
                    yield
                S.dma("sp", sqT[:, :, 0:nq], C.qkT.v((b, "sq", s0), C.qkT.ap[b, 8:12, :, s0:s0 + nq].rearrange("c p t -> p c t")), sres=sqT.res)
                yield
            while prog["s1"] < qb + 1:
                yield
            yb = ytd[st_i % 2]
            nk = (qb + 1) * 128
            qc = slice(qs * 128, (qs + 1) * 128)
            use_topk = nk > TOPK
            Mq = Ms[qb % 2]
            if use_topk:
                for g0 in range(0, qb + 1, 8):
                    ng = min(8, qb + 1 - g0)
                    pm = PB2.get()
                    for i in range(ng):
                        S.tr(bf(pm)[:, i * 128:(i + 1) * 128], Mq[:, (g0 + i) * 128:(g0 + i + 1) * 128], C.identb)
                    S.copy("act", MT[:, g0:g0 + ng, :], v3(bf(pm)[:, 0:ng * 128], ng))
                    yield
            for a in DACC:
                S.memset("dve", a[:, 0:130], 0.0)
                S.memset("dve", a[:, 256:386], 0.0)
            nkb_ = qb + 1

            def st1(kb):
                kc_ = slice(kb * 128, (kb + 1) * 128)
                ps = PB2.get()
                S.mm(ps, skT[:, kc_], sqT[:, :, qc])
                pt_ = PTa[kb % 3]
                rel = kb - (qb - 1)
                if rel >= 0:
                    t_ = tmpf[ctr["T"] % len(tmpf)]
                    ctr["T"] += 1
                    S.tt("dve", v3(t_), v3(ps), rb[:, 4:8, rel, :], ALU.add)
                    S.act(pt_, t_, AF.Exp)
                else:
                    S.act(pt_, ps, AF.Exp)

            def st2(kb):
                if use_topk:
                    pt_ = PTa[kb % 3]
                    S.tt("dve", v3(pt_), v3(pt_), bc_mid(MT[:, kb, :], 4), ALU.mult)

            def st3(kb):
                pt_ = PTa[kb % 3]
                for h in range(4):
                    S.op("pe", lambda e, h=h, pt_=pt_, kb=kb, qb=qb: e.matmul(DACC[h // 2].ap[:, (h % 2) * 256:(h % 2) * 256 + 129], pt_.ap[:, h * 128:(h + 1) * 128],
                                                                             svA.ap[:, kb, 0:129], start=False, stop=(kb == qb), skip_group_check=True),
                         r=[pt_, svA], w=[DACC[h // 2]], inc=(h == 3))

            for t in range(nkb_ + 2):
                if 0 <= t - 2 < nkb_:
                    st3(t - 2)
                if 0 <= t - 1 < nkb_:
                    st2(t - 1)
                if t < nkb_:
                    st1(t)
                yield
            for h in range(4):
                a = DACC[h // 2]
                c0 = (h % 2) * 256
                S.op("dve", lambda e, a=a, c0=c0: e.reciprocal(sc2["rec"].ap, a.ap[:, c0 + 128:c0 + 129]), r=[a], w=[sc2["rec"]])
                S.ts("dve", yb[qs][:, h, :], a[:, c0:c0 + 128], sc2["rec"], ALU.mult)
            prog["s2"] = qb + 1
            yield

    def df_lane(b):
        for st_i in range(NST):
            s0 = st_i * NQ
            sblk = s0 // 128
            nqs = min(NQS, (L - s0) // 128)
            nq = nqs * 128
            yf = ytf[st_i % 2]
            yd = ytd[st_i % 2]
            S.dma("sp", dqT[:, :, 0:nq], C.qkT.v((b, "dq", s0), C.qkT.ap[b, 0:4, :, s0:s0 + nq].rearrange("c p t -> p c t")), sres=dqT.res)
            yield
            last_kb = sblk + nqs - 1
            for h in range(4):
                for m in range(2):
                    pr = slice(m * 64, m * 64 + 64)
                    for a in FACC:
                        S.memset("dve", a[:, 0:130], 0.0)
                        S.memset("dve", a[:, 256:386], 0.0)
                    def d1(kb):
                        kc_ = slice(kb * 128, (kb + 1) * 128)
                        qlo = max(0, kb - sblk)
                        ncol = (nqs - qlo) * 128
                        ps = PB3.get()
                        S.mm(ps[:, 0:ncol], dkT[pr, h, kc_], dqT[pr, h, qlo * 128:nqs * 128])
                        for qs in (kb - sblk, kb - sblk + 1):
                            if 0 <= qs < nqs:
                                rel = kb - (sblk + qs - 1)
                                cc = slice((qs - qlo) * 128, (qs - qlo + 1) * 128)
                                S.tt("dve", ps[:, cc], ps[:, cc], rb[:, h, rel, :], ALU.add)
                        pt_ = PTb[kb % 2]
                        S.act(pt_[:, 0:ncol], ps[:, 0:ncol], AF.Exp)

                    def d2(kb):
                        qlo = max(0, kb - sblk)
                        pt_ = PTb[kb % 2]
                        for qs in range(qlo, nqs):
                            cc = slice((qs - qlo) * 128, (qs - qlo + 1) * 128)
                            S.op("pe", lambda e, qs=qs, cc=cc, pt_=pt_, kb=kb, h=h, sblk=sblk: e.matmul(
                                FACC[qs // 2].ap[:, (qs % 2) * 256:(qs % 2) * 256 + 129], pt_.ap[:, cc], dvA.ap[:, kb, h, 0:129],
                                start=False, stop=(kb == sblk + qs), skip_group_check=True), r=[pt_, dvA], w=[FACC[qs // 2]],
                                inc=(qs == nqs - 1))

                    for t in range(last_kb + 2):
                        if 0 <= t - 1 <= last_kb:
                            d2(t - 1)
                        if t <= last_kb:
                            d1(t)
                        yield
                    for qs in range(nqs):
                        a = FACC[qs // 2]
                        c0 = (qs % 2) * 256
                        S.op("dve", lambda e, a=a, c0=c0: e.reciprocal(sd["rec"].ap, a.ap[:, c0 + 128:c0 + 129]), r=[a], w=[sd["rec"]])
                        if m == 0:
                            S.ts("dve", of0[:, qs, :], a[:, c0:c0 + 128], sd["rec"], ALU.mult)
                        else:
                            S.tt("dve", sd["rec"], sd["rec"], lam["nlam"], ALU.mult)
                            S.stt("dve", of, a[:, c0:c0 + 128], sd["rec"], of0[:, qs, :], ALU.mult, ALU.add)
                            S.act(osq, of, AF.Square, accum=sd["oss"])
                            S.act(sd["ors"], sd["oss"], AF.Sqrt, bias=C.eps, scale=1.0 / 128)
                            S.op("dve", lambda e: e.reciprocal(sd["ors"].ap, sd["ors"].ap), r=[sd["ors"]], w=[sd["ors"]])
                            S.stt("dve", yf[qs][:, h, :], of, sd["ors"], gsub_, ALU.mult, ALU.mult)
                    yield
            while prog["s2"] < min(NQB, (st_i + 1) * NQS):
                yield
            for qs in range(nqs):
                r0 = b * L + s0 + qs * 128
                xt = xs[0]
                load_rows(S, C, xt, src, (r0 // 128), src.ap[r0:r0 + 128, :])
                pt2 = PB3.get()
                for c in range(4):
                    S.tr(bf(pt2)[:, c * 128:(c + 1) * 128], yf[qs][:, c, :], C.identb)
                for c in range(4):
                    S.tr(bf(pt2)[:, (4 + c) * 128:(5 + c) * 128], yd[qs][:, c, :], C.identb)
                yield
                S.copy("act", flat(yT), bf(pt2))
                yield
                for nh in range(2):
                    pso = PB3.get()
                    for kc in range(KC):
                        S.mm(pso, yT[:, kc, :], wout[:, kc, nh * 512:(nh + 1) * 512], start=(kc == 0), stop=(kc == KC - 1))
                    yield
                    S.tt("dve", xt[:, nh * 512:(nh + 1) * 512], xt[:, nh * 512:(nh + 1) * 512], pso, ALU.add)
                S.dma("sp", dst.v((r0 // 128), dst.ap[r0:r0 + 128, :]), xt, sres=xt.res)
                yield
            prog["df"] = st_i + 1
            yield

    for b in range(C.NSEQ):
        S.dma("sp", dkT, C.qkT.v((b, "dk"), C.qkT.ap[b, 4:8, :, :].rearrange("c p t -> p c t")), sres=dkT.res)
        S.dma("sp", skT, C.qkT.v((b, "sk"), C.qkT.ap[b, 12, :, :]), sres=skT.res)
        S.dma("sp", ikT, C.qkT.v((b, "ik"), C.qkT.ap[b, 17, :, :]), sres=ikT.res)
        for kb in range(NKB):
            S.dma("sp", dvA[:, kb, :, 0:128], C.vtok.v((b, "dv", kb), C.vtok.ap[b, kb * 128:(kb + 1) * 128, 0:512].rearrange("p (h d) -> p h d", h=4)), sres=dvA.res)
        S.dma("sp", svA[:, :, 0:128], C.vtok.v((b, "sv"), C.vtok.ap[b, :, 512:640].rearrange("(k p) d -> p k d", p=128)), sres=svA.res)
        prog["s1"] = prog["s2"] = prog["df"] = 0
        run_lanes([s1_lane(b), s2_lane(b), df_lane(b)])
    S.end_phase()


W_SPECS = {
    "ffn_g": [4, 128, KC],
    "ffn_cw": [4, 128, 2 * FC, 3],
    "ffn_cb": [4, 128, 2 * FC],
    "ffn_wup": [4, D, 2 * DFF],
    "ffn_wdn": [4, DFF, D],
    "mix_g": [4, 128, KC],
    "ev_win": [2, D, EVEN_IN],
    "ev_wout": [2, D, D],
    "gm_wT": [2, 128, 4, 128],
    "gm_b": [2, 1, 512],
    "gdn_cw": [2, 128, 12, 4],
    "gdn_alog": [2, 1, 4],
    "gdn_dtb": [2, 1, 4],
    "gdn_ng": [2, 1, 128],
    "od_win": [2, D, OD_EXT],
    "od_wout": [2, D, D],
    "od_gn": [2, 128, 4],
    "od_lam": [2, 1, 256],
    "od_gsub": [2, 1, 128],
    "rb_near": [128, 8, 2, 128],
    "rb_t31": [1, 8],
}


def _rel_bucket_np(dist):
    n = np.maximum(dist, 0)
    nf = np.maximum(n, 16).astype(np.float32)
    far = 16 + (np.log(nf / np.float32(16)) / np.float32(math.log(128 / 16)) * np.float32(16)).astype(np.int32)
    return np.where(n < 16, n, np.minimum(far, 31))


def prep_weights(inp):
    f = lambda a: np.ascontiguousarray(np.asarray(a, dtype=np.float32))
    w = {}
    w["ffn_g"] = f(inp["ffn_norm_g"].reshape(4, KC, 128).transpose(0, 2, 1))
    w["ffn_cw"] = f(inp["ffn_conv_w"].reshape(4, 3, 2 * FC, 128).transpose(0, 3, 2, 1))
    w["ffn_cb"] = f(inp["ffn_conv_b"].reshape(4, 2 * FC, 128).transpose(0, 2, 1))
    w["ffn_wup"] = f(inp["ffn_w_up"])
    w["ffn_wdn"] = f(inp["ffn_w_down"])
    w["mix_g"] = f(inp["mix_norm_g"].reshape(4, KC, 128).transpose(0, 2, 1))
    w["ev_win"] = f(inp["ev_w_in"])
    w["ev_wout"] = f(inp["ev_w_out"])
    w["gm_wT"] = f(inp["gmlp_w_s"].transpose(0, 3, 1, 2))
    w["gm_b"] = f(inp["gmlp_b_s"].reshape(2, 1, 512))
    w["gdn_cw"] = f(inp["gdn_conv_w"].reshape(2, 4, 12, 128).transpose(0, 3, 2, 1))
    w["gdn_alog"] = f(inp["gdn_a_log"].reshape(2, 1, 4))
    w["gdn_dtb"] = f(inp["gdn_dt_bias"].reshape(2, 1, 4))
    w["gdn_ng"] = f(inp["gdn_norm_g"].reshape(2, 1, 128))
    ow = np.asarray(inp["od_w_in"], dtype=np.float32)
    w["od_win"] = f(np.concatenate([ow, ow[:, :, 2816:2880], ow[:, :, 2816:2880]], axis=2))
    w["od_wout"] = f(inp["od_w_out"])
    gq = np.asarray(inp["diff_q_norm_g"], dtype=np.float32)
    gk = np.asarray(inp["diff_k_norm_g"], dtype=np.float32)
    w["od_gn"] = f(np.stack([np.tile(gq, (1, 2)), np.tile(gk, (1, 2)), np.asarray(inp["dsa_q_norm_g"]), np.asarray(inp["dsa_k_norm_g"])], axis=2))
    w["od_lam"] = f(inp["diff_lambda"].reshape(2, 1, 256))
    w["od_gsub"] = f(inp["diff_sub_norm_g"].reshape(2, 1, 128))
    kk = np.arange(128)[:, None]
    qq = np.arange(128)[None, :]
    tab = np.asarray(inp["rel_bias"], dtype=np.float32)
    near = np.zeros((128, 8, 2, 128), np.float32)
    for rel in range(2):
        dist = qq - kk + (128 if rel == 0 else 0)
        near[:, :, rel, :] = tab[_rel_bucket_np(dist)].transpose(0, 2, 1)
    w["rb_near"] = f(near)
    w["rb_t31"] = f(tab[31:32, :])
    return w


def build_program(L, NSEQ, plan):
    nc = bass.Bass("TRN2", target_bir_lowering=False)
    NTOK = L * NSEQ
    C = Ctx()
    C.L, C.NSEQ, C.NTOK = L, NSEQ, NTOK
    x = nc.dram_tensor("x", [NTOK, D], F32, kind="ExternalInput").ap()
    y = nc.dram_tensor("y", [NTOK, D], F32, kind="ExternalOutput").ap()
    for name, shape in W_SPECS.items():
        setattr(C, name, nc.dram_tensor(name, shape, F32, kind="ExternalInput").ap())
    C.wres = Res("weights")
    xdt = DT(x, "x")
    ydt = DT(y, "y")
    C.qkT = DT(nc.dram_tensor("qkT", [NSEQ, NQK, 128, L], BF16, kind="Internal").ap(), "qkT")
    C.vtok = DT(nc.dram_tensor("vtok", [NSEQ, L, 640], BF16, kind="Internal").ap(), "vtok")
    C.iwd = DT(nc.dram_tensor("iwd", [NSEQ, L, 8], F32, kind="Internal").ap(), "iwd")
    with ExitStack() as es:
        S = Sched(nc, es)
        ct = es.enter_context(nc.sbuf_tensor("identb", [128, 128], BF16))
        C.identb = V(ct[:], Res("identb"))
        ct = es.enter_context(nc.sbuf_tensor("identf", [128, 128], F32))
        C.identf = V(ct[:], Res("identf"))
        ct = es.enter_context(nc.sbuf_tensor("eps", [128, 1], F32))
        C.eps = V(ct[:], Res("eps"))
        S.memset("pool", C.identf, 1.0)
        S.asel(C.identf, C.identf, [[-1, 128]], ALU.is_equal, 0.0, base=0, cm=1)
        S.copy("pool", C.identb, C.identf)
        S.memset("pool", C.eps, EPS)
        src = xdt
        for kind, idx in plan:
            if kind == "ffn":
                ffn_phase(S, C, idx, src, ydt)
            elif kind == "even":
                even_phase(S, C, idx, src, ydt)
            elif kind == "odd":
                odd_phase(S, C, idx, src, ydt)
            src = ydt
        S.barrier()
        print("program: ops=%d waits=%d dma_sems=%d" % (S.nops, S.nwaits, S.ndsem))
    return nc


FULL_PLAN = [("even", 0), ("ffn", 0), ("odd", 0), ("ffn", 1), ("even", 1), ("ffn", 2), ("odd", 1), ("ffn", 3)]


N_CORES = 8
_PROG = {}


def kernel(**inputs):
    x = np.asarray(inputs["x"], dtype=np.float32)
    B, L, Dm = x.shape
    nseq = B // N_CORES
    key = (L, nseq)
    if key not in _PROG:
        _PROG[key] = build_program(L, nseq, FULL_PLAN)
    nc = _PROG[key]
    w = prep_weights(inputs)
    in_maps = []
    for c in range(N_CORES):
        m = {"x": np.ascontiguousarray(x[c * nseq:(c + 1) * nseq].reshape(nseq * L, Dm))}
        m.update(w)
        in_maps.append(m)
    res = run_bass_kernel_spmd(nc, in_maps, core_ids=list(range(N_CORES)))
    out = np.concatenate([np.asarray(r["y"]).reshape(nseq, L, Dm) for r in res.results], axis=0)
    return out.astype(np.float32)
```

```python
import math
from contextlib import ExitStack

import numpy as np
import concourse.bass as bass
import concourse.mybir as mybir
from concourse.bass_utils import run_bass_kernel_spmd

F32 = mybir.dt.float32
BF16 = mybir.dt.bfloat16
AF = mybir.ActivationFunctionType
ALU = mybir.AluOpType
AX = mybir.AxisListType


class Res:
    __slots__ = ("name", "last_w", "reads", "dsem", "dcount", "excl")

    def __init__(self, name="r"):
        self.name = name
        self.excl = False
        self.last_w = None
        self.reads = []
        self.dsem = None
        self.dcount = 0


class V:
    __slots__ = ("ap", "res")

    def __init__(self, ap, res):
        self.ap = ap
        self.res = res

    def __getitem__(self, idx):
        return V(self.ap[idx], self.res)

    def r(self, res):
        return V(self.ap, res)


def _res_of(xs):
    out = []
    for x in xs:
        if x is None:
            continue
        out.append(x.res if isinstance(x, V) else x)
    return out


class Sched:
    ENGS = ("pe", "act", "dve", "pool", "sp")

    def __init__(self, nc, es):
        self.nc = nc
        self.es = es
        self.eng = {"pe": nc.tensor, "act": nc.scalar, "dve": nc.vector, "pool": nc.gpsimd, "sp": nc.sync}
        self.sem = {e: es.enter_context(nc.semaphore("sem_" + e)) for e in self.ENGS}
        self.cnt = {e: 0 for e in self.ENGS}
        self.known = {e: {} for e in self.ENGS}
        self.dpool = []
        self.dlive = []
        self.ndsem = 0
        self.nwaits = 0
        self.nops = 0
        self.phase_es = None
        self.uid = 0
        import os
        self.limit = int(os.environ["OPLIMIT"]) if "OPLIMIT" in os.environ else None

    def begin_phase(self):
        self.phase_es = ExitStack()

    def end_phase(self):
        self.barrier()
        for r in self.dlive:
            self.dpool.append((r.dsem, r.dcount))
            r.dsem = None
        self.dlive = []
        self.phase_es.close()
        self.phase_es = None

    def sb(self, name, shape, dt=F32):
        self.uid += 1
        name = "%s_u%d" % (name, self.uid)
        t = self.phase_es.enter_context(self.nc.sbuf_tensor(name, list(shape), dt))
        return V(t[:], Res(name))

    def ps(self, name, shape, dt=F32):
        self.uid += 1
        name = "%s_u%d" % (name, self.uid)
        t = self.phase_es.enter_context(self.nc.psum_tensor(name, list(shape), dt))
        rs = Res(name)
        rs.excl = True
        return V(t[:], rs)

    def _collect(self, eng, r, w, strict):
        waits = {}

        def add(ev, same_ok):
            if ev is None:
                return
            sem, val, src = ev
            if not strict and src == eng and (same_ok or eng == "pe"):
                return
            k = id(sem)
            if k not in waits or waits[k][1] < val:
                waits[k] = (sem, val)

        for res in r:
            add(res.last_w, False)
            if res.excl:
                for ev in res.reads:
                    add(ev, True)
        for res in w:
            add(res.last_w, True)
            for ev in res.reads:
                add(ev, True)
        return waits

    def _emit_waits(self, eng, waits):
        kn = self.known[eng]
        e = self.eng[eng]
        for k, (sem, val) in waits.items():
            if kn.get(k, 0) >= val:
                continue
            kn[k] = val
            e.wait_ge(sem, val)
            self.nwaits += 1

    def op(self, eng, fn, r=(), w=(), inc=True):
        if self.limit is not None and self.nops >= self.limit:
            return None
        r = _res_of(r)
        w = _res_of(w)
        self._emit_waits(eng, self._collect(eng, r, w, False))
        ins = fn(self.eng[eng])
        self.nops += 1
        if inc:
            self.cnt[eng] += 1
            ins.then_inc(self.sem[eng], 1)
            ev = (self.sem[eng], self.cnt[eng], eng)
        else:
            assert eng == "pe"
            ev = (self.sem[eng], self.cnt[eng] + 1, eng)
        for res in r:
            res.reads.append(ev)
        for res in w:
            res.last_w = ev
            res.reads = []
        return ins

    def dma(self, q, out, in_, sres=None, **kw):
        if self.limit is not None and self.nops >= self.limit:
            return None
        sr = sres if sres is not None else out.res
        if sr.dsem is None:
            if self.dpool:
                sr.dsem, sr.dcount = self.dpool.pop()
            else:
                sr.dsem = self.es.enter_context(self.nc.semaphore("dsem%d" % self.ndsem))
                sr.dcount = 0
                self.ndsem += 1
            self.dlive.append(sr)
        waits = self._collect(q, [in_.res], [out.res], True)
        k = id(sr.dsem)
        if sr.dcount > 0 and (k not in waits or waits[k][1] < sr.dcount):
            waits[k] = (sr.dsem, sr.dcount)
        self._emit_waits(q, waits)
        ins = self.eng[q].dma_start(out=out.ap, in_=in_.ap, **kw)
        sr.dcount += 16
        ins.then_inc(sr.dsem, 16)
        self.nops += 1
        ev = (sr.dsem, sr.dcount, "dma")
        in_.res.reads.append(ev)
        out.res.last_w = ev
        out.res.reads = []
        return ins

    def barrier(self):
        evs = [(self.sem[e], self.cnt[e]) for e in self.ENGS if self.cnt[e] > 0]
        evs += [(r.dsem, r.dcount) for r in self.dlive if r.dcount > 0]
        for e in self.ENGS:
            kn = self.known[e]
            for sem, val in evs:
                if sem is self.sem[e]:
                    continue
                if kn.get(id(sem), 0) >= val:
                    continue
                kn[id(sem)] = val
                self.eng[e].wait_ge(sem, val)
                self.nwaits += 1

    def mm(self, out, lhsT, rhs, start=True, stop=True, inc=None, **kw):
        if inc is None:
            inc = bool(stop)
        return self.op("pe", lambda e: e.matmul(out.ap, lhsT.ap, rhs.ap, start=start, stop=stop, **kw),
                       r=[lhsT, rhs], w=[out], inc=inc)

    def tr(self, out, in_, ident):
        return self.op("pe", lambda e: e.transpose(out.ap, in_.ap, ident.ap), r=[in_, ident], w=[out])

    def act(self, out, in_, func, bias=None, scale=None, accum=None, eng="act"):
        kw = {}
        rr = [in_]
        ww = [out]
        if bias is not None:
            if isinstance(bias, V):
                kw["bias"] = bias.ap
                rr.append(bias)
            else:
                kw["bias"] = bias
        if scale is not None:
            if isinstance(scale, V):
                kw["scale"] = scale.ap
                rr.append(scale)
            else:
                kw["scale"] = scale
        if accum is not None:
            kw["accum_out"] = accum.ap
            ww.append(accum)
        return self.op("act", lambda e: e.activation(out.ap, in_.ap, func, **kw), r=rr, w=ww)

    def tt(self, eng, out, a, b, op):
        return self.op(eng, lambda e: e.tensor_tensor(out.ap, a.ap, b.ap, op), r=[a, b], w=[out])

    def ts(self, eng, out, a, s1, op0, s2=None, op1=None, accum=None):
        rr = [a]
        ww = [out]
        a1 = s1
        a2 = s2
        if isinstance(s1, V):
            rr.append(s1)
            a1 = s1.ap
        if isinstance(s2, V):
            rr.append(s2)
            a2 = s2.ap
        kw = {}
        if op1 is not None:
            kw["op1"] = op1
        if accum is not None:
            kw["accum_out"] = accum.ap
            ww.append(accum)
        return self.op(eng, lambda e: e.tensor_scalar(out.ap, a.ap, a1, a2, op0, **kw), r=rr, w=ww)

    def stt(self, eng, out, a, s, b, op0, op1, accum=None):
        rr = [a, b]
        ww = [out]
        sc = s
        if isinstance(s, V):
            rr.append(s)
            sc = s.ap
        kw = {}
        if accum is not None:
            kw["accum_out"] = accum.ap
            ww.append(accum)
        return self.op(eng, lambda e: e.scalar_tensor_tensor(out.ap, a.ap, sc, b.ap, op0, op1, **kw), r=rr, w=ww)

    def copy(self, eng, out, in_):
        if eng == "act":
            return self.op("act", lambda e: e.copy(out.ap, in_.ap), r=[in_], w=[out])
        return self.op(eng, lambda e: e.tensor_copy(out.ap, in_.ap), r=[in_], w=[out])

    def memset(self, eng, out, val):
        return self.op(eng, lambda e: e.memset(out.ap, val), w=[out])

    def reduce(self, eng, out, in_, op, axis=None):
        ax = AX.X if axis is None else axis
        return self.op(eng, lambda e: e.tensor_reduce(out.ap, in_.ap, ax, op), r=[in_], w=[out])

    def asel(self, out, in_, pattern, cmp, fill, base=0, cm=0):
        return self.op("pool", lambda e: e.affine_select(out.ap, in_.ap, pattern=pattern, compare_op=cmp, fill=fill,
                                                         base=base, channel_multiplier=cm), r=[in_], w=[out])


D = 1024
KC = 8
DFF = 2816
FC = 22
EPS = 1e-6
EVEN_IN = 3080
ODD_IN = 2888
NEG = -30000.0


class DT:
    def __init__(self, ap, name):
        self.ap = ap
        self.name = name
        self.res = {}

    def v(self, key, ap):
        if key not in self.res:
            self.res[key] = Res("%s_%s" % (self.name, str(key)))
        return V(ap, self.res[key])


class Ctx:
    pass


def load_rows(S, C, dst, src_dt, key, ap, q="sp"):
    S.dma(q, dst, src_dt.v(key, ap), sres=dst.res)


def rms_to_hnT(S, C, xt, hnT_cols, ptr, hn, ssq, rstd):
    S.act(hn, xt, AF.Square, accum=ssq)
    S.act(rstd, ssq, AF.Sqrt, bias=C.eps, scale=1.0 / D)
    S.op("dve", lambda e: e.reciprocal(rstd.ap, rstd.ap), r=[rstd], w=[rstd])
    S.ts("pool", hn, xt, rstd, ALU.mult, 1.0, ALU.mult)
    for kc in range(KC):
        S.tr(ptr[:, kc * 128:(kc + 1) * 128], hn[:, kc * 128:(kc + 1) * 128], C.identb)
    S.copy("act", hnT_cols, V(ptr.ap.rearrange("p (k t) -> p k t", k=KC), ptr.res))


def rms_to_hnT_gen(S, C, xt, hnT_cols, ptr, hn, ssq, rstd):
    S.act(hn, xt, AF.Square, accum=ssq)
    yield
    S.act(rstd, ssq, AF.Sqrt, bias=C.eps, scale=1.0 / D)
    yield
    S.op("dve", lambda e: e.reciprocal(rstd.ap, rstd.ap), r=[rstd], w=[rstd])
    yield
    S.ts("pool", hn, xt, rstd, ALU.mult, 1.0, ALU.mult)
    yield
    for kc in range(KC):
        S.tr(ptr[:, kc * 128:(kc + 1) * 128], hn[:, kc * 128:(kc + 1) * 128], C.identb)
    yield
    S.copy("act", hnT_cols, V(ptr.ap.rearrange("p (k t) -> p k t", k=KC), ptr.res))
    yield


def load_weight_scaled(S, C, wsb, w_dram_ap, nrows_chunks, ncols, gcol, stg, colchunk, name):
    i = 0
    width = stg[0].ap.shape[1]
    colchunk = min(colchunk, width)
    for rc in range(nrows_chunks):
        for c0 in range(0, ncols, colchunk):
            c1 = min(ncols, c0 + colchunk)
            st = stg[i % len(stg)]
            S.dma(("sp", "act")[i % 2], st[:, 0:c1 - c0], V(w_dram_ap[rc * 128:(rc + 1) * 128, c0:c1], C.wres), sres=st.res)
            eng = ("dve", "act", "dve", "pool")[i % 4]
            if gcol is None:
                S.copy(eng, wsb[:, rc, c0:c1], st[:, 0:c1 - c0])
            elif eng == "act":
                S.act(wsb[:, rc, c0:c1], st[:, 0:c1 - c0], AF.Copy, scale=gcol[:, rc:rc + 1])
            else:
                S.ts(eng, wsb[:, rc, c0:c1], st[:, 0:c1 - c0], gcol[:, rc:rc + 1], ALU.mult, 1.0, ALU.mult)
            i += 1


def ffn_phase(S, C, layer, src, dst):
    NT = 512
    HW = 256
    NSUB = NT // 128
    S.begin_phase()
    wup = S.sb("wup", [128, KC, 2 * DFF], BF16)
    wdn = S.sb("wdn", [128, FC, D], BF16)
    gsb = S.sb("gsb", [128, KC])
    cw = S.sb("cw", [128, 2 * FC, 3])
    cb = S.sb("cb", [128, 2 * FC])
    stg = [S.sb("stg%d" % i, [128, 352]) for i in range(4)]
    S.dma("sp", gsb, V(C.ffn_g[layer], C.wres), sres=gsb.res)
    S.dma("sp", cw, V(C.ffn_cw[layer], C.wres), sres=cw.res)
    S.dma("sp", cb, V(C.ffn_cb[layer], C.wres), sres=cb.res)
    load_weight_scaled(S, C, wup, C.ffn_wup[layer], KC, 2 * DFF, gsb, stg, 352, "wup")
    load_weight_scaled(S, C, wdn, C.ffn_wdn[layer], FC, D, None, stg, 352, "wdn")

    xs = [S.sb("x%d" % i, [128, D]) for i in range(2)]
    xr = [S.sb("xr%d" % i, [128, D]) for i in range(2)]
    hn = [S.sb("hn%d" % i, [128, D], BF16) for i in range(2)]
    ssq = [S.sb("ssq%d" % i, [128, 1]) for i in range(2)]
    rstd = [S.sb("rstd%d" % i, [128, 1]) for i in range(2)]
    hT_ = S.sb("hnT", [128, KC, NT], BF16)
    hT = S.sb("hT", [128, FC, NT], BF16)
    hal = S.sb("hal", [128, 2 * FC, 2])
    rr = [S.sb("rr%d" % i, [128, HW + 2]) for i in range(4)]
    acc = [S.sb("acc%d" % i, [128, HW]) for i in range(4)]
    sg = [S.sb("sg%d" % i, [128, HW]) for i in range(2)]
    ptr = [S.ps("ptr%d" % i, [128, D], BF16) for i in range(2)]
    pup = [S.ps("pup%d" % i, [128, 512]) for i in range(4)]
    pdn = [S.ps("pdn%d" % i, [128, 512]) for i in range(2)]

    gsub = 0
    for b in range(C.NSEQ):
        for t0 in range(0, C.L, NT):
            rows = []
            for sub in range(NSUB):
                r0 = b * C.L + t0 + sub * 128
                xt = xs[gsub % 2]
                rows.append(r0)
                load_rows(S, C, xt, src, (r0 // 128), src.ap[r0:r0 + 128, :])
                j = gsub % 2
                rms_to_hnT(S, C, xt, hT_[:, :, sub * 128:(sub + 1) * 128], ptr[j], hn[j], ssq[j], rstd[j])
                gsub += 1
            items = [(c, hf, which) for c in range(FC) for hf in range(2) for which in range(2)]
            ni = len(items)

            def fA(n):
                c, hf, which = items[n]
                ch = c + which * FC
                ps = pup[(2 * c + which) % 4]
                if hf == 0:
                    for kc in range(KC):
                        S.mm(ps, wup[:, kc, ch * 128:(ch + 1) * 128], hT_[:, kc, :], start=(kc == 0), stop=(kc == KC - 1))
                psh = ps[:, hf * HW:(hf + 1) * HW]
                r = rr[n % 4]
                a = acc[n % 4]
                if t0 == 0 and hf == 0:
                    S.memset("pool", r[:, 0:2], 0.0)
                else:
                    S.copy("pool", r[:, 0:2], hal[:, ch, :])
                S.copy("act", r[:, 2:HW + 2], psh)
                S.act(a, psh, AF.Identity, bias=cb[:, ch:ch + 1], scale=cw[:, ch, 2:3])
                S.copy("pool", hal[:, ch, :], r[:, HW:HW + 2])

            def fB(n):
                c, hf, which = items[n]
                ch = c + which * FC
                r = rr[n % 4]
                a = acc[n % 4]
                S.stt("dve", a, r[:, 1:HW + 1], cw[:, ch, 1:2], a, ALU.mult, ALU.add)
                S.stt("dve", a, r[:, 0:HW], cw[:, ch, 0:1], a, ALU.mult, ALU.add)

            def fC(p):
                S.act(sg[p % 2], acc[(2 * p) % 4], AF.Silu)

            def fD(p):
                c, hf = p // 2, p % 2
                S.tt("dve", hT[:, c, hf * HW:(hf + 1) * HW], sg[p % 2], acc[(2 * p + 1) % 4], ALU.mult)

            npair = ni // 2
            for n in range(ni + 3):
                if n < ni:
                    fA(n)
                if 0 <= n - 1 < ni:
                    fB(n - 1)
                if n >= 2 and n % 2 == 0 and (n - 2) // 2 < npair:
                    fC((n - 2) // 2)
                if n >= 3 and n % 2 == 1 and (n - 3) // 2 < npair:
                    fD((n - 3) // 2)
            for sub in range(NSUB):
                r0 = rows[sub]
                xt = xr[sub % 2]
                load_rows(S, C, xt, src, (r0 // 128), src.ap[r0:r0 + 128, :])
                for nh in range(2):
                    ps2 = pdn[nh]
                    for fc in range(FC):
                        S.mm(ps2, hT[:, fc, sub * 128:(sub + 1) * 128], wdn[:, fc, nh * 512:(nh + 1) * 512],
                             start=(fc == 0), stop=(fc == FC - 1))
                    S.tt("dve", xt[:, nh * 512:(nh + 1) * 512], xt[:, nh * 512:(nh + 1) * 512], ps2, ALU.add)
                S.dma("sp", dst.v((r0 // 128), dst.ap[r0:r0 + 128, :]), xt, sres=xt.res)
    S.end_phase()


def flat(v):
    return V(v.ap.rearrange("p h j -> p (h j)"), v.res)


def v3(v, h=4):
    return V(v.ap.rearrange("p (h j) -> p h j", h=h), v.res)


def bc_last(v, n):
    H = v.ap.shape[1]
    return V(v.ap.unsqueeze(2).to_broadcast([128, H, n]), v.res)


def bc_mid(v, h):
    n = v.ap.shape[1]
    return V(v.ap.unsqueeze(1).to_broadcast([128, h, n]), v.res)


class Banks:
    def __init__(self, S, n=8):
        self.b = [S.ps("bank%d" % i, [128, 512]) for i in range(n)]
        self.i = 0

    def get(self):
        v = self.b[self.i % len(self.b)]
        self.i += 1
        return v


def bf(v):
    return V(v.ap.bitcast(BF16), v.res)


F32R = mybir.dt.float32r


def r32(v):
    return V(v.ap.bitcast(F32R), v.res)


def run_lanes(gens):
    gens = [g for g in gens if g is not None]
    while gens:
        for g in list(gens):
            try:
                next(g)
            except StopIteration:
                gens.remove(g)


def even_phase(S, C, j, src, dst):
    layer = 2 * j
    NT = 256
    NSUB = NT // 128
    H = 4
    S.begin_phase()
    PB = Banks(S)
    win = S.sb("win", [128, KC, EVEN_IN], BF16)
    wout = S.sb("wout", [128, KC, D], BF16)
    gsb = S.sb("gsb", [128, KC])
    stg = [S.sb("stg%d" % i, [128, 770]) for i in range(4)]
    S.dma("sp", gsb, V(C.mix_g[layer], C.wres), sres=gsb.res)
    load_weight_scaled(S, C, win, C.ev_win[j], KC, EVEN_IN, gsb, stg, 1540, "win")
    load_weight_scaled(S, C, wout, C.ev_wout[j], KC, D, None, stg, 1024, "wout")
    wTm = S.sb("wTm", [128, H, 128], BF16)
    wTf = S.sb("wTf", [128, H, 128])
    brow = S.sb("brow", [1, 512])
    cw = S.sb("gcw", [128, 12, 4])
    alog = S.sb("alog", [128, 4])
    dtb = S.sb("dtb", [128, 4])
    nega = S.sb("nega", [128, 4])
    gng = S.sb("gng", [128, 128])
    S.dma("sp", wTf, V(C.gm_wT[j], C.wres), sres=wTf.res)
    S.dma("sp", brow, V(C.gm_b[j], C.wres), sres=brow.res)
    S.dma("sp", cw, V(C.gdn_cw[j], C.wres), sres=cw.res)
    S.dma("sp", alog, V(C.gdn_alog[j].partition_broadcast(128), C.wres), sres=alog.res)
    S.dma("sp", dtb, V(C.gdn_dtb[j].partition_broadcast(128), C.wres), sres=dtb.res)
    S.dma("sp", gng, V(C.gdn_ng[j].partition_broadcast(128), C.wres), sres=gng.res)
    ones = S.sb("ones", [128, 128])
    tri = S.sb("tri", [128, 128])
    ntri = S.sb("ntri", [128, 128])
    strict = S.sb("strict", [128, 128])
    incl = S.sb("incl", [128, 128])
    S.memset("pool", ones, 1.0)
    S.asel(tri, ones, [[1, 128]], ALU.is_ge, 0.0, base=0, cm=-1)
    S.ts("pool", ntri, tri, -1.0, ALU.mult, 1.0, ALU.mult)
    S.asel(strict, ones, [[-1, 128]], ALU.is_ge, 0.0, base=-1, cm=1)
    S.asel(incl, ones, [[-1, 128]], ALU.is_ge, 0.0, base=0, cm=1)
    S.tt("pool", wTm, wTf, bc_mid(tri, H), ALU.mult)
    S.act(nega, alog, AF.Exp)
    S.ts("pool", nega, nega, -1.0, ALU.mult, 1.0, ALU.mult)

    xs = [S.sb("x%d" % i, [128, D]) for i in range(4)]
    hn = [S.sb("hn%d" % i, [128, D], BF16) for i in range(2)]
    ssq = [S.sb("ssq%d" % i, [128, 1]) for i in range(2)]
    rstd = [S.sb("rstd%d" % i, [128, 1]) for i in range(2)]
    hnT = [S.sb("hnT%d" % i, [128, KC, NT], BF16) for i in range(2)]
    uTs = [S.sb("uT%d" % i, [128, H, NT], BF16) for i in range(2)]
    yT = S.sb("yT", [128, KC, NT], BF16)
    qTs = [S.sb("qT%d" % i, [128, H, NT], BF16) for i in range(2)]
    kTs = [S.sb("kT%d" % i, [128, H, NT], BF16) for i in range(2)]
    vTs = [S.sb("vT%d" % i, [128, H, NT], BF16) for i in range(2)]
    XSL = [None, None]
    hal = S.sb("hal", [128, 12, 3])
    rr = [S.sb("rr%d" % i, [128, NT + 3]) for i in range(2)]
    ca = [S.sb("ca%d" % i, [128, NT]) for i in range(2)]
    qs = [S.sb("qs%d" % i, [128, NT]) for i in range(2)]
    sq = S.sb("sq", [128, NT])
    rn = S.sb("rn", [128, NT])
    Sst = S.sb("Sst", [128, H, 128])
    Sr = S.sb("Sr", [128, H, 128])

    def F(name, dt=F32):
        return S.sb(name, [128, H, 128], dt)

    ktok, vtok, vg, sqv, zs, G1, G2, E, nbm, usb, osq, on, gz, t1 = [
        F(n) for n in ("ktok", "vtok", "vg", "sqv", "zs", "G1", "G2", "E", "nbm", "usb", "osq", "on", "gz", "t1")]
    Lp0, Lp1, intra, intraT, U0, U1, TT, vb, kbg, wTs, qgT, kd, vnew = [
        F(n) for n in ("Lp0", "Lp1", "intra", "intraT", "U0", "U1", "TT", "vb", "kbg", "wTs", "qgT", "kd", "vnew")]
    vn = F("vn", BF16)
    onb = F("onb", BF16)
    sm = {n: S.sb(n, [128, 4]) for n in ("vsum", "vvar", "vrs", "beta", "nbeta", "xa", "xe", "xm", "sp", "g", "bk", "edl", "oss", "ors")}
    gcs = S.sb("gcs", [128, 8])
    egs = S.sb("egs", [128, 8])

    st = {"gsub": 0, "ich": 0}

    def proj_task(b, t0, k):
        hT_ = hnT[k]
        uT, qT, kT, vT = uTs[k], qTs[k], kTs[k], vTs[k]
        xsl = []
        for sub in range(NSUB):
            r0 = b * C.L + t0 + sub * 128
            xt = xs[st["gsub"] % 4]
            xsl.append((xt, r0))
            load_rows(S, C, xt, src, (r0 // 128), src.ap[r0:r0 + 128, :])
            jj = st["gsub"] % 2
            pt = PB.get()
            yield from rms_to_hnT_gen(S, C, xt, hT_[:, :, sub * 128:(sub + 1) * 128], bf(pt), hn[jj], ssq[jj], rstd[jj])
            st["gsub"] += 1
        for c in range(H):
            ps = PB.get()
            for kc in range(KC):
                S.mm(ps[:, 0:NT], win[:, kc, c * 128:(c + 1) * 128], hT_[:, kc, :], start=(kc == 0), stop=(kc == KC - 1))
            yield
            S.act(uT[:, c, :], ps[:, 0:NT], AF.Gelu_apprx_tanh)
            yield
        for c in range(12):
            ps = PB.get()
            for kc in range(KC):
                S.mm(ps[:, 0:NT], win[:, kc, 1024 + c * 128:1024 + (c + 1) * 128], hT_[:, kc, :], start=(kc == 0), stop=(kc == KC - 1))
            yield
            r = rr[st["ich"] % 2]
            a = ca[st["ich"] % 2]
            if t0 == 0:
                S.memset("pool", r[:, 0:3], 0.0)
            else:
                S.copy("pool", r[:, 0:3], hal[:, c, :])
            S.copy("act", r[:, 3:NT + 3], ps[:, 0:NT])
            S.act(a, ps[:, 0:NT], AF.Copy, scale=cw[:, c, 3:4])
            S.copy("pool", hal[:, c, :], r[:, NT:NT + 3])
            yield
            for tap in (2, 1, 0):
                S.stt("dve", a, r[:, tap:tap + NT], cw[:, c, tap:tap + 1], a, ALU.mult, ALU.add)
            hh = c % 4
            yield
            if c >= 8:
                S.act(vT[:, hh, :], a, AF.Silu)
            else:
                q_ = qs[st["ich"] % 2]
                S.act(q_, a, AF.Silu)
                S.act(sq, q_, AF.Square)
                yield
                pss = PB.get()
                S.mm(pss[:, 0:NT], ones, sq)
                yield
                S.act(rn, pss[:, 0:NT], AF.Sqrt, bias=C.eps, scale=1.0)
                yield
                S.op("dve", lambda e: e.reciprocal(rn.ap, rn.ap), r=[rn], w=[rn])
                if c < 4:
                    S.stt("dve", qT[:, hh, :], q_, 128.0 ** -0.5, rn, ALU.mult, ALU.mult)
                else:
                    S.tt("dve", kT[:, hh, :], q_, rn, ALU.mult)
            st["ich"] += 1
            yield
        XSL[k] = xsl
        yield

    def chunk_task(b, t0, k):
        hT_ = hnT[k]
        uT, qT, kT, vT = uTs[k], qTs[k], kTs[k], vTs[k]
        xsl = XSL[k]
        if t0 == 0:
            S.memset("pool", Sst, 0.0)
            S.copy("pool", r32(Sr), Sst)
        for sub in range(NSUB):
            cs = slice(sub * 128, (sub + 1) * 128)
            xt, r0 = xsl[sub]
            pv = PB.get()
            pz = PB.get()
            pba = PB.get()
            for kc in range(KC):
                S.mm(pv, hT_[:, kc, cs], win[:, kc, 512:1024], start=(kc == 0), stop=(kc == KC - 1))
            for kc in range(KC):
                S.mm(pz, hT_[:, kc, cs], win[:, kc, 2568:3080], start=(kc == 0), stop=(kc == KC - 1))
            for kc in range(KC):
                S.mm(pba[:, 0:8], hT_[:, kc, cs], win[:, kc, 2560:2568], start=(kc == 0), stop=(kc == KC - 1))
            yield
            S.act(flat(vg), pv, AF.Gelu_apprx_tanh)
            S.act(flat(zs), pz, AF.Silu)
            S.act(sm["beta"], pba[:, 0:4], AF.Sigmoid)
            S.tt("dve", sm["xa"], pba[:, 4:8], dtb, ALU.add)
            S.reduce("dve", sm["vsum"], vg, ALU.add)
            S.stt("dve", vg, bc_last(sm["vsum"], 128), -1.0 / 128, vg, ALU.mult, ALU.add)
            S.tt("pool", sqv, vg, vg, ALU.mult)
            S.reduce("dve", sm["vvar"], sqv, ALU.add)
            S.act(sm["vrs"], sm["vvar"], AF.Sqrt, bias=C.eps, scale=1.0 / 128)
            S.op("dve", lambda e: e.reciprocal(sm["vrs"].ap, sm["vrs"].ap), r=[sm["vrs"]], w=[sm["vrs"]])
            S.tt("dve", vn, vg, bc_last(sm["vrs"], 128), ALU.mult)
            yield
            pm = PB.get()
            for gI in range(H):
                S.mm(pm[:, gI * 128:(gI + 1) * 128], vn[:, gI, :], wTm[:, gI, :], start=True, stop=False)
                S.mm(pm[:, gI * 128:(gI + 1) * 128], ones[0:1, :], brow[0:1, gI * 128:(gI + 1) * 128], start=False, stop=True)
            yield
            S.tt("dve", yT[:, 0:4, cs], v3(pm), uT[:, :, cs], ALU.mult)
            yield
            S.ts("dve", sm["nbeta"], sm["beta"], -1.0, ALU.mult)
            S.ts("dve", sm["xm"], sm["xa"], 30.0, ALU.min)
            S.act(sm["xe"], sm["xm"], AF.Exp)
            S.act(sm["sp"], sm["xe"], AF.Ln, bias=1.0, scale=1.0)
            S.ts("dve", sm["xm"], sm["xa"], -30.0, ALU.add, 0.0, ALU.max)
            S.tt("dve", sm["sp"], sm["sp"], sm["xm"], ALU.add)
            S.tt("dve", sm["g"], sm["sp"], nega, ALU.mult)
            g = sm["g"]
            pg = PB.get()
            S.mm(pg[:, 0:4], tri, g)
            S.mm(pg[:, 4:8], ones, g)
            yield
            S.copy("dve", gcs, pg[:, 0:8])
            S.act(egs, gcs, AF.Exp)
            S.tt("dve", sm["edl"], gcs[:, 4:8], gcs[:, 0:4], ALU.subtract)
            S.act(sm["edl"], sm["edl"], AF.Exp)
            S.tt("dve", sm["bk"], sm["beta"], egs[:, 0:4], ALU.mult)
            yield
            pk = PB.get()
            for h in range(H):
                S.tr(bf(pk)[:, h * 128:(h + 1) * 128], kT[:, h, cs], C.identb)
            yield
            S.copy("act", flat(ktok), bf(pk)[:, 0:512])
            pk = PB.get()
            for h in range(H):
                S.tr(bf(pk)[:, h * 128:(h + 1) * 128], vT[:, h, cs], C.identb)
            yield
            S.copy("act", flat(vtok), bf(pk)[:, 0:512])
            yield
            S.tt("pool", G2, bc_mid(C.identf, H), bc_last(gcs[:, 0:4], 128), ALU.mult)
            pd = PB.get()
            S.mm(pd, ones, flat(G2))
            yield
            S.tt("dve", E, bc_last(gcs[:, 0:4], 128), v3(pd), ALU.subtract)
            S.ts("dve", flat(E), flat(E), 0.0, ALU.min)
            S.act(flat(E), flat(E), AF.Exp)
            yield
            pkk = PB.get()
            pqk = PB.get()
            for h in range(H):
                S.mm(pkk[:, h * 128:(h + 1) * 128], kT[:, h, cs], kT[:, h, cs])
            for h in range(H):
                S.mm(pqk[:, h * 128:(h + 1) * 128], qT[:, h, cs], kT[:, h, cs])
            yield
            S.tt("pool", nbm, bc_mid(strict, H), bc_last(sm["nbeta"], 128), ALU.mult)
            S.tt("dve", flat(t1), pkk, flat(E), ALU.mult)
            S.tt("pool", r32(Lp0), t1, nbm, ALU.mult)
            S.tt("dve", flat(osq), pqk, flat(E), ALU.mult)
            S.tt("pool", intra, osq, bc_mid(incl, H), ALU.mult)
            yield
            pu = PB.get()
            for h in range(H):
                S.tr(pu[:, h * 128:(h + 1) * 128], Lp0[:, h, :], C.identf)
            yield
            S.copy("act", r32(flat(U0)), pu)
            S.tt("dve", r32(TT), v3(pu), bc_mid(C.identf, H), ALU.add)
            pi = PB.get()
            for h in range(H):
                S.tr(pi[:, h * 128:(h + 1) * 128], intra[:, h, :], C.identf)
            yield
            S.copy("act", r32(flat(intraT)), pi)
            yield
            Us = [U0, U1]
            Ls = [Lp0, Lp1]
            for k in range(1, 7):
                Uo, Un = Us[(k - 1) % 2], Us[k % 2]
                Lo, Ln_ = Ls[(k - 1) % 2], Ls[k % 2]
                if k <= 5:
                    p1 = PB.get()
                    for h in range(H):
                        S.mm(p1[:, h * 128:(h + 1) * 128], r32(Lo[:, h, :]), r32(Uo[:, h, :]))
                p2 = PB.get()
                for h in range(H):
                    S.mm(p2[:, h * 128:(h + 1) * 128], r32(Uo[:, h, :]), r32(Lo[:, h, :]))
                yield
                if k <= 5:
                    S.copy("act", r32(flat(Un)), p1)
                S.copy("dve", r32(flat(Ln_)), p2)
                yield
                p3 = PB.get()
                for h in range(H):
                    S.mm(p3[:, h * 128:(h + 1) * 128], r32(Ln_[:, h, :]), r32(TT[:, h, :]))
                yield
                S.tt("dve", r32(flat(TT)), flat(TT), p3, ALU.add)
                yield
            yield
            S.tt("pool", r32(vb), vtok, bc_last(sm["beta"], 128), ALU.mult)
            S.tt("pool", r32(kbg), ktok, bc_last(sm["bk"], 128), ALU.mult)
            S.tt("pool", r32(kd), ktok, bc_last(sm["edl"], 128), ALU.mult)
            pU = PB.get()
            for h in range(H):
                S.mm(pU[:, h * 128:(h + 1) * 128], r32(TT[:, h, :]), r32(vb[:, h, :]))
            yield
            S.copy("act", flat(usb), pU)
            pW = PB.get()
            for h in range(H):
                S.mm(pW[:, h * 128:(h + 1) * 128], r32(kbg[:, h, :]), r32(TT[:, h, :]))
            yield
            S.copy("act", r32(flat(wTs)), pW)
            yield
            S.tt("pool", G1, bc_mid(C.identf, H), bc_last(egs[:, 0:4], 128), ALU.mult)
            pe_ = PB.get()
            S.mm(pe_, ones, flat(G1))
            yield
            S.tt("dve", r32(qgT), qT[:, :, cs], v3(pe_), ALU.mult)
            yield
            pws = PB.get()
            for h in range(H):
                S.mm(pws[:, h * 128:(h + 1) * 128], r32(wTs[:, h, :]), r32(Sr[:, h, :]))
            yield
            S.tt("dve", r32(flat(vnew)), flat(usb), pws, ALU.subtract)
            po = PB.get()
            for h in range(H):
                S.mm(po[:, h * 128:(h + 1) * 128], r32(qgT[:, h, :]), r32(Sr[:, h, :]), start=True, stop=False)
                S.mm(po[:, h * 128:(h + 1) * 128], r32(intraT[:, h, :]), r32(vnew[:, h, :]), start=False, stop=True)
            pS = PB.get()
            for h in range(H):
                S.mm(pS[:, h * 128:(h + 1) * 128], r32(kd[:, h, :]), r32(vnew[:, h, :]))
            yield
            S.tt("dve", Sst, Sst, bc_last(egs[:, 4:8], 128), ALU.mult)
            S.tt("dve", flat(Sst), flat(Sst), pS, ALU.add)
            S.copy("act", r32(Sr), Sst)
            yield
            yield
            S.act(flat(osq), po, AF.Square)
            S.reduce("dve", sm["oss"], osq, ALU.add)
            S.act(sm["ors"], sm["oss"], AF.Sqrt, bias=C.eps, scale=1.0 / 128)
            S.op("dve", lambda e: e.reciprocal(sm["ors"].ap, sm["ors"].ap), r=[sm["ors"]], w=[sm["ors"]])
            S.tt("pool", gz, zs, bc_mid(gng, H), ALU.mult)
            S.tt("dve", on, v3(po), bc_last(sm["ors"], 128), ALU.mult)
            S.tt("dve", onb, on, gz, ALU.mult)
            pt2 = PB.get()
            for h in range(H):
                S.tr(bf(pt2)[:, h * 128:(h + 1) * 128], onb[:, h, :], C.identb)
            yield
            S.copy("act", yT[:, 4:8, cs], v3(bf(pt2)[:, 0:512]))
            yield
            for nh in range(2):
                pso = PB.get()
                for kc in range(KC):
                    S.mm(pso, yT[:, kc, cs], wout[:, kc, nh * 512:(nh + 1) * 512], start=(kc == 0), stop=(kc == KC - 1))
                S.tt("dve", xt[:, nh * 512:(nh + 1) * 512], xt[:, nh * 512:(nh + 1) * 512], pso, ALU.add)
            S.dma("sp", dst.v((r0 // 128), dst.ap[r0:r0 + 128, :]), xt, sres=xt.res)

    tiles = [(b, t0) for b in range(C.NSEQ) for t0 in range(0, C.L, NT)]
    prev = None
    for i, (b, t0) in enumerate(tiles):
        run_lanes([proj_task(b, t0, i % 2), chunk_task(*prev) if prev is not None else None])
        prev = (b, t0, i % 2)
    run_lanes([chunk_task(*prev)])
    S.end_phase()


OD_EXT = 3016
NQK = 18


def odd_phase(S, C, j, src, dst):
    odd_proj_phase(S, C, j, src)
    odd_attn_phase(S, C, j, src, dst)


def odd_proj_phase(S, C, j, src):
    layer = 2 * j + 1
    NT = 512
    NSUB = NT // 128
    S.begin_phase()
    PB = Banks(S)
    win = S.sb("win", [128, KC, OD_EXT], BF16)
    gsb = S.sb("gsb", [128, KC])
    stg = [S.sb("stg%d" % i, [128, 754]) for i in range(4)]
    S.dma("sp", gsb, V(C.mix_g[layer], C.wres), sres=gsb.res)
    load_weight_scaled(S, C, win, C.od_win[j], KC, OD_EXT, gsb, stg, 1508, "win")
    gn = S.sb("gn", [128, 4])
    S.dma("sp", gn, V(C.od_gn[j], C.wres), sres=gn.res)
    S.ts("pool", gn[:, 0:1], gn[:, 0:1], 64.0 ** -0.5, ALU.mult, 1.0, ALU.mult)
    S.ts("pool", gn[:, 2:3], gn[:, 2:3], 128.0 ** -0.5, ALU.mult, 1.0, ALU.mult)
    ones = S.sb("ones", [128, 128])
    bd64 = S.sb("bd64", [128, 128])
    S.memset("pool", ones, 1.0)
    S.memset("pool", bd64, 0.0)
    S.memset("pool", bd64[0:64, 0:64], 1.0)
    S.memset("pool", bd64[64:128, 64:128], 1.0)

    xs = [S.sb("x%d" % i, [128, D]) for i in range(2)]
    hn = [S.sb("hn%d" % i, [128, D], BF16) for i in range(2)]
    ssq = [S.sb("ssq%d" % i, [128, 1]) for i in range(2)]
    rstd = [S.sb("rstd%d" % i, [128, 1]) for i in range(2)]
    hnT = [S.sb("hnT%d" % i, [128, KC, NT], BF16) for i in range(2)]
    oT = [S.sb("oT%d" % i, [128, NQK, NT], BF16) for i in range(2)]
    sqb = [S.sb("sqb%d" % i, [128, NT]) for i in range(2)]
    rn = [S.sb("rn%d" % i, [128, NT]) for i in range(2)]
    tokb = [S.sb("tokb%d" % i, [128, 640], BF16) for i in range(2)]
    iwt = [S.sb("iwt%d" % i, [128, 8]) for i in range(2)]

    chunks = []
    for c in range(4):
        chunks.append((c * 128, "n64", 0))
    for c in range(4):
        chunks.append((512 + c * 128, "n64", 1))
    for c in range(4):
        chunks.append((1536 + c * 128, "n128", 2))
    chunks.append((2048, "n128", 3))
    for c in range(4):
        chunks.append((2304 + c * 128, "scale", None))
    chunks.append((2888, "copy", None))

    gsub = 0
    ist = 0
    inorm = 0
    for b in range(C.NSEQ):
        for t0 in range(0, C.L, NT):
            hT_ = hnT[ist % 2]
            o_ = oT[ist % 2]
            for sub in range(NSUB):
                r0 = b * C.L + t0 + sub * 128
                xt = xs[gsub % 2]
                load_rows(S, C, xt, src, (r0 // 128), src.ap[r0:r0 + 128, :])
                jj = gsub % 2
                pt = PB.get()
                rms_to_hnT(S, C, xt, hT_[:, :, sub * 128:(sub + 1) * 128], bf(pt), hn[jj], ssq[jj], rstd[jj])
                gsub += 1
            for ci, (c0, kind, gi) in enumerate(chunks):
                ps = PB.get()
                for kc in range(KC):
                    S.mm(ps[:, 0:NT], win[:, kc, c0:c0 + 128], hT_[:, kc, :], start=(kc == 0), stop=(kc == KC - 1))
                if kind == "scale":
                    S.act(o_[:, ci, :], ps[:, 0:NT], AF.Copy, scale=0.125)
                elif kind == "copy":
                    S.copy("act", o_[:, ci, :], ps[:, 0:NT])
                else:
                    sq_ = sqb[inorm % 2]
                    rn_ = rn[inorm % 2]
                    inorm += 1
                    S.act(sq_, ps[:, 0:NT], AF.Square)
                    pss = PB.get()
                    S.mm(pss[:, 0:NT], bd64 if kind == "n64" else ones, sq_)
                    dim = 64.0 if kind == "n64" else 128.0
                    S.act(rn_, pss[:, 0:NT], AF.Sqrt, bias=C.eps, scale=1.0 / dim)
                    S.op("dve", lambda e, rn_=rn_: e.reciprocal(rn_.ap, rn_.ap), r=[rn_], w=[rn_])
                    S.stt("dve", o_[:, ci, :], ps[:, 0:NT], gn[:, gi:gi + 1], rn_, ALU.mult, ALU.mult)
            for sub in range(NSUB):
                cs = slice(sub * 128, (sub + 1) * 128)
                r0 = t0 + sub * 128
                tb = tokb[sub % 2]
                iw_ = iwt[sub % 2]
                pv = PB.get()
                for kc in range(KC):
                    S.mm(pv, hT_[:, kc, cs], win[:, kc, 1024:1536], start=(kc == 0), stop=(kc == KC - 1))
                p2 = PB.get()
                for kc in range(KC):
                    S.mm(p2[:, 0:128], hT_[:, kc, cs], win[:, kc, 2176:2304], start=(kc == 0), stop=(kc == KC - 1))
                for kc in range(KC):
                    S.mm(p2[:, 128:136], hT_[:, kc, cs], win[:, kc, 2880:2888], start=(kc == 0), stop=(kc == KC - 1))
                S.copy("act", tb[:, 0:512], pv)
                S.copy("act", tb[:, 512:640], p2[:, 0:128])
                S.ts("dve", iw_, p2[:, 128:136], 8.0 ** -0.5, ALU.mult)
                S.dma("sp", C.vtok.v((b, r0 // 128), C.vtok.ap[b, r0:r0 + 128, :]), tb, sres=tb.res)
                S.dma("sp", C.iwd.v((b, r0 // 128), C.iwd.ap[b, r0:r0 + 128, :]), iw_, sres=iw_.res)
            S.dma("sp", C.qkT.v((b, t0 // NT), C.qkT.ap[b, :, :, t0:t0 + NT].rearrange("c p t -> p c t")), o_, sres=o_.res)
            ist += 1
    S.end_phase()


def odd_attn_phase(S, C, j, src, dst):
    layer = 2 * j + 1
    lambda_init = 0.8 - 0.6 * math.exp(-0.3 * layer)
    L = C.L
    NKB = L // 128
    NQ = 512
    NQS = NQ // 128
    TOPK = min(256, L // 4)
    NIT = 18
    S.begin_phase()
    PB1 = Banks(S, 2)
    PB2 = Banks(S, 1)
    PB3 = Banks(S, 1)
    DACC = [S.ps("dacc%d" % i, [128, 512]) for i in range(2)]
    FACC = [S.ps("facc%d" % i, [128, 512]) for i in range(2)]
    wout = S.sb("wout", [128, KC, D], BF16)
    stg = [S.sb("stg%d" % i, [128, 1024]) for i in range(2)]
    load_weight_scaled(S, C, wout, C.od_wout[j], KC, D, None, stg, 1024, "wout")
    S.barrier()
    R = [V(stg[1].ap[:, 0:512], Res("R0")), V(stg[1].ap[:, 512:1024], Res("R1")), V(stg[0].ap[:, 512:1024], Res("R2")),
         V(stg[0].ap[:, 0:512], Res("R3"))]
    rb = S.sb("rb", [128, 8, 2, 128])
    t31 = S.sb("t31", [128, 8])
    cmT = S.sb("cmT", [128, 128])
    dmask = S.sb("dmask", [128, 128])
    zer = S.sb("zer", [128, 128])
    S.dma("sp", rb, V(C.rb_near, C.wres), sres=rb.res)
    S.dma("sp", t31, V(C.rb_t31.partition_broadcast(128), C.wres), sres=t31.res)
    S.memset("pool", zer, 0.0)
    S.asel(cmT, zer, [[1, 128]], ALU.is_ge, NEG, base=0, cm=-1)
    S.asel(dmask, zer, [[-1, 128]], ALU.is_ge, -1e30, base=0, cm=1)
    rbv = V(rb.ap.rearrange("p h r q -> p h (r q)"), rb.res)
    S.tt("dve", rbv, rbv, bc_last(t31, 256), ALU.subtract)
    S.tt("dve", rb[:, :, 1, :], rb[:, :, 1, :], bc_mid(cmT, 8), ALU.add)
    lf = S.sb("lf", [128, 256])
    lj = S.sb("lj", [128, 64])
    lam = {n: S.sb(n, [128, 1]) for n in ("s01", "s23", "nlam")}
    S.dma("sp", lf, V(C.od_lam[j].partition_broadcast(128), C.wres), sres=lf.res)
    S.memset("dve", lam["s01"], 0.0)
    S.memset("dve", lam["s23"], 0.0)
    S.stt("dve", lj, lf[:, 0:64], 1.0, lf[:, 64:128], ALU.mult, ALU.mult, accum=lam["s01"])
    S.stt("dve", lj, lf[:, 128:192], 1.0, lf[:, 192:256], ALU.mult, ALU.mult, accum=lam["s23"])
    S.act(lam["s01"], lam["s01"], AF.Exp)
    S.act(lam["s23"], lam["s23"], AF.Exp)
    S.tt("dve", lam["nlam"], lam["s23"], lam["s01"], ALU.subtract)
    S.ts("dve", lam["nlam"], lam["nlam"], -lambda_init, ALU.add)
    gsub_ = S.sb("gsubn", [128, 128])
    S.dma("sp", gsub_, V(C.od_gsub[j].partition_broadcast(128), C.wres), sres=gsub_.res)
    S.ts("pool", gsub_, gsub_, 1.0 - lambda_init, ALU.mult, 1.0, ALU.mult)

    dkT = S.sb("dkT", [128, 4, L], BF16)
    skT = S.sb("skT", [128, L], BF16)
    ikT = S.sb("ikT", [128, L], BF16)
    dvA = S.sb("dvA", [128, NKB, 4, 130], BF16)
    svA = S.sb("svA", [128, NKB, 130], BF16)
    S.memset("pool", dvA[:, :, :, 128:130], 1.0)
    S.memset("pool", svA[:, :, 128:130], 1.0)
    dqT = S.sb("dqT", [128, 4, NQ], BF16)
    sqT = S.sb("sqT", [128, 4, NQ], BF16)
    iqT = S.sb("iqT", [128, 4, NQ], BF16)
    iw = S.sb("iw", [128, NQS, 8])
    idx = S.sb("idx", [128, L])
    M = S.sb("M", [128, L], BF16)
    M2 = S.sb("M2", [128, L], BF16)
    MT = S.sb("MT", [128, NKB, 128], BF16)
    PTa = [S.sb("PTa%d" % i, [128, 512], BF16) for i in range(3)]
    PTb = [S.sb("PTb%d" % i, [128, 512], BF16) for i in range(2)]
    xs = [S.sb("x%d" % i, [128, D]) for i in range(1)]
    ytd = [[S.sb("ytd%d_%d" % (a, i), [128, 4, 128], BF16) for i in range(NQS)] for a in range(2)]
    ytf = [[S.sb("ytf%d_%d" % (a, i), [128, 4, 128], BF16) for i in range(NQS)] for a in range(2)]
    yT = S.sb("yT", [128, 8, 128], BF16)
    of0 = S.sb("of0", [128, NQS, 128])
    of = S.sb("of", [128, 128])
    osq = S.sb("osq", [128, 128])
    sc = {n: S.sb(n, [128, 1]) for n in ("rmax", "w0", "lo", "nlo", "nmid", "cnt", "gw", "rec")}
    sd = {n: S.sb(n, [128, 1]) for n in ("rec", "oss", "ors")}
    sc2 = {n: S.sb(n + "2", [128, 1]) for n in ("rec",)}
    ctr = {"R": 0, "Pa": 0, "Pb": 0, "T": 0, "x": 0}

    NQB = L // 128
    NST = (L + NQ - 1) // NQ
    Ms = [M, M2]
    prog = {"s1": 0, "s2": 0, "df": 0}

    def s1_lane(b):
        for qb in range(NQB):
            s0 = (qb // NQS) * NQ
            qs = qb % NQS
            nqs = min(NQS, (L - s0) // 128)
            nq = nqs * 128
            if qs == 0:
                S.dma("sp", iqT[:, :, 0:nq], C.qkT.v((b, "iq", s0), C.qkT.ap[b, 13:17, :, s0:s0 + nq].rearrange("c p t -> p c t")), sres=iqT.res)
                S.dma("sp", iw[:, 0:nqs, :], C.iwd.v((b, "iw", s0), C.iwd.ap[b, s0:s0 + nq, :].rearrange("(s p) e -> p s e", p=128)), sres=iw.res)
                yield
            nk = (qb + 1) * 128
            qc = slice(qs * 128, (qs + 1) * 128)
            if nk > TOPK:
                while prog["s2"] < qb - 1:
                    yield
                Mq = Ms[qb % 2]
                pend = []

                def fma(p):
                    r_, k0, wd, h = p
                    if h == 0:
                        S.ts("dve", idx[:, k0:k0 + wd], r_[:, 0:wd], iw[:, qs, 0:1], ALU.mult)
                    else:
                        S.stt("dve", idx[:, k0:k0 + wd], r_[:, 0:wd], iw[:, qs, h:h + 1], idx[:, k0:k0 + wd], ALU.mult, ALU.add)

                for k0 in range(0, nk, 512):
                    wd = min(512, nk - k0)
                    for h0 in range(0, 8, 2):
                        cur = []
                        pss_ = []
                        for h in (h0, h0 + 1):
                            pr = slice((h % 2) * 64, (h % 2) * 64 + 64)
                            ps = PB1.get()
                            S.mm(ps[:, 0:wd], iqT[pr, h // 2, qc], ikT[pr, k0:k0 + wd])
                            pss_.append(ps)
                        for ps, h in zip(pss_, (h0, h0 + 1)):
                            r_ = R[ctr["R"] % len(R)]
                            ctr["R"] += 1
                            S.act(r_[:, 0:wd], ps[:, 0:wd], AF.Relu)
                            cur.append((r_, k0, wd, h))
                        for p in pend:
                            fma(p)
                        pend = cur
                        yield
                for p in pend:
                    fma(p)
                S.reduce("dve", sc["rmax"], idx[:, 0:nk], ALU.max)
                S.reduce("dve", sc["lo"], idx[:, 0:nk], ALU.min)
                S.tt("dve", sc["w0"], sc["rmax"], sc["lo"], ALU.subtract)
                S.ts("dve", sc["nlo"], sc["lo"], -1.0, ALU.mult)
                S.tt("dve", idx[:, nk - 128:nk], idx[:, nk - 128:nk], dmask, ALU.add)
                yield
                thr = 2.0 * TOPK - nk - 0.5
                for it in range(NIT):
                    hw = 2.0 ** -(it + 1)
                    S.stt("dve", sc["nmid"], sc["w0"], -hw, sc["nlo"], ALU.mult, ALU.add)
                    S.act(Mq[:, 0:nk], idx[:, 0:nk], AF.Sign, bias=sc["nmid"], scale=1.0, accum=sc["cnt"])
                    yield
                    S.stt("dve", sc["gw"], sc["cnt"], thr, sc["w0"], ALU.is_ge, ALU.mult)
                    S.stt("dve", sc["nlo"], sc["gw"], -hw, sc["nlo"], ALU.mult, ALU.add)
                S.ts("dve", sc["lo"], sc["nlo"], -1.0, ALU.mult)
                S.ts("dve", Mq[:, 0:nk], idx[:, 0:nk], sc["lo"], ALU.is_ge)
            prog["s1"] = qb + 1
            yield

    def s2_lane(b):
        for qb in range(NQB):
            st_i = qb // NQS
            s0 = st_i * NQ
            qs = qb % NQS
            nqs = min(NQS, (L - s0) // 128)
            nq = nqs * 128
            if qs == 0:
                while prog["df"] < st_i - 1:
                    yield
                S.dma("sp", sqT[:, :, 0:nq], C.qkT.v((b, "sq", s0), C.qkT.ap[b, 8:12, :, s0:s0 + nq].rearrange("c p t -> p c t")), sres=sqT.res)
                yield
            while prog["s1"] < qb + 1:
                yield
            yb = ytd[st_i % 2]
            nk = (qb + 1) * 128
            qc = slice(qs * 128, (qs + 1) * 128)
            use_topk = nk > TOPK
            Mq = Ms[qb % 2]
            if use_topk:
                for g0 in range(0, qb + 1, 8):
                    ng = min(8, qb + 1 - g0)
                    pm = PB2.get()
                    for i in range(ng):
                        S.tr(bf(pm)[:, i * 128:(i + 1) * 128], Mq[:, (g0 + i) * 128:(g0 + i + 1) * 128], C.identb)
                    S.copy("act", MT[:, g0:g0 + ng, :], v3(bf(pm)[:, 0:ng * 128], ng))
                    yield
            for a in DACC:
                S.memset("dve", a[:, 0:130], 0.0)
                S.memset("dve", a[:, 256:386], 0.0)
            nkb_ = qb + 1

            def st1(kb):
                kc_ = slice(kb * 128, (kb + 1) * 128)
                ps = PB2.get()
                S.mm(ps, skT[:, kc_], sqT[:, :, qc])
                pt_ = PTa[kb % 3]
                rel = kb - (qb - 1)
                if rel >= 0:
                    S.tt("dve", v3(ps), v3(ps), rb[:, 4:8, rel, :], ALU.add)
                S.act(pt_, ps, AF.Exp)

            def st2(kb):
                if use_topk:
                    pt_ = PTa[kb % 3]
                    S.tt("dve", v3(pt_), v3(pt_), bc_mid(MT[:, kb, :], 4), ALU.mult)

            def st3(kb):
                pt_ = PTa[kb % 3]
                for h in range(4):
                    S.op("pe", lambda e, h=h, pt_=pt_, kb=kb, qb=qb: e.matmul(DACC[h // 2].ap[:, (h % 2) * 256:(h % 2) * 256 + 129], pt_.ap[:, h * 128:(h + 1) * 128],
                                                                             svA.ap[:, kb, 0:129], start=False, stop=(kb == qb), skip_group_check=True),
                         r=[pt_, svA], w=[DACC[h // 2]], inc=(h == 3))

            for t in range(nkb_ + 2):
                if 0 <= t - 2 < nkb_:
                    st3(t - 2)
                if 0 <= t - 1 < nkb_:
                    st2(t - 1)
                if t < nkb_:
                    st1(t)
                yield
            for h in range(4):
                a = DACC[h // 2]
                c0 = (h % 2) * 256
                S.op("dve", lambda e, a=a, c0=c0: e.reciprocal(sc2["rec"].ap, a.ap[:, c0 + 128:c0 + 129]), r=[a], w=[sc2["rec"]])
                S.ts("dve", yb[qs][:, h, :], a[:, c0:c0 + 128], sc2["rec"], ALU.mult)
            prog["s2"] = qb + 1
            yield

    def df_lane(b):
        for st_i in range(NST):
            s0 = st_i * NQ
            sblk = s0 // 128
            nqs = min(NQS, (L - s0) // 128)
            nq = nqs * 128
            yf = ytf[st_i % 2]
            yd = ytd[st_i % 2]
            S.dma("sp", dqT[:, :, 0:nq], C.qkT.v((b, "dq", s0), C.qkT.ap[b, 0:4, :, s0:s0 + nq].rearrange("c p t -> p c t")), sres=dqT.res)
            yield
            last_kb = sblk + nqs - 1
            for h in range(4):
                for m in range(2):
                    pr = slice(m * 64, m * 64 + 64)
                    for a in FACC:
                        S.memset("dve", a[:, 0:130], 0.0)
                        S.memset("dve", a[:, 256:386], 0.0)
                    def d1(kb):
                        kc_ = slice(kb * 128, (kb + 1) * 128)
                        qlo = max(0, kb - sblk)
                        ncol = (nqs - qlo) * 128
                        ps = PB3.get()
                        S.mm(ps[:, 0:ncol], dkT[pr, h, kc_], dqT[pr, h, qlo * 128:nqs * 128])
                        for qs in (kb - sblk, kb - sblk + 1):
                            if 0 <= qs < nqs:
                                rel = kb - (sblk + qs - 1)
                                cc = slice((qs - qlo) * 128, (qs - qlo + 1) * 128)
                                S.tt("dve", ps[:, cc], ps[:, cc], rb[:, h, rel, :], ALU.add)
                        pt_ = PTb[kb % 2]
                        S.act(pt_[:, 0:ncol], ps[:, 0:ncol], AF.Exp)

                    def d2(kb):
                        qlo = max(0, kb - sblk)
                        pt_ = PTb[kb % 2]
                        for qs in range(qlo, nqs):
                            cc = slice((qs - qlo) * 128, (qs - qlo + 1) * 128)
                            S.op("pe", lambda e, qs=qs, cc=cc, pt_=pt_, kb=kb, h=h, sblk=sblk: e.matmul(
                                FACC[qs // 2].ap[:, (qs % 2) * 256:(qs % 2) * 256 + 129], pt_.ap[:, cc], dvA.ap[:, kb, h, 0:129],
                                start=False, stop=(kb == sblk + qs), skip_group_check=True), r=[pt_, dvA], w=[FACC[qs // 2]],
                                inc=(qs == nqs - 1))

                    for t in range(last_kb + 2):
                        if 0 <= t - 1 <= last_kb:
                            d2(t - 1)
                        if t <= last_kb:
                            d1(t)
                        yield
                    for qs in range(nqs):
                        a = FACC[qs // 2]
                        c0 = (qs % 2) * 256
                        S.op("dve", lambda e, a=a, c0=c0: e.reciprocal(sd["rec"].ap, a.ap[:, c0 + 128:c0 + 129]), r=[a], w=[sd["rec"]])
                        if m == 0:
                            S.ts("dve", of0[:, qs, :], a[:, c0:c0 + 128], sd["rec"], ALU.mult)
                        else:
                            S.tt("dve", sd["rec"], sd["rec"], lam["nlam"], ALU.mult)
                            S.stt("dve", of, a[:, c0:c0 + 128], sd["rec"], of0[:, qs, :], ALU.mult, ALU.add)
                            S.act(osq, of, AF.Square, accum=sd["oss"])
                            S.act(sd["ors"], sd["oss"], AF.Sqrt, bias=C.eps, scale=1.0 / 128)
                            S.op("dve", lambda e: e.reciprocal(sd["ors"].ap, sd["ors"].ap), r=[sd["ors"]], w=[sd["ors"]])
                            S.stt("dve", yf[qs][:, h, :], of, sd["ors"], gsub_, ALU.mult, ALU.mult)
                    yield
            while prog["s2"] < min(NQB, (st_i + 1) * NQS):
                yield
            for qs in range(nqs):
                r0 = b * L + s0 + qs * 128
                xt = xs[0]
                load_rows(S, C, xt, src, (r0 // 128), src.ap[r0:r0 + 128, :])
                pt2 = PB3.get()
                for c in range(4):
                    S.tr(bf(pt2)[:, c * 128:(c + 1) * 128], yf[qs][:, c, :], C.identb)
                for c in range(4):
                    S.tr(bf(pt2)[:, (4 + c) * 128:(5 + c) * 128], yd[qs][:, c, :], C.identb)
                yield
                S.copy("act", flat(yT), bf(pt2))
                yield
                for nh in range(2):
                    pso = PB3.get()
                    for kc in range(KC):
                        S.mm(pso, yT[:, kc, :], wout[:, kc, nh * 512:(nh + 1) * 512], start=(kc == 0), stop=(kc == KC - 1))
                    yield
                    S.tt("dve", xt[:, nh * 512:(nh + 1) * 512], xt[:, nh * 512:(nh + 1) * 512], pso, ALU.add)
                S.dma("sp", dst.v((r0 // 128), dst.ap[r0:r0 + 128, :]), xt, sres=xt.res)
                yield
            prog["df"] = st_i + 1
            yield

    for b in range(C.NSEQ):
        S.dma("sp", dkT, C.qkT.v((b, "dk"), C.qkT.ap[b, 4:8, :, :].rearrange("c p t -> p c t")), sres=dkT.res)
        S.dma("sp", skT, C.qkT.v((b, "sk"), C.qkT.ap[b, 12, :, :]), sres=skT.res)
        S.dma("sp", ikT, C.qkT.v((b, "ik"), C.qkT.ap[b, 17, :, :]), sres=ikT.res)
        for kb in range(NKB):
            S.dma("sp", dvA[:, kb, :, 0:128], C.vtok.v((b, "dv", kb), C.vtok.ap[b, kb * 128:(kb + 1) * 128, 0:512].rearrange("p (h d) -> p h d", h=4)), sres=dvA.res)
        S.dma("sp", svA[:, :, 0:128], C.vtok.v((b, "sv"), C.vtok.ap[b, :, 512:640].rearrange("(k p) d -> p k d", p=128)), sres=svA.res)
        prog["s1"] = prog["s2"] = prog["df"] = 0
        run_lanes([s1_lane(b), s2_lane(b), df_lane(b)])
    S.end_phase()


W_SPECS = {
    "ffn_g": [4, 128, KC],
    "ffn_cw": [4, 128, 2 * FC, 3],
    "ffn_cb": [4, 128, 2 * FC],
    "ffn_wup": [4, D, 2 * DFF],
    "ffn_wdn": [4, DFF, D],
    "mix_g": [4, 128, KC],
    "ev_win": [2, D, EVEN_IN],
    "ev_wout": [2, D, D],
    "gm_wT": [2, 128, 4, 128],
    "gm_b": [2, 1, 512],
    "gdn_cw": [2, 128, 12, 4],
    "gdn_alog": [2, 1, 4],
    "gdn_dtb": [2, 1, 4],
    "gdn_ng": [2, 1, 128],
    "od_win": [2, D, OD_EXT],
    "od_wout": [2, D, D],
    "od_gn": [2, 128, 4],
    "od_lam": [2, 1, 256],
    "od_gsub": [2, 1, 128],
    "rb_near": [128, 8, 2, 128],
    "rb_t31": [1, 8],
}


def _rel_bucket_np(dist):
    n = np.maximum(dist, 0)
    nf = np.maximum(n, 16).astype(np.float32)
    far = 16 + (np.log(nf / np.float32(16)) / np.float32(math.log(128 / 16)) * np.float32(16)).astype(np.int32)
    return np.where(n < 16, n, np.minimum(far, 31))


def prep_weights(inp):
    f = lambda a: np.ascontiguousarray(np.asarray(a, dtype=np.float32))
    w = {}
    w["ffn_g"] = f(inp["ffn_norm_g"].reshape(4, KC, 128).transpose(0, 2, 1))
    w["ffn_cw"] = f(inp["ffn_conv_w"].reshape(4, 3, 2 * FC, 128).transpose(0, 3, 2, 1))
    w["ffn_cb"] = f(inp["ffn_conv_b"].reshape(4, 2 * FC, 128).transpose(0, 2, 1))
    w["ffn_wup"] = f(inp["ffn_w_up"])
    w["ffn_wdn"] = f(inp["ffn_w_down"])
    w["mix_g"] = f(inp["mix_norm_g"].reshape(4, KC, 128).transpose(0, 2, 1))
    w["ev_win"] = f(inp["ev_w_in"])
    w["ev_wout"] = f(inp["ev_w_out"])
    w["gm_wT"] = f(inp["gmlp_w_s"].transpose(0, 3, 1, 2))
    w["gm_b"] = f(inp["gmlp_b_s"].reshape(2, 1, 512))
    w["gdn_cw"] = f(inp["gdn_conv_w"].reshape(2, 4, 12, 128).transpose(0, 3, 2, 1))
    w["gdn_alog"] = f(inp["gdn_a_log"].reshape(2, 1, 4))
    w["gdn_dtb"] = f(inp["gdn_dt_bias"].reshape(2, 1, 4))
    w["gdn_ng"] = f(inp["gdn_norm_g"].reshape(2, 1, 128))
    ow = np.asarray(inp["od_w_in"], dtype=np.float32)
    w["od_win"] = f(np.concatenate([ow, ow[:, :, 2816:2880], ow[:, :, 2816:2880]], axis=2))
    w["od_wout"] = f(inp["od_w_out"])
    gq = np.asarray(inp["diff_q_norm_g"], dtype=np.float32)
    gk = np.asarray(inp["diff_k_norm_g"], dtype=np.float32)
    w["od_gn"] = f(np.stack([np.tile(gq, (1, 2)), np.tile(gk, (1, 2)), np.asarray(inp["dsa_q_norm_g"]), np.asarray(inp["dsa_k_norm_g"])], axis=2))
    w["od_lam"] = f(inp["diff_lambda"].reshape(2, 1, 256))
    w["od_gsub"] = f(inp["diff_sub_norm_g"].reshape(2, 1, 128))
    kk = np.arange(128)[:, None]
    qq = np.arange(128)[None, :]
    tab = np.asarray(inp["rel_bias"], dtype=np.float32)
    near = np.zeros((128, 8, 2, 128), np.float32)
    for rel in range(2):
        dist = qq - kk + (128 if rel == 0 else 0)
        near[:, :, rel, :] = tab[_rel_bucket_np(dist)].transpose(0, 2, 1)
    w["rb_near"] = f(near)
    w["rb_t31"] = f(tab[31:32, :])
    return w


def build_program(L, NSEQ, plan):
    nc = bass.Bass("TRN2", target_bir_lowering=False)
    NTOK = L * NSEQ
    C = Ctx()
    C.L, C.NSEQ, C.NTOK = L, NSEQ, NTOK
    x = nc.dram_tensor("x", [NTOK, D], F32, kind="ExternalInput").ap()
    y = nc.dram_tensor("y", [NTOK, D], F32, kind="ExternalOutput").ap()
    for name, shape in W_SPECS.items():
        setattr(C, name, nc.dram_tensor(name, shape, F32, kind="ExternalInput").ap())
    C.wres = Res("weights")
    xdt = DT(x, "x")
    ydt = DT(y, "y")
    C.qkT = DT(nc.dram_tensor("qkT", [NSEQ, NQK, 128, L], BF16, kind="Internal").ap(), "qkT")
    C.vtok = DT(nc.dram_tensor("vtok", [NSEQ, L, 640], BF16, kind="Internal").ap(), "vtok")
    C.iwd = DT(nc.dram_tensor("iwd", [NSEQ, L, 8], F32, kind="Internal").ap(), "iwd")
    with ExitStack() as es:
        S = Sched(nc, es)
        ct = es.enter_context(nc.sbuf_tensor("identb", [128, 128], BF16))
        C.identb = V(ct[:], Res("identb"))
        ct = es.enter_context(nc.sbuf_tensor("identf", [128, 128], F32))
        C.identf = V(ct[:], Res("identf"))
        ct = es.enter_context(nc.sbuf_tensor("eps", [128, 1], F32))
        C.eps = V(ct[:], Res("eps"))
        S.memset("pool", C.identf, 1.0)
        S.asel(C.identf, C.identf, [[-1, 128]], ALU.is_equal, 0.0, base=0, cm=1)
        S.copy("pool", C.identb, C.identf)
        S.memset("pool", C.eps, EPS)
        src = xdt
        for kind, idx in plan:
            if kind == "ffn":
                ffn_phase(S, C, idx, src, ydt)
            elif kind == "even":
                even_phase(S, C, idx, src, ydt)
            elif kind == "odd":
                odd_phase(S, C, idx, src, ydt)
            src = ydt
        S.barrier()
        print("program: ops=%d waits=%d dma_sems=%d" % (S.nops, S.nwaits, S.ndsem))
    return nc


FULL_PLAN = [("even", 0), ("ffn", 0), ("odd", 0), ("ffn", 1), ("even", 1), ("ffn", 2), ("odd", 1), ("ffn", 3)]


N_CORES = 8
_PROG = {}


def kernel(**inputs):
    x = np.asarray(inputs["x"], dtype=np.float32)
    B, L, Dm = x.shape
    nseq = B // N_CORES
    key = (L, nseq)
    if key not in _PROG:
        _PROG[key] = build_program(L, nseq, FULL_PLAN)
    nc = _PROG[key]
    w = prep_weights(inputs)
    in_maps = []
    for c in range(N_CORES):
        m = {"x": np.ascontiguousarray(x[c * nseq:(c + 1) * nseq].reshape(nseq * L, Dm))}
        m.update(w)
        in_maps.append(m)
    res = run_bass_kernel_spmd(nc, in_maps, core_ids=list(range(N_CORES)))
    out = np.concatenate([np.asarray(r["y"]).reshape(nseq, L, Dm) for r in res.results], axis=0)
    return out.astype(np.float32)
```

```python
import math
from contextlib import ExitStack

import numpy as np
import concourse.bass as bass
import concourse.mybir as mybir
from concourse.bass_utils import run_bass_kernel_spmd

F32 = mybir.dt.float32
BF16 = mybir.dt.bfloat16
AF = mybir.ActivationFunctionType
ALU = mybir.AluOpType
AX = mybir.AxisListType


class Res:
    __slots__ = ("name", "last_w", "reads", "dsem", "dcount", "excl")

    def __init__(self, name="r"):
        self.name = name
        self.excl = False
        self.last_w = None
        self.reads = []
        self.dsem = None
        self.dcount = 0


class V:
    __slots__ = ("ap", "res")

    def __init__(self, ap, res):
        self.ap = ap
        self.res = res

    def __getitem__(self, idx):
        return V(self.ap[idx], self.res)

    def r(self, res):
        return V(self.ap, res)


def _res_of(xs):
    out = []
    for x in xs:
        if x is None:
            continue
        out.append(x.res if isinstance(x, V) else x)
    return out


class Sched:
    ENGS = ("pe", "act", "dve", "pool", "sp")

    def __init__(self, nc, es):
        self.nc = nc
        self.es = es
        self.eng = {"pe": nc.tensor, "act": nc.scalar, "dve": nc.vector, "pool": nc.gpsimd, "sp": nc.sync}
        self.sem = {e: es.enter_context(nc.semaphore("sem_" + e)) for e in self.ENGS}
        self.cnt = {e: 0 for e in self.ENGS}
        self.known = {e: {} for e in self.ENGS}
        self.dpool = []
        self.dlive = []
        self.ndsem = 0
        self.nwaits = 0
        self.nops = 0
        self.phase_es = None
        self.uid = 0
        import os
        self.limit = int(os.environ["OPLIMIT"]) if "OPLIMIT" in os.environ else None

    def begin_phase(self):
        self.phase_es = ExitStack()

    def end_phase(self):
        self.barrier()
        for r in self.dlive:
            self.dpool.append((r.dsem, r.dcount))
            r.dsem = None
        self.dlive = []
        self.phase_es.close()
        self.phase_es = None

    def sb(self, name, shape, dt=F32):
        self.uid += 1
        name = "%s_u%d" % (name, self.uid)
        t = self.phase_es.enter_context(self.nc.sbuf_tensor(name, list(shape), dt))
        return V(t[:], Res(name))

    def ps(self, name, shape, dt=F32):
        self.uid += 1
        name = "%s_u%d" % (name, self.uid)
        t = self.phase_es.enter_context(self.nc.psum_tensor(name, list(shape), dt))
        rs = Res(name)
        rs.excl = True
        return V(t[:], rs)

    def _collect(self, eng, r, w, strict):
        waits = {}

        def add(ev, same_ok):
            if ev is None:
                return
            sem, val, src = ev
            if not strict and src == eng and (same_ok or eng == "pe"):
                return
            k = id(sem)
            if k not in waits or waits[k][1] < val:
                waits[k] = (sem, val)

        for res in r:
            add(res.last_w, False)
            if res.excl:
                for ev in res.reads:
                    add(ev, True)
        for res in w:
            add(res.last_w, True)
            for ev in res.reads:
                add(ev, True)
        return waits

    def _emit_waits(self, eng, waits):
        kn = self.known[eng]
        e = self.eng[eng]
        for k, (sem, val) in waits.items():
            if kn.get(k, 0) >= val:
                continue
            kn[k] = val
            e.wait_ge(sem, val)
            self.nwaits += 1

    def op(self, eng, fn, r=(), w=(), inc=True):
        if self.limit is not None and self.nops >= self.limit:
            return None
        r = _res_of(r)
        w = _res_of(w)
        self._emit_waits(eng, self._collect(eng, r, w, False))
        ins = fn(self.eng[eng])
        self.nops += 1
        if inc:
            self.cnt[eng] += 1
            ins.then_inc(self.sem[eng], 1)
            ev = (self.sem[eng], self.cnt[eng], eng)
        else:
            assert eng == "pe"
            ev = (self.sem[eng], self.cnt[eng] + 1, eng)
        for res in r:
            res.reads.append(ev)
        for res in w:
            res.last_w = ev
            res.reads = []
        return ins

    def dma(self, q, out, in_, sres=None, **kw):
        if self.limit is not None and self.nops >= self.limit:
            return None
        sr = sres if sres is not None else out.res
        if sr.dsem is None:
            if self.dpool:
                sr.dsem, sr.dcount = self.dpool.pop()
            else:
                sr.dsem = self.es.enter_context(self.nc.semaphore("dsem%d" % self.ndsem))
                sr.dcount = 0
                self.ndsem += 1
            self.dlive.append(sr)
        waits = self._collect(q, [in_.res], [out.res], True)
        k = id(sr.dsem)
        if sr.dcount > 0 and (k not in waits or waits[k][1] < sr.dcount):
            waits[k] = (sr.dsem, sr.dcount)
        self._emit_waits(q, waits)
        ins = self.eng[q].dma_start(out=out.ap, in_=in_.ap, **kw)
        sr.dcount += 16
        ins.then_inc(sr.dsem, 16)
        self.nops += 1
        ev = (sr.dsem, sr.dcount, "dma")
        in_.res.reads.append(ev)
        out.res.last_w = ev
        out.res.reads = []
        return ins

    def barrier(self):
        evs = [(self.sem[e], self.cnt[e]) for e in self.ENGS if self.cnt[e] > 0]
        evs += [(r.dsem, r.dcount) for r in self.dlive if r.dcount > 0]
        for e in self.ENGS:
            kn = self.known[e]
            for sem, val in evs:
                if sem is self.sem[e]:
                    continue
                if kn.get(id(sem), 0) >= val:
                    continue
                kn[id(sem)] = val
                self.eng[e].wait_ge(sem, val)
                self.nwaits += 1

    def mm(self, out, lhsT, rhs, start=True, stop=True, inc=None, **kw):
        if inc is None:
            inc = bool(stop)
        return self.op("pe", lambda e: e.matmul(out.ap, lhsT.ap, rhs.ap, start=start, stop=stop, **kw),
                       r=[lhsT, rhs], w=[out], inc=inc)

    def tr(self, out, in_, ident):
        return self.op("pe", lambda e: e.transpose(out.ap, in_.ap, ident.ap), r=[in_, ident], w=[out])

    def act(self, out, in_, func, bias=None, scale=None, accum=None, eng="act"):
        kw = {}
        rr = [in_]
        ww = [out]
        if bias is not None:
            if isinstance(bias, V):
                kw["bias"] = bias.ap
                rr.append(bias)
            else:
                kw["bias"] = bias
        if scale is not None:
            if isinstance(scale, V):
                kw["scale"] = scale.ap
                rr.append(scale)
            else:
                kw["scale"] = scale
        if accum is not None:
            kw["accum_out"] = accum.ap
            ww.append(accum)
        return self.op("act", lambda e: e.activation(out.ap, in_.ap, func, **kw), r=rr, w=ww)

    def tt(self, eng, out, a, b, op):
        return self.op(eng, lambda e: e.tensor_tensor(out.ap, a.ap, b.ap, op), r=[a, b], w=[out])

    def ts(self, eng, out, a, s1, op0, s2=None, op1=None, accum=None):
        rr = [a]
        ww = [out]
        a1 = s1
        a2 = s2
        if isinstance(s1, V):
            rr.append(s1)
            a1 = s1.ap
        if isinstance(s2, V):
            rr.append(s2)
            a2 = s2.ap
        kw = {}
        if op1 is not None:
            kw["op1"] = op1
        if accum is not None:
            kw["accum_out"] = accum.ap
            ww.append(accum)
        return self.op(eng, lambda e: e.tensor_scalar(out.ap, a.ap, a1, a2, op0, **kw), r=rr, w=ww)

    def stt(self, eng, out, a, s, b, op0, op1, accum=None):
        rr = [a, b]
        ww = [out]
        sc = s
        if isinstance(s, V):
            rr.append(s)
            sc = s.ap
        kw = {}
        if accum is not None:
            kw["accum_out"] = accum.ap
            ww.append(accum)
        return self.op(eng, lambda e: e.scalar_tensor_tensor(out.ap, a.ap, sc, b.ap, op0, op1, **kw), r=rr, w=ww)

    def copy(self, eng, out, in_):
        if eng == "act":
            return self.op("act", lambda e: e.copy(out.ap, in_.ap), r=[in_], w=[out])
        return self.op(eng, lambda e: e.tensor_copy(out.ap, in_.ap), r=[in_], w=[out])

    def memset(self, eng, out, val):
        return self.op(eng, lambda e: e.memset(out.ap, val), w=[out])

    def reduce(self, eng, out, in_, op, axis=None):
        ax = AX.X if axis is None else axis
        return self.op(eng, lambda e: e.tensor_reduce(out.ap, in_.ap, ax, op), r=[in_], w=[out])

    def asel(self, out, in_, pattern, cmp, fill, base=0, cm=0):
        return self.op("pool", lambda e: e.affine_select(out.ap, in_.ap, pattern=pattern, compare_op=cmp, fill=fill,
                                                         base=base, channel_multiplier=cm), r=[in_], w=[out])


D = 1024
KC = 8
DFF = 2816
FC = 22
EPS = 1e-6
EVEN_IN = 3080
ODD_IN = 2888
NEG = -30000.0


class DT:
    def __init__(self, ap, name):
        self.ap = ap
        self.name = name
        self.res = {}

    def v(self, key, ap):
        if key not in self.res:
            self.res[key] = Res("%s_%s" % (self.name, str(key)))
        return V(ap, self.res[key])


class Ctx:
    pass


def load_rows(S, C, dst, src_dt, key, ap, q="sp"):
    S.dma(q, dst, src_dt.v(key, ap), sres=dst.res)


def rms_to_hnT(S, C, xt, hnT_cols, ptr, hn, ssq, rstd):
    S.act(hn, xt, AF.Square, accum=ssq)
    S.act(rstd, ssq, AF.Sqrt, bias=C.eps, scale=1.0 / D)
    S.op("dve", lambda e: e.reciprocal(rstd.ap, rstd.ap), r=[rstd], w=[rstd])
    S.ts("pool", hn, xt, rstd, ALU.mult, 1.0, ALU.mult)
    for kc in range(KC):
        S.tr(ptr[:, kc * 128:(kc + 1) * 128], hn[:, kc * 128:(kc + 1) * 128], C.identb)
    S.copy("act", hnT_cols, V(ptr.ap.rearrange("p (k t) -> p k t", k=KC), ptr.res))


def rms_to_hnT_gen(S, C, xt, hnT_cols, ptr, hn, ssq, rstd):
    S.act(hn, xt, AF.Square, accum=ssq)
    yield
    S.act(rstd, ssq, AF.Sqrt, bias=C.eps, scale=1.0 / D)
    yield
    S.op("dve", lambda e: e.reciprocal(rstd.ap, rstd.ap), r=[rstd], w=[rstd])
    yield
    S.ts("pool", hn, xt, rstd, ALU.mult, 1.0, ALU.mult)
    yield
    for kc in range(KC):
        S.tr(ptr[:, kc * 128:(kc + 1) * 128], hn[:, kc * 128:(kc + 1) * 128], C.identb)
    yield
    S.copy("act", hnT_cols, V(ptr.ap.rearrange("p (k t) -> p k t", k=KC), ptr.res))
    yield


def load_weight_scaled(S, C, wsb, w_dram_ap, nrows_chunks, ncols, gcol, stg, colchunk, name):
    i = 0
    width = stg[0].ap.shape[1]
    colchunk = min(colchunk, width)
    for rc in range(nrows_chunks):
        for c0 in range(0, ncols, colchunk):
            c1 = min(ncols, c0 + colchunk)
            st = stg[i % len(stg)]
            S.dma(("sp", "act")[i % 2], st[:, 0:c1 - c0], V(w_dram_ap[rc * 128:(rc + 1) * 128, c0:c1], C.wres), sres=st.res)
            eng = ("dve", "act", "dve", "pool")[i % 4]
            if gcol is None:
                S.copy(eng, wsb[:, rc, c0:c1], st[:, 0:c1 - c0])
            elif eng == "act":
                S.act(wsb[:, rc, c0:c1], st[:, 0:c1 - c0], AF.Copy, scale=gcol[:, rc:rc + 1])
            else:
                S.ts(eng, wsb[:, rc, c0:c1], st[:, 0:c1 - c0], gcol[:, rc:rc + 1], ALU.mult, 1.0, ALU.mult)
            i += 1


def ffn_phase(S, C, layer, src, dst):
    NT = 512
    HW = 256
    NSUB = NT // 128
    S.begin_phase()
    wup = S.sb("wup", [128, KC, 2 * DFF], BF16)
    wdn = S.sb("wdn", [128, FC, D], BF16)
    gsb = S.sb("gsb", [128, KC])
    cw = S.sb("cw", [128, 2 * FC, 3])
    cb = S.sb("cb", [128, 2 * FC])
    stg = [S.sb("stg%d" % i, [128, 352]) for i in range(4)]
    S.dma("sp", gsb, V(C.ffn_g[layer], C.wres), sres=gsb.res)
    S.dma("sp", cw, V(C.ffn_cw[layer], C.wres), sres=cw.res)
    S.dma("sp", cb, V(C.ffn_cb[layer], C.wres), sres=cb.res)
    load_weight_scaled(S, C, wup, C.ffn_wup[layer], KC, 2 * DFF, gsb, stg, 352, "wup")
    load_weight_scaled(S, C, wdn, C.ffn_wdn[layer], FC, D, None, stg, 352, "wdn")

    xs = [S.sb("x%d" % i, [128, D]) for i in range(2)]
    xr = [S.sb("xr%d" % i, [128, D]) for i in range(2)]
    hn = [S.sb("hn%d" % i, [128, D], BF16) for i in range(2)]
    ssq = [S.sb("ssq%d" % i, [128, 1]) for i in range(2)]
    rstd = [S.sb("rstd%d" % i, [128, 1]) for i in range(2)]
    hT_ = S.sb("hnT", [128, KC, NT], BF16)
    hT = S.sb("hT", [128, FC, NT], BF16)
    hal = S.sb("hal", [128, 2 * FC, 2])
    rr = [S.sb("rr%d" % i, [128, HW + 2]) for i in range(4)]
    acc = [S.sb("acc%d" % i, [128, HW]) for i in range(4)]
    sg = [S.sb("sg%d" % i, [128, HW]) for i in range(2)]
    ptr = [S.ps("ptr%d" % i, [128, D], BF16) for i in range(2)]
    pup = [S.ps("pup%d" % i, [128, 512]) for i in range(4)]
    pdn = [S.ps("pdn%d" % i, [128, 512]) for i in range(2)]

    gsub = 0
    for b in range(C.NSEQ):
        for t0 in range(0, C.L, NT):
            rows = []
            for sub in range(NSUB):
                r0 = b * C.L + t0 + sub * 128
                xt = xs[gsub % 2]
                rows.append(r0)
                load_rows(S, C, xt, src, (r0 // 128), src.ap[r0:r0 + 128, :])
                j = gsub % 2
                rms_to_hnT(S, C, xt, hT_[:, :, sub * 128:(sub + 1) * 128], ptr[j], hn[j], ssq[j], rstd[j])
                gsub += 1
            items = [(c, hf, which) for c in range(FC) for hf in range(2) for which in range(2)]
            ni = len(items)

            def fA(n):
                c, hf, which = items[n]
                ch = c + which * FC
                ps = pup[(2 * c + which) % 4]
                if hf == 0:
                    for kc in range(KC):
                        S.mm(ps, wup[:, kc, ch * 128:(ch + 1) * 128], hT_[:, kc, :], start=(kc == 0), stop=(kc == KC - 1))
                psh = ps[:, hf * HW:(hf + 1) * HW]
                r = rr[n % 4]
                a = acc[n % 4]
                if t0 == 0 and hf == 0:
                    S.memset("pool", r[:, 0:2], 0.0)
                else:
                    S.copy("pool", r[:, 0:2], hal[:, ch, :])
                S.copy("act", r[:, 2:HW + 2], psh)
                S.act(a, psh, AF.Identity, bias=cb[:, ch:ch + 1], scale=cw[:, ch, 2:3])
                S.copy("pool", hal[:, ch, :], r[:, HW:HW + 2])

            def fB(n):
                c, hf, which = items[n]
                ch = c + which * FC
                r = rr[n % 4]
                a = acc[n % 4]
                S.stt("dve", a, r[:, 1:HW + 1], cw[:, ch, 1:2], a, ALU.mult, ALU.add)
                S.stt("dve", a, r[:, 0:HW], cw[:, ch, 0:1], a, ALU.mult, ALU.add)

            def fC(p):
                S.act(sg[p % 2], acc[(2 * p) % 4], AF.Silu)

            def fD(p):
                c, hf = p // 2, p % 2
                S.tt("dve", hT[:, c, hf * HW:(hf + 1) * HW], sg[p % 2], acc[(2 * p + 1) % 4], ALU.mult)

            npair = ni // 2
            for n in range(ni + 3):
                if n < ni:
                    fA(n)
                if 0 <= n - 1 < ni:
                    fB(n - 1)
                if n >= 2 and n % 2 == 0 and (n - 2) // 2 < npair:
                    fC((n - 2) // 2)
                if n >= 3 and n % 2 == 1 and (n - 3) // 2 < npair:
                    fD((n - 3) // 2)
            for sub in range(NSUB):
                r0 = rows[sub]
                xt = xr[sub % 2]
                load_rows(S, C, xt, src, (r0 // 128), src.ap[r0:r0 + 128, :])
                for nh in range(2):
                    ps2 = pdn[nh]
                    for fc in range(FC):
                        S.mm(ps2, hT[:, fc, sub * 128:(sub + 1) * 128], wdn[:, fc, nh * 512:(nh + 1) * 512],
                             start=(fc == 0), stop=(fc == FC - 1))
                    S.tt("dve", xt[:, nh * 512:(nh + 1) * 512], xt[:, nh * 512:(nh + 1) * 512], ps2, ALU.add)
                S.dma("sp", dst.v((r0 // 128), dst.ap[r0:r0 + 128, :]), xt, sres=xt.res)
    S.end_phase()


def flat(v):
    return V(v.ap.rearrange("p h j -> p (h j)"), v.res)


def v3(v, h=4):
    return V(v.ap.rearrange("p (h j) -> p h j", h=h), v.res)


def bc_last(v, n):
    H = v.ap.shape[1]
    return V(v.ap.unsqueeze(2).to_broadcast([128, H, n]), v.res)


def bc_mid(v, h):
    n = v.ap.shape[1]
    return V(v.ap.unsqueeze(1).to_broadcast([128, h, n]), v.res)


class Banks:
    def __init__(self, S, n=8):
        self.b = [S.ps("bank%d" % i, [128, 512]) for i in range(n)]
        self.i = 0

    def get(self):
        v = self.b[self.i % len(self.b)]
        self.i += 1
        return v


def bf(v):
    return V(v.ap.bitcast(BF16), v.res)


F32R = mybir.dt.float32r


def r32(v):
    return V(v.ap.bitcast(F32R), v.res)


def run_lanes(gens):
    gens = [g for g in gens if g is not None]
    while gens:
        for g in list(gens):
            try:
                next(g)
            except StopIteration:
                gens.remove(g)


def even_phase(S, C, j, src, dst):
    layer = 2 * j
    NT = 256
    NSUB = NT // 128
    H = 4
    S.begin_phase()
    PB = Banks(S)
    win = S.sb("win", [128, KC, EVEN_IN], BF16)
    wout = S.sb("wout", [128, KC, D], BF16)
    gsb = S.sb("gsb", [128, KC])
    stg = [S.sb("stg%d" % i, [128, 770]) for i in range(4)]
    S.dma("sp", gsb, V(C.mix_g[layer], C.wres), sres=gsb.res)
    load_weight_scaled(S, C, win, C.ev_win[j], KC, EVEN_IN, gsb, stg, 1540, "win")
    load_weight_scaled(S, C, wout, C.ev_wout[j], KC, D, None, stg, 1024, "wout")
    wTm = S.sb("wTm", [128, H, 128], BF16)
    wTf = S.sb("wTf", [128, H, 128])
    brow = S.sb("brow", [1, 512])
    cw = S.sb("gcw", [128, 12, 4])
    alog = S.sb("alog", [128, 4])
    dtb = S.sb("dtb", [128, 4])
    nega = S.sb("nega", [128, 4])
    gng = S.sb("gng", [128, 128])
    S.dma("sp", wTf, V(C.gm_wT[j], C.wres), sres=wTf.res)
    S.dma("sp", brow, V(C.gm_b[j], C.wres), sres=brow.res)
    S.dma("sp", cw, V(C.gdn_cw[j], C.wres), sres=cw.res)
    S.dma("sp", alog, V(C.gdn_alog[j].partition_broadcast(128), C.wres), sres=alog.res)
    S.dma("sp", dtb, V(C.gdn_dtb[j].partition_broadcast(128), C.wres), sres=dtb.res)
    S.dma("sp", gng, V(C.gdn_ng[j].partition_broadcast(128), C.wres), sres=gng.res)
    ones = S.sb("ones", [128, 128])
    tri = S.sb("tri", [128, 128])
    ntri = S.sb("ntri", [128, 128])
    strict = S.sb("strict", [128, 128])
    incl = S.sb("incl", [128, 128])
    S.memset("pool", ones, 1.0)
    ones_r = S.sb("ones_r", [128, 128])
    S.copy("pool", r32(ones_r), ones)
    S.asel(tri, ones, [[1, 128]], ALU.is_ge, 0.0, base=0, cm=-1)
    S.ts("pool", ntri, tri, -1.0, ALU.mult, 1.0, ALU.mult)
    S.asel(strict, ones, [[-1, 128]], ALU.is_ge, 0.0, base=-1, cm=1)
    S.asel(incl, ones, [[-1, 128]], ALU.is_ge, 0.0, base=0, cm=1)
    S.tt("pool", wTm, wTf, bc_mid(tri, H), ALU.mult)
    S.act(nega, alog, AF.Exp)
    S.ts("pool", nega, nega, -1.0, ALU.mult, 1.0, ALU.mult)

    xs = [S.sb("x%d" % i, [128, D]) for i in range(4)]
    hn = [S.sb("hn%d" % i, [128, D], BF16) for i in range(2)]
    ssq = [S.sb("ssq%d" % i, [128, 1]) for i in range(2)]
    rstd = [S.sb("rstd%d" % i, [128, 1]) for i in range(2)]
    hnT = [S.sb("hnT%d" % i, [128, KC, NT], BF16) for i in range(2)]
    uTs = [S.sb("uT%d" % i, [128, H, NT], BF16) for i in range(2)]
    yT = S.sb("yT", [128, KC, NT], BF16)
    qTs = [S.sb("qT%d" % i, [128, H, NT], BF16) for i in range(2)]
    kTs = [S.sb("kT%d" % i, [128, H, NT], BF16) for i in range(2)]
    vTs = [S.sb("vT%d" % i, [128, H, NT], BF16) for i in range(2)]
    XSL = [None, None]
    hal = S.sb("hal", [128, 12, 3])
    rr = [S.sb("rr%d" % i, [128, NT + 3]) for i in range(2)]
    ca = [S.sb("ca%d" % i, [128, NT]) for i in range(2)]
    qs = [S.sb("qs%d" % i, [128, NT]) for i in range(2)]
    sq = S.sb("sq", [128, NT])
    rn = S.sb("rn", [128, NT])
    Sst = S.sb("Sst", [128, H, 128])
    Sr = S.sb("Sr", [128, H, 128])

    def F(name, dt=F32):
        return S.sb(name, [128, H, 128], dt)

    ktok, vtok, vg, sqv, zs, G1, G2, E, nbm, usb, osq, on, gz, t1 = [
        F(n) for n in ("ktok", "vtok", "vg", "sqv", "zs", "G1", "G2", "E", "nbm", "usb", "osq", "on", "gz", "t1")]
    Lp0, Lp1, intra, intraT, U0, U1, TT, vb, kbg, wTs, qgT, kd, vnew = [
        F(n) for n in ("Lp0", "Lp1", "intra", "intraT", "U0", "U1", "TT", "vb", "kbg", "wTs", "qgT", "kd", "vnew")]
    vn = F("vn", BF16)
    onb = F("onb", BF16)
    sm = {n: S.sb(n, [128, 4]) for n in ("vsum", "vvar", "vrs", "beta", "nbeta", "xa", "xe", "xm", "sp", "g", "bk", "edl", "oss", "ors")}
    gcs = S.sb("gcs", [128, 8])
    egs = S.sb("egs", [128, 8])

    st = {"gsub": 0, "ich": 0}

    def proj_task(b, t0, k):
        hT_ = hnT[k]
        uT, qT, kT, vT = uTs[k], qTs[k], kTs[k], vTs[k]
        xsl = []
        for sub in range(NSUB):
            r0 = b * C.L + t0 + sub * 128
            xt = xs[st["gsub"] % 4]
            xsl.append((xt, r0))
            load_rows(S, C, xt, src, (r0 // 128), src.ap[r0:r0 + 128, :])
            jj = st["gsub"] % 2
            pt = PB.get()
            yield from rms_to_hnT_gen(S, C, xt, hT_[:, :, sub * 128:(sub + 1) * 128], bf(pt), hn[jj], ssq[jj], rstd[jj])
            st["gsub"] += 1
        for c in range(H):
            ps = PB.get()
            for kc in range(KC):
                S.mm(ps[:, 0:NT], win[:, kc, c * 128:(c + 1) * 128], hT_[:, kc, :], start=(kc == 0), stop=(kc == KC - 1))
            yield
            S.act(uT[:, c, :], ps[:, 0:NT], AF.Gelu_apprx_tanh)
            yield
        for c in range(12):
            ps = PB.get()
            for kc in range(KC):
                S.mm(ps[:, 0:NT], win[:, kc, 1024 + c * 128:1024 + (c + 1) * 128], hT_[:, kc, :], start=(kc == 0), stop=(kc == KC - 1))
            yield
            r = rr[st["ich"] % 2]
            a = ca[st["ich"] % 2]
            if t0 == 0:
                S.memset("pool", r[:, 0:3], 0.0)
            else:
                S.copy("pool", r[:, 0:3], hal[:, c, :])
            S.copy("act", r[:, 3:NT + 3], ps[:, 0:NT])
            S.act(a, ps[:, 0:NT], AF.Copy, scale=cw[:, c, 3:4])
            S.copy("pool", hal[:, c, :], r[:, NT:NT + 3])
            yield
            for tap in (2, 1, 0):
                S.stt("dve", a, r[:, tap:tap + NT], cw[:, c, tap:tap + 1], a, ALU.mult, ALU.add)
            hh = c % 4
            yield
            if c >= 8:
                S.act(vT[:, hh, :], a, AF.Silu)
            else:
                q_ = qs[st["ich"] % 2]
                S.act(q_, a, AF.Silu)
                S.act(r32(sq), q_, AF.Square)
                yield
                pss = PB.get()
                S.mm(pss[:, 0:NT], r32(ones_r), r32(sq))
                yield
                S.act(rn, pss[:, 0:NT], AF.Sqrt, bias=C.eps, scale=1.0)
                yield
                S.op("dve", lambda e: e.reciprocal(rn.ap, rn.ap), r=[rn], w=[rn])
                if c < 4:
                    S.stt("dve", qT[:, hh, :], q_, 128.0 ** -0.5, rn, ALU.mult, ALU.mult)
                else:
                    S.tt("dve", kT[:, hh, :], q_, rn, ALU.mult)
            st["ich"] += 1
            yield
        XSL[k] = xsl
        yield

    def chunk_task(b, t0, k):
        hT_ = hnT[k]
        uT, qT, kT, vT = uTs[k], qTs[k], kTs[k], vTs[k]
        xsl = XSL[k]
        if t0 == 0:
            S.memset("pool", Sst, 0.0)
            S.copy("pool", r32(Sr), Sst)
        for sub in range(NSUB):
            cs = slice(sub * 128, (sub + 1) * 128)
            xt, r0 = xsl[sub]
            pv = PB.get()
            pz = PB.get()
            pba = PB.get()
            for kc in range(KC):
                S.mm(pv, hT_[:, kc, cs], win[:, kc, 512:1024], start=(kc == 0), stop=(kc == KC - 1))
            for kc in range(KC):
                S.mm(pz, hT_[:, kc, cs], win[:, kc, 2568:3080], start=(kc == 0), stop=(kc == KC - 1))
            for kc in range(KC):
                S.mm(pba[:, 0:8], hT_[:, kc, cs], win[:, kc, 2560:2568], start=(kc == 0), stop=(kc == KC - 1))
            yield
            S.act(flat(vg), pv, AF.Gelu_apprx_tanh)
            S.act(flat(zs), pz, AF.Silu)
            S.act(sm["beta"], pba[:, 0:4], AF.Sigmoid)
            S.tt("dve", sm["xa"], pba[:, 4:8], dtb, ALU.add)
            S.reduce("dve", sm["vsum"], vg, ALU.add)
            S.stt("dve", vg, bc_last(sm["vsum"], 128), -1.0 / 128, vg, ALU.mult, ALU.add)
            S.tt("pool", sqv, vg, vg, ALU.mult)
            S.reduce("dve", sm["vvar"], sqv, ALU.add)
            S.act(sm["vrs"], sm["vvar"], AF.Sqrt, bias=C.eps, scale=1.0 / 128)
            S.op("dve", lambda e: e.reciprocal(sm["vrs"].ap, sm["vrs"].ap), r=[sm["vrs"]], w=[sm["vrs"]])
            S.tt("dve", vn, vg, bc_last(sm["vrs"], 128), ALU.mult)
            yield
            pm = PB.get()
            for gI in range(H):
                S.mm(pm[:, gI * 128:(gI + 1) * 128], vn[:, gI, :], wTm[:, gI, :], start=True, stop=False)
                S.mm(pm[:, gI * 128:(gI + 1) * 128], ones[0:1, :], brow[0:1, gI * 128:(gI + 1) * 128], start=False, stop=True)
            yield
            S.tt("dve", yT[:, 0:4, cs], v3(pm), uT[:, :, cs], ALU.mult)
            yield
            S.ts("dve", sm["nbeta"], sm["beta"], -1.0, ALU.mult)
            S.ts("dve", sm["xm"], sm["xa"], 30.0, ALU.min)
            S.act(sm["xe"], sm["xm"], AF.Exp)
            S.act(sm["sp"], sm["xe"], AF.Ln, bias=1.0, scale=1.0)
            S.ts("dve", sm["xm"], sm["xa"], -30.0, ALU.add, 0.0, ALU.max)
            S.tt("dve", sm["sp"], sm["sp"], sm["xm"], ALU.add)
            S.tt("dve", sm["g"], sm["sp"], nega, ALU.mult)
            g = sm["g"]
            pg = PB.get()
            S.mm(pg[:, 0:4], tri, g)
            S.mm(pg[:, 4:8], ones, g)
            yield
            S.copy("dve", gcs, pg[:, 0:8])
            S.act(egs, gcs, AF.Exp)
            S.tt("dve", sm["edl"], gcs[:, 4:8], gcs[:, 0:4], ALU.subtract)
            S.act(sm["edl"], sm["edl"], AF.Exp)
            S.tt("dve", sm["bk"], sm["beta"], egs[:, 0:4], ALU.mult)
            yield
            pk = PB.get()
            for h in range(H):
                S.tr(bf(pk)[:, h * 128:(h + 1) * 128], kT[:, h, cs], C.identb)
            yield
            S.copy("act", flat(ktok), bf(pk)[:, 0:512])
            pk = PB.get()
            for h in range(H):
                S.tr(bf(pk)[:, h * 128:(h + 1) * 128], vT[:, h, cs], C.identb)
            yield
            S.copy("act", flat(vtok), bf(pk)[:, 0:512])
            yield
            S.tt("pool", G2, bc_mid(C.identf, H), bc_last(gcs[:, 0:4], 128), ALU.mult)
            pd = PB.get()
            S.mm(pd, ones, flat(G2))
            yield
            S.tt("dve", E, bc_last(gcs[:, 0:4], 128), v3(pd), ALU.subtract)
            S.ts("dve", flat(E), flat(E), 0.0, ALU.min)
            S.act(flat(E), flat(E), AF.Exp)
            yield
            pkk = PB.get()
            pqk = PB.get()
            for h in range(H):
                S.mm(pkk[:, h * 128:(h + 1) * 128], kT[:, h, cs], kT[:, h, cs])
            for h in range(H):
                S.mm(pqk[:, h * 128:(h + 1) * 128], qT[:, h, cs], kT[:, h, cs])
            yield
            S.tt("pool", nbm, bc_mid(strict, H), bc_last(sm["nbeta"], 128), ALU.mult)
            S.tt("dve", flat(t1), pkk, flat(E), ALU.mult)
            S.tt("pool", r32(Lp0), t1, nbm, ALU.mult)
            S.tt("dve", flat(osq), pqk, flat(E), ALU.mult)
            S.tt("pool", intra, osq, bc_mid(incl, H), ALU.mult)
            yield
            pu = PB.get()
            for h in range(H):
                S.tr(pu[:, h * 128:(h + 1) * 128], Lp0[:, h, :], C.identf)
            yield
            S.copy("act", r32(flat(U0)), pu)
            S.tt("dve", r32(TT), v3(pu), bc_mid(C.identf, H), ALU.add)
            pi = PB.get()
            for h in range(H):
                S.tr(pi[:, h * 128:(h + 1) * 128], intra[:, h, :], C.identf)
            yield
            S.copy("act", r32(flat(intraT)), pi)
            yield
            Us = [U0, U1]
            Ls = [Lp0, Lp1]
            for k in range(1, 7):
                Uo, Un = Us[(k - 1) % 2], Us[k % 2]
                Lo, Ln_ = Ls[(k - 1) % 2], Ls[k % 2]
                if k <= 5:
                    p1 = PB.get()
                    for h in range(H):
                        S.mm(p1[:, h * 128:(h + 1) * 128], r32(Lo[:, h, :]), r32(Uo[:, h, :]))
                p2 = PB.get()
                for h in range(H):
                    S.mm(p2[:, h * 128:(h + 1) * 128], r32(Uo[:, h, :]), r32(Lo[:, h, :]))
                yield
                if k <= 5:
                    S.copy("act", r32(flat(Un)), p1)
                S.copy("dve", r32(flat(Ln_)), p2)
                yield
                p3 = PB.get()
                for h in range(H):
                    S.mm(p3[:, h * 128:(h + 1) * 128], r32(Ln_[:, h, :]), r32(TT[:, h, :]))
                yield
                S.tt("dve", r32(flat(TT)), flat(TT), p3, ALU.add)
                yield
            yield
            S.tt("pool", r32(vb), vtok, bc_last(sm["beta"], 128), ALU.mult)
            S.tt("pool", r32(kbg), ktok, bc_last(sm["bk"], 128), ALU.mult)
            S.tt("pool", r32(kd), ktok, bc_last(sm["edl"], 128), ALU.mult)
            pU = PB.get()
            for h in range(H):
                S.mm(pU[:, h * 128:(h + 1) * 128], r32(TT[:, h, :]), r32(vb[:, h, :]))
            yield
            S.copy("act", flat(usb), pU)
            pW = PB.get()
            for h in range(H):
                S.mm(pW[:, h * 128:(h + 1) * 128], r32(kbg[:, h, :]), r32(TT[:, h, :]))
            yield
            S.copy("act", r32(flat(wTs)), pW)
            yield
            S.tt("pool", G1, bc_mid(C.identf, H), bc_last(egs[:, 0:4], 128), ALU.mult)
            pe_ = PB.get()
            S.mm(pe_, ones, flat(G1))
            yield
            S.tt("dve", r32(qgT), qT[:, :, cs], v3(pe_), ALU.mult)
            yield
            pws = PB.get()
            for h in range(H):
                S.mm(pws[:, h * 128:(h + 1) * 128], r32(wTs[:, h, :]), r32(Sr[:, h, :]))
            yield
            S.tt("dve", r32(flat(vnew)), flat(usb), pws, ALU.subtract)
            po = PB.get()
            for h in range(H):
                S.mm(po[:, h * 128:(h + 1) * 128], r32(qgT[:, h, :]), r32(Sr[:, h, :]), start=True, stop=False)
                S.mm(po[:, h * 128:(h + 1) * 128], r32(intraT[:, h, :]), r32(vnew[:, h, :]), start=False, stop=True)
            pS = PB.get()
            for h in range(H):
                S.mm(pS[:, h * 128:(h + 1) * 128], r32(kd[:, h, :]), r32(vnew[:, h, :]))
            yield
            S.tt("dve", Sst, Sst, bc_last(egs[:, 4:8], 128), ALU.mult)
            S.tt("dve", flat(Sst), flat(Sst), pS, ALU.add)
            S.copy("act", r32(Sr), Sst)
            yield
            yield
            S.act(flat(osq), po, AF.Square)
            S.reduce("dve", sm["oss"], osq, ALU.add)
            S.act(sm["ors"], sm["oss"], AF.Sqrt, bias=C.eps, scale=1.0 / 128)
            S.op("dve", lambda e: e.reciprocal(sm["ors"].ap, sm["ors"].ap), r=[sm["ors"]], w=[sm["ors"]])
            S.tt("pool", gz, zs, bc_mid(gng, H), ALU.mult)
            S.tt("dve", on, v3(po), bc_last(sm["ors"], 128), ALU.mult)
            S.tt("dve", onb, on, gz, ALU.mult)
            pt2 = PB.get()
            for h in range(H):
                S.tr(bf(pt2)[:, h * 128:(h + 1) * 128], onb[:, h, :], C.identb)
            yield
            S.copy("act", yT[:, 4:8, cs], v3(bf(pt2)[:, 0:512]))
            yield
            for nh in range(2):
                pso = PB.get()
                for kc in range(KC):
                    S.mm(pso, yT[:, kc, cs], wout[:, kc, nh * 512:(nh + 1) * 512], start=(kc == 0), stop=(kc == KC - 1))
                S.tt("dve", xt[:, nh * 512:(nh + 1) * 512], xt[:, nh * 512:(nh + 1) * 512], pso, ALU.add)
            S.dma("sp", dst.v((r0 // 128), dst.ap[r0:r0 + 128, :]), xt, sres=xt.res)

    tiles = [(b, t0) for b in range(C.NSEQ) for t0 in range(0, C.L, NT)]
    prev = None
    for i, (b, t0) in enumerate(tiles):
        run_lanes([proj_task(b, t0, i % 2), chunk_task(*prev) if prev is not None else None])
        prev = (b, t0, i % 2)
    run_lanes([chunk_task(*prev)])
    S.end_phase()


OD_EXT = 3016
NQK = 18


def odd_phase(S, C, j, src, dst):
    odd_proj_phase(S, C, j, src)
    odd_attn_phase(S, C, j, src, dst)


def odd_proj_phase(S, C, j, src):
    layer = 2 * j + 1
    NT = 512
    NSUB = NT // 128
    S.begin_phase()
    PB = Banks(S)
    win = S.sb("win", [128, KC, OD_EXT], BF16)
    gsb = S.sb("gsb", [128, KC])
    stg = [S.sb("stg%d" % i, [128, 754]) for i in range(4)]
    S.dma("sp", gsb, V(C.mix_g[layer], C.wres), sres=gsb.res)
    load_weight_scaled(S, C, win, C.od_win[j], KC, OD_EXT, gsb, stg, 1508, "win")
    gn = S.sb("gn", [128, 4])
    S.dma("sp", gn, V(C.od_gn[j], C.wres), sres=gn.res)
    S.ts("pool", gn[:, 0:1], gn[:, 0:1], 64.0 ** -0.5, ALU.mult, 1.0, ALU.mult)
    S.ts("pool", gn[:, 2:3], gn[:, 2:3], 128.0 ** -0.5, ALU.mult, 1.0, ALU.mult)
    ones = S.sb("ones", [128, 128])
    bd64 = S.sb("bd64", [128, 128])
    S.memset("pool", ones, 1.0)
    S.memset("pool", bd64, 0.0)
    S.memset("pool", bd64[0:64, 0:64], 1.0)
    S.memset("pool", bd64[64:128, 64:128], 1.0)

    xs = [S.sb("x%d" % i, [128, D]) for i in range(2)]
    hn = [S.sb("hn%d" % i, [128, D], BF16) for i in range(2)]
    ssq = [S.sb("ssq%d" % i, [128, 1]) for i in range(2)]
    rstd = [S.sb("rstd%d" % i, [128, 1]) for i in range(2)]
    hnT = [S.sb("hnT%d" % i, [128, KC, NT], BF16) for i in range(2)]
    oT = [S.sb("oT%d" % i, [128, NQK, NT], BF16) for i in range(2)]
    sqb = [S.sb("sqb%d" % i, [128, NT]) for i in range(2)]
    rn = [S.sb("rn%d" % i, [128, NT]) for i in range(2)]
    tokb = [S.sb("tokb%d" % i, [128, 640], BF16) for i in range(2)]
    iwt = [S.sb("iwt%d" % i, [128, 8]) for i in range(2)]

    chunks = []
    for c in range(4):
        chunks.append((c * 128, "n64", 0))
    for c in range(4):
        chunks.append((512 + c * 128, "n64", 1))
    for c in range(4):
        chunks.append((1536 + c * 128, "n128", 2))
    chunks.append((2048, "n128", 3))
    for c in range(4):
        chunks.append((2304 + c * 128, "scale", None))
    chunks.append((2888, "copy", None))

    gsub = 0
    ist = 0
    inorm = 0
    for b in range(C.NSEQ):
        for t0 in range(0, C.L, NT):
            hT_ = hnT[ist % 2]
            o_ = oT[ist % 2]
            for sub in range(NSUB):
                r0 = b * C.L + t0 + sub * 128
                xt = xs[gsub % 2]
                load_rows(S, C, xt, src, (r0 // 128), src.ap[r0:r0 + 128, :])
                jj = gsub % 2
                pt = PB.get()
                rms_to_hnT(S, C, xt, hT_[:, :, sub * 128:(sub + 1) * 128], bf(pt), hn[jj], ssq[jj], rstd[jj])
                gsub += 1
            for ci, (c0, kind, gi) in enumerate(chunks):
                ps = PB.get()
                for kc in range(KC):
                    S.mm(ps[:, 0:NT], win[:, kc, c0:c0 + 128], hT_[:, kc, :], start=(kc == 0), stop=(kc == KC - 1))
                if kind == "scale":
                    S.act(o_[:, ci, :], ps[:, 0:NT], AF.Copy, scale=0.125)
                elif kind == "copy":
                    S.copy("act", o_[:, ci, :], ps[:, 0:NT])
                else:
                    sq_ = sqb[inorm % 2]
                    rn_ = rn[inorm % 2]
                    inorm += 1
                    S.act(sq_, ps[:, 0:NT], AF.Square)
                    pss = PB.get()
                    S.mm(pss[:, 0:NT], bd64 if kind == "n64" else ones, sq_)
                    dim = 64.0 if kind == "n64" else 128.0
                    S.act(rn_, pss[:, 0:NT], AF.Sqrt, bias=C.eps, scale=1.0 / dim)
                    S.op("dve", lambda e, rn_=rn_: e.reciprocal(rn_.ap, rn_.ap), r=[rn_], w=[rn_])
                    S.stt("dve", o_[:, ci, :], ps[:, 0:NT], gn[:, gi:gi + 1], rn_, ALU.mult, ALU.mult)
            for sub in range(NSUB):
                cs = slice(sub * 128, (sub + 1) * 128)
                r0 = t0 + sub * 128
                tb = tokb[sub % 2]
                iw_ = iwt[sub % 2]
                pv = PB.get()
                for kc in range(KC):
                    S.mm(pv, hT_[:, kc, cs], win[:, kc, 1024:1536], start=(kc == 0), stop=(kc == KC - 1))
                p2 = PB.get()
                for kc in range(KC):
                    S.mm(p2[:, 0:128], hT_[:, kc, cs], win[:, kc, 2176:2304], start=(kc == 0), stop=(kc == KC - 1))
                for kc in range(KC):
                    S.mm(p2[:, 128:136], hT_[:, kc, cs], win[:, kc, 2880:2888], start=(kc == 0), stop=(kc == KC - 1))
                S.copy("act", tb[:, 0:512], pv)
                S.copy("act", tb[:, 512:640], p2[:, 0:128])
                S.ts("dve", iw_, p2[:, 128:136], 8.0 ** -0.5, ALU.mult)
                S.dma("sp", C.vtok.v((b, r0 // 128), C.vtok.ap[b, r0:r0 + 128, :]), tb, sres=tb.res)
                S.dma("sp", C.iwd.v((b, r0 // 128), C.iwd.ap[b, r0:r0 + 128, :]), iw_, sres=iw_.res)
            S.dma("sp", C.qkT.v((b, t0 // NT), C.qkT.ap[b, :, :, t0:t0 + NT].rearrange("c p t -> p c t")), o_, sres=o_.res)
            ist += 1
    S.end_phase()


def odd_attn_phase(S, C, j, src, dst):
    layer = 2 * j + 1
    lambda_init = 0.8 - 0.6 * math.exp(-0.3 * layer)
    L = C.L
    NKB = L // 128
    NQ = 512
    NQS = NQ // 128
    TOPK = min(256, L // 4)
    NIT = 18
    S.begin_phase()
    PB1 = Banks(S, 2)
    PB2 = Banks(S, 1)
    PB3 = Banks(S, 1)
    DACC = [S.ps("dacc%d" % i, [128, 512]) for i in range(2)]
    FACC = [S.ps("facc%d" % i, [128, 512]) for i in range(2)]
    wout = S.sb("wout", [128, KC, D], BF16)
    stg = [S.sb("stg%d" % i, [128, 1024]) for i in range(2)]
    load_weight_scaled(S, C, wout, C.od_wout[j], KC, D, None, stg, 1024, "wout")
    S.barrier()
    R = [V(stg[1].ap[:, 0:512], Res("R0")), V(stg[1].ap[:, 512:1024], Res("R1")), V(stg[0].ap[:, 512:1024], Res("R2")),
         V(stg[0].ap[:, 0:512], Res("R3"))]
    rb = S.sb("rb", [128, 8, 2, 128])
    t31 = S.sb("t31", [128, 8])
    cmT = S.sb("cmT", [128, 128])
    dmask = S.sb("dmask", [128, 128])
    zer = S.sb("zer", [128, 128])
    S.dma("sp", rb, V(C.rb_near, C.wres), sres=rb.res)
    S.dma("sp", t31, V(C.rb_t31.partition_broadcast(128), C.wres), sres=t31.res)
    S.memset("pool", zer, 0.0)
    S.asel(cmT, zer, [[1, 128]], ALU.is_ge, NEG, base=0, cm=-1)
    S.asel(dmask, zer, [[-1, 128]], ALU.is_ge, -1e30, base=0, cm=1)
    rbv = V(rb.ap.rearrange("p h r q -> p h (r q)"), rb.res)
    S.tt("dve", rbv, rbv, bc_last(t31, 256), ALU.subtract)
    S.tt("dve", rb[:, :, 1, :], rb[:, :, 1, :], bc_mid(cmT, 8), ALU.add)
    lf = S.sb("lf", [128, 256])
    lj = S.sb("lj", [128, 64])
    lam = {n: S.sb(n, [128, 1]) for n in ("s01", "s23", "nlam")}
    S.dma("sp", lf, V(C.od_lam[j].partition_broadcast(128), C.wres), sres=lf.res)
    S.memset("dve", lam["s01"], 0.0)
    S.memset("dve", lam["s23"], 0.0)
    S.stt("dve", lj, lf[:, 0:64], 1.0, lf[:, 64:128], ALU.mult, ALU.mult, accum=lam["s01"])
    S.stt("dve", lj, lf[:, 128:192], 1.0, lf[:, 192:256], ALU.mult, ALU.mult, accum=lam["s23"])
    S.act(lam["s01"], lam["s01"], AF.Exp)
    S.act(lam["s23"], lam["s23"], AF.Exp)
    S.tt("dve", lam["nlam"], lam["s23"], lam["s01"], ALU.subtract)
    S.ts("dve", lam["nlam"], lam["nlam"], -lambda_init, ALU.add)
    gsub_ = S.sb("gsubn", [128, 128])
    S.dma("sp", gsub_, V(C.od_gsub[j].partition_broadcast(128), C.wres), sres=gsub_.res)
    S.ts("pool", gsub_, gsub_, 1.0 - lambda_init, ALU.mult, 1.0, ALU.mult)

    dkT = S.sb("dkT", [128, 4, L], BF16)
    skT = S.sb("skT", [128, L], BF16)
    ikT = S.sb("ikT", [128, L], BF16)
    dvA = S.sb("dvA", [128, NKB, 4, 130], BF16)
    svA = S.sb("svA", [128, NKB, 130], BF16)
    S.memset("pool", dvA[:, :, :, 128:130], 1.0)
    S.memset("pool", svA[:, :, 128:130], 1.0)
    dqT = S.sb("dqT", [128, 4, NQ], BF16)
    sqT = S.sb("sqT", [128, 4, NQ], BF16)
    iqT = S.sb("iqT", [128, 4, NQ], BF16)
    iw = S.sb("iw", [128, NQS, 8])
    idx = S.sb("idx", [128, L])
    M = S.sb("M", [128, L], BF16)
    M2 = S.sb("M2", [128, L], BF16)
    MT = S.sb("MT", [128, NKB, 128], BF16)
    PTa = [S.sb("PTa%d" % i, [128, 512], BF16) for i in range(3)]
    PTb = [S.sb("PTb%d" % i, [128, 512], BF16) for i in range(2)]
    xs = [S.sb("x%d" % i, [128, D]) for i in range(1)]
    ytd = [[S.sb("ytd%d_%d" % (a, i), [128, 4, 128], BF16) for i in range(NQS)] for a in range(2)]
    ytf = [[S.sb("ytf%d_%d" % (a, i), [128, 4, 128], BF16) for i in range(NQS)] for a in range(2)]
    yT = S.sb("yT", [128, 8, 128], BF16)
    of0 = S.sb("of0", [128, NQS, 128])
    of = S.sb("of", [128, 128])
    osq = S.sb("osq", [128, 128])
    sc = {n: S.sb(n, [128, 1]) for n in ("rmax", "w0", "lo", "nlo", "nmid", "cnt", "gw", "rec")}
    sd = {n: S.sb(n, [128, 1]) for n in ("rec", "oss", "ors")}
    sc2 = {n: S.sb(n + "2", [128, 1]) for n in ("rec",)}
    ctr = {"R": 0, "Pa": 0, "Pb": 0, "T": 0, "x": 0}

    NQB = L // 128
    NST = (L + NQ - 1) // NQ
    Ms = [M, M2]
    prog = {"s1": 0, "s2": 0, "df": 0}

    def s1_lane(b):
        for qb in range(NQB):
            s0 = (qb // NQS) * NQ
            qs = qb % NQS
            nqs = min(NQS, (L - s0) // 128)
            nq = nqs * 128
            if qs == 0:
                S.dma("sp", iqT[:, :, 0:nq], C.qkT.v((b, "iq", s0), C.qkT.ap[b, 13:17, :, s0:s0 + nq].rearrange("c p t -> p c t")), sres=iqT.res)
                S.dma("sp", iw[:, 0:nqs, :], C.iwd.v((b, "iw", s0), C.iwd.ap[b, s0:s0 + nq, :].rearrange("(s p) e -> p s e", p=128)), sres=iw.res)
                yield
            nk = (qb + 1) * 128
            qc = slice(qs * 128, (qs + 1) * 128)
            if nk > TOPK:
                while prog["s2"] < qb - 1:
                    yield
                Mq = Ms[qb % 2]
                pend = []

                def fma(p):
                    r_, k0, wd, h = p
                    if h == 0:
                        S.ts("dve", idx[:, k0:k0 + wd], r_[:, 0:wd], iw[:, qs, 0:1], ALU.mult)
                    else:
                        S.stt("dve", idx[:, k0:k0 + wd], r_[:, 0:wd], iw[:, qs, h:h + 1], idx[:, k0:k0 + wd], ALU.mult, ALU.add)

                for k0 in range(0, nk, 512):
                    wd = min(512, nk - k0)
                    for h0 in range(0, 8, 2):
                        cur = []
                        pss_ = []
                        for h in (h0, h0 + 1):
                            pr = slice((h % 2) * 64, (h % 2) * 64 + 64)
                            ps = PB1.get()
                            S.mm(ps[:, 0:wd], iqT[pr, h // 2, qc], ikT[pr, k0:k0 + wd])
                            pss_.append(ps)
                        for ps, h in zip(pss_, (h0, h0 + 1)):
                            r_ = R[ctr["R"] % len(R)]
                            ctr["R"] += 1
                            S.act(r_[:, 0:wd], ps[:, 0:wd], AF.Relu)
                            cur.append((r_, k0, wd, h))
                        for p in pend:
                            fma(p)
                        pend = cur
                        yield
                for p in pend:
                    fma(p)
                S.reduce("dve", sc["rmax"], idx[:, 0:nk], ALU.max)
                S.reduce("dve", sc["lo"], idx[:, 0:nk], ALU.min)
                S.tt("dve", sc["w0"], sc["rmax"], sc["lo"], ALU.subtract)
                S.ts("dve", sc["nlo"], sc["lo"], -1.0, ALU.mult)
                S.tt("dve", idx[:, nk - 128:nk], idx[:, nk - 128:nk], dmask, ALU.add)
                yield
                thr = 2.0 * TOPK - nk - 0.5
                for it in range(NIT):
                    hw = 2.0 ** -(it + 1)
                    S.stt("dve", sc["nmid"], sc["w0"], -hw, sc["nlo"], ALU.mult, ALU.add)
                    S.act(Mq[:, 0:nk], idx[:, 0:nk], AF.Sign, bias=sc["nmid"], scale=1.0, accum=sc["cnt"])
                    yield
                    S.stt("dve", sc["gw"], sc["cnt"], thr, sc["w0"], ALU.is_ge, ALU.mult)
                    S.stt("dve", sc["nlo"], sc["gw"], -hw, sc["nlo"], ALU.mult, ALU.add)
                S.ts("dve", sc["lo"], sc["nlo"], -1.0, ALU.mult)
                S.ts("dve", Mq[:, 0:nk], idx[:, 0:nk], sc["lo"], ALU.is_ge)
            prog["s1"] = qb + 1
            yield

    def s2_lane(b):
        for qb in range(NQB):
            st_i = qb // NQS
            s0 = st_i * NQ
            qs = qb % NQS
            nqs = min(NQS, (L - s0) // 128)
            nq = nqs * 128
            if qs == 0:
                while prog["df"] < st_i - 1:
                    yield
                S.dma("sp", sqT[:, :, 0:nq], C.qkT.v((b, "sq", s0), C.qkT.ap[b, 8:12, :, s0:s0 + nq].rearrange("c p t -> p c t")), sres=sqT.res)
                yield
            while prog["s1"] < qb + 1:
                yield
            yb = ytd[st_i % 2]
            nk = (qb + 1) * 128
            qc = slice(qs * 128, (qs + 1) * 128)
            use_topk = nk > TOPK
            Mq = Ms[qb % 2]
            if use_topk:
                for g0 in range(0, qb + 1, 8):
                    ng = min(8, qb + 1 - g0)
                    pm = PB2.get()
                    for i in range(ng):
                        S.tr(bf(pm)[:, i * 128:(i + 1) * 128], Mq[:, (g0 + i) * 128:(g0 + i + 1) * 128], C.identb)
                    S.copy("act", MT[:, g0:g0 + ng, :], v3(bf(pm)[:, 0:ng * 128], ng))
                    yield
            for a in DACC:
                S.memset("dve", a[:, 0:130], 0.0)
                S.memset("dve", a[:, 256:386], 0.0)
            nkb_ = qb + 1

            def st1(kb):
                kc_ = slice(kb * 128, (kb + 1) * 128)
                ps = PB2.get()
                S.mm(ps, skT[:, kc_], sqT[:, :, qc])
                pt_ = PTa[kb % 3]
                rel = kb - (qb - 1)
                if rel >= 0:
                    S.tt("dve", v3(ps), v3(ps), rb[:, 4:8, rel, :], ALU.add)
                S.act(pt_, ps, AF.Exp)

            def st2(kb):
                if use_topk:
                    pt_ = PTa[kb % 3]
                    S.tt("dve", v3(pt_), v3(pt_), bc_mid(MT[:, kb, :], 4), ALU.mult)

            def st3(kb):
                pt_ = PTa[kb % 3]
                for h in range(4):
                    S.op("pe", lambda e, h=h, pt_=pt_, kb=kb, qb=qb: e.matmul(DACC[h // 2].ap[:, (h % 2) * 256:(h % 2) * 256 + 129], pt_.ap[:, h * 128:(h + 1) * 128],
                                                                             svA.ap[:, kb, 0:129], start=False, stop=(kb == qb), skip_group_check=True),
                         r=[pt_, svA], w=[DACC[h // 2]], inc=(h == 3))

            for t in range(nkb_ + 2):
                if 0 <= t - 2 < nkb_:
                    st3(t - 2)
                if 0 <= t - 1 < nkb_:
                    st2(t - 1)
                if t < nkb_:
                    st1(t)
                yield
            for h in range(4):
                a = DACC[h // 2]
                c0 = (h % 2) * 256
                S.op("dve", lambda e, a=a, c0=c0: e.reciprocal(sc2["rec"].ap, a.ap[:, c0 + 128:c0 + 129]), r=[a], w=[sc2["rec"]])
                S.ts("dve", yb[qs][:, h, :], a[:, c0:c0 + 128], sc2["rec"], ALU.mult)
            prog["s2"] = qb + 1
            yield

    def df_lane(b):
        for st_i in range(NST):
            s0 = st_i * NQ
            sblk = s0 // 128
            nqs = min(NQS, (L - s0) // 128)
            nq = nqs * 128
            yf = ytf[st_i % 2]
            yd = ytd[st_i % 2]
            S.dma("sp", dqT[:, :, 0:nq], C.qkT.v((b, "dq", s0), C.qkT.ap[b, 0:4, :, s0:s0 + nq].rearrange("c p t -> p c t")), sres=dqT.res)
            yield
            last_kb = sblk + nqs - 1
            for h in range(4):
                for m in range(2):
                    pr = slice(m * 64, m * 64 + 64)
                    for a in FACC:
                        S.memset("dve", a[:, 0:130], 0.0)
                        S.memset("dve", a[:, 256:386], 0.0)
                    def d1(kb):
                        kc_ = slice(kb * 128, (kb + 1) * 128)
                        qlo = max(0, kb - sblk)
                        ncol = (nqs - qlo) * 128
                        ps = PB3.get()
                        S.mm(ps[:, 0:ncol], dkT[pr, h, kc_], dqT[pr, h, qlo * 128:nqs * 128])
                        for qs in (kb - sblk, kb - sblk + 1):
                            if 0 <= qs < nqs:
                                rel = kb - (sblk + qs - 1)
                                cc = slice((qs - qlo) * 128, (qs - qlo + 1) * 128)
                                S.tt("dve", ps[:, cc], ps[:, cc], rb[:, h, rel, :], ALU.add)
                        pt_ = PTb[kb % 2]
                        S.act(pt_[:, 0:ncol], ps[:, 0:ncol], AF.Exp)

                    def d2(kb):
                        qlo = max(0, kb - sblk)
                        pt_ = PTb[kb % 2]
                        for qs in range(qlo, nqs):
                            cc = slice((qs - qlo) * 128, (qs - qlo + 1) * 128)
                            S.op("pe", lambda e, qs=qs, cc=cc, pt_=pt_, kb=kb, h=h, sblk=sblk: e.matmul(
                                FACC[qs // 2].ap[:, (qs % 2) * 256:(qs % 2) * 256 + 129], pt_.ap[:, cc], dvA.ap[:, kb, h, 0:129],
                                start=False, stop=(kb == sblk + qs), skip_group_check=True), r=[pt_, dvA], w=[FACC[qs // 2]],
                                inc=(qs == nqs - 1))

                    for t in range(last_kb + 2):
                        if 0 <= t - 1 <= last_kb:
                            d2(t - 1)
                        if t <= last_kb:
                            d1(t)
                        yield
                    for qs in range(nqs):
                        a = FACC[qs // 2]
                        c0 = (qs % 2) * 256
                        S.op("dve", lambda e, a=a, c0=c0: e.reciprocal(sd["rec"].ap, a.ap[:, c0 + 128:c0 + 129]), r=[a], w=[sd["rec"]])
                        if m == 0:
                            S.ts("dve", of0[:, qs, :], a[:, c0:c0 + 128], sd["rec"], ALU.mult)
                        else:
                            S.tt("dve", sd["rec"], sd["rec"], lam["nlam"], ALU.mult)
                            S.stt("dve", of, a[:, c0:c0 + 128], sd["rec"], of0[:, qs, :], ALU.mult, ALU.add)
                            S.act(osq, of, AF.Square, accum=sd["oss"])
                            S.act(sd["ors"], sd["oss"], AF.Sqrt, bias=C.eps, scale=1.0 / 128)
                            S.op("dve", lambda e: e.reciprocal(sd["ors"].ap, sd["ors"].ap), r=[sd["ors"]], w=[sd["ors"]])
                            S.stt("dve", yf[qs][:, h, :], of, sd["ors"], gsub_, ALU.mult, ALU.mult)
                    yield
            while prog["s2"] < min(NQB, (st_i + 1) * NQS):
                yield
            for qs in range(nqs):
                r0 = b * L + s0 + qs * 128
                xt = xs[0]
                load_rows(S, C, xt, src, (r0 // 128), src.ap[r0:r0 + 128, :])
                pt2 = PB3.get()
                for c in range(4):
                    S.tr(bf(pt2)[:, c * 128:(c + 1) * 128], yf[qs][:, c, :], C.identb)
                for c in range(4):
                    S.tr(bf(pt2)[:, (4 + c) * 128:(5 + c) * 128], yd[qs][:, c, :], C.identb)
                yield
                S.copy("act", flat(yT), bf(pt2))
                yield
                for nh in range(2):
                    pso = PB3.get()
                    for kc in range(KC):
                        S.mm(pso, yT[:, kc, :], wout[:, kc, nh * 512:(nh + 1) * 512], start=(kc == 0), stop=(kc == KC - 1))
                    yield
                    S.tt("dve", xt[:, nh * 512:(nh + 1) * 512], xt[:, nh * 512:(nh + 1) * 512], pso, ALU.add)
                S.dma("sp", dst.v((r0 // 128), dst.ap[r0:r0 + 128, :]), xt, sres=xt.res)
                yield
            prog["df"] = st_i + 1
            yield

    for b in range(C.NSEQ):
        S.dma("sp", dkT, C.qkT.v((b, "dk"), C.qkT.ap[b, 4:8, :, :].rearrange("c p t -> p c t")), sres=dkT.res)
        S.dma("sp", skT, C.qkT.v((b, "sk"), C.qkT.ap[b, 12, :, :]), sres=skT.res)
        S.dma("sp", ikT, C.qkT.v((b, "ik"), C.qkT.ap[b, 17, :, :]), sres=ikT.res)
        for kb in range(NKB):
            S.dma("sp", dvA[:, kb, :, 0:128], C.vtok.v((b, "dv", kb), C.vtok.ap[b, kb * 128:(kb + 1) * 128, 0:512].rearrange("p (h d) -> p h d", h=4)), sres=dvA.res)
        S.dma("sp", svA[:, :, 0:128], C.vtok.v((b, "sv"), C.vtok.ap[b, :, 512:640].rearrange("(k p) d -> p k d", p=128)), sres=svA.res)
        prog["s1"] = prog["s2"] = prog["df"] = 0
        run_lanes([s1_lane(b), s2_lane(b), df_lane(b)])
    S.end_phase()


W_SPECS = {
    "ffn_g": [4, 128, KC],
    "ffn_cw": [4, 128, 2 * FC, 3],
    "ffn_cb": [4, 128, 2 * FC],
    "ffn_wup": [4, D, 2 * DFF],
    "ffn_wdn": [4, DFF, D],
    "mix_g": [4, 128, KC],
    "ev_win": [2, D, EVEN_IN],
    "ev_wout": [2, D, D],
    "gm_wT": [2, 128, 4, 128],
    "gm_b": [2, 1, 512],
    "gdn_cw": [2, 128, 12, 4],
    "gdn_alog": [2, 1, 4],
    "gdn_dtb": [2, 1, 4],
    "gdn_ng": [2, 1, 128],
    "od_win": [2, D, OD_EXT],
    "od_wout": [2, D, D],
    "od_gn": [2, 128, 4],
    "od_lam": [2, 1, 256],
    "od_gsub": [2, 1, 128],
    "rb_near": [128, 8, 2, 128],
    "rb_t31": [1, 8],
}


def _rel_bucket_np(dist):
    n = np.maximum(dist, 0)
    nf = np.maximum(n, 16).astype(np.float32)
    far = 16 + (np.log(nf / np.float32(16)) / np.float32(math.log(128 / 16)) * np.float32(16)).astype(np.int32)
    return np.where(n < 16, n, np.minimum(far, 31))


def prep_weights(inp):
    f = lambda a: np.ascontiguousarray(np.asarray(a, dtype=np.float32))
    w = {}
    w["ffn_g"] = f(inp["ffn_norm_g"].reshape(4, KC, 128).transpose(0, 2, 1))
    w["ffn_cw"] = f(inp["ffn_conv_w"].reshape(4, 3, 2 * FC, 128).transpose(0, 3, 2, 1))
    w["ffn_cb"] = f(inp["ffn_conv_b"].reshape(4, 2 * FC, 128).transpose(0, 2, 1))
    w["ffn_wup"] = f(inp["ffn_w_up"])
    w["ffn_wdn"] = f(inp["ffn_w_down"])
    w["mix_g"] = f(inp["mix_norm_g"].reshape(4, KC, 128).transpose(0, 2, 1))
    w["ev_win"] = f(inp["ev_w_in"])
    w["ev_wout"] = f(inp["ev_w_out"])
    w["gm_wT"] = f(inp["gmlp_w_s"].transpose(0, 3, 1, 2))
    w["gm_b"] = f(inp["gmlp_b_s"].reshape(2, 1, 512))
    w["gdn_cw"] = f(inp["gdn_conv_w"].reshape(2, 4, 12, 128).transpose(0, 3, 2, 1))
    w["gdn_alog"] = f(inp["gdn_a_log"].reshape(2, 1, 4))
    w["gdn_dtb"] = f(inp["gdn_dt_bias"].reshape(2, 1, 4))
    w["gdn_ng"] = f(inp["gdn_norm_g"].reshape(2, 1, 128))
    ow = np.asarray(inp["od_w_in"], dtype=np.float32)
    w["od_win"] = f(np.concatenate([ow, ow[:, :, 2816:2880], ow[:, :, 2816:2880]], axis=2))
    w["od_wout"] = f(inp["od_w_out"])
    gq = np.asarray(inp["diff_q_norm_g"], dtype=np.float32)
    gk = np.asarray(inp["diff_k_norm_g"], dtype=np.float32)
    w["od_gn"] = f(np.stack([np.tile(gq, (1, 2)), np.tile(gk, (1, 2)), np.asarray(inp["dsa_q_norm_g"]), np.asarray(inp["dsa_k_norm_g"])], axis=2))
    w["od_lam"] = f(inp["diff_lambda"].reshape(2, 1, 256))
    w["od_gsub"] = f(inp["diff_sub_norm_g"].reshape(2, 1, 128))
    kk = np.arange(128)[:, None]
    qq = np.arange(128)[None, :]
    tab = np.asarray(inp["rel_bias"], dtype=np.float32)
    near = np.zeros((128, 8, 2, 128), np.float32)
    for rel in range(2):
        dist = qq - kk + (128 if rel == 0 else 0)
        near[:, :, rel, :] = tab[_rel_bucket_np(dist)].transpose(0, 2, 1)
    w["rb_near"] = f(near)
    w["rb_t31"] = f(tab[31:32, :])
    return w


def build_program(L, NSEQ, plan):
    nc = bass.Bass("TRN2", target_bir_lowering=False)
    NTOK = L * NSEQ
    C = Ctx()
    C.L, C.NSEQ, C.NTOK = L, NSEQ, NTOK
    x = nc.dram_tensor("x", [NTOK, D], F32, kind="ExternalInput").ap()
    y = nc.dram_tensor("y", [NTOK, D], F32, kind="ExternalOutput").ap()
    for name, shape in W_SPECS.items():
        setattr(C, name, nc.dram_tensor(name, shape, F32, kind="ExternalInput").ap())
    C.wres = Res("weights")
    xdt = DT(x, "x")
    ydt = DT(y, "y")
    C.qkT = DT(nc.dram_tensor("qkT", [NSEQ, NQK, 128, L], BF16, kind="Internal").ap(), "qkT")
    C.vtok = DT(nc.dram_tensor("vtok", [NSEQ, L, 640], BF16, kind="Internal").ap(), "vtok")
    C.iwd = DT(nc.dram_tensor("iwd", [NSEQ, L, 8], F32, kind="Internal").ap(), "iwd")
    with ExitStack() as es:
        S = Sched(nc, es)
        ct = es.enter_context(nc.sbuf_tensor("identb", [128, 128], BF16))
        C.identb = V(ct[:], Res("identb"))
        ct = es.enter_context(nc.sbuf_tensor("identf", [128, 128], F32))
        C.identf = V(ct[:], Res("identf"))
        ct = es.enter_context(nc.sbuf_tensor("eps", [128, 1], F32))
        C.eps = V(ct[:], Res("eps"))
        S.memset("pool", C.identf, 1.0)
        S.asel(C.identf, C.identf, [[-1, 128]], ALU.is_equal, 0.0, base=0, cm=1)
        S.copy("pool", C.identb, C.identf)
        S.memset("pool", C.eps, EPS)
        src = xdt
        for kind, idx in plan:
            if kind == "ffn":
                ffn_phase(S, C, idx, src, ydt)
            elif kind == "even":
                even_phase(S, C, idx, src, ydt)
            elif kind == "odd":
                odd_phase(S, C, idx, src, ydt)
            src = ydt
        S.barrier()
        print("program: ops=%d waits=%d dma_sems=%d" % (S.nops, S.nwaits, S.ndsem))
    return nc


FULL_PLAN = [("even", 0), ("ffn", 0), ("odd", 0), ("ffn", 1), ("even", 1), ("ffn", 2), ("odd", 1), ("ffn", 3)]


N_CORES = 8
_PROG = {}


def kernel(**inputs):
    x = np.asarray(inputs["x"], dtype=np.float32)
    B, L, Dm = x.shape
    nseq = B // N_CORES
    key = (L, nseq)
    if key not in _PROG:
        _PROG[key] = build_program(L, nseq, FULL_PLAN)
    nc = _PROG[key]
    w = prep_weights(inputs)
    in_maps = []
    for c in range(N_CORES):
        m = {"x": np.ascontiguousarray(x[c * nseq:(c + 1) * nseq].reshape(nseq * L, Dm))}
        m.update(w)
        in_maps.append(m)
    res = run_bass_kernel_spmd(nc, in_maps, core_ids=list(range(N_CORES)))
    out = np.concatenate([np.asarray(r["y"]).reshape(nseq, L, Dm) for r in res.results], axis=0)
    return out.astype(np.float32)
```
